# Optimizing a Trainium2 kernel written in Bass

```python
import math
import jax
import jax.numpy as jnp
from jax import lax
import numpy as np

D_MODEL = 1024
BATCH = 2
SEQ = 8192
DEPTH = 1

N_MEM = 256
HEAD_DIM = 64
NSA_HEADS = 8
NSA_KV_HEADS = 2
NSA_GROUP = NSA_HEADS // NSA_KV_HEADS
NSA_WIDTH = NSA_HEADS * HEAD_DIM
KV_WIDTH = NSA_KV_HEADS * HEAD_DIM
RWKV_HEADS = 8
RWKV_WIDTH = RWKV_HEADS * HEAD_DIM
MIX_WIDTH = NSA_WIDTH + RWKV_WIDTH
CMP_LEN = 32
CMP_STRIDE = 16
CMP_HIDDEN = 256
SEL_BLOCK = 64
SEL_TOP = 16
WINDOW = 512
Q_BLOCK = 128
DECAY_LORA = 64
AAA_LORA = 64
GATE_LORA = 128
N_BUCKETS = 32
MAX_DISTANCE = 2048
XATTN_HEADS = 4
XATTN_HEAD = D_MODEL // XATTN_HEADS
D_FF = -(-(8 * D_MODEL) // (3 * 256)) * 256
RMS_EPS = 1e-6
LNX_EPS = 64e-5
FORCE_SCORE = 1e4
NEG_SCORE = -1e9
NSA_SIZES = (NSA_WIDTH,) + (KV_WIDTH,) * 6 + (NSA_HEADS * 3,)
RWKV_SIZES = (RWKV_WIDTH,) * 3 + (DECAY_LORA, AAA_LORA, GATE_LORA)
NSA_COLS = sum(NSA_SIZES)
RWKV_COLS = sum(RWKV_SIZES)
IN_COLS = NSA_COLS + RWKV_COLS

kernel_name = 'hybrid_nsa_rwkv7_block'


def _split(z, sizes):
    return jnp.split(z, np.cumsum(sizes)[:-1].tolist(), axis=-1)


def rms_norm(x, g):
    x32 = x.astype(jnp.float32)
    y = x32 * lax.rsqrt(jnp.mean(x32 * x32, axis=-1, keepdims=True) + RMS_EPS)
    return (y * g.astype(jnp.float32)).astype(x.dtype)


def masked_softmax(s, mask):
    s = jnp.where(mask, s.astype(jnp.float32), -1e30)
    m = jnp.max(s, axis=-1, keepdims=True)
    e = jnp.where(mask, jnp.exp(s - m), 0.0)
    return e / jnp.maximum(jnp.sum(e, axis=-1, keepdims=True), 1e-30)


def t5_bucket(dist):
    n = jnp.maximum(dist, 0)
    max_exact = N_BUCKETS // 2
    nf = jnp.maximum(n, 1).astype(jnp.float32)
    large = max_exact + (jnp.log(nf / max_exact) / math.log(MAX_DISTANCE / max_exact)
                         * (N_BUCKETS - max_exact)).astype(jnp.int32)
    return jnp.where(n < max_exact, n, jnp.minimum(large, N_BUCKETS - 1))


def nsa_compress(kv, pe, w1, b1, w2):
    B, T = kv.shape[:2]
    n_cmp = (T - CMP_LEN) // CMP_STRIDE + 1
    idx = np.arange(n_cmp)[:, None] * CMP_STRIDE + np.arange(CMP_LEN)[None, :]
    blk = kv[:, idx] + pe[None, None, :, None, :]
    blk = jnp.transpose(blk, (0, 3, 1, 2, 4)).reshape(B, NSA_KV_HEADS, n_cmp, CMP_LEN * HEAD_DIM)
    return jax.nn.gelu(blk @ w1 + b1) @ w2


def nsa_attention(q, k_cmp, v_cmp, k_sel, v_sel, k_win, v_win, gate_logits, rel_bias,
                  pe_k, pe_v, ck_w1, ck_b1, ck_w2, cv_w1, cv_b1, cv_w2):
    B, T = q.shape[:2]
    Hkv, G, dh = NSA_KV_HEADS, NSA_GROUP, HEAD_DIM
    qh = (q.reshape(B, T, Hkv, G, dh) * (dh ** -0.5)).transpose(0, 2, 3, 1, 4)
    kvh = lambda a: a.reshape(B, T, Hkv, dh)
    Kc = nsa_compress(kvh(k_cmp), pe_k, ck_w1, ck_b1, ck_w2)
    Vc = nsa_compress(kvh(v_cmp), pe_v, cv_w1, cv_b1, cv_w2)
    n_cmp = Kc.shape[2]
    cmp_start = np.arange(n_cmp) * CMP_STRIDE
    cmp_end = cmp_start + CMP_LEN - 1
    n_sel = T // SEL_BLOCK
    n_top = min(SEL_TOP, n_sel)
    sel_start = np.arange(n_sel) * SEL_BLOCK
    overlap = jnp.asarray(((cmp_start[:, None] <= sel_start[None, :] + SEL_BLOCK - 1)
                           & (cmp_end[:, None] >= sel_start[None, :])).astype(np.float32))
    Ks = kvh(k_sel).reshape(B, n_sel, SEL_BLOCK, Hkv, dh).transpose(0, 3, 1, 2, 4)
    Vs = kvh(v_sel).reshape(B, n_sel, SEL_BLOCK, Hkv, dh).transpose(0, 3, 1, 2, 4)
    pad = ((0, 0), (0, 0), (WINDOW, 0), (0, 0))
    Kw = jnp.pad(kvh(k_win).transpose(0, 2, 1, 3), pad)
    Vw = jnp.pad(kvh(v_win).transpose(0, 2, 1, 3), pad)
    gates = jax.nn.sigmoid(gate_logits).reshape(B, T, Hkv, G, 3).transpose(0, 2, 3, 1, 4)
    table = rel_bias.T.reshape(Hkv, G, N_BUCKETS)
    bi = jnp.arange(B)[:, None, None, None]
    hi = jnp.arange(Hkv)[None, :, None, None]
    h6 = jnp.arange(Hkv)[None, :, None, None, None, None]
    g6 = jnp.arange(G)[None, None, :, None, None, None]
    blk_id = jnp.arange(n_sel)

    def query_block(qb):
        q0 = qb * Q_BLOCK
        qblk = lax.dynamic_slice_in_dim(qh, q0, Q_BLOCK, axis=3)
        t = q0 + jnp.arange(Q_BLOCK)
        dist_c = t[:, None] - cmp_end[None, :]
        s = jnp.einsum('bhgqd,bhcd->bhgqc', qblk, Kc).astype(jnp.float32) + table[:, :, t5_bucket(dist_c)]
        p_c = masked_softmax(s, dist_c >= 0)
        o_c = jnp.einsum('bhgqc,bhcd->bhgqd', p_c.astype(Vc.dtype), Vc)
        imp = jnp.einsum('bhgqc,cj->bhqj', p_c, overlap)
        cur = t // SEL_BLOCK
        forced = (blk_id[None] == 0) | (blk_id[None] == cur[:, None]) | (blk_id[None] == cur[:, None] - 1)
        valid = blk_id[None] * SEL_BLOCK <= t[:, None]
        score = jnp.where(valid, jnp.where(forced, FORCE_SCORE, imp), NEG_SCORE)
        _, idx = lax.top_k(score, n_top)
        kg = Ks[bi, hi, idx]
        vg = Vs[bi, hi, idx]
        kpos = idx[..., None] * SEL_BLOCK + jnp.arange(SEL_BLOCK)
        dist_s = t[None, None, :, None, None] - kpos
        bias_s = table[h6, g6, t5_bucket(dist_s)[:, :, None]]
        s = jnp.einsum('bhgqd,bhqnkd->bhgqnk', qblk, kg).astype(jnp.float32) + bias_s
        s = s.reshape(B, Hkv, G, Q_BLOCK, -1)
        mask_s = (dist_s >= 0)[:, :, None].reshape(B, Hkv, 1, Q_BLOCK, -1)
        p_s = masked_softmax(s, mask_s)
        o_s = jnp.einsum('bhgqm,bhqmd->bhgqd', p_s.astype(vg.dtype), vg.reshape(B, Hkv, Q_BLOCK, -1, dh))
        kw = lax.dynamic_slice_in_dim(Kw, q0, Q_BLOCK + WINDOW, axis=2)
        vw = lax.dynamic_slice_in_dim(Vw, q0, Q_BLOCK + WINDOW, axis=2)
        spos = q0 - WINDOW + jnp.arange(Q_BLOCK + WINDOW)
        dist_w = t[:, None] - spos[None, :]
        mask_w = (dist_w >= 0) & (dist_w < WINDOW) & (spos[None, :] >= 0)
        s = jnp.einsum('bhgqd,bhkd->bhgqk', qblk, kw).astype(jnp.float32) + table[:, :, t5_bucket(dist_w)]
        p_w = masked_softmax(s, mask_w)
        o_w = jnp.einsum('bhgqk,bhkd->bhgqd', p_w.astype(vw.dtype), vw)
        g = lax.dynamic_slice_in_dim(gates, q0, Q_BLOCK, axis=3)
        return g[..., 0:1] * o_c + g[..., 1:2] * o_s + g[..., 2:3] * o_w

    out = lax.map(query_block, jnp.arange(T // Q_BLOCK))
    return out.transpose(1, 0, 4, 2, 3, 5).reshape(B, T, NSA_WIDTH)


def rwkv7_time_mix(r, k, v, wd, ad, gd, w0, w_up, a0, a_up, g_up, k_k, k_a, r_k, lnx_w, lnx_b):
    B, T, _ = r.shape
    f32 = jnp.float32
    heads = lambda z: z.reshape(B, T, RWKV_HEADS, HEAD_DIM)
    w_log = -jax.nn.softplus(-(w0 + jnp.tanh(wd) @ w_up)) - 0.5
    decay = jnp.exp(-jnp.exp(w_log.astype(f32)))
    a = jax.nn.sigmoid(a0 + ad @ a_up)
    g = jax.nn.sigmoid(gd) @ g_up
    kk = heads(k * k_k).astype(f32)
    kk = kk * lax.rsqrt(jnp.maximum(jnp.sum(kk * kk, axis=-1, keepdims=True), 1e-24))
    k = k * (1 + (a - 1) * k_a)
    tm = lambda z: jnp.swapaxes(heads(z).astype(f32), 0, 1)
    xs = (tm(r), tm(decay), tm(k), tm(v), jnp.swapaxes(-kk, 0, 1),
          jnp.swapaxes(kk * heads(a).astype(f32), 0, 1))

    def step(S, inp):
        r_t, w_t, k_t, v_t, a_t, b_t = inp
        sa = jnp.einsum('bhvk,bhk->bhv', S, a_t)
        S = S * w_t[:, :, None, :] + sa[..., None] * b_t[:, :, None, :] + v_t[..., None] * k_t[:, :, None, :]
        return S, jnp.einsum('bhvk,bhk->bhv', S, r_t)

    S0 = jnp.zeros((B, RWKV_HEADS, HEAD_DIM, HEAD_DIM), f32)
    _, y = lax.scan(step, S0, xs)
    y = jnp.swapaxes(y, 0, 1)
    mean = jnp.mean(y, axis=-1, keepdims=True)
    var = jnp.mean(jnp.square(y - mean), axis=-1, keepdims=True)
    y = ((y - mean) * lax.rsqrt(var + LNX_EPS)).reshape(B, T, RWKV_WIDTH) * lnx_w + lnx_b
    bonus = jnp.sum(heads(r) * heads(k) * r_k, axis=-1, keepdims=True) * heads(v)
    y = (y + bonus.reshape(B, T, RWKV_WIDTH).astype(f32)) * g.astype(f32)
    return y.astype(r.dtype)


def memory_cross_attention(h, mem, w_q, w_kv, w_o):
    B, T, _ = h.shape
    M = mem.shape[1]
    q = (h @ w_q).reshape(B, T, XATTN_HEADS, XATTN_HEAD)
    kv = mem @ w_kv
    k = kv[..., :D_MODEL].reshape(B, M, XATTN_HEADS, XATTN_HEAD)
    v = kv[..., D_MODEL:].reshape(B, M, XATTN_HEADS, XATTN_HEAD)
    s = jnp.einsum('bthd,bmhd->bhtm', q, k).astype(jnp.float32) * (XATTN_HEAD ** -0.5)
    p = jax.nn.softmax(s, axis=-1)
    o = jnp.einsum('bhtm,bmhd->bthd', p.astype(v.dtype), v).reshape(B, T, D_MODEL)
    return o @ w_o


def hybrid_layer(x, mem, rel_bias, norm_mix_g, w_in, nsa_gate_b, cmp_pe_k, cmp_pe_v,
                 cmp_k_w1, cmp_k_b1, cmp_k_w2, cmp_v_w1, cmp_v_b1, cmp_v_w2,
                 rwkv_mu, rwkv_w0, rwkv_w_up, rwkv_a0, rwkv_a_up, rwkv_g_up,
                 rwkv_k_k, rwkv_k_a, rwkv_r_k, rwkv_lnx_w, rwkv_lnx_b, w_out,
                 norm_x_g, norm_mem_g, w_q_x, w_kv_x, w_o_x,
                 norm_ffn_g, w_gate, w_up, w_down):
    proj = rms_norm(x, norm_mix_g) @ w_in
    nsa_part = proj[..., :NSA_COLS]
    rw = proj[..., NSA_COLS:]
    rw_prev = jnp.pad(rw[:, :-1], ((0, 0), (1, 0), (0, 0)))
    rw = rw + (rw_prev - rw) * rwkv_mu
    q, kc, vc, ks, vs, kw, vw, gl = _split(nsa_part, NSA_SIZES)
    r, k, v, wd, ad, gd = _split(rw, RWKV_SIZES)
    o_nsa = nsa_attention(q, kc, vc, ks, vs, kw, vw, gl + nsa_gate_b, rel_bias,
                          cmp_pe_k, cmp_pe_v, cmp_k_w1, cmp_k_b1, cmp_k_w2, cmp_v_w1, cmp_v_b1, cmp_v_w2)
    o_rwkv = rwkv7_time_mix(r, k, v, wd, ad, gd, rwkv_w0, rwkv_w_up, rwkv_a0, rwkv_a_up, rwkv_g_up,
                            rwkv_k_k, rwkv_k_a, rwkv_r_k, rwkv_lnx_w, rwkv_lnx_b)
    x = x + jnp.concatenate([o_nsa, o_rwkv], axis=-1) @ w_out
    x = x + memory_cross_attention(rms_norm(x, norm_x_g), rms_norm(mem, norm_mem_g), w_q_x, w_kv_x, w_o_x)
    h = rms_norm(x, norm_ffn_g)
    return x + (jax.nn.silu(h @ w_gate) * (h @ w_up)) @ w_down


def setup_inputs(seed: int = 0) -> dict:
    key = jax.random.key(seed)
    keys = iter(jax.random.split(key, 48))
    f32 = jnp.float32

    def nrm(shape, scale, stacked=True):
        shp = ((DEPTH,) + shape) if stacked else shape
        return scale * jax.random.normal(next(keys), shp, f32)

    def gain(n):
        return 1.0 + nrm((n,), 0.01)

    flat = CMP_LEN * HEAD_DIM
    return {
        'x': nrm((BATCH, SEQ, D_MODEL), 1.0, stacked=False),
        'mem': nrm((BATCH, N_MEM, D_MODEL), 1.0, stacked=False),
        'rel_bias': nrm((N_BUCKETS, NSA_HEADS), 0.2, stacked=False),
        'norm_f_g': 1.0 + nrm((D_MODEL,), 0.01, stacked=False),
        'norm_mix_g': gain(D_MODEL),
        'w_in': nrm((D_MODEL, IN_COLS), D_MODEL ** -0.5),
        'nsa_gate_b': nrm((NSA_HEADS * 3,), 0.01),
        'cmp_pe_k': nrm((CMP_LEN, HEAD_DIM), 0.02),
        'cmp_pe_v': nrm((CMP_LEN, HEAD_DIM), 0.02),
        'cmp_k_w1': nrm((flat, CMP_HIDDEN), flat ** -0.5),
        'cmp_k_b1': nrm((CMP_HIDDEN,), 0.01),
        'cmp_k_w2': nrm((CMP_HIDDEN, HEAD_DIM), CMP_HIDDEN ** -0.5),
        'cmp_v_w1': nrm((flat, CMP_HIDDEN), flat ** -0.5),
        'cmp_v_b1': nrm((CMP_HIDDEN,), 0.01),
        'cmp_v_w2': nrm((CMP_HIDDEN, HEAD_DIM), CMP_HIDDEN ** -0.5),
        'rwkv_mu': jax.random.uniform(next(keys), (DEPTH, RWKV_COLS), f32, 0.0, 1.0),
        'rwkv_w0': jax.random.uniform(next(keys), (DEPTH, RWKV_WIDTH), f32, -6.0, 0.0),
        'rwkv_w_up': nrm((DECAY_LORA, RWKV_WIDTH), 0.5 * DECAY_LORA ** -0.5),
        'rwkv_a0': nrm((RWKV_WIDTH,), 0.5),
        'rwkv_a_up': nrm((AAA_LORA, RWKV_WIDTH), 0.5 * AAA_LORA ** -0.5),
        'rwkv_g_up': nrm((GATE_LORA, RWKV_WIDTH), GATE_LORA ** -0.5),
        'rwkv_k_k': 0.85 + nrm((RWKV_WIDTH,), 0.05),
        'rwkv_k_a': 1.0 + nrm((RWKV_WIDTH,), 0.05),
        'rwkv_r_k': nrm((RWKV_HEADS, HEAD_DIM), 0.1),
        'rwkv_lnx_w': gain(RWKV_WIDTH),
        'rwkv_lnx_b': nrm((RWKV_WIDTH,), 0.01),
        'w_out': nrm((MIX_WIDTH, D_MODEL), MIX_WIDTH ** -0.5),
        'norm_x_g': gain(D_MODEL),
        'norm_mem_g': gain(D_MODEL),
        'w_q_x': nrm((D_MODEL, D_MODEL), D_MODEL ** -0.5),
        'w_kv_x': nrm((D_MODEL, 2 * D_MODEL), D_MODEL ** -0.5),
        'w_o_x': nrm((D_MODEL, D_MODEL), D_MODEL ** -0.5),
        'norm_ffn_g': gain(D_MODEL),
        'w_gate': nrm((D_MODEL, D_FF), D_MODEL ** -0.5),
        'w_up': nrm((D_MODEL, D_FF), D_MODEL ** -0.5),
        'w_down': nrm((D_FF, D_MODEL), D_FF ** -0.5),
    }


def reference(x, mem, rel_bias, norm_f_g, norm_mix_g, w_in, nsa_gate_b, cmp_pe_k, cmp_pe_v,
              cmp_k_w1, cmp_k_b1, cmp_k_w2, cmp_v_w1, cmp_v_b1, cmp_v_w2,
              rwkv_mu, rwkv_w0, rwkv_w_up, rwkv_a0, rwkv_a_up, rwkv_g_up,
              rwkv_k_k, rwkv_k_a, rwkv_r_k, rwkv_lnx_w, rwkv_lnx_b, w_out,
              norm_x_g, norm_mem_g, w_q_x, w_kv_x, w_o_x,
              norm_ffn_g, w_gate, w_up, w_down):
    for l in range(DEPTH):
        x = hybrid_layer(x, mem, rel_bias, norm_mix_g[l], w_in[l], nsa_gate_b[l], cmp_pe_k[l], cmp_pe_v[l],
                         cmp_k_w1[l], cmp_k_b1[l], cmp_k_w2[l], cmp_v_w1[l], cmp_v_b1[l], cmp_v_w2[l],
                         rwkv_mu[l], rwkv_w0[l], rwkv_w_up[l], rwkv_a0[l], rwkv_a_up[l], rwkv_g_up[l],
                         rwkv_k_k[l], rwkv_k_a[l], rwkv_r_k[l], rwkv_lnx_w[l], rwkv_lnx_b[l], w_out[l],
                         norm_x_g[l], norm_mem_g[l], w_q_x[l], w_kv_x[l], w_o_x[l],
                         norm_ffn_g[l], w_gate[l], w_up[l], w_down[l])
    return rms_norm(x, norm_f_g)
```

```python
import math
import numpy as np
from contextlib import ExitStack
import concourse.bass as bass
import concourse.mybir as mybir
from concourse.bass_utils import run_bass_kernel_spmd

F32 = mybir.dt.float32
BF16 = mybir.dt.bfloat16
AF = mybir.ActivationFunctionType
ALU = mybir.AluOpType
AX = mybir.AxisListType

D = 1024
DFF = 2816
NOWN = 16
RMS_EPS = 1e-6


class Sync:
    EPOCH = 24000
    NDMA = 12

    def __init__(self, nc):
        self.nc = nc
        self.eng = {"pe": nc.tensor, "act": nc.scalar, "dve": nc.vector, "pool": nc.gpsimd, "sp": nc.sync}
        self.cnt = {e: 0 for e in self.eng}
        self.sems = {}
        self.known = {e: {} for e in self.eng}
        self.vc = {}
        self.dma_sems = {}
        self.dma_val = {}
        self.dma_rr = {}
        self.state = {}

    def _sem(self, e, ep):
        k = (e, ep)
        if k not in self.sems:
            self.sems[k] = self.nc.alloc_semaphore("s_%s_%d" % (e, ep))
        return self.sems[k]

    def _wait(self, e, tok):
        kn = self.known[e]
        if tok[0] == "c":
            _, f, n = tok
            if f == "pe" and e == "pe":
                return
            if kn.get(f, 0) >= n:
                return
            ep = (n - 1) // self.EPOCH
            self.eng[e].wait_ge(self._sem(f, ep), n - ep * self.EPOCH)
            kn[f] = n
        else:
            _, q, slot, v = tok
            key = ("d", q, slot)
            if kn.get(key, 0) >= v:
                return
            self.eng[e].wait_ge(self.dma_sems[(q, slot)], v)
            kn[key] = v
        snap = self.vc.get(tok)
        if snap:
            for k2, v2 in snap.items():
                if kn.get(k2, 0) < v2:
                    kn[k2] = v2

    @staticmethod
    def _rkey(tok):
        return tok[1] if tok[0] == "c" else (tok[1], tok[2])

    def _deps(self, reads, writes):
        deps = []
        for (name, sub) in reads:
            st = self.state.get(name)
            if st:
                for s2, ent in st.items():
                    if (sub is None or s2 is None or s2 == sub) and ent[0] is not None:
                        deps.append(ent[0])
        for (name, sub) in writes:
            st = self.state.get(name)
            if st:
                for s2, ent in st.items():
                    if sub is None or s2 is None or s2 == sub:
                        if ent[0] is not None:
                            deps.append(ent[0])
                        deps.extend(ent[1].values())
        return deps

    def _update(self, tok, reads, writes):
        rk = self._rkey(tok)
        for (name, sub) in reads:
            st = self.state.setdefault(name, {})
            ent = st.get(sub)
            if ent is None:
                ent = st[sub] = [None, {}]
            ent[1][rk] = tok
        for (name, sub) in writes:
            st = self.state.setdefault(name, {})
            if sub is None:
                st.clear()
            st[sub] = [tok, {}]

    PSUM = ("pA", "pB", "pC", "pD", "pE", "pF", "pG", "pT", "pJ0", "pJ1", "pM0", "pM1", "pS0", "pS1", "pS2")

    def op(self, e, fn, reads=(), writes=()):
        if e != "pe":
            writes = list(writes) + [(r[0], None) for r in reads if r[0] in self.PSUM]
        for d in self._deps(reads, writes):
            self._wait(e, d)
        ins = fn()
        self.cnt[e] += 1
        n = self.cnt[e]
        ep = (n - 1) // self.EPOCH
        ins.then_inc(self._sem(e, ep), 1)
        tok = ("c", e, n)
        self.vc[tok] = dict(self.known[e])
        self._update(tok, reads, writes)
        return tok

    def dma(self, q, out, in_, reads=(), writes=()):
        slot = self.dma_rr.get(q, 0) % self.NDMA
        self.dma_rr[q] = self.dma_rr.get(q, 0) + 1
        k = (q, slot)
        if k not in self.dma_sems:
            self.dma_sems[k] = self.nc.alloc_semaphore("d_%s_%d" % (q, slot))
            self.dma_val[k] = 0
        if self.dma_val[k] > 0:
            self._wait(q, ("d", q, slot, self.dma_val[k]))
        for d in self._deps(reads, writes):
            self._wait(q, d)
        ins = self.eng[q].dma_start(out=out, in_=in_)
        self.dma_val[k] += 16
        ins.then_inc(self.dma_sems[k], 16)
        tok = ("d", q, slot, self.dma_val[k])
        self.vc[tok] = dict(self.known[q])
        self._update(tok, reads, writes)
        return tok

    def fence(self):
        for e in self.eng:
            for f in self.eng:
                if self.cnt[f] > 0:
                    self._wait(e, ("c", f, self.cnt[f])) if not (e == "pe" and f == "pe") else None
            for (q, slot), v in list(self.dma_val.items()):
                if v > 0:
                    self._wait(e, ("d", q, slot, v))

    def finish(self):
        for (q, slot), v in list(self.dma_val.items()):
            if v > 0:
                self._wait(q, ("d", q, slot, v))


class K:
    def __init__(self, nc):
        self.nc = nc
        self.S = Sync(nc)
        self.rr = 0

    def mm(self, out, lhsT, rhs, start, stop, r, w):
        nc = self.nc
        return self.S.op("pe", lambda: nc.tensor.matmul(out, lhsT=lhsT, rhs=rhs, start=start, stop=stop), r, w)

    def tr(self, out, in_, ident, r, w):
        nc = self.nc
        return self.S.op("pe", lambda: nc.tensor.transpose(out, in_, ident), r + [("ident", None)], w)

    def act(self, out, in_, func, r, w, bias=None, scale=None):
        nc = self.nc
        kw = {}
        if bias is not None:
            kw["bias"] = bias
        if scale is not None:
            kw["scale"] = scale
        return self.S.op("act", lambda: nc.scalar.activation(out=out, in_=in_, func=func, **kw), r, w)

    def ve(self, e, name, r, w, *a, **kw):
        eng = self.S.eng[e]
        return self.S.op(e, lambda: getattr(eng, name)(*a, **kw), r, w)

    def alt(self):
        self.rr += 1
        return "dve" if self.rr % 2 else "pool"


def build_program(dbg=None):
    dbg = dbg or {}
    nc = bass.Bass("TRN2", target_bir_lowering=False)
    k = K(nc)
    S = k.S

    def din(name, shape, dt=F32):
        return nc.dram_tensor(name, list(shape), dt, kind="ExternalInput").ap()

    xs = din("xs", [8192, D])
    ident_d = din("ident", [128, 128])
    g_f = din("g_f", [1, D])
    gl = {n: din(n, [128, 8]) for n in ["g_x", "g_ffn", "g_mem"]}
    mem = din("mem", [256, D])
    wl = {
        "w_out": din("w_out", [128, 8, D]),
        "w_q": din("w_q", [128, 8, D]),
        "w_kv": din("w_kv", [128, 8, 2 * D]),
        "w_o": din("w_o", [128, 8, D]),
        "w_gate": din("w_gate", [128, 8, DFF]),
        "w_up": din("w_up", [128, 8, DFF]),
        "w_down": din("w_down", [128, 22, D]),
    }
    y = nc.dram_tensor("y", [NOWN * 128, D], F32, kind="ExternalOutput").ap()
    if "omix" in dbg:
        omix_d = din("omix", [NOWN * 128, D])
        dx1 = nc.dram_tensor("dx1", [NOWN * 128, D], F32, kind="ExternalOutput").ap()
        dx2 = nc.dram_tensor("dx2", [NOWN * 128, D], F32, kind="ExternalOutput").ap()
        dq = nc.dram_tensor("dq", [128, 1024], BF16, kind="ExternalOutput").ap()
        dk = nc.dram_tensor("dk", [128, 2048], BF16, kind="ExternalOutput").ap()
        dP = nc.dram_tensor("dP", [128, 256], BF16, kind="ExternalOutput").ap()
        dr = nc.dram_tensor("dr", [128, 128], F32, kind="ExternalOutput").ap()
        dh = nc.dram_tensor("dh", [128, 1024], BF16, kind="ExternalOutput").ap()
        dV = nc.dram_tensor("dV", [128, 2048], BF16, kind="ExternalOutput").ap()
        do = nc.dram_tensor("do", [128, 1024], BF16, kind="ExternalOutput").ap()

    ident = nc.alloc_sbuf_tensor("ident_b", [128, 128], BF16)
    identf = nc.alloc_sbuf_tensor("ident_f", [128, 128], F32)
    S.dma("sp", identf[:], ident_d[:, :], [], [("identf", None)])
    k.ve("dve", "tensor_copy", [("identf", None)], [("ident", None)], out=ident[:], in_=identf[:])
    ones_b = nc.alloc_sbuf_tensor("ones_b", [128, 128], BF16)
    k.ve("dve", "memset", [], [("ones_b", None)], ones_b[:], 1.0)
    epsc = nc.alloc_sbuf_tensor("epsc", [128, 1], F32)
    k.ve("dve", "memset", [], [("epsc", None)], epsc[:], RMS_EPS)
    ones_f = nc.alloc_sbuf_tensor("ones_f", [1, 128], F32)
    k.ve("dve", "memset", [], [("ones_f", None)], ones_f[:], 1.0)

    pT = nc.alloc_psum_tensor("pT", [128, 1024], BF16)
    pA = nc.alloc_psum_tensor("pA", [128, 512], F32)
    pB = nc.alloc_psum_tensor("pB", [128, 512], F32)
    pC = nc.alloc_psum_tensor("pC", [128, 512], F32)
    pD = nc.alloc_psum_tensor("pD", [128, 512], F32)
    pE = nc.alloc_psum_tensor("pE", [128, 512], F32)
    pF = nc.alloc_psum_tensor("pF", [128, 512], F32)
    pG = nc.alloc_psum_tensor("pG", [128, 512], F32)

    stg = [None, None]

    def alloc_stg(es, tag, side=None):
        for i in range(2):
            kw = {"side": side} if side else {}
            stg[i] = es.enter_context(nc.sbuf_tensor("stg%d%s" % (i, tag), [128, 1024], F32, **kw))
    stg_i = [0]

    def load_w(dst, dname, src, KC, ncols, gcol=None, gname=None, c_lo=0, kc_lo=0):
        for kc in range(KC):
            for c0 in range(0, ncols, 1024):
                c1 = min(ncols, c0 + 1024)
                i = stg_i[0] % 2
                stg_i[0] += 1
                q = "sp" if i == 0 else "pool"
                S.dma(q, stg[i][:, 0:c1 - c0], src[:, kc_lo + kc, c_lo + c0:c_lo + c1], [], [("stg%d" % i, None)])
                e = "dve" if i == 0 else "act"
                if gcol is not None:
                    if e == "dve":
                        k.ve("dve", "tensor_scalar", [("stg%d" % i, None), (gname, None)], [(dname, kc)],
                             out=dst[:, kc, c0:c1], in0=stg[i][:, 0:c1 - c0], scalar1=gcol[:, kc_lo + kc:kc_lo + kc + 1],
                             scalar2=None, op0=ALU.mult)
                    else:
                        k.act(dst[:, kc, c0:c1], stg[i][:, 0:c1 - c0], AF.Copy, [("stg%d" % i, None), (gname, None)],
                              [(dname, kc)], scale=gcol[:, kc_lo + kc:kc_lo + kc + 1])
                else:
                    if e == "dve":
                        k.ve("dve", "tensor_copy", [("stg%d" % i, None)], [(dname, kc)], out=dst[:, kc, c0:c1],
                             in_=stg[i][:, 0:c1 - c0])
                    else:
                        k.act(dst[:, kc, c0:c1], stg[i][:, 0:c1 - c0], AF.Copy, [("stg%d" % i, None)], [(dname, kc)])

    def rmsnorm(src, sname, dst, dname, sq, ss, tag):
        k.act(sq[:], src, AF.Square, [sname], [("sq" + tag, None)])
        k.ve("dve", "reduce_sum", [("sq" + tag, None)], [("ss" + tag, None)], out=ss[:, 0:1], in_=sq[:], axis=AX.X)
        k.act(ss[:, 1:2], ss[:, 0:1], AF.Ln, [("ss" + tag, None)], [("ss" + tag, None)], bias=epsc[:, 0:1], scale=1.0 / D)
        k.act(ss[:, 1:2], ss[:, 1:2], AF.Exp, [("ss" + tag, None)], [("ss" + tag, None)], scale=-0.5)
        k.ve("dve", "tensor_scalar", [sname, ("ss" + tag, None)], [dname], out=dst, in0=src, scalar1=ss[:, 1:2],
             scalar2=None, op0=ALU.mult)

    def transpose8(src, sname, dstT, dname, nchunk=8):
        for c in range(nchunk):
            k.tr(pT[:, c * 128:(c + 1) * 128], src[:, c * 128:(c + 1) * 128], ident[:], [sname], [("pT", c)])
        k.ve("dve", "tensor_copy", [("pT", None)], [dname], out=dstT, in_=pT[:, 0:nchunk * 128])

    pJ0, pJ1, pM0, pM1, pS0, pS1, pS2 = pA, pB, pC, pD, pE, pF, pG
    epsl = nc.alloc_sbuf_tensor("epsl", [128, 1], F32)
    k.ve("dve", "memset", [], [("epsl", None)], epsl[:], 64e-5)
    sq2 = [nc.alloc_sbuf_tensor("sq0", [128, D], BF16), None]
    ss2 = [nc.alloc_sbuf_tensor("ss0", [128, 2], F32), None]
    hb2 = [nc.alloc_sbuf_tensor("hb0", [128, D], BF16), None]
    sq, ss, hb = sq2[0], ss2[0], hb2[0]
    eo = ExitStack()
    oT_r = eo.enter_context(nc.sbuf_tensor("oT_r", [128, NOWN, 4, 128], BF16, side="right"))
    LNX_EPS = 64e-5
    if "rwkv" in dbg or "nsa" in dbg:
        gmix_d = din("g_mix", [128, 8])
    if "rwkv" in dbg:
        w_in_r = din("w_in_r", [128, 8, 1792])
        mu_d = din("mu", [128, 14])
        rv_d = din("rvecs", [128, 5, 4])
        lora_up_d = din("lora_up", [128, 512])
        g_up_d = din("g_up", [128, 512])
        lnw_d = din("lnw", [128, 4, 64])
        lnb_d = din("lnb", [128, 4, 64])
        rmask_d = din("rmasks", [128, 4, 512])
        bones_d = din("blockones", [128, 128])
        er = ExitStack()
        GT = 256
        NG = 8192 // GT
        NCH = GT // 64
        NSLOT = 3

        def T(name, shape, dt):
            return er.enter_context(nc.sbuf_tensor("r_" + name, shape, dt))

        alloc_stg(er, "r")
        sq2[1] = T("sq1", [128, D], BF16)
        ss2[1] = T("ss1", [128, 2], F32)
        hb2[1] = T("hb1", [128, D], BF16)
        w_in_b = T("w_in_b", [128, 8, 1792], BF16)
        gmix = T("gmix", [128, 8], F32)
        mu = T("mu", [128, 14], F32)
        rv = T("rv", [128, 5, 4], F32)
        omk = T("omk", [128, 4], F32)
        lup_b = T("lup_b", [128, 512], BF16)
        gup_b = T("gup_b", [128, 512], BF16)
        lnw = T("lnw", [128, 4, 64], F32)
        lnb = T("lnb", [128, 4, 64], F32)
        RSTt = T("RSTt", [128, GT], F32)
        rmb = T("rmb", [128, 3, 512], BF16)
        bones = T("bones", [128, 128], F32)
        onesc = T("onesc", [128, 1], BF16)
        xnT1 = T("xnT0", [128, 8, GT], BF16)
        xnT = [xnT1, xnT1]
        xr1 = T("xr0", [128, D], F32)
        xr = [xr1, xr1]
        carry = T("carry", [128, 14], F32)
        RWl = T("RWl", [128, 2, GT], F32)
        Pstl = T("Pstl", [128, GT + 1], F32)
        LT = [T("LT%d" % i, [128, GT], BF16) for i in range(2)]
        SGD = [T("SGD%d" % i, [128, NCH, 2, 64], BF16) for i in range(2)]
        WS = []
        for i in range(2):
            WS.append(dict(
                i=i,
                Pst=T("Pst%d" % i, [128, GT + 1], F32),
                RW3=T("RW3_%d" % i, [128, 3, GT], F32),
                Wk=[T("Wk%d_%d" % (i, j), [128, GT], F32) for j in range(7)],
            ))
        BTm = [T("BT%d" % i, [128, 4, 4, NCH, 128], BF16) for i in range(2)]
        BTo = T("BTo", [128, 2, 4, 2, 128], BF16)
        GC = [T("GC%d" % i, [128, 4, NCH], F32) for i in range(2)]
        SL = []
        for i in range(NSLOT):
            SL.append(dict(
                i=i,
                TT=T("TT%d" % i, [128, 3, 512], BF16),
                X1=T("X1_%d" % i, [128, 2, 2, 2, 128], BF16),
                X2=T("X2_%d" % i, [128, 2, 2, 2, 128], BF16),
                X3=T("X3_%d" % i, [128, 512], BF16),
                Ak=[T("Ak%d_%d" % (i, j), [128, 512], BF16) for j in range(2)],
                Bk=[T("Bk%d_%d" % (i, j), [128, 512], BF16) for j in range(2)],
                Wt=[T("Wt%d_%d" % (i, j), [128, 512], BF16) for j in range(2)],
                Zt=[T("Zt%d_%d" % (i, j), [128, 512], BF16) for j in range(2)],
            ))
        Hf = T("Hf", [128, 512], F32)
        Hb = T("Hb", [128, 512], BF16)
        Hs = T("Hs", [128, 512], F32)
        RHSb = T("RHSb", [128, 512], BF16)
        Ub = T("Ub", [128, 512], BF16)
        Yc = T("Yc", [128, 4, 64], F32)
        Gc = T("Gc", [128, 4, 64], F32)
        V2c = T("V2c", [128, 4, 64], F32)
        Ycen = [T("Ycen%d" % i, [128, 64], F32) for i in range(4)]
        Ysq = [T("Ysq%d" % i, [128, 64], F32) for i in range(4)]
        st = [T("st%d" % i, [128, 8], F32) for i in range(4)]
        stb = T("stb", [128, 4], F32)
        Oblk = T("Oblk", [128, 4, 128], BF16)

        S.dma("sp", gmix[:], gmix_d[:, :], [], [("gmix", None)])
        S.dma("sp", mu[:], mu_d[:, :], [], [("mu", None)])
        S.dma("sp", rv[:], rv_d[:, :, :], [], [("rv", None)])
        S.dma("sp", stg[0][:, 0:512], lora_up_d[:, :], [], [("stg0", None)])
        k.ve("dve", "tensor_copy", [("stg0", None)], [("lup_b", None)], out=lup_b[:], in_=stg[0][:, 0:512])
        S.dma("sp", stg[1][:, 0:512], g_up_d[:, :], [], [("stg1", None)])
        k.ve("dve", "tensor_copy", [("stg1", None)], [("gup_b", None)], out=gup_b[:], in_=stg[1][:, 0:512])
        S.dma("sp", lnw[:], lnw_d[:, :, :], [], [("lnw", None)])
        S.dma("sp", lnb[:], lnb_d[:, :, :], [], [("lnb", None)])
        S.dma("sp", RSTt[:], rmask_d[:, 0, 0:GT], [], [("rmask", None)])
        for mi in range(3):
            S.dma("sp", stg[mi % 2][:, 0:512], rmask_d[:, 1 + mi, :], [], [("stg%d" % (mi % 2), None)])
            k.ve("dve", "tensor_copy", [("stg%d" % (mi % 2), None)], [("rmb", mi)], out=rmb[:, mi, :], in_=stg[mi % 2][:, 0:512])
        S.dma("pool", bones[:], bones_d[:, :], [], [("bones", None)])
        k.ve("dve", "memset", [], [("onesc", None)], onesc[:], 1.0)
        k.ve("dve", "memset", [], [("carry", None)], carry[:], 0.0)
        k.ve("dve", "memset", [], [("Hf", None)], Hf[:], 0.0)
        k.ve("dve", "memset", [], [("Hb", None)], Hb[:], 0.0)
        for par in range(2):
            for ty in range(4):
                k.ve("pool", "memset", [], [("BT%d" % par, None)], BTm[par][:, ty, :, :, :].rearrange("p b c d -> p (b c d)"), 0.0)
        k.ve("pool", "memset", [], [("BTo", None)], BTo[:].rearrange("p a b c d -> p (a b c d)"), 0.0)
        k.ve("pool", "memset", [], [("Oblk", None)], Oblk[:].rearrange("p a b -> p (a b)"), 0.0)
        k.ve("dve", "tensor_scalar", [("rv", None)], [("omk", None)], out=omk[:], in0=rv[:, 3, :], scalar1=-1.0,
             scalar2=1.0, op0=ALU.mult, op1=ALU.add)
        load_w(w_in_b, "w_in_b", w_in_r, 8, 1792, gcol=gmix, gname="gmix")

        MA = rmb[:, 0, :]
        ML4 = rmb[:, 1, :]
        I4 = rmb[:, 2, :]
        RST = RSTt[:, :]
        mm_i = [0]
        pj_i = [0]

        rbanks = [(pJ0, "pJ0"), (pJ1, "pJ1"), (pM0, "pM0"), (pM1, "pM1"), (pS0, "pS0"), (pS1, "pS1"), (pS2, "pS2")]

        def pM():
            mm_i[0] += 1
            return rbanks[mm_i[0] % 7]

        pJ = pM

        def v3(ap):
            return ap.rearrange("q (c t) -> q c t", c=NCH)

        def halves(e, name, r, w, out3, mk):
            for hh in range(2):
                lo, hi = hh * 64, hh * 64 + 64
                k.ve(e, name, r, w, out=out3[lo:hi, :, lo:hi], **mk(lo, hi))

        def inproj(par, ci, Pst, pstn, dst, dname):
            pp, pn = pJ()
            for kc in range(8):
                k.mm(pp[:, 0:GT], w_in_b[:, kc, ci * 128:(ci + 1) * 128], xnT[par][:, kc, :], kc == 0, kc == 7,
                     [("w_in_b", kc), ("xnT0", None)], [(pn, None)])
            k.act(Pst[:, 0:1], carry[:, ci:ci + 1], AF.Copy, [("carry", ci)], [(pstn, 0)])
            k.act(Pst[:, 1:GT + 1], pp[:, 0:GT], AF.Copy, [(pn, None)], [(pstn, 1)])
            k.act(carry[:, ci:ci + 1], pp[:, GT - 1:GT], AF.Copy, [(pn, None)], [("carry", ci)])
            k.ve("dve", "tensor_tensor", [(pstn, None)], [dname], out=dst, in0=Pst[:, 0:GT], in1=Pst[:, 1:GT + 1],
                 op=ALU.subtract)
            k.ve("dve", "scalar_tensor_tensor", [dname, (pstn, None), ("mu", None)], [dname], out=dst, in0=dst,
                 scalar=mu[:, ci:ci + 1], in1=Pst[:, 1:GT + 1], op0=ALU.mult, op1=ALU.add)

        def gen_load(g):
            par = g % 2
            for tl in range(GT // 128):
                pos = (GT // 128) * g + tl
                b = tl % 2
                S.dma("sp", xr[b][:], xs[pos * 128:(pos + 1) * 128, :], [], [("xr0", None)])
                rmsnorm(xr[b][:], ("xr0", None), hb2[b][:], ("hb%d" % b, None), sq2[b], ss2[b], str(b))
                yield
                for c in range(8):
                    k.tr(pT[:, c * 128:(c + 1) * 128], hb2[b][:, c * 128:(c + 1) * 128], ident[:], [("hb%d" % b, None)], [("pT", c)])
                k.ve("dve", "tensor_copy", [("pT", None)], [("xnT0", tl)], out=xnT[par][:, :, tl * 128:(tl + 1) * 128],
                     in_=pT[:, :].rearrange("p (c t) -> p c t", c=8))
                yield
            inproj(par, 12, Pstl, "Pstl", RWl[:, 0, :], ("RWl", 0))
            yield
            inproj(par, 13, Pstl, "Pstl", RWl[:, 1, :], ("RWl", 1))
            yield
            k.act(LT[par][0:64, :], RWl[0:64, 0, :], AF.Tanh, [("RWl", 0)], [("LT%d" % par, None)])
            k.act(LT[par][64:128, :], RWl[64:128, 0, :], AF.Copy, [("RWl", 0)], [("LT%d" % par, None)])
            for dup in range(2):
                k.act(SGD[par][:, :, dup, :], RWl[:, 1, :].rearrange("q (c t) -> q c t", c=NCH), AF.Sigmoid, [("RWl", 1)],
                      [("SGD%d" % par, None)])
            yield

        def gen_prep(g, p, ws):
            par = g % 2
            wi = ws["i"]
            Wk = ws["Wk"]
            RW3 = ws["RW3"]
            wn = lambda j: ("Wk%d_%d" % (wi, j), None)
            rn = lambda j: ("RW3_%d" % wi, j)
            btn = lambda ty: ("BT%d" % par, (ty, p))
            B_ = BTm[par]
            inproj(par, p, ws["Pst"], "Pst%d" % wi, RW3[:, 0, :], rn(0))
            yield
            inproj(par, 4 + p, ws["Pst"], "Pst%d" % wi, RW3[:, 1, :], rn(1))
            yield
            inproj(par, 8 + p, ws["Pst"], "Pst%d" % wi, RW3[:, 2, :], rn(2))
            yield
            k_ = RW3[:, 1, :]
            cs = slice(p * 128, (p + 1) * 128)
            pp, pn = pJ()
            k.mm(pp[:, 0:GT], lup_b[0:64, cs], LT[par][0:64, :], True, True, [("lup_b", None), ("LT%d" % par, None)], [(pn, None)])
            k.act(Wk[0][:], pp[:, 0:GT], AF.Sigmoid, [(pn, None), ("rv", None)], [wn(0)], bias=rv[:, 0, p:p + 1])
            yield
            k.ve("dve", "tensor_scalar", [wn(0)], [wn(0)], out=Wk[0][:], in0=Wk[0][:], scalar1=-0.6065306597126334,
                 scalar2=None, op0=ALU.mult)
            yield
            k.ve("dve", "tensor_tensor_scan", [wn(0), ("rmask", None)], [wn(1)], out=Wk[1][:], data0=RST, data1=Wk[0][:],
                 initial=0.0, op0=ALU.mult, op1=ALU.add)
            yield
            k.ve("pool", "tensor_tensor", [wn(0), wn(1)], [wn(0)], out=Wk[0][:], in0=Wk[1][:], in1=Wk[0][:], op=ALU.subtract)
            k.act(Wk[2][:], Wk[1][:], AF.Exp, [wn(1)], [wn(2)])
            k.act(Wk[3][:], Wk[1][:], AF.Exp, [wn(1)], [wn(3)], scale=-1.0)
            yield
            k.act(Wk[0][:], Wk[0][:], AF.Exp, [wn(0)], [wn(0)])
            k.ve("pool", "tensor_copy", [wn(2)], [("GC%d" % par, p)], out=GC[par][:, p, :],
                 in_=Wk[2][:].rearrange("q (c t) -> q c t", c=NCH)[:, :, 63])
            yield
            pp, pn = pJ()
            k.mm(pp[:, 0:GT], lup_b[64:128, cs], LT[par][64:128, :], True, True, [("lup_b", None), ("LT%d" % par, None)],
                 [(pn, None)])
            k.act(Wk[4][:], pp[:, 0:GT], AF.Sigmoid, [(pn, None), ("rv", None)], [wn(4)], bias=rv[:, 1, p:p + 1])
            k.ve("dve", "tensor_scalar", [rn(1), ("rv", None)], [wn(5)], out=Wk[5][:], in0=k_, scalar1=rv[:, 2, p:p + 1],
                 scalar2=None, op0=ALU.mult)
            yield
            k.ve("pool", "tensor_tensor", [wn(5)], [wn(6)], out=Wk[6][:], in0=Wk[5][:], in1=Wk[5][:], op=ALU.mult)
            yield
            pp, pn = pJ()
            k.mm(pp[:, 0:GT], bones[:, :], Wk[6][:], True, True, [("bones", None), wn(6)], [(pn, None)])
            k.ve("dve", "tensor_scalar", [(pn, None)], [wn(6)], out=Wk[6][:], in0=pp[:, 0:GT], scalar1=1e-24, scalar2=None,
                 op0=ALU.max)
            yield
            k.act(Wk[6][:], Wk[6][:], AF.Ln, [wn(6)], [wn(6)])
            yield
            k.act(Wk[6][:], Wk[6][:], AF.Exp, [wn(6)], [wn(6)], scale=-0.5)
            yield
            k.ve("dve", "tensor_tensor", [wn(5), wn(6)], [wn(5)], out=Wk[5][:], in0=Wk[5][:], in1=Wk[6][:], op=ALU.mult)
            k.ve("dve", "tensor_scalar", [wn(4), ("rv", None), ("omk", None)], [wn(6)], out=Wk[6][:], in0=Wk[4][:],
                 scalar1=rv[:, 3, p:p + 1], scalar2=omk[:, p:p + 1], op0=ALU.mult, op1=ALU.add)
            yield
            k.ve("pool", "tensor_tensor", [wn(6), rn(1)], [wn(6)], out=Wk[6][:], in0=Wk[6][:], in1=k_, op=ALU.mult)
            k.ve("pool", "tensor_tensor", [wn(5), wn(4)], [wn(4)], out=Wk[4][:], in0=Wk[5][:], in1=Wk[4][:], op=ALU.mult)
            yield
            if g % 2 == 1:
                halves("dve", "tensor_tensor", [rn(0), wn(2)], [("BTo", (0, p))], BTo[:, 0, p, :, :],
                       lambda lo, hi: dict(in0=v3(RW3[lo:hi, 0, :])[:, 2:4, :], in1=v3(Wk[2][lo:hi, :])[:, 2:4, :], op=ALU.mult))
            halves("dve", "scalar_tensor_tensor", [wn(5), wn(0)], [btn(1)], B_[:, 0, p, :, :],
                   lambda lo, hi: dict(in0=v3(Wk[5][lo:hi, :]), scalar=-1.0, in1=v3(Wk[0][lo:hi, :]), op0=ALU.mult,
                                       op1=ALU.mult))
            yield
            halves("pool", "tensor_tensor", [wn(4), wn(3)], [btn(2)], B_[:, 1, p, :, :],
                   lambda lo, hi: dict(in0=v3(Wk[4][lo:hi, :]), in1=v3(Wk[3][lo:hi, :]), op=ALU.mult))
            halves("pool", "tensor_tensor", [wn(6), wn(3)], [btn(3)], B_[:, 2, p, :, :],
                   lambda lo, hi: dict(in0=v3(Wk[6][lo:hi, :]), in1=v3(Wk[3][lo:hi, :]), op=ALU.mult))
            yield
            halves("pool", "tensor_copy", [rn(2)], [btn(4)], B_[:, 3, p, :, :],
                   lambda lo, hi: dict(in_=v3(RW3[lo:hi, 2, :])))
            if g % 2 == 1:
                halves("dve", "scalar_tensor_tensor", [rn(0), wn(6), ("rv", None)], [("BTo", (1, p))], BTo[:, 1, p, :, :],
                       lambda lo, hi: dict(in0=v3(RW3[lo:hi, 0, :])[:, 2:4, :], scalar=rv[lo:hi, 4, p:p + 1],
                                           in1=v3(Wk[6][lo:hi, :])[:, 2:4, :], op0=ALU.mult, op1=ALU.mult))
            yield

        def gen_tt(g, c, sl):
            par = g % 2
            B_ = BTm[par]
            si_ = sl["i"]
            btn = lambda ty, p: ("BT%d" % par, (ty, p))
            TT, X2 = sl["TT"], sl["X2"]
            own = (g % 2 == 1) and c >= 2
            for j, ty in enumerate((2, 3, 4)):
                pp, pn = pM()
                for p in range(4):
                    k.mm(pp[:, p * 128:(p + 1) * 128], B_[:, ty - 1, p, c, :], ident[:], True, True,
                         [btn(ty, p), ("ident", None)], [(pn, None)])
                if j >= 1:
                    k.act(TT[:, j, :], pp[:, :], AF.Copy, [(pn, None)], [("TT%d" % si_, j)])
                else:
                    k.ve("dve", "tensor_copy", [(pn, None)], [("TT%d" % si_, j)], out=TT[:, j, :], in_=pp[:, :])
                yield
            for j in range(2):
                pp, pn = pM()
                for a in range(2):
                    p = 2 * j + a
                    k.mm(pp[:, a * 256:a * 256 + 128], B_[:, 2, p, c, :], B_[:, 0, p, c, :], True, True, [btn(3, p), btn(1, p)],
                         [(pn, None)])
                    if own:
                        k.mm(pp[:, a * 256 + 128:a * 256 + 256], B_[:, 2, p, c, :], BTo[:, 0, p, c - 2, :], True, True,
                             [btn(3, p), ("BTo", (0, p))], [(pn, None)])
                if own:
                    k.ve("dve", "tensor_tensor", [(pn, None), ("rmb", None)], [("X2_%d" % si_, j)],
                         out=X2[:, j, :, :, :].rearrange("q a t s -> q (a t s)"), in0=pp[:, :], in1=MA, op=ALU.mult)
                else:
                    k.ve("dve", "tensor_tensor", [(pn, None), ("rmb", None)], [("X2_%d" % si_, j)],
                         out=X2[:, j, :, 0, :], in0=pp[:, :].rearrange("q (a t s) -> q a t s", a=2, t=2)[:, :, 0, :],
                         in1=MA.rearrange("q (a t s) -> q a t s", a=2, t=2)[:, :, 0, :], op=ALU.mult)
                yield

        def gen_pre(g, c, sl):
            par = g % 2
            B_ = BTm[par]
            si_ = sl["i"]
            btn = lambda ty, p: ("BT%d" % par, (ty, p))
            X1, X3 = sl["X1"], sl["X3"]
            Ak, Bk, Wt, Zt = sl["Ak"], sl["Bk"], sl["Wt"], sl["Zt"]
            own = (g % 2 == 1) and c >= 2
            n_ = lambda base, j=None: ("%s%d" % (base, si_) if j is None else "%s%d_%d" % (base, si_, j), None)
            pp, pn = pM()
            for p in range(4):
                k.mm(pp[:, p * 128:(p + 1) * 128], B_[:, 0, p, c, :], B_[:, 1, p, c, :], True, True, [btn(1, p), btn(2, p)],
                     [(pn, None)])
            k.ve("dve", "tensor_tensor", [(pn, None), ("rmb", None)], [("X3_%d" % si_, None)], out=X3[:], in0=pp[:, :], in1=ML4,
                 op=ALU.mult)
            for j in range(2):
                pp, pn = pM()
                for a in range(2):
                    p = 2 * j + a
                    k.mm(pp[:, a * 256:a * 256 + 128], B_[:, 1, p, c, :], B_[:, 0, p, c, :], True, True, [btn(2, p), btn(1, p)],
                         [(pn, None)])
                    if own:
                        k.mm(pp[:, a * 256 + 128:a * 256 + 256], B_[:, 1, p, c, :], BTo[:, 0, p, c - 2, :], True, True,
                             [btn(2, p), ("BTo", (0, p))], [(pn, None)])
                if own:
                    k.ve("dve", "tensor_tensor", [(pn, None), ("rmb", None)], [("X1_%d" % si_, j)],
                         out=X1[:, j, :, :, :].rearrange("q a t s -> q (a t s)"), in0=pp[:, :], in1=MA, op=ALU.mult)
                else:
                    k.ve("dve", "tensor_tensor", [(pn, None), ("rmb", None)], [("X1_%d" % si_, j)],
                         out=X1[:, j, :, 0, :], in0=pp[:, :].rearrange("q (a t s) -> q a t s", a=2, t=2)[:, :, 0, :],
                         in1=MA.rearrange("q (a t s) -> q a t s", a=2, t=2)[:, :, 0, :], op=ALU.mult)
            yield
            B1 = X1[:, :, :, 0, :]
            k.ve("dve", "tensor_tensor", [("X1_%d" % si_, None), ("rmb", None)], [n_("Wt", 0)],
                 out=Wt[0][:].rearrange("q (j a s) -> q j a s", j=2, a=2), in0=B1,
                 in1=I4.rearrange("q (j a s) -> q j a s", j=2, a=2), op=ALU.add)
            k.ve("dve", "tensor_tensor", [("X3_%d" % si_, None), ("rmb", None)], [n_("Zt", 0)], out=Zt[0][:], in0=X3[:], in1=I4,
                 op=ALU.add)

            def Bsl(lv, p):
                if lv == 0:
                    return X1[:, p // 2, p % 2, 0, :], ("X1_%d" % si_, None)
                return Bk[lv % 2][:, p * 128:(p + 1) * 128], n_("Bk", lv % 2)

            def Asl(lv, p):
                if lv == 0:
                    return X3[:, p * 128:(p + 1) * 128], ("X3_%d" % si_, None)
                return Ak[lv % 2][:, p * 128:(p + 1) * 128], n_("Ak", lv % 2)

            def squares(lv):
                ppB, pnB = pM()
                for p in range(4):
                    a_ap, a_n = Asl(lv - 1, p)
                    b_ap, b_n = Bsl(lv - 1, p)
                    k.mm(ppB[:, p * 128:(p + 1) * 128], a_ap, b_ap, True, True, [a_n, b_n], [(pnB, None)])
                k.act(Bk[lv % 2][:], ppB[:, :], AF.Copy, [(pnB, None)], [n_("Bk", lv % 2)])
                if lv < 5:
                    ppA, pnA = pM()
                    for p in range(4):
                        a_ap, a_n = Asl(lv - 1, p)
                        b_ap, b_n = Bsl(lv - 1, p)
                        k.mm(ppA[:, p * 128:(p + 1) * 128], b_ap, a_ap, True, True, [a_n, b_n], [(pnA, None)])
                    k.act(Ak[lv % 2][:], ppA[:, :], AF.Copy, [(pnA, None)], [n_("Ak", lv % 2)])

            def products(lv):
                wi, wo = (lv - 1) % 2, lv % 2
                ppW, pnW = pM()
                for p in range(4):
                    b_ap, b_n = Bsl(lv, p)
                    k.mm(ppW[:, p * 128:(p + 1) * 128], Zt[wi][:, p * 128:(p + 1) * 128], b_ap, True, True,
                         [n_("Zt", wi), b_n], [(pnW, None)])
                k.ve("dve", "tensor_tensor", [(pnW, None), n_("Wt", wi)], [n_("Wt", wo)], out=Wt[wo][:], in0=ppW[:, :],
                     in1=Wt[wi][:], op=ALU.add)
                if lv < 5:
                    ppZ, pnZ = pM()
                    for p in range(4):
                        a_ap, a_n = Asl(lv, p)
                        k.mm(ppZ[:, p * 128:(p + 1) * 128], Wt[wi][:, p * 128:(p + 1) * 128], a_ap, True, True,
                             [n_("Wt", wi), a_n], [(pnZ, None)])
                    k.ve("dve", "tensor_tensor", [(pnZ, None), n_("Zt", wi)], [n_("Zt", wo)], out=Zt[wo][:], in0=ppZ[:, :],
                         in1=Zt[wi][:], op=ALU.add)

            squares(1)
            yield
            for lv in range(1, 6):
                if lv < 5:
                    squares(lv + 1)
                products(lv)
                yield

        def gen_seq(g, c, sl):
            par = g % 2
            B_ = BTm[par]
            si_ = sl["i"]
            btn = lambda ty, p: ("BT%d" % par, (ty, p))
            TT, X1, X2 = sl["TT"], sl["X1"], sl["X2"]
            Wf, Wfn = sl["Wt"][1], ("Wt%d_1" % si_, None)
            ttn = lambda j: ("TT%d" % si_, j)
            x1n, x2n = ("X1_%d" % si_, None), ("X2_%d" % si_, None)
            BBT, KBT, VBT = TT[:, 0, :], TT[:, 1, :], TT[:, 2, :]
            own = (g % 2 == 1) and c >= 2
            it = g // 2
            b0, b0n = pM()
            for p in range(4):
                ps = slice(p * 128, (p + 1) * 128)
                k.mm(b0[:, ps], B_[:, 0, p, c, :], Hb[:, ps], True, False, [btn(1, p), ("Hb", None)], [(b0n, None)])
                k.mm(b0[:, ps], X2[:, p // 2, p % 2, 0, :], VBT[:, ps], False, True, [x2n, ttn(2)], [(b0n, None)])
            k.act(RHSb[:], b0[:, :], AF.Copy, [(b0n, None)], [("RHSb", None)])
            yield
            b1, b1n = pM()
            for p in range(4):
                ps = slice(p * 128, (p + 1) * 128)
                k.mm(b1[:, ps], Wf[:, ps], RHSb[:, ps], True, True, [Wfn, ("RHSb", None)], [(b1n, None)])
            k.act(Ub[:], b1[:, :], AF.Copy, [(b1n, None)], [("Ub", None)])
            yield
            if own:
                ch = c - 2
                pS2, pS2n = pM()
                for p in range(4):
                    ps = slice(p * 128, (p + 1) * 128)
                    k.mm(pS2[:, ps], BTo[:, 0, p, c - 2, :], Hb[:, ps], True, False, [("BTo", (0, p)), ("Hb", None)], [(pS2n, None)])
                    k.mm(pS2[:, ps], X1[:, p // 2, p % 2, 1, :], Ub[:, ps], False, False, [x1n, ("Ub", None)], [(pS2n, None)])
                    k.mm(pS2[:, ps], X2[:, p // 2, p % 2, 1, :], VBT[:, ps], False, True, [x2n, ttn(2)], [(pS2n, None)])
            b3, b3n = pM()
            for p in range(4):
                ps = slice(p * 128, (p + 1) * 128)
                k.mm(b3[:, ps], BBT[:, ps], Ub[:, ps], True, False, [ttn(0), ("Ub", None)], [(b3n, None)])
                k.mm(b3[:, ps], KBT[:, ps], VBT[:, ps], False, True, [ttn(1), ttn(2)], [(b3n, None)])
            k.ve("dve", "tensor_tensor", [(b3n, None), ("Hf", None)], [("Hs", None)], out=Hs[:], in0=b3[:, :], in1=Hf[:],
                 op=ALU.add)
            yield
            for p in range(4):
                ps = slice(p * 128, (p + 1) * 128)
                k.act(Hf[:, ps], Hs[:, ps], AF.Copy, [("Hs", None), ("GC%d" % par, p)], [("Hf", None)], scale=GC[par][:, p, c:c + 1])
            k.ve("dve", "tensor_copy", [("Hf", None)], [("Hb", None)], out=Hb[:], in_=Hf[:])
            yield
            if own:
                for hh in range(2):
                    lo, hi = hh * 64, hh * 64 + 64
                    k.ve("dve", "tensor_copy", [(pS2n, None)], [("Yc", None)], out=Yc[lo:hi, :, :],
                         in_=pS2[lo:hi, :].rearrange("q (p s) -> q p s", p=4)[:, :, lo:hi])
                    k.ve("pool", "tensor_copy", [ttn(2)], [("V2c", None)], out=V2c[lo:hi, :, :],
                         in_=TT[lo:hi, 2, :].rearrange("q (p s) -> q p s", p=4)[:, :, lo:hi])
                yield
                pp, pn = pJ()
                for p in range(4):
                    k.mm(pp[:, p * 128:(p + 1) * 128], SGD[par][:, c, :, :].rearrange("q a t -> q (a t)"),
                         gup_b[:, p * 128:(p + 1) * 128], True, True, [("SGD%d" % par, None), ("gup_b", None)], [(pn, None)])
                for hh in range(2):
                    lo, hi = hh * 64, hh * 64 + 64
                    k.act(Gc[lo:hi, :, :], pp[lo:hi, :].rearrange("q (p s) -> q p s", p=4)[:, :, lo:hi], AF.Copy,
                          [(pn, None)], [("Gc", None)])
                yield
                pp, pn = pJ()
                for p in range(4):
                    k.mm(pp[:, p:p + 1], BTo[:, 1, p, c - 2, :], onesc[:, 0:1], True, True, [("BTo", (1, p)), ("onesc", None)], [(pn, None)])
                k.ve("dve", "tensor_copy", [(pn, None)], [("stb", None)], out=stb[:, 0:4], in_=pp[:, 0:4])
                yield
                steps = []
                for p in range(4):
                    yc = Yc[:, p, :]
                    sn, cn, qn = ("st%d" % p, None), ("Ycen%d" % p, None), ("Ysq%d" % p, None)
                    s_, c_, q_ = st[p], Ycen[p], Ysq[p]
                    steps.append([
                        lambda yc=yc, s_=s_, sn=sn: k.ve("dve", "reduce_sum", [("Yc", None)], [sn], out=s_[:, 0:1], in_=yc, axis=AX.X),
                        lambda s_=s_, sn=sn: k.ve("dve", "tensor_scalar", [sn], [sn], out=s_[:, 0:1], in0=s_[:, 0:1],
                                                  scalar1=-1.0 / 64, scalar2=None, op0=ALU.mult),
                        lambda yc=yc, s_=s_, sn=sn, c_=c_, cn=cn: k.ve("dve", "tensor_scalar", [("Yc", None), sn], [cn], out=c_[:],
                                                                      in0=yc, scalar1=s_[:, 0:1], scalar2=None, op0=ALU.add),
                        lambda c_=c_, cn=cn, q_=q_, qn=qn: k.ve("pool", "tensor_tensor", [cn], [qn], out=q_[:], in0=c_[:], in1=c_[:],
                                                              op=ALU.mult),
                        lambda q_=q_, qn=qn, s_=s_, sn=sn: k.ve("dve", "reduce_sum", [qn], [sn], out=s_[:, 1:2], in_=q_[:], axis=AX.X),
                        lambda s_=s_, sn=sn: k.act(s_[:, 2:3], s_[:, 1:2], AF.Ln, [sn, ("epsl", None)], [sn], bias=epsl[:, 0:1],
                                                   scale=1.0 / 64),
                        lambda s_=s_, sn=sn: k.act(s_[:, 2:3], s_[:, 2:3], AF.Exp, [sn], [sn], scale=-0.5),
                        lambda s_=s_, sn=sn, c_=c_, cn=cn: k.ve("dve", "tensor_scalar", [cn, sn], [cn], out=c_[:], in0=c_[:],
                                                              scalar1=s_[:, 2:3], scalar2=None, op0=ALU.mult),
                        lambda c_=c_, cn=cn, p=p: k.ve("pool", "tensor_tensor", [cn, ("lnw", None)], [cn], out=c_[:], in0=c_[:],
                                                      in1=lnw[:, p, :], op=ALU.mult),
                        lambda c_=c_, cn=cn, p=p: k.ve("pool", "tensor_tensor", [cn, ("lnb", None)], [cn], out=c_[:], in0=c_[:],
                                                      in1=lnb[:, p, :], op=ALU.add),
                        lambda c_=c_, cn=cn, p=p: k.ve("dve", "scalar_tensor_tensor", [("V2c", None), ("stb", None), cn], [cn],
                                                      out=c_[:], in0=V2c[:, p, :], scalar=stb[:, p:p + 1], in1=c_[:],
                                                      op0=ALU.mult, op1=ALU.add),
                    ])
                for si2 in range(len(steps[0])):
                    for p in range(4):
                        steps[p][si2]()
                    yield
                for p in range(4):
                    for hh in range(2):
                        lo, hi = hh * 64, hh * 64 + 64
                        k.ve("dve" if hh == 0 else "pool", "tensor_tensor", [("Ycen%d" % p, None), ("Gc", None)], [("Oblk", None)],
                             out=Oblk[lo:hi, p, lo:hi], in0=Ycen[p][lo:hi, :], in1=Gc[lo:hi, p, :], op=ALU.mult)
                yield
                for p in range(4):
                    k.tr(pT[:, p * 128:(p + 1) * 128], Oblk[:, p, :], ident[:], [("Oblk", None)], [("pT", p)])
                for hh in range(2):
                    lo, hi = hh * 64, hh * 64 + 64
                    k.ve("dve", "tensor_copy", [("pT", None)], [("oT_r", it)], out=oT_r[lo:hi, it, :, ch * 64:(ch + 1) * 64],
                         in_=pT[lo:hi, 0:512].rearrange("q (p s) -> q p s", p=4)[:, :, lo:hi])
                yield

        NCC = NG * NCH
        done = {"load": -1, "prep": -1, "pre": -1, "seq": -1}
        active = []
        nxt = {"load": 0, "prepg": 0, "prepp": 0, "pre": 0, "seq": 0}
        prep_done_pairs = {}

        def admit():
            if nxt["load"] < NG and not any(a[0] == "load" for a in active) and nxt["load"] <= done["prep"] + 1 \
                    and done["seq"] >= (nxt["load"] - 1) * NCH - 1:
                g = nxt["load"]
                active.append(("load", g, gen_load(g)))
                nxt["load"] += 1
            while nxt["prepg"] < NG:
                g = nxt["prepg"]
                if done["load"] < g or done["seq"] < (g - 1) * NCH - 1:
                    break
                inflight = [a for a in active if a[0] == "prep"]
                if len(inflight) >= 2:
                    break
                used = {a[3] for a in inflight}
                wsi = 0 if 0 not in used else 1
                p = nxt["prepp"]
                active.append(("prep", (g, p), gen_prep(g, p, WS[wsi]), wsi))
                nxt["prepp"] += 1
                if nxt["prepp"] == 4:
                    nxt["prepp"] = 0
                    nxt["prepg"] += 1
            while nxt["pre"] < NCC:
                cc = nxt["pre"]
                if done["prep"] < cc // NCH:
                    break
                if cc - (done["seq"] + 1) >= NSLOT:
                    break
                if len([a for a in active if a[0] == "pre"]) >= 2:
                    break
                active.append(("pre", cc, gen_pre(cc // NCH, cc % NCH, SL[cc % NSLOT])))
                active.append(("tt", cc, gen_tt(cc // NCH, cc % NCH, SL[cc % NSLOT])))
                nxt["pre"] += 1
            if nxt["seq"] < NCC and not any(a[0] == "seq" for a in active) and done["pre"] >= nxt["seq"]:
                cc = nxt["seq"]
                active.append(("seq", cc, gen_seq(cc // NCH, cc % NCH, SL[cc % NSLOT])))
                nxt["seq"] += 1

        pre_done_set = set()
        pre_cnt = {}
        while True:
            admit()
            if not active:
                break
            for a in list(active):
                try:
                    next(a[2])
                except StopIteration:
                    active.remove(a)
                    kind = a[0]
                    if kind == "load":
                        done["load"] = a[1]
                    elif kind == "prep":
                        gq, pq = a[1]
                        prep_done_pairs[gq] = prep_done_pairs.get(gq, 0) + 1
                        if prep_done_pairs[gq] == 4:
                            done["prep"] = gq
                    elif kind in ("pre", "tt"):
                        pre_cnt[a[1]] = pre_cnt.get(a[1], 0) + 1
                        if pre_cnt[a[1]] == 2:
                            pre_done_set.add(a[1])
                        while done["pre"] + 1 in pre_done_set:
                            done["pre"] += 1
                    else:
                        done["seq"] = a[1]
        assert done["seq"] == NCC - 1, done
        if "orw" in dbg:
            dorw = nc.dram_tensor("dorw", [128, NOWN * 4 * 128], BF16, kind="ExternalOutput").ap()
            S.dma("sp", dorw[:, :], oT_r[:].rearrange("q a b c -> q (a b c)"), [("oT_r", None)], [])
        S.fence()
        er.close()
        S.fence()

    oT_n = eo.enter_context(nc.sbuf_tensor("oT_n", [128, NOWN, 4, 128], BF16, side="right"))
    if "nsa" in dbg:
        w_in_n = din("w_in_n", [128, 8, 1304])
        gateb_d = din("gate_b", [128, 24])
        realtok_d = din("realtok", [128, 64])
        w1_d = {"k": din("cmp_w1k", [128, 32, 256]), "v": din("cmp_w1v", [128, 32, 256])}
        peT_d = {"k": din("cmp_peTk", [128, 32]), "v": din("cmp_peTv", [128, 32])}
        b1_d = {"k": din("cmp_b1k", [128, 2]), "v": din("cmp_b1v", [128, 2])}
        w2k_d = din("cmp_w2k", [128, 2, 128])
        w2v_d = din("cmp_w2v", [128, 2, 64])
        ovlm_d = din("ovlm", [128, 4, 129])
        relb_d = din("relb_rep", [32, 8, 128])
        OHa_d = din("OHa", [32, 2048])
        OHw_d = din("OHw", [32, 256])
        OHc_d = din("OHc", [32, 6144])
        Ebig_d = din("Ebig", [128, 64, 128])
        addm_d = din("addmask", [128, 16, 128])
        scr_a = nc.dram_tensor("scr_a", [8, 128, 2048], BF16)
        scr_w = nc.dram_tensor("scr_w", [8, 128, 256], BF16)
        scr_c = nc.dram_tensor("scr_c", [8, 128, 6144], BF16)
        en = ExitStack()
        ens = ExitStack()
        alloc_stg(ens, "n", side="right")

        def TN(name, shape, dt):
            return en.enter_context(nc.sbuf_tensor("n_" + name, shape, dt))

        kselT = TN("kselT", [128, 8192], BF16)
        kwinT = TN("kwinT", [128, 8192], BF16)
        Vs = TN("Vs", [128, 64, 2, 65], BF16)
        Vw = TN("Vw", [128, 64, 2, 65], BF16)
        QT = TN("QT", [128, NOWN, 4, 128], BF16)
        GATES = TN("GATES", [128, NOWN, 24], F32)
        KcT = TN("KcT", [128, 512], BF16)
        Vca = TN("Vca", [128, 4, 2, 65], BF16)
        Ovl = TN("Ovl", [128, 4, 128], BF16)

        en1 = ExitStack()

        def TN1(name, shape, dt):
            return en1.enter_context(nc.sbuf_tensor("n_" + name, shape, dt))

        ovlm = TN1("ovlm", [128, 4, 129], F32)
        realtok = TN1("realtok", [128, 64], F32)
        gateb = TN1("gateb", [128, 24], F32)
        S.dma("sp", gateb[:], gateb_d[:, :], [], [("gateb", None)])
        S.dma("sp", realtok[:], realtok_d[:, :], [], [("realtok", None)])
        S.dma("sp", ovlm[:], ovlm_d[:, :, :], [], [("ovlm", None)])
        k.ve("pool", "tensor_copy", [("ovlm", None)], [("Ovl", None)], out=Ovl[:], in_=ovlm[:, :, 1:129])
        kcmpT = TN1("kcmpT", [128, 8192 + 32], BF16)
        vcmpT = TN1("vcmpT", [128, 8192 + 32], BF16)
        en1b = ExitStack()
        w_in_nb = en1b.enter_context(nc.sbuf_tensor("n_w_in_nb", [128, 8, 1304], BF16))
        gmixn = en1b.enter_context(nc.sbuf_tensor("n_gmixn", [128, 8], F32))
        xnT2 = [en1b.enter_context(nc.sbuf_tensor("n_xnT%d" % i, [128, 8, 512], BF16)) for i in range(2)]
        xr_n = en1b.enter_context(nc.sbuf_tensor("n_xr0", [128, D], F32))
        xr2 = [xr_n, xr_n]
        sqn = [sq, sq]
        ssn = [ss, ss]
        hbn = [hb, en1b.enter_context(nc.sbuf_tensor("n_hbx1", [128, D], BF16))]
        S.dma("sp", gmixn[:], gmix_d[:, :], [], [("gmixn", None)])
        load_w(w_in_nb, "w_in_nb", w_in_n, 8, 1304, gcol=gmixn, gname="gmixn")
        k.ve("pool", "memset", [], [("kcmpT", None)], kcmpT[:, 8192:8224], 0.0)
        k.ve("pool", "memset", [], [("vcmpT", None)], vcmpT[:, 8192:8224], 0.0)
        nbanks = [(pA, "pA"), (pB, "pB"), (pC, "pC"), (pD, "pD"), (pE, "pE"), (pF, "pF"), (pG, "pG")]
        nb_i = [0]

        def nbank():
            nb_i[0] += 1
            return nbanks[nb_i[0] % 7]

        def gen_n1(g):
            par = g % 2
            xnT = xnT2[par]
            xn_ = "n_xnT%d" % par
            for tl in range(4):
                pos = 4 * g + tl
                b2 = par
                S.dma("sp", xr2[b2][:], xs[pos * 128:(pos + 1) * 128, :], [], [("n_xr0", None)])
                rmsnorm(xr2[b2][:], ("n_xr0", None), hbn[b2][:], ("n_hb%d" % b2, None), sqn[b2], ssn[b2], "n")
                yield
                for c in range(8):
                    k.tr(pT[:, c * 128:(c + 1) * 128], hbn[b2][:, c * 128:(c + 1) * 128], ident[:], [("n_hb%d" % b2, None)], [("pT", c)])
                k.ve("dve", "tensor_copy", [("pT", None)], [(xn_, tl)], out=xnT[:, :, tl * 128:(tl + 1) * 128],
                     in_=pT[:, :].rearrange("p (c t) -> p c t", c=8))
                yield
            tok = slice(g * 512, (g + 1) * 512)
            for ci, (dst, dn) in enumerate(((kcmpT, "kcmpT"), (vcmpT, "vcmpT"), (kselT, "kselT"), (kwinT, "kwinT"))):
                pp, pn = nbank()
                for kc in range(8):
                    k.mm(pp[:, :], w_in_nb[:, kc, 512 + ci * 128:512 + (ci + 1) * 128], xnT[:, kc, :], kc == 0, kc == 7,
                         [("w_in_nb", kc), (xn_, None)], [(pn, None)])
                if ci % 2 == 0:
                    k.act(dst[:, tok], pp[:, :], AF.Copy, [(pn, None)], [(dn, g)])
                else:
                    k.ve("dve", "tensor_copy", [(pn, None)], [(dn, g)], out=dst[:, tok], in_=pp[:, :])
                yield
            for tl in range(4):
                pos = 4 * g + tl
                pp, pn = nbank()
                for kc in range(8):
                    k.mm(pp[:, 0:256], xnT[:, kc, tl * 128:(tl + 1) * 128], w_in_nb[:, kc, 1024:1280], kc == 0, kc == 7,
                         [("w_in_nb", kc), (xn_, None)], [(pn, None)])
                k.act(Vs[:, pos, :, 0:64], pp[:, 0:128].rearrange("p (a b) -> p a b", a=2), AF.Copy, [(pn, None)], [("Vs", pos)])
                k.ve("dve", "tensor_copy", [(pn, None)], [("Vw", pos)], out=Vw[:, pos, :, 0:64],
                     in_=pp[:, 128:256].rearrange("p (a b) -> p a b", a=2))
                for kv in range(2):
                    k.ve("pool", "tensor_copy", [("realtok", None)], [("Vs", pos)], out=Vs[:, pos, kv, 64:65],
                         in_=realtok[:, pos:pos + 1])
                    k.ve("pool", "tensor_copy", [("realtok", None)], [("Vw", pos)], out=Vw[:, pos, kv, 64:65],
                         in_=realtok[:, pos:pos + 1])
                yield
            for a in range(4):
                pp, pn = nbank()
                for kc in range(8):
                    k.mm(pp[:, 0:128], w_in_nb[:, kc, a * 128:(a + 1) * 128], xnT[:, kc, 384:512], kc == 0, kc == 7,
                         [("w_in_nb", kc), (xn_, None)], [(pn, None)])
                k.act(QT[:, g, a, :], pp[:, 0:128], AF.Copy, [(pn, None)], [("QT", g)])
                if a % 2 == 1:
                    yield
            pp, pn = nbank()
            for kc in range(8):
                k.mm(pp[:, 0:24], xnT[:, kc, 384:512], w_in_nb[:, kc, 1280:1304], kc == 0, kc == 7,
                     [("w_in_nb", kc), (xn_, None)], [(pn, None)])
            k.ve("dve", "tensor_tensor", [(pn, None), ("gateb", None)], [("GATES", g)], out=GATES[:, g, :], in0=pp[:, 0:24],
                 in1=gateb[:], op=ALU.add)
            k.act(GATES[:, g, :], GATES[:, g, :], AF.Sigmoid, [("GATES", g)], [("GATES", g)])
            yield

        actn = []
        nxtn = 0
        while True:
            while nxtn < 16 and len(actn) < 2:
                actn.append((nxtn, gen_n1(nxtn)))
                nxtn += 1
            if not actn:
                break
            for a in list(actn):
                try:
                    next(a[1])
                except StopIteration:
                    actn.remove(a)

        S.fence()
        en1b.close()
        S.fence()
        with nc.sbuf_tensor("n_w1b", [128, 32, 256], BF16) as w1b, \
                nc.sbuf_tensor("n_peT", [128, 32], F32) as peTf, \
                nc.sbuf_tensor("n_peTb", [128, 32], BF16) as peTb, \
                nc.sbuf_tensor("n_b1", [128, 2], F32) as b1t, \
                nc.sbuf_tensor("n_w2f", [128, 2, 128], F32) as w2f, \
                nc.sbuf_tensor("n_w2b", [128, 2, 128], BF16) as w2b, \
                nc.sbuf_tensor("n_hb1", [128, 2], F32) as hbias, \
                nc.sbuf_tensor("n_z", [128, 512], F32) as zt, \
                nc.sbuf_tensor("n_z2", [128, 512], F32) as z2t, \
                nc.sbuf_tensor("n_ge", [128, 2, 512], BF16) as ge:
            for X, srcT, sn in (("k", kcmpT, "kcmpT"), ("v", vcmpT, "vcmpT")):
                for part in range(8):
                    i = part % 2
                    S.dma("sp" if i == 0 else "pool", stg[i][:, :],
                          w1_d[X][:, part * 4:(part + 1) * 4, :].rearrange("p a b -> p (a b)"), [], [("stg%d" % i, None)])
                    k.ve("dve" if i == 0 else "pool", "tensor_copy", [("stg%d" % i, None)], [("w1b", part)],
                         out=w1b[:, part * 4:(part + 1) * 4, :].rearrange("p a b -> p (a b)"), in_=stg[i][:, :])
                S.dma("sp", peTf[:], peT_d[X][:, :], [], [("peTf", None)])
                k.ve("dve", "tensor_copy", [("peTf", None)], [("peTb", None)], out=peTb[:], in_=peTf[:])
                S.dma("sp", b1t[:], b1_d[X][:, :], [], [("b1t", None)])
                if X == "k":
                    S.dma("sp", w2f[:], w2k_d[:, :, :], [], [("w2f", None)])
                    k.ve("dve", "tensor_copy", [("w2f", None)], [("w2b", None)], out=w2b[:], in_=w2f[:])
                else:
                    S.dma("sp", w2f[:, :, 0:64], w2v_d[:, :, :], [], [("w2f", None)])
                    k.ve("dve", "tensor_copy", [("w2f", None)], [("w2b", None)], out=w2b[:, :, 0:64], in_=w2f[:, :, 0:64])
                for hc in range(2):
                    for l in range(32):
                        k.mm(pD[:, hc:hc + 1], w1b[0:64, l, hc * 128:(hc + 1) * 128], peTb[0:64, l:l + 1], l == 0, l == 31,
                             [("w1b", None), ("peTb", None)], [("pD", None)])
                k.ve("dve", "tensor_tensor", [("pD", None), ("b1t", None)], [("hbias", None)], out=hbias[:], in0=pD[:, 0:2],
                     in1=b1t[:], op=ALU.add)
                for kv in range(2):
                    rows = slice(kv * 64, kv * 64 + 64)
                    for hc in range(2):
                        pp, pn = (pA, "pA") if hc == 0 else (pB, "pB")
                        for l in range(32):
                            rhs = srcT[rows, l:l + 8192].rearrange("q (c s) -> q c s", s=16)[:, :, 0]
                            k.mm(pp[:, :], w1b[rows, l, hc * 128:(hc + 1) * 128], rhs, l == 0, l == 31,
                                 [("w1b", None), (sn, None)], [(pn, None)])
                        k.act(zt[:], pp[:, :], AF.Identity, [(pn, None), ("hbias", None)], [("zt", None)], bias=hbias[:, hc:hc + 1])
                        k.ve("dve", "tensor_tensor", [("zt", None)], [("z2t", None)], out=z2t[:], in0=zt[:], in1=zt[:], op=ALU.mult)
                        k.ve("dve", "tensor_scalar", [("z2t", None)], [("z2t", None)], out=z2t[:], in0=z2t[:], scalar1=0.044715,
                             scalar2=1.0, op0=ALU.mult, op1=ALU.add)
                        k.ve("pool", "tensor_tensor", [("z2t", None), ("zt", None)], [("z2t", None)], out=z2t[:], in0=z2t[:],
                             in1=zt[:], op=ALU.mult)
                        k.act(z2t[:], z2t[:], AF.Sigmoid, [("z2t", None)], [("z2t", None)], scale=1.5957691216057308)
                        k.ve("dve", "tensor_tensor", [("z2t", None), ("zt", None)], [("ge", hc)], out=ge[:, hc, :], in0=z2t[:],
                             in1=zt[:], op=ALU.mult)
                    if X == "k":
                        for hc in range(2):
                            k.mm(pC[:, :], w2b[:, hc, :], ge[:, hc, :], hc == 0, hc == 1, [("w2b", None), ("ge", None)],
                                 [("pC", None)])
                        k.ve("dve", "tensor_copy", [("pC", None)], [("KcT", kv)], out=KcT[rows, :], in_=pC[rows, :])
                    else:
                        for ct in range(4):
                            for hc in range(2):
                                k.mm(pC[:, 0:64], ge[:, hc, ct * 128:(ct + 1) * 128], w2b[:, hc, 0:64], hc == 0, hc == 1,
                                     [("w2b", None), ("ge", None)], [("pC", None)])
                            k.ve("dve", "tensor_scalar", [("pC", None), ("ovlm", None)], [("Vca", None)],
                                 out=Vca[:, ct, kv, 0:64], in0=pC[:, 0:64], scalar1=ovlm[:, ct, 0:1], scalar2=None, op0=ALU.mult)
                            k.ve("pool", "tensor_copy", [("ovlm", None)], [("Vca", None)], out=Vca[:, ct, kv, 64:65],
                                 in_=ovlm[:, ct, 0:1])
        S.fence()
        en1.close()
        S.fence()

        MBs = TN("MBs", [128, 8, 14, 128], BF16)
        MBw4 = TN("MBw4", [128, 8, 128], BF16)
        MBc = TN("MBc", [128, 8, 8, 128], BF16)
        Ebig = TN("Ebig", [128, 64, 128], BF16)
        for c8 in range(8):
            i = c8 % 2
            S.dma("sp" if i == 0 else "pool", stg[i][:, :], Ebig_d[:, c8 * 8:(c8 + 1) * 8, :].rearrange("p a b -> p (a b)"),
                  [], [("stg%d" % i, None)])
            k.ve("dve" if i == 0 else "pool", "tensor_copy", [("stg%d" % i, None)], [("Ebig", c8)],
                 out=Ebig[:, c8 * 8:(c8 + 1) * 8, :].rearrange("p a b -> p (a b)"), in_=stg[i][:, :])
        with nc.sbuf_tensor("n_relb", [32, 8, 128], F32) as relb, \
                nc.sbuf_tensor("n_OH0", [32, 512], F32) as OH0, \
                nc.sbuf_tensor("n_OH1", [32, 512], F32) as OH1, \
                nc.sbuf_tensor("n_Fv0", [128, 512], BF16) as Fv0, \
                nc.sbuf_tensor("n_Fv1", [128, 512], BF16) as Fv1:
            S.dma("sp", relb[:], relb_d[:, :, :], [], [("relb", None)])
            k.act(relb[:].rearrange("p a b -> p (a b)"), relb[:].rearrange("p a b -> p (a b)"), AF.Exp, [("relb", None)],
                  [("relb", None)])
            oi = 0
            for (OHd, L, scr, sname) in ((OHa_d, 2048, scr_a, "scr_a"), (OHw_d, 256, scr_w, "scr_w"), (OHc_d, 6144, scr_c, "scr_c")):
                for h in range(8):
                    for c0 in range(0, L, 512):
                        c1 = min(L, c0 + 512)
                        oi += 1
                        OH, OHn = (OH0, "OH0") if oi % 2 else (OH1, "OH1")
                        Fv, Fvn = (Fv0, "Fv0") if oi % 2 else (Fv1, "Fv1")
                        pp, pn = (pA, "pA") if oi % 2 else (pB, "pB")
                        S.dma("sp", OH[:, 0:c1 - c0], OHd[:, c0:c1], [], [(OHn, None)])
                        k.mm(pp[:, 0:c1 - c0], relb[:, h, :], OH[:, 0:c1 - c0], True, True, [("relb", None), (OHn, None)],
                             [(pn, None)])
                        if oi % 2:
                            k.act(Fv[:, 0:c1 - c0], pp[:, 0:c1 - c0], AF.Copy, [(pn, None)], [(Fvn, None)])
                        else:
                            k.ve("dve", "tensor_copy", [(pn, None)], [(Fvn, None)], out=Fv[:, 0:c1 - c0], in_=pp[:, 0:c1 - c0])
                        S.dma("pool", scr.ap()[h, :, c0:c1], Fv[:, 0:c1 - c0], [(Fvn, None)], [(sname, h)])
            for h in range(8):
                src = bass.AP(scr_a, h * 128 * 2048 + 127, [[2047, 128], [128, 14], [1, 128]])
                S.dma("pool", MBs[:, h, :, :], src, [("scr_a", h)], [("MBs", h)])
                src = bass.AP(scr_w, h * 128 * 256 + 127, [[255, 128], [1, 128]])
                S.dma("pool", MBw4[:, h, :], src, [("scr_w", h)], [("MBw4", h)])
                src = bass.AP(scr_c, h * 128 * 6144 + 128 * 3 + 2033, [[6144 - 16, 128], [512, 8], [1, 128]])
                S.dma("pool", MBc[:, h, :, :], src, [("scr_c", h)], [("MBc", h)])
        S.fence()

        S.fence()
        ens.close()
        S.fence()
        en2 = ExitStack()

        def TN2(name, shape, dt):
            return en2.enter_context(nc.sbuf_tensor("n_" + name, shape, dt))

        Pt = [TN2("Pt%d" % i, [128, 4, 128], BF16) for i in range(4)]
        AB_ = []
        for i in range(2):
            AB_.append(dict(
                i=i,
                Eall=TN2("Eall%d" % i, [128, 4, 4, 128], BF16),
                score=TN2("score%d" % i, [128, 128], F32),
                scr2=TN2("scr2%d" % i, [128, 128], F32),
                mx8=TN2("mx8%d" % i, [128, 8], F32),
                thr=TN2("thr%d" % i, [128, 1], F32),
                selb=TN2("selb%d" % i, [128, 128], BF16),
                selbT=TN2("selbT%d" % i, [128, 4, 128], BF16),
                densA=TN2("densA%d" % i, [128, 4], F32),
                coefA=TN2("coefA%d" % i, [128, 4], F32),
                Ccon=TN2("Ccon%d" % i, [128, 4, 64], F32),
            ))
        addm2 = [TN2("addm%d" % i, [128, 128], F32) for i in range(2)]
        dens = TN2("dens", [128, 8], F32)
        coef = TN2("coef", [128, 8], F32)
        Oacc = TN2("Oacc", [128, 8, 64], F32)
        Ob = TN2("Ob", [128, 512], BF16)
        sbanks = ((pA, "pA"), (pB, "pB"), (pF, "pF"), (pG, "pG"))
        sb_i = [0]

        def sbank():
            sb_i[0] += 1
            return sbanks[sb_i[0] % 4], sb_i[0] % 4

        def gen_A(n):
            it, gk = n // 2, n % 2
            Q = 4 * it + 3
            B = AB_[n % 2]
            bi = B["i"]
            nm = lambda s: (s + str(bi), None)
            Eall, score, scr2, mx8, thr, selb, selbT = B["Eall"], B["score"], B["scr2"], B["mx8"], B["thr"], B["selb"], B["selbT"]
            densA, coefA, Ccon = B["densA"], B["coefA"], B["Ccon"]
            addm = addm2[it % 2]
            addn = ("addm%d" % (it % 2), None)
            if gk == 0:
                S.dma("sp", addm[:], addm_d[:, it, :], [], [addn])
            rows = slice(gk * 64, gk * 64 + 64)
            qrhs = QT[rows, it, :, :].rearrange("q a t -> q (a t)")
            hs = slice(4 * gk, 4 * gk + 4)
            nct = min(4, Q // 16 + 1)
            for ct in range(nct):
                pp, pn = (pC, "pC") if ct % 2 == 0 else (pD, "pD")
                k.mm(pp[:, :], KcT[rows, ct * 128:(ct + 1) * 128], qrhs, True, True, [("KcT", None), ("QT", it)], [(pn, None)])
                k.act(Eall[:, ct, :, :].rearrange("p a t -> p (a t)"), pp[:, :], AF.Exp, [(pn, None)], [("Eall%d" % bi, ct)],
                      scale=0.125)
                n_ = min((Q - 16 * ct - 3) // 4, 7)
                k.ve("dve", "tensor_tensor", [("Eall%d" % bi, ct), ("MBc", None)], [("Eall%d" % bi, ct)], out=Eall[:, ct, :, :],
                     in0=Eall[:, ct, :, :], in1=MBc[:, hs, n_, :], op=ALU.mult)
                yield
            k.ve("dve", "memset", [], [("pC", None)], pC[:, 0:260], 0.0)
            k.ve("dve", "memset", [], [("pD", None)], pD[:, :], 0.0)
            for a in range(4):
                for ct in range(nct):
                    k.mm(pC[:, a * 65:(a + 1) * 65], Eall[:, ct, a, :], Vca[:, ct, gk, :], False, ct == nct - 1,
                         [("Eall%d" % bi, ct), ("Vca", None)], [("pC", None)])
                for ct in range(nct):
                    k.mm(pD[:, a * 128:(a + 1) * 128], Eall[:, ct, a, :], Ovl[:, ct, :], False, ct == nct - 1,
                         [("Eall%d" % bi, ct), ("Ovl", None)], [("pD", None)])
            yield
            k.ve("dve", "tensor_scalar", [("pC", None)], [nm("densA")], out=densA[:, 0:4],
                 in0=pC[:, 0:260].rearrange("p (a d) -> p a d", a=4)[:, :, 64], scalar1=1e-30, scalar2=None, op0=ALU.max)
            yield
            k.ve("dve", "reciprocal", [nm("densA")], [nm("densA")], out=densA[:, 0:4], in_=densA[:, 0:4])
            k.ve("pool", "tensor_copy", [addn], [nm("score")], out=score[:], in_=addm[:])
            yield
            for a in range(4):
                k.ve("dve", "scalar_tensor_tensor", [("pD", None), nm("densA"), nm("score")], [nm("score")],
                     out=score[:], in0=pD[:, a * 128:(a + 1) * 128], scalar=densA[:, a:a + 1], in1=score[:], op0=ALU.mult,
                     op1=ALU.add)
                yield
            k.ve("dve", "max", [nm("score")], [nm("mx8")], out=mx8[:], in_=score[:])
            yield
            k.ve("dve", "match_replace", [nm("score"), nm("mx8")], [nm("scr2")], out=scr2[:],
                 in_to_replace=mx8[:], in_values=score[:], imm_value=-3.0e9)
            yield
            k.ve("dve", "max", [nm("scr2")], [nm("mx8")], out=mx8[:], in_=scr2[:])
            yield
            k.ve("dve", "tensor_reduce", [nm("mx8")], [nm("thr")], out=thr[:, 0:1], in_=mx8[:], axis=AX.X, op=ALU.min)
            yield
            k.ve("dve", "tensor_scalar", [nm("score"), nm("thr")], [nm("selb")], out=selb[:], in0=score[:],
                 scalar1=thr[:, 0:1], scalar2=-30000.0, op0=ALU.is_lt, op1=ALU.mult)
            yield
            for a in range(4):
                k.tr(pT[:, a * 128:(a + 1) * 128], selb[:], ident[:], [nm("selb")], [("pT", a)])
            k.ve("dve", "tensor_copy", [("pT", None)], [nm("selbT")], out=selbT[:].rearrange("p a t -> p (a t)"),
                 in_=pT[:, 0:512])
            yield
            k.ve("dve", "tensor_tensor", [nm("densA"), ("GATES", it)], [nm("coefA")], out=coefA[:, 0:4], in0=densA[:, 0:4],
                 in1=GATES[:, it, :].rearrange("p (h j) -> p h j", j=3)[:, hs, 0], op=ALU.mult)
            yield
            for a in range(4):
                k.ve("dve", "tensor_scalar", [("pC", None), nm("coefA")], [nm("Ccon")], out=Ccon[:, a, :],
                     in0=pC[:, a * 65:a * 65 + 64], scalar1=coefA[:, a:a + 1], scalar2=None, op0=ALU.mult)
            yield

        def gen_B(n):
            it, gk = n // 2, n % 2
            Q = 4 * it + 3
            B = AB_[n % 2]
            bi = B["i"]
            selbT, Ccon = B["selbT"], B["Ccon"]
            rows = slice(gk * 64, gk * 64 + 64)
            qrhs = QT[rows, it, :, :].rearrange("q a t -> q (a t)")
            hs = slice(4 * gk, 4 * gk + 4)
            for bq, br in enumerate((1, 0)):
                if br == 0:
                    kT, kTn, Vt, Vn, kts = kselT, "kselT", Vs, "Vs", list(range(0, Q + 1))
                else:
                    kT, kTn, Vt, Vn, kts = kwinT, "kwinT", Vw, "Vw", list(range(max(Q - 4, 0), Q + 1))
                k.ve("dve", "memset", [], [("pE", None)], pE[:, 0:260], 0.0)
                SK = 3
                nk = len(kts)
                slots = []
                for step in range(nk + SK):
                    if step < nk:
                        kt = kts[step]
                        m = Q - kt
                        (pp, pn), bidx = sbank()
                        slots.append(bidx)
                        k.mm(pp[:, :], kT[rows, kt * 128:(kt + 1) * 128], qrhs, True, br == 1, [(kTn, None), ("QT", it)],
                             [(pn, None)])
                        if br == 0:
                            k.mm(pp[:, :], Ebig[:, kt, :], selbT[:].rearrange("p a t -> p (a t)"), False, True,
                                 [("Ebig", None), ("selbT%d" % bi, None)], [(pn, None)])
                        P = Pt[bidx]
                        Pn = "Pt%d" % bidx
                        k.act(P[:].rearrange("p a t -> p (a t)"), pp[:, :], AF.Exp, [(pn, None)], [(Pn, None)], scale=0.125)
                        if br == 1 and m == 4:
                            mb = MBw4[:, hs, :]
                            mbn = ("MBw4", None)
                        else:
                            mb = MBs[:, hs, min(m, 13), :]
                            mbn = ("MBs", None)
                        k.ve("dve", "tensor_tensor", [(Pn, None), mbn], [(Pn, None)], out=P[:],
                             in0=P[:], in1=mb, op=ALU.mult)
                    if step >= SK:
                        kt = kts[step - SK]
                        bidx = slots[step - SK]
                        P = Pt[bidx]
                        Pn = "Pt%d" % bidx
                        for a in range(4):
                            k.mm(pE[:, a * 65:(a + 1) * 65], P[:, a, :], Vt[:, kt, gk, :], False, kt == kts[-1],
                                 [(Pn, None), (Vn, kt)], [("pE", None)])
                    yield
                c0 = 4 * bq
                k.ve("dve", "tensor_scalar", [("pE", None)], [("dens", None)], out=dens[:, c0:c0 + 4],
                     in0=pE[:, 0:260].rearrange("p (a d) -> p a d", a=4)[:, :, 64], scalar1=1e-30, scalar2=None, op0=ALU.max)
                k.ve("dve", "reciprocal", [("dens", None)], [("dens", None)], out=dens[:, c0:c0 + 4], in_=dens[:, c0:c0 + 4])
                k.ve("dve", "tensor_tensor", [("dens", None), ("GATES", it)], [("coef", None)], out=coef[:, c0:c0 + 4],
                     in0=dens[:, c0:c0 + 4], in1=GATES[:, it, :].rearrange("p (h j) -> p h j", j=3)[:, hs, 1 + br], op=ALU.mult)
                for a in range(4):
                    if bq == 0:
                        in1, in1n = Ccon[:, a, :], ("Ccon%d" % bi, None)
                    else:
                        in1, in1n = Oacc[:, 4 * gk + a, :], ("Oacc", None)
                    k.ve("dve", "scalar_tensor_tensor", [("pE", None), ("coef", None), in1n], [("Oacc", None)],
                         out=Oacc[:, 4 * gk + a, :], in0=pE[:, a * 65:a * 65 + 64], scalar=coef[:, c0 + a:c0 + a + 1],
                         in1=in1, op0=ALU.mult, op1=ALU.add)
                yield
            if gk == 1:
                k.ve("dve", "tensor_copy", [("Oacc", None)], [("Ob", None)], out=Ob[:], in_=Oacc[:].rearrange("p h d -> p (h d)"))
                for c in range(4):
                    k.tr(pT[:, c * 128:(c + 1) * 128], Ob[:, c * 128:(c + 1) * 128], ident[:], [("Ob", None)], [("pT", c)])
                k.ve("dve", "tensor_copy", [("pT", None)], [("oT_n", it)], out=oT_n[:, it, :, :].rearrange("p c t -> p (c t)"),
                     in_=pT[:, 0:512])
                yield

        NN = 2 * NOWN
        doneA, doneB = -1, -1
        nxtA, nxtB = 0, 0
        activeN = []
        while True:
            if nxtA < NN and not any(a[0] == "A" for a in activeN) and nxtA <= doneB + 2:
                activeN.append(("A", nxtA, gen_A(nxtA)))
                nxtA += 1
            if nxtB < NN and not any(a[0] == "B" for a in activeN) and doneA >= nxtB:
                activeN.append(("B", nxtB, gen_B(nxtB)))
                nxtB += 1
            if not activeN:
                break
            for a in list(activeN):
                try:
                    next(a[2])
                except StopIteration:
                    activeN.remove(a)
                    if a[0] == "A":
                        doneA = a[1]
                    else:
                        doneB = a[1]
        assert doneB == NN - 1
        if "onsa" in dbg:
            donsa = nc.dram_tensor("donsa", [128, NOWN * 4 * 128], BF16, kind="ExternalOutput").ap()
            S.dma("sp", donsa[:, :], oT_n[:].rearrange("q a b c -> q (a b c)"), [("oT_n", None)], [])
        S.fence()
        en2.close()
        en.close()
        S.fence()

    if "stop_r" in dbg:
        S.finish()
        return nc
    es_stgc = ExitStack()
    alloc_stg(es_stgc, "c")
    gx = nc.alloc_sbuf_tensor("gx", [128, 8], F32)
    gffn = nc.alloc_sbuf_tensor("gffn", [128, 8], F32)
    gmem = nc.alloc_sbuf_tensor("gmem", [128, 8], F32)
    S.dma("sp", gx[:], gl["g_x"][:, :], [], [("gx", None)])
    S.dma("sp", gffn[:], gl["g_ffn"][:, :], [], [("gffn", None)])
    S.dma("sp", gmem[:], gl["g_mem"][:, :], [], [("gmem", None)])

    gf1 = nc.alloc_sbuf_tensor("gf1", [1, D], F32)
    gfb = nc.alloc_sbuf_tensor("gfb", [128, D], F32)
    S.dma("sp", gf1[:], g_f[:, :], [], [("gf1", None)])
    for h in range(2):
        pp = pA if h == 0 else pB
        k.mm(pp[:, :], ones_f[0:1, :], gf1[0:1, h * 512:(h + 1) * 512], True, True,
             [("ones_f", None), ("gf1", None)], [("pA" if h == 0 else "pB", None)])
        k.ve("dve", "tensor_copy", [("pA" if h == 0 else "pB", None)], [("gfb", None)],
             out=gfb[:, h * 512:(h + 1) * 512], in_=pp[:, :])

    x2all = nc.alloc_sbuf_tensor("x2all", [128, NOWN, D], F32)
    h2T = nc.alloc_sbuf_tensor("h2T", [128, NOWN, 8, 128], BF16)
    hT = nc.alloc_sbuf_tensor("hT", [128, 8, 128], BF16)

    if "omix" not in dbg:
        with nc.sbuf_tensor("w_out_b0", [128, 8, D], BF16) as w_out_b0, \
                nc.sbuf_tensor("xt0", [128, D], F32) as xt0:
            load_w(w_out_b0, "w_out_b0", wl["w_out"], 8, D)
            for it in range(NOWN):
                pos = 4 * it + 3
                S.dma("sp", xt0[:], xs[pos * 128:(pos + 1) * 128, :], [], [("xt0", None)])
                for hf in range(2):
                    pp, pn = (pA, "pA") if hf == 0 else (pB, "pB")
                    for kc in range(8):
                        if kc < 4:
                            lh, ln = oT_n[:, it, kc, :], ("oT_n", it)
                        else:
                            lh, ln = oT_r[:, it, kc - 4, :], ("oT_r", it)
                        k.mm(pp[:, :], lh, w_out_b0[:, kc, hf * 512:(hf + 1) * 512], kc == 0, kc == 7,
                             [ln, ("w_out_b0", kc)], [(pn, None)])
                    k.ve("dve", "tensor_tensor", [(pn, None), ("xt0", None)], [("x2all", it)],
                         out=x2all[:, it, hf * 512:(hf + 1) * 512], in0=pp[:, :], in1=xt0[:, hf * 512:(hf + 1) * 512],
                         op=ALU.add)
        S.fence()
        eo.close()
        S.fence()
    es_c1 = ExitStack()
    kmT = es_c1.enter_context(nc.sbuf_tensor("kmT", [128, 8, 256], BF16))
    Vm = es_c1.enter_context(nc.sbuf_tensor("Vm", [128, 2, D], BF16))
    with nc.sbuf_tensor("w_kv_b", [128, 8, 2 * D], BF16) as w_kv_b, \
            nc.sbuf_tensor("memf", [128, D], F32) as memf, \
            nc.sbuf_tensor("memT", [128, 8, 256], BF16) as memT:
        load_w(w_kv_b, "w_kv_b", wl["w_kv"], 8, 2 * D, gcol=gmem, gname="gmem")
        for mt in range(2):
            S.dma("sp", memf[:], mem[mt * 128:(mt + 1) * 128, :], [], [("memf", None)])
            rmsnorm(memf[:], ("memf", None), hb[:], ("hb", None), sq, ss, "")
            for c in range(8):
                k.tr(pT[:, c * 128:(c + 1) * 128], hb[:, c * 128:(c + 1) * 128], ident[:], [("hb", None)], [("pT", c)])
            k.ve("dve", "tensor_copy", [("pT", None)], [("memT", mt)],
                 out=memT[:, :, mt * 128:(mt + 1) * 128], in_=pT[:, :].rearrange("p (c t) -> p c t", c=8))
        for c in range(8):
            for kc in range(8):
                k.mm(pA[:, 0:256], w_kv_b[:, kc, c * 128:(c + 1) * 128], memT[:, kc, :], kc == 0, kc == 7,
                     [("w_kv_b", kc), ("memT", None)], [("pA", None)])
            k.ve("dve", "tensor_copy", [("pA", None)], [("kmT", c)], out=kmT[:, c, :], in_=pA[:, 0:256])
        for mt in range(2):
            for hf in range(2):
                for kc in range(8):
                    k.mm(pA[:, :], memT[:, kc, mt * 128:(mt + 1) * 128],
                         w_kv_b[:, kc, D + hf * 512:D + (hf + 1) * 512], kc == 0, kc == 7,
                         [("w_kv_b", kc), ("memT", None)], [("pA", None)])
                k.ve("dve", "tensor_copy", [("pA", None)], [("Vm", mt)], out=Vm[:, mt, hf * 512:(hf + 1) * 512],
                     in_=pA[:, :])
    S.fence()
    with nc.sbuf_tensor("w_q_b", [128, 8, D], BF16) as w_q_b, \
            nc.sbuf_tensor("w_o_b", [128, 8, D], BF16) as w_o_b, \
            nc.sbuf_tensor("c_sq1", [128, D], BF16) as c_sq1, \
            nc.sbuf_tensor("c_ss1", [128, 2], F32) as c_ss1, \
            nc.sbuf_tensor("c_hb1", [128, D], BF16) as c_hb1, \
            nc.sbuf_tensor("c_hT1", [128, 8, 128], BF16) as c_hT1, \
            nc.sbuf_tensor("qT0", [128, 8, 128], BF16) as qT0, \
            nc.sbuf_tensor("qT1", [128, 8, 128], BF16) as qT1, \
            nc.sbuf_tensor("PT0", [128, 2, 128], BF16) as PT0, \
            nc.sbuf_tensor("PT1", [128, 2, 128], BF16) as PT1, \
            nc.sbuf_tensor("rden0", [128, 128], F32) as rden0, \
            nc.sbuf_tensor("rden1", [128, 128], F32) as rden1, \
            nc.sbuf_tensor("oT0", [128, 8, 128], BF16) as oT0, \
            nc.sbuf_tensor("oT1", [128, 8, 128], BF16) as oT1:
        load_w(w_q_b, "w_q_b", wl["w_q"], 8, D, gcol=gx, gname="gx")
        load_w(w_o_b, "w_o_b", wl["w_o"], 8, D)
        CS = [dict(i=0, sq=sq, ss=ss, hb=hb, hT=hT, qT=qT0, PT=PT0, rden=rden0, oT=oT0),
              dict(i=1, sq=c_sq1, ss=c_ss1, hb=c_hb1, hT=c_hT1, qT=qT1, PT=PT1, rden=rden1, oT=oT1)]
        cbanks = [(pA, "pA"), (pB, "pB"), (pC, "pC"), (pD, "pD"), (pE, "pE"), (pF, "pF"), (pG, "pG")]
        cb_i = [0]

        def cbank():
            cb_i[0] += 1
            return cbanks[cb_i[0] % 7]

        def c_norm(src, sname, B):
            bi = B["i"]
            k.act(B["sq"][:], src, AF.Square, [sname], [("c_sq%d" % bi, None)])
            k.ve("dve", "reduce_sum", [("c_sq%d" % bi, None)], [("c_ss%d" % bi, None)], out=B["ss"][:, 0:1], in_=B["sq"][:], axis=AX.X)
            k.act(B["ss"][:, 1:2], B["ss"][:, 0:1], AF.Ln, [("c_ss%d" % bi, None)], [("c_ss%d" % bi, None)], bias=epsc[:, 0:1],
                  scale=1.0 / D)
            k.act(B["ss"][:, 1:2], B["ss"][:, 1:2], AF.Exp, [("c_ss%d" % bi, None)], [("c_ss%d" % bi, None)], scale=-0.5)
            k.ve("dve", "tensor_scalar", [sname, ("c_ss%d" % bi, None)], [("c_hb%d" % bi, None)], out=B["hb"][:], in0=src,
                 scalar1=B["ss"][:, 1:2], scalar2=None, op0=ALU.mult)

        def gen_c1(it, B):
            bi = B["i"]
            hbn, hTn, qTn, PTn, rdn, oTn = [("c_%s%d" % (s, bi), None) for s in ("hb", "hT", "qT", "PT", "rden", "oT")]
            c_norm(x2all[:, it, :], ("x2all", it), B)
            yield
            for c in range(8):
                k.tr(pT[:, c * 128:(c + 1) * 128], B["hb"][:, c * 128:(c + 1) * 128], ident[:], [hbn], [("pT", c)])
            k.ve("dve", "tensor_copy", [("pT", None)], [hTn], out=B["hT"][:].rearrange("p c t -> p (c t)"), in_=pT[:, :])
            yield
            for c in range(8):
                pp, pn = cbank()
                for kc in range(8):
                    k.mm(pp[:, 0:128], w_q_b[:, kc, c * 128:(c + 1) * 128], B["hT"][:, kc, :], kc == 0, kc == 7,
                         [("w_q_b", kc), hTn], [(pn, None)])
                k.act(B["qT"][:, c, :], pp[:, 0:128], AF.Copy, [(pn, None)], [("c_qT%d" % bi, c)])
                if c % 2 == 1:
                    yield
            for h in range(4):
                pp, pn = cbank()
                for mc in range(2):
                    for dc in range(2):
                        k.mm(pp[:, mc * 128:(mc + 1) * 128], kmT[:, 2 * h + dc, mc * 128:(mc + 1) * 128],
                             B["qT"][:, 2 * h + dc, :], dc == 0, dc == 1, [("kmT", None), qTn], [(pn, None)])
                k.act(B["PT"][:].rearrange("p c t -> p (c t)"), pp[:, 0:256], AF.Exp, [(pn, None)], [PTn], scale=1.0 / 16.0)
                yield
                pp, pn = cbank()
                for mc in range(2):
                    k.mm(pp[:, 0:128], ones_b[:, :], B["PT"][:, mc, :], mc == 0, mc == 1, [("ones_b", None), PTn], [(pn, None)])
                k.ve("dve", "reciprocal", [(pn, None)], [rdn], out=B["rden"][:], in_=pp[:, 0:128])
                for dc in range(2):
                    pp, pn = cbank()
                    for mc in range(2):
                        k.mm(pp[:, 0:128], Vm[:, mc, h * 256 + dc * 128:h * 256 + (dc + 1) * 128], B["PT"][:, mc, :], mc == 0,
                             mc == 1, [("Vm", None), PTn], [(pn, None)])
                    k.ve("dve", "tensor_tensor", [(pn, None), rdn], [("c_oT%d" % bi, 2 * h + dc)], out=B["oT"][:, 2 * h + dc, :],
                         in0=pp[:, 0:128], in1=B["rden"][:], op=ALU.mult)
                yield
            for hf in range(2):
                pp, pn = cbank()
                for kc in range(8):
                    k.mm(pp[:, :], B["oT"][:, kc, :], w_o_b[:, kc, hf * 512:(hf + 1) * 512], kc == 0, kc == 7,
                         [oTn, ("w_o_b", kc)], [(pn, None)])
                k.ve("dve", "tensor_tensor", [(pn, None), ("x2all", it)], [("x2all", it)],
                     out=x2all[:, it, hf * 512:(hf + 1) * 512], in0=pp[:, :], in1=x2all[:, it, hf * 512:(hf + 1) * 512],
                     op=ALU.add)
                yield
            c_norm(x2all[:, it, :], ("x2all", it), B)
            yield
            for c in range(8):
                k.tr(pT[:, c * 128:(c + 1) * 128], B["hb"][:, c * 128:(c + 1) * 128], ident[:], [hbn], [("pT", c)])
            k.ve("dve", "tensor_copy", [("pT", None)], [("h2T", it)], out=h2T[:, it, :, :].rearrange("p c t -> p (c t)"),
                 in_=pT[:, :])
            yield

        actC = []
        nxtC = 0
        while True:
            while nxtC < NOWN and len(actC) < 2:
                used = {a[0] for a in actC}
                bsi = 0 if 0 not in used else 1
                actC.append((bsi, gen_c1(nxtC, CS[bsi])))
                nxtC += 1
            if not actC:
                break
            for a in list(actC):
                try:
                    next(a[1])
                except StopIteration:
                    actC.remove(a)

    es_c1.close()
    eo.close()
    S.fence()
    with nc.sbuf_tensor("wg_b", [128, 8, 1408], BF16) as wg_b, \
            nc.sbuf_tensor("wu_b", [128, 8, 1408], BF16) as wu_b, \
            nc.sbuf_tensor("wd_b", [128, 11, D], BF16) as wd_b, \
            nc.sbuf_tensor("sg", [128, 128], F32) as sg, \
            nc.sbuf_tensor("sg2", [128, 128], F32) as sg2, \
            nc.sbuf_tensor("aT", [128, 11, 128], BF16) as aT, \
            nc.sbuf_tensor("yo", [128, D], F32) as yo:
        for half in range(2):
            load_w(wg_b, "wg_b", wl["w_gate"], 8, 1408, gcol=gffn, gname="gffn", c_lo=half * 1408)
            load_w(wu_b, "wu_b", wl["w_up"], 8, 1408, gcol=gffn, gname="gffn", c_lo=half * 1408)
            load_w(wd_b, "wd_b", wl["w_down"], 11, D, kc_lo=half * 11)
            for it in range(NOWN):
                for fc in range(11):
                    pg, pgn = (pC, "pC") if fc % 2 == 0 else (pE, "pE")
                    pu, pun = (pD, "pD") if fc % 2 == 0 else (pF, "pF")
                    sgt, sgn = (sg, "sg") if fc % 2 == 0 else (sg2, "sg2")
                    for kc in range(8):
                        k.mm(pg[:, 0:128], wg_b[:, kc, fc * 128:(fc + 1) * 128], h2T[:, it, kc, :], kc == 0, kc == 7,
                             [("wg_b", kc), ("h2T", it)], [(pgn, None)])
                    for kc in range(8):
                        k.mm(pu[:, 0:128], wu_b[:, kc, fc * 128:(fc + 1) * 128], h2T[:, it, kc, :], kc == 0, kc == 7,
                             [("wu_b", kc), ("h2T", it)], [(pun, None)])
                    k.act(sgt[:], pg[:, 0:128], AF.Silu, [(pgn, None)], [(sgn, None)])
                    k.ve("dve", "tensor_tensor", [(sgn, None), (pun, None)], [("aT", fc)], out=aT[:, fc, :], in0=sgt[:],
                         in1=pu[:, 0:128], op=ALU.mult)
                for hf in range(2):
                    pp, pn = (pA, "pA") if hf == 0 else (pB, "pB")
                    for fc in range(11):
                        k.mm(pp[:, :], aT[:, fc, :], wd_b[:, fc, hf * 512:(hf + 1) * 512], fc == 0, fc == 10,
                             [("aT", None), ("wd_b", fc)], [(pn, None)])
                    k.ve("dve", "tensor_tensor", [(pn, None), ("x2all", it)], [("x2all", it)],
                         out=x2all[:, it, hf * 512:(hf + 1) * 512], in0=pp[:, :],
                         in1=x2all[:, it, hf * 512:(hf + 1) * 512], op=ALU.add)
                if half == 1:
                    k.act(sq[:], x2all[:, it, :], AF.Square, [("x2all", it)], [("sq", None)])
                    k.ve("dve", "reduce_sum", [("sq", None)], [("ss", None)], out=ss[:, 0:1], in_=sq[:], axis=AX.X)
                    k.act(ss[:, 1:2], ss[:, 0:1], AF.Ln, [("ss", None)], [("ss", None)], bias=epsc[:, 0:1], scale=1.0 / D)
                    k.act(ss[:, 1:2], ss[:, 1:2], AF.Exp, [("ss", None)], [("ss", None)], scale=-0.5)
                    k.ve("dve", "tensor_scalar", [("x2all", it), ("ss", None)], [("yo", None)], out=yo[:],
                         in0=x2all[:, it, :], scalar1=ss[:, 1:2], scalar2=None, op0=ALU.mult)
                    k.ve("dve", "tensor_tensor", [("yo", None), ("gfb", None)], [("yo", None)], out=yo[:], in0=yo[:],
                         in1=gfb[:], op=ALU.mult)
                    S.dma("sp", y[it * 128:(it + 1) * 128, :], yo[:], [("yo", None)], [])
    S.finish()
    return nc


def _lay_w(w, kc):
    return np.ascontiguousarray(w.reshape(kc, 128, w.shape[1]).transpose(1, 0, 2))


def _lay_v(v):
    return np.ascontiguousarray(v.reshape(-1, 128).T)


NSA_COLS = 512 + 6 * 128 + 24


def _rwkv_consts():
    f = np.float32
    r = np.arange(128)
    t = np.arange(512)
    rst = np.broadcast_to((t % 64 != 0).astype(f), (128, 512))
    same = (r[:, None] // 64) == (r[None, :] // 64)
    mus = (same & ((r[:, None] % 64) < (r[None, :] % 64))).astype(f)
    mui = (same & ((r[:, None] % 64) <= (r[None, :] % 64))).astype(f)
    mls = (same & ((r[:, None] % 64) > (r[None, :] % 64))).astype(f)
    ma = np.concatenate([mus, mui, mus, mui], 1)
    ml4 = np.concatenate([mls] * 4, 1)
    i4 = np.concatenate([np.eye(128, dtype=f)] * 4, 1)
    return np.ascontiguousarray(np.stack([rst, ma, ml4, i4], 1)), same.astype(f)


def _rwkv_inputs(inp):
    f = np.float32
    g = lambda n: np.asarray(inp[n], f)[0]
    w_in = g("w_in")
    rm, bo = _rwkv_consts()
    lnw = g("rwkv_lnx_w").reshape(4, 2, 64)
    lnb = g("rwkv_lnx_b").reshape(4, 2, 64)
    rows_h = np.arange(128) // 64
    return {
        "w_in_r": _lay_w(w_in[:, NSA_COLS:NSA_COLS + 1792], 8),
        "g_mix": _lay_v(g("norm_mix_g")),
        "mu": _lay_v(g("rwkv_mu")),
        "rvecs": np.ascontiguousarray(np.stack([_lay_v(g("rwkv_w0")), _lay_v(g("rwkv_a0")), _lay_v(g("rwkv_k_k")),
                                                _lay_v(g("rwkv_k_a")), _lay_v(g("rwkv_r_k").reshape(-1))], 1)),
        "lora_up": np.ascontiguousarray(np.concatenate([g("rwkv_w_up"), g("rwkv_a_up")], 0)),
        "g_up": np.ascontiguousarray(g("rwkv_g_up")),
        "lnw": np.ascontiguousarray(lnw[:, rows_h, :].transpose(1, 0, 2)),
        "lnb": np.ascontiguousarray(lnb[:, rows_h, :].transpose(1, 0, 2)),
        "rmasks": rm,
        "blockones": bo,
    }


def _t5_bucket(n):
    n = np.asarray(n)
    nf = np.maximum(n, 1).astype(np.float32)
    large = 16 + (np.log(nf / np.float32(16)) / np.float32(math.log(2048 / 16)) * np.float32(16)).astype(np.int32)
    return np.where(n < 16, n, np.minimum(large, 31))


def _nsa_consts():
    f = np.float32
    def onehot(delta, valid):
        b = _t5_bucket(np.maximum(delta, 0))
        oh = np.zeros((32, delta.shape[0]), f)
        idx = np.nonzero(valid)[0]
        oh[b[idx], idx] = 1.0
        return oh
    u = np.arange(2048)
    OHa = onehot(u - 127, (u - 127) >= 0)
    u = np.arange(256)
    OHw = onehot(512 + u - 127, (u - 127) < 0)
    u = np.arange(6144)
    OHc = onehot(u - 2064, (u - 2064) >= 0)
    jj = np.arange(128)[:, None, None]
    kt = np.arange(64)[None, :, None]
    kk = np.arange(128)[None, None, :]
    Ebig = (jj == 2 * kt + kk // 64).astype(f)
    return OHa, OHw, OHc, np.ascontiguousarray(Ebig)


def _nsa_inputs(inp, j):
    f = np.float32
    g = lambda n: np.asarray(inp[n], f)[0]
    w_in = g("w_in")
    nd = (3 - j) * 128
    qperm = np.concatenate([np.r_[a * 64:(a + 1) * 64, (4 + a) * 64:(5 + a) * 64] for a in range(4)])
    cols = np.concatenate([qperm, np.arange(512, 640), np.arange(640, 768), np.arange(768, 896), np.arange(1024, 1152),
                           np.arange(896, 1024), np.arange(1152, 1280), np.arange(1280, 1304)])
    OHa, OHw, OHc, Ebig = _nsa_consts()
    def w1lay(w):
        a = w.reshape(32, 64, 256).transpose(1, 0, 2)
        return np.ascontiguousarray(np.concatenate([a, a], 0))
    def pelay(pe):
        a = pe.T
        return np.ascontiguousarray(np.concatenate([a, a], 0))
    w2k = g("cmp_k_w2").reshape(2, 128, 64).transpose(1, 0, 2)
    w2v = g("cmp_v_w2").reshape(2, 128, 64).transpose(1, 0, 2)
    tpos = np.arange(64)[None, :] * 128 + np.arange(128)[:, None]
    realtok = (tpos >= nd).astype(f)
    c = np.arange(4)[None, :] * 128 + np.arange(128)[:, None]
    realblk = ((16 * c >= nd) & (c <= 510)).astype(f)
    jb = np.arange(128)[None, None, :]
    ov = ((16 * c[:, :, None] <= 64 * jb + 63) & (16 * c[:, :, None] + 31 >= 64 * jb)).astype(f)
    ovlm = np.concatenate([realblk[:, :, None], ov * realblk[:, :, None]], 2)
    it = np.arange(16)[None, :, None]
    q = np.arange(128)[:, None, None]
    t = 128 * (4 * it + 3) + q
    cur = t // 64
    blk0 = nd // 64
    bad = (jb < blk0) | (64 * jb > t)
    forced = (jb == blk0) | (jb == cur) | (jb == cur - 1)
    addmask = np.where(bad, -1e9, np.where(forced, 1e4, 0.0)).astype(f)
    return {
        "w_in_n": _lay_w(np.ascontiguousarray(w_in[:, cols]), 8),
        "g_mix": _lay_v(g("norm_mix_g")),
        "gate_b": np.ascontiguousarray(np.broadcast_to(g("nsa_gate_b")[None, :], (128, 24))),
        "realtok": np.ascontiguousarray(realtok),
        "cmp_w1k": w1lay(g("cmp_k_w1")), "cmp_w1v": w1lay(g("cmp_v_w1")),
        "cmp_peTk": pelay(g("cmp_pe_k")), "cmp_peTv": pelay(g("cmp_pe_v")),
        "cmp_b1k": _lay_v(g("cmp_k_b1")), "cmp_b1v": _lay_v(g("cmp_v_b1")),
        "cmp_w2k": np.ascontiguousarray(np.concatenate([w2k, w2k], 2)),
        "cmp_w2v": np.ascontiguousarray(w2v),
        "ovlm": np.ascontiguousarray(ovlm.astype(f)),
        "relb_rep": np.ascontiguousarray(np.broadcast_to(np.asarray(inp["rel_bias"], f)[:, :, None], (32, 8, 128))),
        "OHa": OHa, "OHw": OHw, "OHc": OHc, "Ebig": Ebig,
        "addmask": np.ascontiguousarray(addmask),
    }


def _core_inputs(c, inp):
    b, j = c // 4, c % 4
    f = np.float32
    x = np.asarray(inp["x"], f)[b]
    nd = (3 - j) * 128
    xs = np.zeros((8192, D), f)
    xs[nd:] = x[: 8192 - nd]
    m = {
        "xs": xs,
        "ident": np.eye(128, dtype=f),
        "g_f": np.asarray(inp["norm_f_g"], f).reshape(1, D),
        "g_x": _lay_v(np.asarray(inp["norm_x_g"], f)[0]),
        "g_ffn": _lay_v(np.asarray(inp["norm_ffn_g"], f)[0]),
        "g_mem": _lay_v(np.asarray(inp["norm_mem_g"], f)[0]),
        "mem": np.ascontiguousarray(np.asarray(inp["mem"], f)[b]),
        "w_out": _lay_w(np.asarray(inp["w_out"], f)[0], 8),
        "w_q": _lay_w(np.asarray(inp["w_q_x"], f)[0], 8),
        "w_kv": _lay_w(np.asarray(inp["w_kv_x"], f)[0], 8),
        "w_o": _lay_w(np.asarray(inp["w_o_x"], f)[0], 8),
        "w_gate": _lay_w(np.asarray(inp["w_gate"], f)[0], 8),
        "w_up": _lay_w(np.asarray(inp["w_up"], f)[0], 8),
        "w_down": _lay_w(np.asarray(inp["w_down"], f)[0], 22),
    }
    return m


def kernel(**inp):
    nc = build_program(dbg={"rwkv": 1, "nsa": 1})
    rin = _rwkv_inputs(inp)
    in_maps = []
    for c in range(8):
        m = _core_inputs(c, inp)
        m.update(rin)
        m.update(_nsa_inputs(inp, c % 4))
        in_maps.append(m)
    res = run_bass_kernel_spmd(nc, in_maps, core_ids=list(range(8)))
    out = np.zeros((2, 8192, D), np.float32)
    for c in range(8):
        b, j = c // 4, c % 4
        yv = np.asarray(res.results[c]["y"]).reshape(NOWN, 128, D)
        out[b].reshape(64, 128, D)[j::4] = yv
    return out
```

```python
import math
import numpy as np
from contextlib import ExitStack
import concourse.bass as bass
import concourse.mybir as mybir
from concourse.bass_utils import run_bass_kernel_spmd

F32 = mybir.dt.float32
BF16 = mybir.dt.bfloat16
AF = mybir.ActivationFunctionType
ALU = mybir.AluOpType
AX = mybir.AxisListType

D = 1024
DFF = 2816
NOWN = 16
RMS_EPS = 1e-6


class Sync:
    EPOCH = 24000
    NDMA = 12

    def __init__(self, nc):
        self.nc = nc
        self.eng = {"pe": nc.tensor, "act": nc.scalar, "dve": nc.vector, "pool": nc.gpsimd, "sp": nc.sync}
        self.cnt = {e: 0 for e in self.eng}
        self.sems = {}
        self.known = {e: {} for e in self.eng}
        self.vc = {}
        self.dma_sems = {}
        self.dma_val = {}
        self.dma_rr = {}
        self.state = {}

    def _sem(self, e, ep):
        k = (e, ep)
        if k not in self.sems:
            self.sems[k] = self.nc.alloc_semaphore("s_%s_%d" % (e, ep))
        return self.sems[k]

    def _wait(self, e, tok):
        kn = self.known[e]
        if tok[0] == "c":
            _, f, n = tok
            if f == "pe" and e == "pe":
                return
            if kn.get(f, 0) >= n:
                return
            ep = (n - 1) // self.EPOCH
            self.eng[e].wait_ge(self._sem(f, ep), n - ep * self.EPOCH)
            kn[f] = n
        else:
            _, q, slot, v = tok
            key = ("d", q, slot)
            if kn.get(key, 0) >= v:
                return
            self.eng[e].wait_ge(self.dma_sems[(q, slot)], v)
            kn[key] = v
        snap = self.vc.get(tok)
        if snap:
            for k2, v2 in snap.items():
                if kn.get(k2, 0) < v2:
                    kn[k2] = v2

    @staticmethod
    def _rkey(tok):
        return tok[1] if tok[0] == "c" else (tok[1], tok[2])

    def _deps(self, reads, writes):
        deps = []
        for (name, sub) in reads:
            st = self.state.get(name)
            if st:
                for s2, ent in st.items():
                    if (sub is None or s2 is None or s2 == sub) and ent[0] is not None:
                        deps.append(ent[0])
        for (name, sub) in writes:
            st = self.state.get(name)
            if st:
                for s2, ent in st.items():
                    if sub is None or s2 is None or s2 == sub:
                        if ent[0] is not None:
                            deps.append(ent[0])
                        deps.extend(ent[1].values())
        return deps

    def _update(self, tok, reads, writes):
        rk = self._rkey(tok)
        for (name, sub) in reads:
            st = self.state.setdefault(name, {})
            ent = st.get(sub)
            if ent is None:
                ent = st[sub] = [None, {}]
            ent[1][rk] = tok
        for (name, sub) in writes:
            st = self.state.setdefault(name, {})
            if sub is None:
                st.clear()
            st[sub] = [tok, {}]

    PSUM = ("pA", "pB", "pC", "pD", "pE", "pF", "pG", "pT", "pJ0", "pJ1", "pM0", "pM1", "pS0", "pS1", "pS2")

    def op(self, e, fn, reads=(), writes=()):
        if e != "pe":
            writes = list(writes) + [(r[0], None) for r in reads if r[0] in self.PSUM]
        for d in self._deps(reads, writes):
            self._wait(e, d)
        ins = fn()
        self.cnt[e] += 1
        n = self.cnt[e]
        ep = (n - 1) // self.EPOCH
        ins.then_inc(self._sem(e, ep), 1)
        tok = ("c", e, n)
        self.vc[tok] = dict(self.known[e])
        self._update(tok, reads, writes)
        return tok

    def dma(self, q, out, in_, reads=(), writes=()):
        slot = self.dma_rr.get(q, 0) % self.NDMA
        self.dma_rr[q] = self.dma_rr.get(q, 0) + 1
        k = (q, slot)
        if k not in self.dma_sems:
            self.dma_sems[k] = self.nc.alloc_semaphore("d_%s_%d" % (q, slot))
            self.dma_val[k] = 0
        if self.dma_val[k] > 0:
            self._wait(q, ("d", q, slot, self.dma_val[k]))
        for d in self._deps(reads, writes):
            self._wait(q, d)
        ins = self.eng[q].dma_start(out=out, in_=in_)
        self.dma_val[k] += 16
        ins.then_inc(self.dma_sems[k], 16)
        tok = ("d", q, slot, self.dma_val[k])
        self.vc[tok] = dict(self.known[q])
        self._update(tok, reads, writes)
        return tok

    def fence(self):
        for e in self.eng:
            for f in self.eng:
                if self.cnt[f] > 0:
                    self._wait(e, ("c", f, self.cnt[f])) if not (e == "pe" and f == "pe") else None
            for (q, slot), v in list(self.dma_val.items()):
                if v > 0:
                    self._wait(e, ("d", q, slot, v))

    def finish(self):
        for (q, slot), v in list(self.dma_val.items()):
            if v > 0:
                self._wait(q, ("d", q, slot, v))


class K:
    def __init__(self, nc):
        self.nc = nc
        self.S = Sync(nc)
        self.rr = 0

    def mm(self, out, lhsT, rhs, start, stop, r, w):
        nc = self.nc
        return self.S.op("pe", lambda: nc.tensor.matmul(out, lhsT=lhsT, rhs=rhs, start=start, stop=stop), r, w)

    def tr(self, out, in_, ident, r, w):
        nc = self.nc
        return self.S.op("pe", lambda: nc.tensor.transpose(out, in_, ident), r + [("ident", None)], w)

    def act(self, out, in_, func, r, w, bias=None, scale=None):
        nc = self.nc
        kw = {}
        if bias is not None:
            kw["bias"] = bias
        if scale is not None:
            kw["scale"] = scale
        return self.S.op("act", lambda: nc.scalar.activation(out=out, in_=in_, func=func, **kw), r, w)

    def ve(self, e, name, r, w, *a, **kw):
        eng = self.S.eng[e]
        return self.S.op(e, lambda: getattr(eng, name)(*a, **kw), r, w)

    def alt(self):
        self.rr += 1
        return "dve" if self.rr % 2 else "pool"


def build_program(dbg=None):
    dbg = dbg or {}
    nc = bass.Bass("TRN2", target_bir_lowering=False)
    k = K(nc)
    S = k.S

    def din(name, shape, dt=F32):
        return nc.dram_tensor(name, list(shape), dt, kind="ExternalInput").ap()

    xs = din("xs", [8192, D])
    ident_d = din("ident", [128, 128])
    g_f = din("g_f", [1, D])
    gl = {n: din(n, [128, 8]) for n in ["g_x", "g_ffn", "g_mem"]}
    mem = din("mem", [256, D])
    wl = {
        "w_out": din("w_out", [128, 8, D]),
        "w_q": din("w_q", [128, 8, D]),
        "w_kv": din("w_kv", [128, 8, 2 * D]),
        "w_o": din("w_o", [128, 8, D]),
        "w_gate": din("w_gate", [128, 8, DFF]),
        "w_up": din("w_up", [128, 8, DFF]),
        "w_down": din("w_down", [128, 22, D]),
    }
    y = nc.dram_tensor("y", [NOWN * 128, D], F32, kind="ExternalOutput").ap()
    if "omix" in dbg:
        omix_d = din("omix", [NOWN * 128, D])
        dx1 = nc.dram_tensor("dx1", [NOWN * 128, D], F32, kind="ExternalOutput").ap()
        dx2 = nc.dram_tensor("dx2", [NOWN * 128, D], F32, kind="ExternalOutput").ap()
        dq = nc.dram_tensor("dq", [128, 1024], BF16, kind="ExternalOutput").ap()
        dk = nc.dram_tensor("dk", [128, 2048], BF16, kind="ExternalOutput").ap()
        dP = nc.dram_tensor("dP", [128, 256], BF16, kind="ExternalOutput").ap()
        dr = nc.dram_tensor("dr", [128, 128], F32, kind="ExternalOutput").ap()
        dh = nc.dram_tensor("dh", [128, 1024], BF16, kind="ExternalOutput").ap()
        dV = nc.dram_tensor("dV", [128, 2048], BF16, kind="ExternalOutput").ap()
        do = nc.dram_tensor("do", [128, 1024], BF16, kind="ExternalOutput").ap()

    ident = nc.alloc_sbuf_tensor("ident_b", [128, 128], BF16)
    identf = nc.alloc_sbuf_tensor("ident_f", [128, 128], F32)
    S.dma("sp", identf[:], ident_d[:, :], [], [("identf", None)])
    k.ve("dve", "tensor_copy", [("identf", None)], [("ident", None)], out=ident[:], in_=identf[:])
    ones_b = nc.alloc_sbuf_tensor("ones_b", [128, 128], BF16)
    k.ve("dve", "memset", [], [("ones_b", None)], ones_b[:], 1.0)
    epsc = nc.alloc_sbuf_tensor("epsc", [128, 1], F32)
    k.ve("dve", "memset", [], [("epsc", None)], epsc[:], RMS_EPS)
    ones_f = nc.alloc_sbuf_tensor("ones_f", [1, 128], F32)
    k.ve("dve", "memset", [], [("ones_f", None)], ones_f[:], 1.0)

    pT = nc.alloc_psum_tensor("pT", [128, 1024], BF16)
    pA = nc.alloc_psum_tensor("pA", [128, 512], F32)
    pB = nc.alloc_psum_tensor("pB", [128, 512], F32)
    pC = nc.alloc_psum_tensor("pC", [128, 512], F32)
    pD = nc.alloc_psum_tensor("pD", [128, 512], F32)
    pE = nc.alloc_psum_tensor("pE", [128, 512], F32)
    pF = nc.alloc_psum_tensor("pF", [128, 512], F32)
    pG = nc.alloc_psum_tensor("pG", [128, 512], F32)

    stg = [None, None]

    def alloc_stg(es, tag, side=None):
        for i in range(2):
            kw = {"side": side} if side else {}
            stg[i] = es.enter_context(nc.sbuf_tensor("stg%d%s" % (i, tag), [128, 1024], F32, **kw))
    stg_i = [0]

    def load_w(dst, dname, src, KC, ncols, gcol=None, gname=None, c_lo=0, kc_lo=0):
        for kc in range(KC):
            for c0 in range(0, ncols, 1024):
                c1 = min(ncols, c0 + 1024)
                i = stg_i[0] % 2
                stg_i[0] += 1
                q = "sp"
                S.dma(q, stg[i][:, 0:c1 - c0], src[:, kc_lo + kc, c_lo + c0:c_lo + c1], [], [("stg%d" % i, None)])
                e = "dve" if i == 0 else "act"
                if gcol is not None:
                    if e == "dve":
                        k.ve("dve", "tensor_scalar", [("stg%d" % i, None), (gname, None)], [(dname, kc)],
                             out=dst[:, kc, c0:c1], in0=stg[i][:, 0:c1 - c0], scalar1=gcol[:, kc_lo + kc:kc_lo + kc + 1],
                             scalar2=None, op0=ALU.mult)
                    else:
                        k.act(dst[:, kc, c0:c1], stg[i][:, 0:c1 - c0], AF.Copy, [("stg%d" % i, None), (gname, None)],
                              [(dname, kc)], scale=gcol[:, kc_lo + kc:kc_lo + kc + 1])
                else:
                    if e == "dve":
                        k.ve("dve", "tensor_copy", [("stg%d" % i, None)], [(dname, kc)], out=dst[:, kc, c0:c1],
                             in_=stg[i][:, 0:c1 - c0])
                    else:
                        k.act(dst[:, kc, c0:c1], stg[i][:, 0:c1 - c0], AF.Copy, [("stg%d" % i, None)], [(dname, kc)])

    def rmsnorm(src, sname, dst, dname, sq, ss, tag):
        k.act(sq[:], src, AF.Square, [sname], [("sq" + tag, None)])
        k.ve("dve", "reduce_sum", [("sq" + tag, None)], [("ss" + tag, None)], out=ss[:, 0:1], in_=sq[:], axis=AX.X)
        k.act(ss[:, 1:2], ss[:, 0:1], AF.Ln, [("ss" + tag, None)], [("ss" + tag, None)], bias=epsc[:, 0:1], scale=1.0 / D)
        k.act(ss[:, 1:2], ss[:, 1:2], AF.Exp, [("ss" + tag, None)], [("ss" + tag, None)], scale=-0.5)
        k.ve("dve", "tensor_scalar", [sname, ("ss" + tag, None)], [dname], out=dst, in0=src, scalar1=ss[:, 1:2],
             scalar2=None, op0=ALU.mult)

    def transpose8(src, sname, dstT, dname, nchunk=8):
        for c in range(nchunk):
            k.tr(pT[:, c * 128:(c + 1) * 128], src[:, c * 128:(c + 1) * 128], ident[:], [sname], [("pT", c)])
        k.ve("dve", "tensor_copy", [("pT", None)], [dname], out=dstT, in_=pT[:, 0:nchunk * 128])

    pJ0, pJ1, pM0, pM1, pS0, pS1, pS2 = pA, pB, pC, pD, pE, pF, pG
    epsl = nc.alloc_sbuf_tensor("epsl", [128, 1], F32)
    k.ve("dve", "memset", [], [("epsl", None)], epsl[:], 64e-5)
    sq2 = [nc.alloc_sbuf_tensor("sq0", [128, D], BF16), None]
    ss2 = [nc.alloc_sbuf_tensor("ss0", [128, 2], F32), None]
    hb2 = [nc.alloc_sbuf_tensor("hb0", [128, D], BF16), None]
    sq, ss, hb = sq2[0], ss2[0], hb2[0]
    eo = ExitStack()
    oT_r = eo.enter_context(nc.sbuf_tensor("oT_r", [128, NOWN, 4, 128], BF16, side="right"))
    LNX_EPS = 64e-5
    if "rwkv" in dbg or "nsa" in dbg:
        gmix_d = din("g_mix", [128, 8])
    if "rwkv" in dbg:
        w_in_r = din("w_in_r", [128, 8, 1792])
        mu_d = din("mu", [128, 14])
        rv_d = din("rvecs", [128, 5, 4])
        lora_up_d = din("lora_up", [128, 512])
        g_up_d = din("g_up", [128, 512])
        lnw_d = din("lnw", [128, 4, 64])
        lnb_d = din("lnb", [128, 4, 64])
        rmask_d = din("rmasks", [128, 4, 512])
        bones_d = din("blockones", [128, 128])
        er = ExitStack()
        GT = 256
        NG = 8192 // GT
        NCH = GT // 64
        NSLOT = 3

        def T(name, shape, dt):
            return er.enter_context(nc.sbuf_tensor("r_" + name, shape, dt))

        alloc_stg(er, "r")
        sq2[1] = T("sq1", [128, D], BF16)
        ss2[1] = T("ss1", [128, 2], F32)
        hb2[1] = T("hb1", [128, D], BF16)
        w_in_b = T("w_in_b", [128, 8, 1792], BF16)
        gmix = T("gmix", [128, 8], F32)
        mu = T("mu", [128, 14], F32)
        rv = T("rv", [128, 5, 4], F32)
        omk = T("omk", [128, 4], F32)
        lup_b = T("lup_b", [128, 512], BF16)
        gup_b = T("gup_b", [128, 512], BF16)
        lnw = T("lnw", [128, 4, 64], F32)
        lnb = T("lnb", [128, 4, 64], F32)
        RSTt = T("RSTt", [128, GT], F32)
        rmb = T("rmb", [128, 3, 512], BF16)
        bones = T("bones", [128, 128], F32)
        onesc = T("onesc", [128, 1], BF16)
        xnT1 = T("xnT0", [128, 8, GT], BF16)
        xnT = [xnT1, xnT1]
        xr1 = T("xr0", [128, D], F32)
        xr = [xr1, xr1]
        carry = T("carry", [128, 14], F32)
        RWl = T("RWl", [128, 2, GT], F32)
        Pstl = T("Pstl", [128, GT + 1], F32)
        LT = [T("LT%d" % i, [128, GT], BF16) for i in range(2)]
        SGD = [T("SGD%d" % i, [128, NCH, 2, 64], BF16) for i in range(2)]
        WS = []
        for i in range(2):
            WS.append(dict(
                i=i,
                Pst=T("Pst%d" % i, [128, GT + 1], F32),
                RW3=T("RW3_%d" % i, [128, 3, GT], F32),
                Wk=[T("Wk%d_%d" % (i, j), [128, GT], F32) for j in range(7)],
            ))
        BTm = [T("BT%d" % i, [128, 4, 4, NCH, 128], BF16) for i in range(2)]
        BTo = T("BTo", [128, 2, 4, 2, 128], BF16)
        GC = [T("GC%d" % i, [128, 4, NCH], F32) for i in range(2)]
        SL = []
        for i in range(NSLOT):
            SL.append(dict(
                i=i,
                TT=T("TT%d" % i, [128, 3, 512], BF16),
                X1=T("X1_%d" % i, [128, 2, 2, 2, 128], BF16),
                X2=T("X2_%d" % i, [128, 2, 2, 2, 128], BF16),
                X3=T("X3_%d" % i, [128, 512], BF16),
                Ak=[T("Ak%d_%d" % (i, j), [128, 512], BF16) for j in range(2)],
                Bk=[T("Bk%d_%d" % (i, j), [128, 512], BF16) for j in range(2)],
                Wt=[T("Wt%d_%d" % (i, j), [128, 512], BF16) for j in range(2)],
                Zt=[T("Zt%d_%d" % (i, j), [128, 512], BF16) for j in range(2)],
            ))
        Hf = T("Hf", [128, 512], F32)
        Hb = T("Hb", [128, 512], BF16)
        Hs = T("Hs", [128, 512], F32)
        RHSb = T("RHSb", [128, 512], BF16)
        Ub = T("Ub", [128, 512], BF16)
        Yc = T("Yc", [128, 4, 64], F32)
        Gc = T("Gc", [128, 4, 64], F32)
        V2c = T("V2c", [128, 4, 64], F32)
        Ycen = [T("Ycen%d" % i, [128, 64], F32) for i in range(4)]
        Ysq = [T("Ysq%d" % i, [128, 64], F32) for i in range(4)]
        st = [T("st%d" % i, [128, 8], F32) for i in range(4)]
        stb = T("stb", [128, 4], F32)
        Oblk = T("Oblk", [128, 4, 128], BF16)

        S.dma("sp", gmix[:], gmix_d[:, :], [], [("gmix", None)])
        S.dma("sp", mu[:], mu_d[:, :], [], [("mu", None)])
        S.dma("sp", rv[:], rv_d[:, :, :], [], [("rv", None)])
        S.dma("sp", stg[0][:, 0:512], lora_up_d[:, :], [], [("stg0", None)])
        k.ve("dve", "tensor_copy", [("stg0", None)], [("lup_b", None)], out=lup_b[:], in_=stg[0][:, 0:512])
        S.dma("sp", stg[1][:, 0:512], g_up_d[:, :], [], [("stg1", None)])
        k.ve("dve", "tensor_copy", [("stg1", None)], [("gup_b", None)], out=gup_b[:], in_=stg[1][:, 0:512])
        S.dma("sp", lnw[:], lnw_d[:, :, :], [], [("lnw", None)])
        S.dma("sp", lnb[:], lnb_d[:, :, :], [], [("lnb", None)])
        S.dma("sp", RSTt[:], rmask_d[:, 0, 0:GT], [], [("rmask", None)])
        for mi in range(3):
            S.dma("sp", stg[mi % 2][:, 0:512], rmask_d[:, 1 + mi, :], [], [("stg%d" % (mi % 2), None)])
            k.ve("dve", "tensor_copy", [("stg%d" % (mi % 2), None)], [("rmb", mi)], out=rmb[:, mi, :], in_=stg[mi % 2][:, 0:512])
        S.dma("pool", bones[:], bones_d[:, :], [], [("bones", None)])
        k.ve("dve", "memset", [], [("onesc", None)], onesc[:], 1.0)
        k.ve("dve", "memset", [], [("carry", None)], carry[:], 0.0)
        k.ve("dve", "memset", [], [("Hf", None)], Hf[:], 0.0)
        k.ve("dve", "memset", [], [("Hb", None)], Hb[:], 0.0)
        for par in range(2):
            for ty in range(4):
                k.ve("pool", "memset", [], [("BT%d" % par, None)], BTm[par][:, ty, :, :, :].rearrange("p b c d -> p (b c d)"), 0.0)
        k.ve("pool", "memset", [], [("BTo", None)], BTo[:].rearrange("p a b c d -> p (a b c d)"), 0.0)
        k.ve("pool", "memset", [], [("Oblk", None)], Oblk[:].rearrange("p a b -> p (a b)"), 0.0)
        k.ve("dve", "tensor_scalar", [("rv", None)], [("omk", None)], out=omk[:], in0=rv[:, 3, :], scalar1=-1.0,
             scalar2=1.0, op0=ALU.mult, op1=ALU.add)
        load_w(w_in_b, "w_in_b", w_in_r, 8, 1792, gcol=gmix, gname="gmix")

        MA = rmb[:, 0, :]
        ML4 = rmb[:, 1, :]
        I4 = rmb[:, 2, :]
        RST = RSTt[:, :]
        mm_i = [0]
        pj_i = [0]

        rbanks = [(pJ0, "pJ0"), (pJ1, "pJ1"), (pM0, "pM0"), (pM1, "pM1"), (pS0, "pS0"), (pS1, "pS1"), (pS2, "pS2")]

        def pM():
            mm_i[0] += 1
            return rbanks[mm_i[0] % 7]

        pJ = pM

        def v3(ap):
            return ap.rearrange("q (c t) -> q c t", c=NCH)

        def halves(e, name, r, w, out3, mk):
            for hh in range(2):
                lo, hi = hh * 64, hh * 64 + 64
                k.ve(e, name, r, w, out=out3[lo:hi, :, lo:hi], **mk(lo, hi))

        def inproj(par, ci, Pst, pstn, dst, dname):
            pp, pn = pJ()
            for kc in range(8):
                k.mm(pp[:, 0:GT], w_in_b[:, kc, ci * 128:(ci + 1) * 128], xnT[par][:, kc, :], kc == 0, kc == 7,
                     [("w_in_b", kc), ("xnT0", None)], [(pn, None)])
            k.act(Pst[:, 0:1], carry[:, ci:ci + 1], AF.Copy, [("carry", ci)], [(pstn, 0)])
            k.act(Pst[:, 1:GT + 1], pp[:, 0:GT], AF.Copy, [(pn, None)], [(pstn, 1)])
            k.act(carry[:, ci:ci + 1], pp[:, GT - 1:GT], AF.Copy, [(pn, None)], [("carry", ci)])
            k.ve("dve", "tensor_tensor", [(pstn, None)], [dname], out=dst, in0=Pst[:, 0:GT], in1=Pst[:, 1:GT + 1],
                 op=ALU.subtract)
            k.ve("dve", "scalar_tensor_tensor", [dname, (pstn, None), ("mu", None)], [dname], out=dst, in0=dst,
                 scalar=mu[:, ci:ci + 1], in1=Pst[:, 1:GT + 1], op0=ALU.mult, op1=ALU.add)

        def gen_load(g):
            par = g % 2
            for tl in range(GT // 128):
                pos = (GT // 128) * g + tl
                b = tl % 2
                S.dma("sp", xr[b][:], xs[pos * 128:(pos + 1) * 128, :], [], [("xr0", None)])
                rmsnorm(xr[b][:], ("xr0", None), hb2[b][:], ("hb%d" % b, None), sq2[b], ss2[b], str(b))
                yield
                for c in range(8):
                    k.tr(pT[:, c * 128:(c + 1) * 128], hb2[b][:, c * 128:(c + 1) * 128], ident[:], [("hb%d" % b, None)], [("pT", c)])
                k.ve("dve", "tensor_copy", [("pT", None)], [("xnT0", tl)], out=xnT[par][:, :, tl * 128:(tl + 1) * 128],
                     in_=pT[:, :].rearrange("p (c t) -> p c t", c=8))
                yield
            inproj(par, 12, Pstl, "Pstl", RWl[:, 0, :], ("RWl", 0))
            yield
            inproj(par, 13, Pstl, "Pstl", RWl[:, 1, :], ("RWl", 1))
            yield
            k.act(LT[par][0:64, :], RWl[0:64, 0, :], AF.Tanh, [("RWl", 0)], [("LT%d" % par, None)])
            k.act(LT[par][64:128, :], RWl[64:128, 0, :], AF.Copy, [("RWl", 0)], [("LT%d" % par, None)])
            for dup in range(2):
                k.act(SGD[par][:, :, dup, :], RWl[:, 1, :].rearrange("q (c t) -> q c t", c=NCH), AF.Sigmoid, [("RWl", 1)],
                      [("SGD%d" % par, None)])
            yield

        def gen_prep(g, p, ws):
            par = g % 2
            wi = ws["i"]
            Wk = ws["Wk"]
            RW3 = ws["RW3"]
            wn = lambda j: ("Wk%d_%d" % (wi, j), None)
            rn = lambda j: ("RW3_%d" % wi, j)
            btn = lambda ty: ("BT%d" % par, (ty, p))
            B_ = BTm[par]
            inproj(par, p, ws["Pst"], "Pst%d" % wi, RW3[:, 0, :], rn(0))
            yield
            inproj(par, 4 + p, ws["Pst"], "Pst%d" % wi, RW3[:, 1, :], rn(1))
            yield
            inproj(par, 8 + p, ws["Pst"], "Pst%d" % wi, RW3[:, 2, :], rn(2))
            yield
            k_ = RW3[:, 1, :]
            cs = slice(p * 128, (p + 1) * 128)
            pp, pn = pJ()
            k.mm(pp[:, 0:GT], lup_b[0:64, cs], LT[par][0:64, :], True, True, [("lup_b", None), ("LT%d" % par, None)], [(pn, None)])
            k.act(Wk[0][:], pp[:, 0:GT], AF.Sigmoid, [(pn, None), ("rv", None)], [wn(0)], bias=rv[:, 0, p:p + 1])
            yield
            k.ve("dve", "tensor_scalar", [wn(0)], [wn(0)], out=Wk[0][:], in0=Wk[0][:], scalar1=-0.6065306597126334,
                 scalar2=None, op0=ALU.mult)
            yield
            k.ve("dve", "tensor_tensor_scan", [wn(0), ("rmask", None)], [wn(1)], out=Wk[1][:], data0=RST, data1=Wk[0][:],
                 initial=0.0, op0=ALU.mult, op1=ALU.add)
            yield
            k.ve("pool", "tensor_tensor", [wn(0), wn(1)], [wn(0)], out=Wk[0][:], in0=Wk[1][:], in1=Wk[0][:], op=ALU.subtract)
            k.act(Wk[2][:], Wk[1][:], AF.Exp, [wn(1)], [wn(2)])
            k.act(Wk[3][:], Wk[1][:], AF.Exp, [wn(1)], [wn(3)], scale=-1.0)
            yield
            k.act(Wk[0][:], Wk[0][:], AF.Exp, [wn(0)], [wn(0)])
            k.ve("pool", "tensor_copy", [wn(2)], [("GC%d" % par, p)], out=GC[par][:, p, :],
                 in_=Wk[2][:].rearrange("q (c t) -> q c t", c=NCH)[:, :, 63])
            yield
            pp, pn = pJ()
            k.mm(pp[:, 0:GT], lup_b[64:128, cs], LT[par][64:128, :], True, True, [("lup_b", None), ("LT%d" % par, None)],
                 [(pn, None)])
            k.act(Wk[4][:], pp[:, 0:GT], AF.Sigmoid, [(pn, None), ("rv", None)], [wn(4)], bias=rv[:, 1, p:p + 1])
            k.ve("dve", "tensor_scalar", [rn(1), ("rv", None)], [wn(5)], out=Wk[5][:], in0=k_, scalar1=rv[:, 2, p:p + 1],
                 scalar2=None, op0=ALU.mult)
            yield
            k.ve("pool", "tensor_tensor", [wn(5)], [wn(6)], out=Wk[6][:], in0=Wk[5][:], in1=Wk[5][:], op=ALU.mult)
            yield
            pp, pn = pJ()
            k.mm(pp[:, 0:GT], bones[:, :], Wk[6][:], True, True, [("bones", None), wn(6)], [(pn, None)])
            k.ve("dve", "tensor_scalar", [(pn, None)], [wn(6)], out=Wk[6][:], in0=pp[:, 0:GT], scalar1=1e-24, scalar2=None,
                 op0=ALU.max)
            yield
            k.act(Wk[6][:], Wk[6][:], AF.Ln, [wn(6)], [wn(6)])
            yield
            k.act(Wk[6][:], Wk[6][:], AF.Exp, [wn(6)], [wn(6)], scale=-0.5)
            yield
            k.ve("dve", "tensor_tensor", [wn(5), wn(6)], [wn(5)], out=Wk[5][:], in0=Wk[5][:], in1=Wk[6][:], op=ALU.mult)
            k.ve("dve", "tensor_scalar", [wn(4), ("rv", None), ("omk", None)], [wn(6)], out=Wk[6][:], in0=Wk[4][:],
                 scalar1=rv[:, 3, p:p + 1], scalar2=omk[:, p:p + 1], op0=ALU.mult, op1=ALU.add)
            yield
            k.ve("pool", "tensor_tensor", [wn(6), rn(1)], [wn(6)], out=Wk[6][:], in0=Wk[6][:], in1=k_, op=ALU.mult)
            k.ve("pool", "tensor_tensor", [wn(5), wn(4)], [wn(4)], out=Wk[4][:], in0=Wk[5][:], in1=Wk[4][:], op=ALU.mult)
            yield
            if g % 2 == 1:
                halves("dve", "tensor_tensor", [rn(0), wn(2)], [("BTo", (0, p))], BTo[:, 0, p, :, :],
                       lambda lo, hi: dict(in0=v3(RW3[lo:hi, 0, :])[:, 2:4, :], in1=v3(Wk[2][lo:hi, :])[:, 2:4, :], op=ALU.mult))
            halves("dve", "scalar_tensor_tensor", [wn(5), wn(0)], [btn(1)], B_[:, 0, p, :, :],
                   lambda lo, hi: dict(in0=v3(Wk[5][lo:hi, :]), scalar=-1.0, in1=v3(Wk[0][lo:hi, :]), op0=ALU.mult,
                                       op1=ALU.mult))
            yield
            halves("pool", "tensor_tensor", [wn(4), wn(3)], [btn(2)], B_[:, 1, p, :, :],
                   lambda lo, hi: dict(in0=v3(Wk[4][lo:hi, :]), in1=v3(Wk[3][lo:hi, :]), op=ALU.mult))
            halves("pool", "tensor_tensor", [wn(6), wn(3)], [btn(3)], B_[:, 2, p, :, :],
                   lambda lo, hi: dict(in0=v3(Wk[6][lo:hi, :]), in1=v3(Wk[3][lo:hi, :]), op=ALU.mult))
            yield
            halves("pool", "tensor_copy", [rn(2)], [btn(4)], B_[:, 3, p, :, :],
                   lambda lo, hi: dict(in_=v3(RW3[lo:hi, 2, :])))
            if g % 2 == 1:
                halves("dve", "scalar_tensor_tensor", [rn(0), wn(6), ("rv", None)], [("BTo", (1, p))], BTo[:, 1, p, :, :],
                       lambda lo, hi: dict(in0=v3(RW3[lo:hi, 0, :])[:, 2:4, :], scalar=rv[lo:hi, 4, p:p + 1],
                                           in1=v3(Wk[6][lo:hi, :])[:, 2:4, :], op0=ALU.mult, op1=ALU.mult))
            yield

        def gen_tt(g, c, sl):
            par = g % 2
            B_ = BTm[par]
            si_ = sl["i"]
            btn = lambda ty, p: ("BT%d" % par, (ty, p))
            TT, X2 = sl["TT"], sl["X2"]
            own = (g % 2 == 1) and c >= 2
            for j, ty in enumerate((2, 3, 4)):
                pp, pn = pM()
                for p in range(4):
                    k.mm(pp[:, p * 128:(p + 1) * 128], B_[:, ty - 1, p, c, :], ident[:], True, True,
                         [btn(ty, p), ("ident", None)], [(pn, None)])
                if j >= 1:
                    k.act(TT[:, j, :], pp[:, :], AF.Copy, [(pn, None)], [("TT%d" % si_, j)])
                else:
                    k.ve("dve", "tensor_copy", [(pn, None)], [("TT%d" % si_, j)], out=TT[:, j, :], in_=pp[:, :])
                yield
            for j in range(2):
                pp, pn = pM()
                for a in range(2):
                    p = 2 * j + a
                    k.mm(pp[:, a * 256:a * 256 + 128], B_[:, 2, p, c, :], B_[:, 0, p, c, :], True, True, [btn(3, p), btn(1, p)],
                         [(pn, None)])
                    if own:
                        k.mm(pp[:, a * 256 + 128:a * 256 + 256], B_[:, 2, p, c, :], BTo[:, 0, p, c - 2, :], True, True,
                             [btn(3, p), ("BTo", (0, p))], [(pn, None)])
                if own:
                    k.ve("dve", "tensor_tensor", [(pn, None), ("rmb", None)], [("X2_%d" % si_, j)],
                         out=X2[:, j, :, :, :].rearrange("q a t s -> q (a t s)"), in0=pp[:, :], in1=MA, op=ALU.mult)
                else:
                    k.ve("dve", "tensor_tensor", [(pn, None), ("rmb", None)], [("X2_%d" % si_, j)],
                         out=X2[:, j, :, 0, :], in0=pp[:, :].rearrange("q (a t s) -> q a t s", a=2, t=2)[:, :, 0, :],
                         in1=MA.rearrange("q (a t s) -> q a t s", a=2, t=2)[:, :, 0, :], op=ALU.mult)
                yield

        def gen_pre(g, c, sl):
            par = g % 2
            B_ = BTm[par]
            si_ = sl["i"]
            btn = lambda ty, p: ("BT%d" % par, (ty, p))
            X1, X3 = sl["X1"], sl["X3"]
            Ak, Bk, Wt, Zt = sl["Ak"], sl["Bk"], sl["Wt"], sl["Zt"]
            own = (g % 2 == 1) and c >= 2
            n_ = lambda base, j=None: ("%s%d" % (base, si_) if j is None else "%s%d_%d" % (base, si_, j), None)
            pp, pn = pM()
            for p in range(4):
                k.mm(pp[:, p * 128:(p + 1) * 128], B_[:, 0, p, c, :], B_[:, 1, p, c, :], True, True, [btn(1, p), btn(2, p)],
                     [(pn, None)])
            k.ve("dve", "tensor_tensor", [(pn, None), ("rmb", None)], [("X3_%d" % si_, None)], out=X3[:], in0=pp[:, :], in1=ML4,
                 op=ALU.mult)
            for j in range(2):
                pp, pn = pM()
                for a in range(2):
                    p = 2 * j + a
                    k.mm(pp[:, a * 256:a * 256 + 128], B_[:, 1, p, c, :], B_[:, 0, p, c, :], True, True, [btn(2, p), btn(1, p)],
                         [(pn, None)])
                    if own:
                        k.mm(pp[:, a * 256 + 128:a * 256 + 256], B_[:, 1, p, c, :], BTo[:, 0, p, c - 2, :], True, True,
                             [btn(2, p), ("BTo", (0, p))], [(pn, None)])
                if own:
                    k.ve("dve", "tensor_tensor", [(pn, None), ("rmb", None)], [("X1_%d" % si_, j)],
                         out=X1[:, j, :, :, :].rearrange("q a t s -> q (a t s)"), in0=pp[:, :], in1=MA, op=ALU.mult)
                else:
                    k.ve("dve", "tensor_tensor", [(pn, None), ("rmb", None)], [("X1_%d" % si_, j)],
                         out=X1[:, j, :, 0, :], in0=pp[:, :].rearrange("q (a t s) -> q a t s", a=2, t=2)[:, :, 0, :],
                         in1=MA.rearrange("q (a t s) -> q a t s", a=2, t=2)[:, :, 0, :], op=ALU.mult)
            yield
            B1 = X1[:, :, :, 0, :]
            k.ve("dve", "tensor_tensor", [("X1_%d" % si_, None), ("rmb", None)], [n_("Wt", 0)],
                 out=Wt[0][:].rearrange("q (j a s) -> q j a s", j=2, a=2), in0=B1,
                 in1=I4.rearrange("q (j a s) -> q j a s", j=2, a=2), op=ALU.add)
            k.ve("dve", "tensor_tensor", [("X3_%d" % si_, None), ("rmb", None)], [n_("Zt", 0)], out=Zt[0][:], in0=X3[:], in1=I4,
                 op=ALU.add)

            def Bsl(lv, p):
                if lv == 0:
                    return X1[:, p // 2, p % 2, 0, :], ("X1_%d" % si_, None)
                return Bk[lv % 2][:, p * 128:(p + 1) * 128], n_("Bk", lv % 2)

            def Asl(lv, p):
                if lv == 0:
                    return X3[:, p * 128:(p + 1) * 128], ("X3_%d" % si_, None)
                return Ak[lv % 2][:, p * 128:(p + 1) * 128], n_("Ak", lv % 2)

            def squares(lv):
                ppB, pnB = pM()
                for p in range(4):
                    a_ap, a_n = Asl(lv - 1, p)
                    b_ap, b_n = Bsl(lv - 1, p)
                    k.mm(ppB[:, p * 128:(p + 1) * 128], a_ap, b_ap, True, True, [a_n, b_n], [(pnB, None)])
                k.act(Bk[lv % 2][:], ppB[:, :], AF.Copy, [(pnB, None)], [n_("Bk", lv % 2)])
                if lv < 5:
                    ppA, pnA = pM()
                    for p in range(4):
                        a_ap, a_n = Asl(lv - 1, p)
                        b_ap, b_n = Bsl(lv - 1, p)
                        k.mm(ppA[:, p * 128:(p + 1) * 128], b_ap, a_ap, True, True, [a_n, b_n], [(pnA, None)])
                    k.act(Ak[lv % 2][:], ppA[:, :], AF.Copy, [(pnA, None)], [n_("Ak", lv % 2)])

            def products(lv):
                wi, wo = (lv - 1) % 2, lv % 2
                ppW, pnW = pM()
                for p in range(4):
                    b_ap, b_n = Bsl(lv, p)
                    k.mm(ppW[:, p * 128:(p + 1) * 128], Zt[wi][:, p * 128:(p + 1) * 128], b_ap, True, True,
                         [n_("Zt", wi), b_n], [(pnW, None)])
                k.ve("dve", "tensor_tensor", [(pnW, None), n_("Wt", wi)], [n_("Wt", wo)], out=Wt[wo][:], in0=ppW[:, :],
                     in1=Wt[wi][:], op=ALU.add)
                if lv < 5:
                    ppZ, pnZ = pM()
                    for p in range(4):
                        a_ap, a_n = Asl(lv, p)
                        k.mm(ppZ[:, p * 128:(p + 1) * 128], Wt[wi][:, p * 128:(p + 1) * 128], a_ap, True, True,
                             [n_("Wt", wi), a_n], [(pnZ, None)])
                    k.ve("dve", "tensor_tensor", [(pnZ, None), n_("Zt", wi)], [n_("Zt", wo)], out=Zt[wo][:], in0=ppZ[:, :],
                         in1=Zt[wi][:], op=ALU.add)

            squares(1)
            yield
            for lv in range(1, 6):
                if lv < 5:
                    squares(lv + 1)
                products(lv)
                yield

        def gen_seq(g, c, sl):
            par = g % 2
            B_ = BTm[par]
            si_ = sl["i"]
            btn = lambda ty, p: ("BT%d" % par, (ty, p))
            TT, X1, X2 = sl["TT"], sl["X1"], sl["X2"]
            Wf, Wfn = sl["Wt"][1], ("Wt%d_1" % si_, None)
            ttn = lambda j: ("TT%d" % si_, j)
            x1n, x2n = ("X1_%d" % si_, None), ("X2_%d" % si_, None)
            BBT, KBT, VBT = TT[:, 0, :], TT[:, 1, :], TT[:, 2, :]
            own = (g % 2 == 1) and c >= 2
            it = g // 2
            b0, b0n = pM()
            for p in range(4):
                ps = slice(p * 128, (p + 1) * 128)
                k.mm(b0[:, ps], B_[:, 0, p, c, :], Hb[:, ps], True, False, [btn(1, p), ("Hb", None)], [(b0n, None)])
                k.mm(b0[:, ps], X2[:, p // 2, p % 2, 0, :], VBT[:, ps], False, True, [x2n, ttn(2)], [(b0n, None)])
            k.act(RHSb[:], b0[:, :], AF.Copy, [(b0n, None)], [("RHSb", None)])
            yield
            b1, b1n = pM()
            for p in range(4):
                ps = slice(p * 128, (p + 1) * 128)
                k.mm(b1[:, ps], Wf[:, ps], RHSb[:, ps], True, True, [Wfn, ("RHSb", None)], [(b1n, None)])
            k.act(Ub[:], b1[:, :], AF.Copy, [(b1n, None)], [("Ub", None)])
            yield
            if own:
                ch = c - 2
                pS2, pS2n = pM()
                for p in range(4):
                    ps = slice(p * 128, (p + 1) * 128)
                    k.mm(pS2[:, ps], BTo[:, 0, p, c - 2, :], Hb[:, ps], True, False, [("BTo", (0, p)), ("Hb", None)], [(pS2n, None)])
                    k.mm(pS2[:, ps], X1[:, p // 2, p % 2, 1, :], Ub[:, ps], False, False, [x1n, ("Ub", None)], [(pS2n, None)])
                    k.mm(pS2[:, ps], X2[:, p // 2, p % 2, 1, :], VBT[:, ps], False, True, [x2n, ttn(2)], [(pS2n, None)])
            b3, b3n = pM()
            for p in range(4):
                ps = slice(p * 128, (p + 1) * 128)
                k.mm(b3[:, ps], BBT[:, ps], Ub[:, ps], True, False, [ttn(0), ("Ub", None)], [(b3n, None)])
                k.mm(b3[:, ps], KBT[:, ps], VBT[:, ps], False, True, [ttn(1), ttn(2)], [(b3n, None)])
            k.ve("dve", "tensor_tensor", [(b3n, None), ("Hf", None)], [("Hs", None)], out=Hs[:], in0=b3[:, :], in1=Hf[:],
                 op=ALU.add)
            yield
            for p in range(4):
                ps = slice(p * 128, (p + 1) * 128)
                k.act(Hf[:, ps], Hs[:, ps], AF.Copy, [("Hs", None), ("GC%d" % par, p)], [("Hf", None)], scale=GC[par][:, p, c:c + 1])
            k.ve("dve", "tensor_copy", [("Hf", None)], [("Hb", None)], out=Hb[:], in_=Hf[:])
            yield
            if own:
                for hh in range(2):
                    lo, hi = hh * 64, hh * 64 + 64
                    k.ve("dve", "tensor_copy", [(pS2n, None)], [("Yc", None)], out=Yc[lo:hi, :, :],
                         in_=pS2[lo:hi, :].rearrange("q (p s) -> q p s", p=4)[:, :, lo:hi])
                    k.ve("pool", "tensor_copy", [ttn(2)], [("V2c", None)], out=V2c[lo:hi, :, :],
                         in_=TT[lo:hi, 2, :].rearrange("q (p s) -> q p s", p=4)[:, :, lo:hi])
                yield
                pp, pn = pJ()
                for p in range(4):
                    k.mm(pp[:, p * 128:(p + 1) * 128], SGD[par][:, c, :, :].rearrange("q a t -> q (a t)"),
                         gup_b[:, p * 128:(p + 1) * 128], True, True, [("SGD%d" % par, None), ("gup_b", None)], [(pn, None)])
                for hh in range(2):
                    lo, hi = hh * 64, hh * 64 + 64
                    k.act(Gc[lo:hi, :, :], pp[lo:hi, :].rearrange("q (p s) -> q p s", p=4)[:, :, lo:hi], AF.Copy,
                          [(pn, None)], [("Gc", None)])
                yield
                pp, pn = pJ()
                for p in range(4):
                    k.mm(pp[:, p:p + 1], BTo[:, 1, p, c - 2, :], onesc[:, 0:1], True, True, [("BTo", (1, p)), ("onesc", None)], [(pn, None)])
                k.ve("dve", "tensor_copy", [(pn, None)], [("stb", None)], out=stb[:, 0:4], in_=pp[:, 0:4])
                yield
                steps = []
                for p in range(4):
                    yc = Yc[:, p, :]
                    sn, cn, qn = ("st%d" % p, None), ("Ycen%d" % p, None), ("Ysq%d" % p, None)
                    s_, c_, q_ = st[p], Ycen[p], Ysq[p]
                    steps.append([
                        lambda yc=yc, s_=s_, sn=sn: k.ve("dve", "reduce_sum", [("Yc", None)], [sn], out=s_[:, 0:1], in_=yc, axis=AX.X),
                        lambda s_=s_, sn=sn: k.ve("dve", "tensor_scalar", [sn], [sn], out=s_[:, 0:1], in0=s_[:, 0:1],
                                                  scalar1=-1.0 / 64, scalar2=None, op0=ALU.mult),
                        lambda yc=yc, s_=s_, sn=sn, c_=c_, cn=cn: k.ve("dve", "tensor_scalar", [("Yc", None), sn], [cn], out=c_[:],
                                                                      in0=yc, scalar1=s_[:, 0:1], scalar2=None, op0=ALU.add),
                        lambda c_=c_, cn=cn, q_=q_, qn=qn: k.ve("pool", "tensor_tensor", [cn], [qn], out=q_[:], in0=c_[:], in1=c_[:],
                                                              op=ALU.mult),
                        lambda q_=q_, qn=qn, s_=s_, sn=sn: k.ve("dve", "reduce_sum", [qn], [sn], out=s_[:, 1:2], in_=q_[:], axis=AX.X),
                        lambda s_=s_, sn=sn: k.act(s_[:, 2:3], s_[:, 1:2], AF.Ln, [sn, ("epsl", None)], [sn], bias=epsl[:, 0:1],
                                                   scale=1.0 / 64),
                        lambda s_=s_, sn=sn: k.act(s_[:, 2:3], s_[:, 2:3], AF.Exp, [sn], [sn], scale=-0.5),
                        lambda s_=s_, sn=sn, c_=c_, cn=cn: k.ve("dve", "tensor_scalar", [cn, sn], [cn], out=c_[:], in0=c_[:],
                                                              scalar1=s_[:, 2:3], scalar2=None, op0=ALU.mult),
                        lambda c_=c_, cn=cn, p=p: k.ve("pool", "tensor_tensor", [cn, ("lnw", None)], [cn], out=c_[:], in0=c_[:],
                                                      in1=lnw[:, p, :], op=ALU.mult),
                        lambda c_=c_, cn=cn, p=p: k.ve("pool", "tensor_tensor", [cn, ("lnb", None)], [cn], out=c_[:], in0=c_[:],
                                                      in1=lnb[:, p, :], op=ALU.add),
                        lambda c_=c_, cn=cn, p=p: k.ve("dve", "scalar_tensor_tensor", [("V2c", None), ("stb", None), cn], [cn],
                                                      out=c_[:], in0=V2c[:, p, :], scalar=stb[:, p:p + 1], in1=c_[:],
                                                      op0=ALU.mult, op1=ALU.add),
                    ])
                for si2 in range(len(steps[0])):
                    for p in range(4):
                        steps[p][si2]()
                    yield
                for p in range(4):
                    for hh in range(2):
                        lo, hi = hh * 64, hh * 64 + 64
                        k.ve("dve" if hh == 0 else "pool", "tensor_tensor", [("Ycen%d" % p, None), ("Gc", None)], [("Oblk", None)],
                             out=Oblk[lo:hi, p, lo:hi], in0=Ycen[p][lo:hi, :], in1=Gc[lo:hi, p, :], op=ALU.mult)
                yield
                for p in range(4):
                    k.tr(pT[:, p * 128:(p + 1) * 128], Oblk[:, p, :], ident[:], [("Oblk", None)], [("pT", p)])
                for hh in range(2):
                    lo, hi = hh * 64, hh * 64 + 64
                    k.ve("dve", "tensor_copy", [("pT", None)], [("oT_r", it)], out=oT_r[lo:hi, it, :, ch * 64:(ch + 1) * 64],
                         in_=pT[lo:hi, 0:512].rearrange("q (p s) -> q p s", p=4)[:, :, lo:hi])
                yield

        NCC = NG * NCH
        done = {"load": -1, "prep": -1, "pre": -1, "seq": -1}
        active = []
        nxt = {"load": 0, "prepg": 0, "prepp": 0, "pre": 0, "seq": 0}
        prep_done_pairs = {}

        def admit():
            if nxt["load"] < NG and not any(a[0] == "load" for a in active) and nxt["load"] <= done["prep"] + 1 \
                    and done["seq"] >= (nxt["load"] - 1) * NCH - 1:
                g = nxt["load"]
                active.append(("load", g, gen_load(g)))
                nxt["load"] += 1
            while nxt["prepg"] < NG:
                g = nxt["prepg"]
                if done["load"] < g or done["seq"] < (g - 1) * NCH - 1:
                    break
                inflight = [a for a in active if a[0] == "prep"]
                if len(inflight) >= 2:
                    break
                used = {a[3] for a in inflight}
                wsi = 0 if 0 not in used else 1
                p = nxt["prepp"]
                active.append(("prep", (g, p), gen_prep(g, p, WS[wsi]), wsi))
                nxt["prepp"] += 1
                if nxt["prepp"] == 4:
                    nxt["prepp"] = 0
                    nxt["prepg"] += 1
            while nxt["pre"] < NCC:
                cc = nxt["pre"]
                if done["prep"] < cc // NCH:
                    break
                if cc - (done["seq"] + 1) >= NSLOT:
                    break
                if len([a for a in active if a[0] == "pre"]) >= 2:
                    break
                active.append(("pre", cc, gen_pre(cc // NCH, cc % NCH, SL[cc % NSLOT])))
                active.append(("tt", cc, gen_tt(cc // NCH, cc % NCH, SL[cc % NSLOT])))
                nxt["pre"] += 1
            if nxt["seq"] < NCC and not any(a[0] == "seq" for a in active) and done["pre"] >= nxt["seq"]:
                cc = nxt["seq"]
                active.append(("seq", cc, gen_seq(cc // NCH, cc % NCH, SL[cc % NSLOT])))
                nxt["seq"] += 1

        pre_done_set = set()
        pre_cnt = {}
        while True:
            admit()
            if not active:
                break
            for a in list(active):
                try:
                    next(a[2])
                except StopIteration:
                    active.remove(a)
                    kind = a[0]
                    if kind == "load":
                        done["load"] = a[1]
                    elif kind == "prep":
                        gq, pq = a[1]
                        prep_done_pairs[gq] = prep_done_pairs.get(gq, 0) + 1
                        if prep_done_pairs[gq] == 4:
                            done["prep"] = gq
                    elif kind in ("pre", "tt"):
                        pre_cnt[a[1]] = pre_cnt.get(a[1], 0) + 1
                        if pre_cnt[a[1]] == 2:
                            pre_done_set.add(a[1])
                        while done["pre"] + 1 in pre_done_set:
                            done["pre"] += 1
                    else:
                        done["seq"] = a[1]
        assert done["seq"] == NCC - 1, done
        if "orw" in dbg:
            dorw = nc.dram_tensor("dorw", [128, NOWN * 4 * 128], BF16, kind="ExternalOutput").ap()
            S.dma("sp", dorw[:, :], oT_r[:].rearrange("q a b c -> q (a b c)"), [("oT_r", None)], [])
        S.fence()
        er.close()
        S.fence()

    oT_n = eo.enter_context(nc.sbuf_tensor("oT_n", [128, NOWN, 4, 128], BF16, side="right"))
    if "nsa" in dbg:
        w_in_n = din("w_in_n", [128, 8, 1304])
        gateb_d = din("gate_b", [128, 24])
        realtok_d = din("realtok", [128, 64])
        w1_d = {"k": din("cmp_w1k", [128, 32, 256]), "v": din("cmp_w1v", [128, 32, 256])}
        peT_d = {"k": din("cmp_peTk", [128, 32]), "v": din("cmp_peTv", [128, 32])}
        b1_d = {"k": din("cmp_b1k", [128, 2]), "v": din("cmp_b1v", [128, 2])}
        w2k_d = din("cmp_w2k", [128, 2, 128])
        w2v_d = din("cmp_w2v", [128, 2, 64])
        ovlm_d = din("ovlm", [128, 4, 129])
        relb_d = din("relb_rep", [32, 8, 128])
        OHa_d = din("OHa", [32, 2048])
        OHw_d = din("OHw", [32, 256])
        OHc_d = din("OHc", [32, 6144])
        Ebig_d = din("Ebig", [128, 64, 128])
        addm_d = din("addmask", [128, 16, 128])
        scr_a = nc.dram_tensor("scr_a", [8, 128, 2048], BF16)
        scr_w = nc.dram_tensor("scr_w", [8, 128, 256], BF16)
        scr_c = nc.dram_tensor("scr_c", [8, 128, 6144], BF16)
        en = ExitStack()
        ens = ExitStack()
        alloc_stg(ens, "n", side="right")

        def TN(name, shape, dt):
            return en.enter_context(nc.sbuf_tensor("n_" + name, shape, dt))

        kselT = TN("kselT", [128, 8192], BF16)
        kwinT = TN("kwinT", [128, 8192], BF16)
        Vs = TN("Vs", [128, 64, 2, 65], BF16)
        Vw = TN("Vw", [128, 64, 2, 65], BF16)
        QT = TN("QT", [128, NOWN, 4, 128], BF16)
        GATES = TN("GATES", [128, NOWN, 24], F32)
        KcT = TN("KcT", [128, 512], BF16)
        Vca = TN("Vca", [128, 4, 2, 65], BF16)
        Ovl = TN("Ovl", [128, 4, 128], BF16)

        en1 = ExitStack()

        def TN1(name, shape, dt):
            return en1.enter_context(nc.sbuf_tensor("n_" + name, shape, dt))

        ovlm = TN1("ovlm", [128, 4, 129], F32)
        realtok = TN1("realtok", [128, 64], F32)
        gateb = TN1("gateb", [128, 24], F32)
        S.dma("sp", gateb[:], gateb_d[:, :], [], [("gateb", None)])
        S.dma("sp", realtok[:], realtok_d[:, :], [], [("realtok", None)])
        S.dma("sp", ovlm[:], ovlm_d[:, :, :], [], [("ovlm", None)])
        k.ve("pool", "tensor_copy", [("ovlm", None)], [("Ovl", None)], out=Ovl[:], in_=ovlm[:, :, 1:129])
        kcmpT = TN1("kcmpT", [128, 8192 + 32], BF16)
        vcmpT = TN1("vcmpT", [128, 8192 + 32], BF16)
        en1b = ExitStack()
        w_in_nb = en1b.enter_context(nc.sbuf_tensor("n_w_in_nb", [128, 8, 1304], BF16))
        gmixn = en1b.enter_context(nc.sbuf_tensor("n_gmixn", [128, 8], F32))
        xnT2 = [en1b.enter_context(nc.sbuf_tensor("n_xnT%d" % i, [128, 8, 512], BF16)) for i in range(2)]
        xr_n = en1b.enter_context(nc.sbuf_tensor("n_xr0", [128, D], F32))
        xr2 = [xr_n, xr_n]
        sqn = [sq, sq]
        ssn = [ss, ss]
        hbn = [hb, en1b.enter_context(nc.sbuf_tensor("n_hbx1", [128, D], BF16))]
        S.dma("sp", gmixn[:], gmix_d[:, :], [], [("gmixn", None)])
        load_w(w_in_nb, "w_in_nb", w_in_n, 8, 1304, gcol=gmixn, gname="gmixn")
        k.ve("pool", "memset", [], [("kcmpT", None)], kcmpT[:, 8192:8224], 0.0)
        k.ve("pool", "memset", [], [("vcmpT", None)], vcmpT[:, 8192:8224], 0.0)
        nbanks = [(pA, "pA"), (pB, "pB"), (pC, "pC"), (pD, "pD"), (pE, "pE"), (pF, "pF"), (pG, "pG")]
        nb_i = [0]

        def nbank():
            nb_i[0] += 1
            return nbanks[nb_i[0] % 7]

        def gen_n1(g):
            par = g % 2
            xnT = xnT2[par]
            xn_ = "n_xnT%d" % par
            for tl in range(4):
                pos = 4 * g + tl
                b2 = par
                S.dma("sp", xr2[b2][:], xs[pos * 128:(pos + 1) * 128, :], [], [("n_xr0", None)])
                rmsnorm(xr2[b2][:], ("n_xr0", None), hbn[b2][:], ("n_hb%d" % b2, None), sqn[b2], ssn[b2], "n")
                yield
                for c in range(8):
                    k.tr(pT[:, c * 128:(c + 1) * 128], hbn[b2][:, c * 128:(c + 1) * 128], ident[:], [("n_hb%d" % b2, None)], [("pT", c)])
                k.ve("dve", "tensor_copy", [("pT", None)], [(xn_, tl)], out=xnT[:, :, tl * 128:(tl + 1) * 128],
                     in_=pT[:, :].rearrange("p (c t) -> p c t", c=8))
                yield
            tok = slice(g * 512, (g + 1) * 512)
            for ci, (dst, dn) in enumerate(((kcmpT, "kcmpT"), (vcmpT, "vcmpT"), (kselT, "kselT"), (kwinT, "kwinT"))):
                pp, pn = nbank()
                for kc in range(8):
                    k.mm(pp[:, :], w_in_nb[:, kc, 512 + ci * 128:512 + (ci + 1) * 128], xnT[:, kc, :], kc == 0, kc == 7,
                         [("w_in_nb", kc), (xn_, None)], [(pn, None)])
                if ci % 2 == 0:
                    k.act(dst[:, tok], pp[:, :], AF.Copy, [(pn, None)], [(dn, g)])
                else:
                    k.ve("dve", "tensor_copy", [(pn, None)], [(dn, g)], out=dst[:, tok], in_=pp[:, :])
                yield
            for tl in range(4):
                pos = 4 * g + tl
                pp, pn = nbank()
                for kc in range(8):
                    k.mm(pp[:, 0:256], xnT[:, kc, tl * 128:(tl + 1) * 128], w_in_nb[:, kc, 1024:1280], kc == 0, kc == 7,
                         [("w_in_nb", kc), (xn_, None)], [(pn, None)])
                k.act(Vs[:, pos, :, 0:64], pp[:, 0:128].rearrange("p (a b) -> p a b", a=2), AF.Copy, [(pn, None)], [("Vs", pos)])
                k.ve("dve", "tensor_copy", [(pn, None)], [("Vw", pos)], out=Vw[:, pos, :, 0:64],
                     in_=pp[:, 128:256].rearrange("p (a b) -> p a b", a=2))
                for kv in range(2):
                    k.ve("pool", "tensor_copy", [("realtok", None)], [("Vs", pos)], out=Vs[:, pos, kv, 64:65],
                         in_=realtok[:, pos:pos + 1])
                    k.ve("pool", "tensor_copy", [("realtok", None)], [("Vw", pos)], out=Vw[:, pos, kv, 64:65],
                         in_=realtok[:, pos:pos + 1])
                yield
            for a in range(4):
                pp, pn = nbank()
                for kc in range(8):
                    k.mm(pp[:, 0:128], w_in_nb[:, kc, a * 128:(a + 1) * 128], xnT[:, kc, 384:512], kc == 0, kc == 7,
                         [("w_in_nb", kc), (xn_, None)], [(pn, None)])
                k.act(QT[:, g, a, :], pp[:, 0:128], AF.Copy, [(pn, None)], [("QT", g)])
                if a % 2 == 1:
                    yield
            pp, pn = nbank()
            for kc in range(8):
                k.mm(pp[:, 0:24], xnT[:, kc, 384:512], w_in_nb[:, kc, 1280:1304], kc == 0, kc == 7,
                     [("w_in_nb", kc), (xn_, None)], [(pn, None)])
            k.ve("dve", "tensor_tensor", [(pn, None), ("gateb", None)], [("GATES", g)], out=GATES[:, g, :], in0=pp[:, 0:24],
                 in1=gateb[:], op=ALU.add)
            k.act(GATES[:, g, :], GATES[:, g, :], AF.Sigmoid, [("GATES", g)], [("GATES", g)])
            yield

        actn = []
        nxtn = 0
        while True:
            while nxtn < 16 and len(actn) < 2:
                actn.append((nxtn, gen_n1(nxtn)))
                nxtn += 1
            if not actn:
                break
            for a in list(actn):
                try:
                    next(a[1])
                except StopIteration:
                    actn.remove(a)

        S.fence()
        en1b.close()
        S.fence()
        with nc.sbuf_tensor("n_w1b", [128, 32, 256], BF16) as w1b, \
                nc.sbuf_tensor("n_peT", [128, 32], F32) as peTf, \
                nc.sbuf_tensor("n_peTb", [128, 32], BF16) as peTb, \
                nc.sbuf_tensor("n_b1", [128, 2], F32) as b1t, \
                nc.sbuf_tensor("n_w2f", [128, 2, 128], F32) as w2f, \
                nc.sbuf_tensor("n_w2b", [128, 2, 128], BF16) as w2b, \
                nc.sbuf_tensor("n_hb1", [128, 2], F32) as hbias, \
                nc.sbuf_tensor("n_z", [128, 512], F32) as zt, \
                nc.sbuf_tensor("n_z2", [128, 512], F32) as z2t, \
                nc.sbuf_tensor("n_ge", [128, 2, 512], BF16) as ge:
            for X, srcT, sn in (("k", kcmpT, "kcmpT"), ("v", vcmpT, "vcmpT")):
                for part in range(8):
                    i = part % 2
                    S.dma("sp" if i == 0 else "pool", stg[i][:, :],
                          w1_d[X][:, part * 4:(part + 1) * 4, :].rearrange("p a b -> p (a b)"), [], [("stg%d" % i, None)])
                    k.ve("dve" if i == 0 else "pool", "tensor_copy", [("stg%d" % i, None)], [("w1b", part)],
                         out=w1b[:, part * 4:(part + 1) * 4, :].rearrange("p a b -> p (a b)"), in_=stg[i][:, :])
                S.dma("sp", peTf[:], peT_d[X][:, :], [], [("peTf", None)])
                k.ve("dve", "tensor_copy", [("peTf", None)], [("peTb", None)], out=peTb[:], in_=peTf[:])
                S.dma("sp", b1t[:], b1_d[X][:, :], [], [("b1t", None)])
                if X == "k":
                    S.dma("sp", w2f[:], w2k_d[:, :, :], [], [("w2f", None)])
                    k.ve("dve", "tensor_copy", [("w2f", None)], [("w2b", None)], out=w2b[:], in_=w2f[:])
                else:
                    S.dma("sp", w2f[:, :, 0:64], w2v_d[:, :, :], [], [("w2f", None)])
                    k.ve("dve", "tensor_copy", [("w2f", None)], [("w2b", None)], out=w2b[:, :, 0:64], in_=w2f[:, :, 0:64])
                for hc in range(2):
                    for l in range(32):
                        k.mm(pD[:, hc:hc + 1], w1b[0:64, l, hc * 128:(hc + 1) * 128], peTb[0:64, l:l + 1], l == 0, l == 31,
                             [("w1b", None), ("peTb", None)], [("pD", None)])
                k.ve("dve", "tensor_tensor", [("pD", None), ("b1t", None)], [("hbias", None)], out=hbias[:], in0=pD[:, 0:2],
                     in1=b1t[:], op=ALU.add)
                for kv in range(2):
                    rows = slice(kv * 64, kv * 64 + 64)
                    for hc in range(2):
                        pp, pn = (pA, "pA") if hc == 0 else (pB, "pB")
                        for l in range(32):
                            rhs = srcT[rows, l:l + 8192].rearrange("q (c s) -> q c s", s=16)[:, :, 0]
                            k.mm(pp[:, :], w1b[rows, l, hc * 128:(hc + 1) * 128], rhs, l == 0, l == 31,
                                 [("w1b", None), (sn, None)], [(pn, None)])
                        k.act(zt[:], pp[:, :], AF.Identity, [(pn, None), ("hbias", None)], [("zt", None)], bias=hbias[:, hc:hc + 1])
                        k.ve("dve", "tensor_tensor", [("zt", None)], [("z2t", None)], out=z2t[:], in0=zt[:], in1=zt[:], op=ALU.mult)
                        k.ve("dve", "tensor_scalar", [("z2t", None)], [("z2t", None)], out=z2t[:], in0=z2t[:], scalar1=0.044715,
                             scalar2=1.0, op0=ALU.mult, op1=ALU.add)
                        k.ve("pool", "tensor_tensor", [("z2t", None), ("zt", None)], [("z2t", None)], out=z2t[:], in0=z2t[:],
                             in1=zt[:], op=ALU.mult)
                        k.act(z2t[:], z2t[:], AF.Sigmoid, [("z2t", None)], [("z2t", None)], scale=1.5957691216057308)
                        k.ve("dve", "tensor_tensor", [("z2t", None), ("zt", None)], [("ge", hc)], out=ge[:, hc, :], in0=z2t[:],
                             in1=zt[:], op=ALU.mult)
                    if X == "k":
                        for hc in range(2):
                            k.mm(pC[:, :], w2b[:, hc, :], ge[:, hc, :], hc == 0, hc == 1, [("w2b", None), ("ge", None)],
                                 [("pC", None)])
                        k.ve("dve", "tensor_copy", [("pC", None)], [("KcT", kv)], out=KcT[rows, :], in_=pC[rows, :])
                    else:
                        for ct in range(4):
                            for hc in range(2):
                                k.mm(pC[:, 0:64], ge[:, hc, ct * 128:(ct + 1) * 128], w2b[:, hc, 0:64], hc == 0, hc == 1,
                                     [("w2b", None), ("ge", None)], [("pC", None)])
                            k.ve("dve", "tensor_scalar", [("pC", None), ("ovlm", None)], [("Vca", None)],
                                 out=Vca[:, ct, kv, 0:64], in0=pC[:, 0:64], scalar1=ovlm[:, ct, 0:1], scalar2=None, op0=ALU.mult)
                            k.ve("pool", "tensor_copy", [("ovlm", None)], [("Vca", None)], out=Vca[:, ct, kv, 64:65],
                                 in_=ovlm[:, ct, 0:1])
        S.fence()
        en1.close()
        S.fence()

        MBs = TN("MBs", [128, 8, 14, 128], BF16)
        MBw4 = TN("MBw4", [128, 8, 128], BF16)
        MBc = TN("MBc", [128, 8, 8, 128], BF16)
        Ebig = TN("Ebig", [128, 64, 128], BF16)
        for c8 in range(8):
            i = c8 % 2
            S.dma("sp" if i == 0 else "pool", stg[i][:, :], Ebig_d[:, c8 * 8:(c8 + 1) * 8, :].rearrange("p a b -> p (a b)"),
                  [], [("stg%d" % i, None)])
            k.ve("dve" if i == 0 else "pool", "tensor_copy", [("stg%d" % i, None)], [("Ebig", c8)],
                 out=Ebig[:, c8 * 8:(c8 + 1) * 8, :].rearrange("p a b -> p (a b)"), in_=stg[i][:, :])
        with nc.sbuf_tensor("n_relb", [32, 8, 128], F32) as relb, \
                nc.sbuf_tensor("n_OH0", [32, 512], F32) as OH0, \
                nc.sbuf_tensor("n_OH1", [32, 512], F32) as OH1, \
                nc.sbuf_tensor("n_Fv0", [128, 512], BF16) as Fv0, \
                nc.sbuf_tensor("n_Fv1", [128, 512], BF16) as Fv1:
            S.dma("sp", relb[:], relb_d[:, :, :], [], [("relb", None)])
            k.act(relb[:].rearrange("p a b -> p (a b)"), relb[:].rearrange("p a b -> p (a b)"), AF.Exp, [("relb", None)],
                  [("relb", None)])
            oi = 0
            for (OHd, L, scr, sname) in ((OHa_d, 2048, scr_a, "scr_a"), (OHw_d, 256, scr_w, "scr_w"), (OHc_d, 6144, scr_c, "scr_c")):
                for h in range(8):
                    for c0 in range(0, L, 512):
                        c1 = min(L, c0 + 512)
                        oi += 1
                        OH, OHn = (OH0, "OH0") if oi % 2 else (OH1, "OH1")
                        Fv, Fvn = (Fv0, "Fv0") if oi % 2 else (Fv1, "Fv1")
                        pp, pn = (pA, "pA") if oi % 2 else (pB, "pB")
                        S.dma("sp", OH[:, 0:c1 - c0], OHd[:, c0:c1], [], [(OHn, None)])
                        k.mm(pp[:, 0:c1 - c0], relb[:, h, :], OH[:, 0:c1 - c0], True, True, [("relb", None), (OHn, None)],
                             [(pn, None)])
                        if oi % 2:
                            k.act(Fv[:, 0:c1 - c0], pp[:, 0:c1 - c0], AF.Copy, [(pn, None)], [(Fvn, None)])
                        else:
                            k.ve("dve", "tensor_copy", [(pn, None)], [(Fvn, None)], out=Fv[:, 0:c1 - c0], in_=pp[:, 0:c1 - c0])
                        S.dma("pool", scr.ap()[h, :, c0:c1], Fv[:, 0:c1 - c0], [(Fvn, None)], [(sname, h)])
            for h in range(8):
                src = bass.AP(scr_a, h * 128 * 2048 + 127, [[2047, 128], [128, 14], [1, 128]])
                S.dma("pool", MBs[:, h, :, :], src, [("scr_a", h)], [("MBs", h)])
                src = bass.AP(scr_w, h * 128 * 256 + 127, [[255, 128], [1, 128]])
                S.dma("pool", MBw4[:, h, :], src, [("scr_w", h)], [("MBw4", h)])
                src = bass.AP(scr_c, h * 128 * 6144 + 128 * 3 + 2033, [[6144 - 16, 128], [512, 8], [1, 128]])
                S.dma("pool", MBc[:, h, :, :], src, [("scr_c", h)], [("MBc", h)])
        S.fence()

        S.fence()
        ens.close()
        S.fence()
        en2 = ExitStack()

        def TN2(name, shape, dt):
            return en2.enter_context(nc.sbuf_tensor("n_" + name, shape, dt))

        Pt = [TN2("Pt%d" % i, [128, 4, 128], BF16) for i in range(4)]
        AB_ = []
        for i in range(2):
            AB_.append(dict(
                i=i,
                Eall=TN2("Eall%d" % i, [128, 4, 4, 128], BF16),
                score=TN2("score%d" % i, [128, 128], F32),
                scr2=TN2("scr2%d" % i, [128, 128], F32),
                mx8=TN2("mx8%d" % i, [128, 8], F32),
                thr=TN2("thr%d" % i, [128, 1], F32),
                selb=TN2("selb%d" % i, [128, 128], BF16),
                selbT=TN2("selbT%d" % i, [128, 4, 128], BF16),
                densA=TN2("densA%d" % i, [128, 4], F32),
                coefA=TN2("coefA%d" % i, [128, 4], F32),
                Ccon=TN2("Ccon%d" % i, [128, 4, 64], F32),
            ))
        addm2 = [TN2("addm%d" % i, [128, 128], F32) for i in range(2)]
        dens = TN2("dens", [128, 8], F32)
        coef = TN2("coef", [128, 8], F32)
        Oacc = TN2("Oacc", [128, 8, 64], F32)
        Ob = TN2("Ob", [128, 512], BF16)
        sbanks = ((pA, "pA"), (pB, "pB"), (pF, "pF"), (pG, "pG"))
        sb_i = [0]

        def sbank():
            sb_i[0] += 1
            return sbanks[sb_i[0] % 4], sb_i[0] % 4

        def gen_A(n):
            it, gk = n // 2, n % 2
            Q = 4 * it + 3
            B = AB_[n % 2]
            bi = B["i"]
            nm = lambda s: (s + str(bi), None)
            Eall, score, scr2, mx8, thr, selb, selbT = B["Eall"], B["score"], B["scr2"], B["mx8"], B["thr"], B["selb"], B["selbT"]
            densA, coefA, Ccon = B["densA"], B["coefA"], B["Ccon"]
            addm = addm2[it % 2]
            addn = ("addm%d" % (it % 2), None)
            if gk == 0:
                S.dma("sp", addm[:], addm_d[:, it, :], [], [addn])
            rows = slice(gk * 64, gk * 64 + 64)
            qrhs = QT[rows, it, :, :].rearrange("q a t -> q (a t)")
            hs = slice(4 * gk, 4 * gk + 4)
            nct = min(4, Q // 16 + 1)
            for ct in range(nct):
                pp, pn = (pC, "pC") if ct % 2 == 0 else (pD, "pD")
                k.mm(pp[:, :], KcT[rows, ct * 128:(ct + 1) * 128], qrhs, True, True, [("KcT", None), ("QT", it)], [(pn, None)])
                k.act(Eall[:, ct, :, :].rearrange("p a t -> p (a t)"), pp[:, :], AF.Exp, [(pn, None)], [("Eall%d" % bi, ct)],
                      scale=0.125)
                n_ = min((Q - 16 * ct - 3) // 4, 7)
                k.ve("pool", "tensor_tensor", [("Eall%d" % bi, ct), ("MBc", None)], [("Eall%d" % bi, ct)], out=Eall[:, ct, :, :],
                     in0=Eall[:, ct, :, :], in1=MBc[:, hs, n_, :], op=ALU.mult)
                yield
            k.ve("dve", "memset", [], [("pC", None)], pC[:, 0:260], 0.0)
            k.ve("dve", "memset", [], [("pD", None)], pD[:, :], 0.0)
            for a in range(4):
                for ct in range(nct):
                    k.mm(pC[:, a * 65:(a + 1) * 65], Eall[:, ct, a, :], Vca[:, ct, gk, :], False, ct == nct - 1,
                         [("Eall%d" % bi, ct), ("Vca", None)], [("pC", None)])
                for ct in range(nct):
                    k.mm(pD[:, a * 128:(a + 1) * 128], Eall[:, ct, a, :], Ovl[:, ct, :], False, ct == nct - 1,
                         [("Eall%d" % bi, ct), ("Ovl", None)], [("pD", None)])
            yield
            k.ve("dve", "tensor_scalar", [("pC", None)], [nm("densA")], out=densA[:, 0:4],
                 in0=pC[:, 0:260].rearrange("p (a d) -> p a d", a=4)[:, :, 64], scalar1=1e-30, scalar2=None, op0=ALU.max)
            yield
            k.ve("dve", "reciprocal", [nm("densA")], [nm("densA")], out=densA[:, 0:4], in_=densA[:, 0:4])
            k.ve("pool", "tensor_copy", [addn], [nm("score")], out=score[:], in_=addm[:])
            yield
            for a in range(4):
                k.ve("dve", "scalar_tensor_tensor", [("pD", None), nm("densA"), nm("score")], [nm("score")],
                     out=score[:], in0=pD[:, a * 128:(a + 1) * 128], scalar=densA[:, a:a + 1], in1=score[:], op0=ALU.mult,
                     op1=ALU.add)
                yield
            k.ve("dve", "max", [nm("score")], [nm("mx8")], out=mx8[:], in_=score[:])
            yield
            k.ve("dve", "match_replace", [nm("score"), nm("mx8")], [nm("scr2")], out=scr2[:],
                 in_to_replace=mx8[:], in_values=score[:], imm_value=-3.0e9)
            yield
            k.ve("dve", "max", [nm("scr2")], [nm("mx8")], out=mx8[:], in_=scr2[:])
            yield
            k.ve("dve", "tensor_reduce", [nm("mx8")], [nm("thr")], out=thr[:, 0:1], in_=mx8[:], axis=AX.X, op=ALU.min)
            yield
            k.ve("dve", "tensor_scalar", [nm("score"), nm("thr")], [nm("selb")], out=selb[:], in0=score[:],
                 scalar1=thr[:, 0:1], scalar2=-30000.0, op0=ALU.is_lt, op1=ALU.mult)
            yield
            for a in range(4):
                k.tr(pT[:, a * 128:(a + 1) * 128], selb[:], ident[:], [nm("selb")], [("pT", a)])
            k.ve("dve", "tensor_copy", [("pT", None)], [nm("selbT")], out=selbT[:].rearrange("p a t -> p (a t)"),
                 in_=pT[:, 0:512])
            yield
            k.ve("pool", "tensor_tensor", [nm("densA"), ("GATES", it)], [nm("coefA")], out=coefA[:, 0:4], in0=densA[:, 0:4],
                 in1=GATES[:, it, :].rearrange("p (h j) -> p h j", j=3)[:, hs, 0], op=ALU.mult)
            yield
            for a in range(4):
                k.ve("dve", "tensor_scalar", [("pC", None), nm("coefA")], [nm("Ccon")], out=Ccon[:, a, :],
                     in0=pC[:, a * 65:a * 65 + 64], scalar1=coefA[:, a:a + 1], scalar2=None, op0=ALU.mult)
            yield

        def gen_B(n):
            it, gk = n // 2, n % 2
            Q = 4 * it + 3
            B = AB_[n % 2]
            bi = B["i"]
            selbT, Ccon = B["selbT"], B["Ccon"]
            rows = slice(gk * 64, gk * 64 + 64)
            qrhs = QT[rows, it, :, :].rearrange("q a t -> q (a t)")
            hs = slice(4 * gk, 4 * gk + 4)
            for bq, br in enumerate((1, 0)):
                if br == 0:
                    kT, kTn, Vt, Vn, kts = kselT, "kselT", Vs, "Vs", list(range(0, Q + 1))
                else:
                    kT, kTn, Vt, Vn, kts = kwinT, "kwinT", Vw, "Vw", list(range(max(Q - 4, 0), Q + 1))
                k.ve("dve", "memset", [], [("pE", None)], pE[:, 0:260], 0.0)
                SK = 3
                nk = len(kts)
                slots = []
                for step in range(nk + SK):
                    if step < nk:
                        kt = kts[step]
                        m = Q - kt
                        (pp, pn), bidx = sbank()
                        slots.append(bidx)
                        k.mm(pp[:, :], kT[rows, kt * 128:(kt + 1) * 128], qrhs, True, br == 1, [(kTn, None), ("QT", it)],
                             [(pn, None)])
                        if br == 0:
                            k.mm(pp[:, :], Ebig[:, kt, :], selbT[:].rearrange("p a t -> p (a t)"), False, True,
                                 [("Ebig", None), ("selbT%d" % bi, None)], [(pn, None)])
                        P = Pt[bidx]
                        Pn = "Pt%d" % bidx
                        k.act(P[:].rearrange("p a t -> p (a t)"), pp[:, :], AF.Exp, [(pn, None)], [(Pn, None)], scale=0.125)
                        if br == 1 and m == 4:
                            mb = MBw4[:, hs, :]
                            mbn = ("MBw4", None)
                        else:
                            mb = MBs[:, hs, min(m, 13), :]
                            mbn = ("MBs", None)
                        k.ve("dve", "tensor_tensor", [(Pn, None), mbn], [(Pn, None)], out=P[:],
                             in0=P[:], in1=mb, op=ALU.mult)
                    if step >= SK:
                        kt = kts[step - SK]
                        bidx = slots[step - SK]
                        P = Pt[bidx]
                        Pn = "Pt%d" % bidx
                        for a in range(4):
                            k.mm(pE[:, a * 65:(a + 1) * 65], P[:, a, :], Vt[:, kt, gk, :], False, kt == kts[-1],
                                 [(Pn, None), (Vn, kt)], [("pE", None)])
                    yield
                c0 = 4 * bq
                k.ve("dve", "tensor_scalar", [("pE", None)], [("dens", None)], out=dens[:, c0:c0 + 4],
                     in0=pE[:, 0:260].rearrange("p (a d) -> p a d", a=4)[:, :, 64], scalar1=1e-30, scalar2=None, op0=ALU.max)
                k.ve("dve", "reciprocal", [("dens", None)], [("dens", None)], out=dens[:, c0:c0 + 4], in_=dens[:, c0:c0 + 4])
                k.ve("dve", "tensor_tensor", [("dens", None), ("GATES", it)], [("coef", None)], out=coef[:, c0:c0 + 4],
                     in0=dens[:, c0:c0 + 4], in1=GATES[:, it, :].rearrange("p (h j) -> p h j", j=3)[:, hs, 1 + br], op=ALU.mult)
                for a in range(4):
                    if bq == 0:
                        in1, in1n = Ccon[:, a, :], ("Ccon%d" % bi, None)
                    else:
                        in1, in1n = Oacc[:, 4 * gk + a, :], ("Oacc", None)
                    k.ve("dve", "scalar_tensor_tensor", [("pE", None), ("coef", None), in1n], [("Oacc", None)],
                         out=Oacc[:, 4 * gk + a, :], in0=pE[:, a * 65:a * 65 + 64], scalar=coef[:, c0 + a:c0 + a + 1],
                         in1=in1, op0=ALU.mult, op1=ALU.add)
                yield
            if gk == 1:
                k.ve("dve", "tensor_copy", [("Oacc", None)], [("Ob", None)], out=Ob[:], in_=Oacc[:].rearrange("p h d -> p (h d)"))
                for c in range(4):
                    k.tr(pT[:, c * 128:(c + 1) * 128], Ob[:, c * 128:(c + 1) * 128], ident[:], [("Ob", None)], [("pT", c)])
                k.ve("dve", "tensor_copy", [("pT", None)], [("oT_n", it)], out=oT_n[:, it, :, :].rearrange("p c t -> p (c t)"),
                     in_=pT[:, 0:512])
                yield

        NN = 2 * NOWN
        doneA, doneB = -1, -1
        nxtA, nxtB = 0, 0
        activeN = []
        while True:
            if nxtA < NN and not any(a[0] == "A" for a in activeN) and nxtA <= doneB + 2:
                activeN.append(("A", nxtA, gen_A(nxtA)))
                nxtA += 1
            if nxtB < NN and not any(a[0] == "B" for a in activeN) and doneA >= nxtB:
                activeN.append(("B", nxtB, gen_B(nxtB)))
                nxtB += 1
            if not activeN:
                break
            for a in list(activeN):
                try:
                    next(a[2])
                except StopIteration:
                    activeN.remove(a)
                    if a[0] == "A":
                        doneA = a[1]
                    else:
                        doneB = a[1]
        assert doneB == NN - 1
        if "onsa" in dbg:
            donsa = nc.dram_tensor("donsa", [128, NOWN * 4 * 128], BF16, kind="ExternalOutput").ap()
            S.dma("sp", donsa[:, :], oT_n[:].rearrange("q a b c -> q (a b c)"), [("oT_n", None)], [])
        S.fence()
        en2.close()
        en.close()
        S.fence()

    if "stop_r" in dbg:
        S.finish()
        return nc
    es_stgc = ExitStack()
    alloc_stg(es_stgc, "c")
    gx = nc.alloc_sbuf_tensor("gx", [128, 8], F32)
    gffn = nc.alloc_sbuf_tensor("gffn", [128, 8], F32)
    gmem = nc.alloc_sbuf_tensor("gmem", [128, 8], F32)
    S.dma("sp", gx[:], gl["g_x"][:, :], [], [("gx", None)])
    S.dma("sp", gffn[:], gl["g_ffn"][:, :], [], [("gffn", None)])
    S.dma("sp", gmem[:], gl["g_mem"][:, :], [], [("gmem", None)])

    gf1 = nc.alloc_sbuf_tensor("gf1", [1, D], F32)
    gfb = nc.alloc_sbuf_tensor("gfb", [128, D], F32)
    S.dma("sp", gf1[:], g_f[:, :], [], [("gf1", None)])
    for h in range(2):
        pp = pA if h == 0 else pB
        k.mm(pp[:, :], ones_f[0:1, :], gf1[0:1, h * 512:(h + 1) * 512], True, True,
             [("ones_f", None), ("gf1", None)], [("pA" if h == 0 else "pB", None)])
        k.ve("dve", "tensor_copy", [("pA" if h == 0 else "pB", None)], [("gfb", None)],
             out=gfb[:, h * 512:(h + 1) * 512], in_=pp[:, :])

    x2all = nc.alloc_sbuf_tensor("x2all", [128, NOWN, D], F32)
    h2T = nc.alloc_sbuf_tensor("h2T", [128, NOWN, 8, 128], BF16)
    hT = nc.alloc_sbuf_tensor("hT", [128, 8, 128], BF16)

    if "omix" not in dbg:
        with nc.sbuf_tensor("w_out_b0", [128, 8, D], BF16) as w_out_b0, \
                nc.sbuf_tensor("xt0", [128, D], F32) as xt0:
            load_w(w_out_b0, "w_out_b0", wl["w_out"], 8, D)
            for it in range(NOWN):
                pos = 4 * it + 3
                S.dma("sp", xt0[:], xs[pos * 128:(pos + 1) * 128, :], [], [("xt0", None)])
                for hf in range(2):
                    pp, pn = (pA, "pA") if hf == 0 else (pB, "pB")
                    for kc in range(8):
                        if kc < 4:
                            lh, ln = oT_n[:, it, kc, :], ("oT_n", it)
                        else:
                            lh, ln = oT_r[:, it, kc - 4, :], ("oT_r", it)
                        k.mm(pp[:, :], lh, w_out_b0[:, kc, hf * 512:(hf + 1) * 512], kc == 0, kc == 7,
                             [ln, ("w_out_b0", kc)], [(pn, None)])
                    k.ve("dve", "tensor_tensor", [(pn, None), ("xt0", None)], [("x2all", it)],
                         out=x2all[:, it, hf * 512:(hf + 1) * 512], in0=pp[:, :], in1=xt0[:, hf * 512:(hf + 1) * 512],
                         op=ALU.add)
        S.fence()
        eo.close()
        S.fence()
    es_c1 = ExitStack()
    kmT = es_c1.enter_context(nc.sbuf_tensor("kmT", [128, 8, 256], BF16))
    Vm = es_c1.enter_context(nc.sbuf_tensor("Vm", [128, 2, D], BF16))
    with nc.sbuf_tensor("w_kv_b", [128, 8, 2 * D], BF16) as w_kv_b, \
            nc.sbuf_tensor("memf", [128, D], F32) as memf, \
            nc.sbuf_tensor("memT", [128, 8, 256], BF16) as memT:
        load_w(w_kv_b, "w_kv_b", wl["w_kv"], 8, 2 * D, gcol=gmem, gname="gmem")
        for mt in range(2):
            S.dma("sp", memf[:], mem[mt * 128:(mt + 1) * 128, :], [], [("memf", None)])
            rmsnorm(memf[:], ("memf", None), hb[:], ("hb", None), sq, ss, "")
            for c in range(8):
                k.tr(pT[:, c * 128:(c + 1) * 128], hb[:, c * 128:(c + 1) * 128], ident[:], [("hb", None)], [("pT", c)])
            k.ve("dve", "tensor_copy", [("pT", None)], [("memT", mt)],
                 out=memT[:, :, mt * 128:(mt + 1) * 128], in_=pT[:, :].rearrange("p (c t) -> p c t", c=8))
        for c in range(8):
            for kc in range(8):
                k.mm(pA[:, 0:256], w_kv_b[:, kc, c * 128:(c + 1) * 128], memT[:, kc, :], kc == 0, kc == 7,
                     [("w_kv_b", kc), ("memT", None)], [("pA", None)])
            k.ve("dve", "tensor_copy", [("pA", None)], [("kmT", c)], out=kmT[:, c, :], in_=pA[:, 0:256])
        for mt in range(2):
            for hf in range(2):
                for kc in range(8):
                    k.mm(pA[:, :], memT[:, kc, mt * 128:(mt + 1) * 128],
                         w_kv_b[:, kc, D + hf * 512:D + (hf + 1) * 512], kc == 0, kc == 7,
                         [("w_kv_b", kc), ("memT", None)], [("pA", None)])
                k.ve("dve", "tensor_copy", [("pA", None)], [("Vm", mt)], out=Vm[:, mt, hf * 512:(hf + 1) * 512],
                     in_=pA[:, :])
    S.fence()
    with nc.sbuf_tensor("w_q_b", [128, 8, D], BF16) as w_q_b, \
            nc.sbuf_tensor("w_o_b", [128, 8, D], BF16) as w_o_b, \
            nc.sbuf_tensor("c_sq1", [128, D], BF16) as c_sq1, \
            nc.sbuf_tensor("c_ss1", [128, 2], F32) as c_ss1, \
            nc.sbuf_tensor("c_hb1", [128, D], BF16) as c_hb1, \
            nc.sbuf_tensor("c_hT1", [128, 8, 128], BF16) as c_hT1, \
            nc.sbuf_tensor("qT0", [128, 8, 128], BF16) as qT0, \
            nc.sbuf_tensor("qT1", [128, 8, 128], BF16) as qT1, \
            nc.sbuf_tensor("PT0", [128, 2, 128], BF16) as PT0, \
            nc.sbuf_tensor("PT1", [128, 2, 128], BF16) as PT1, \
            nc.sbuf_tensor("rden0", [128, 128], F32) as rden0, \
            nc.sbuf_tensor("rden1", [128, 128], F32) as rden1, \
            nc.sbuf_tensor("oT0", [128, 8, 128], BF16) as oT0, \
            nc.sbuf_tensor("oT1", [128, 8, 128], BF16) as oT1:
        load_w(w_q_b, "w_q_b", wl["w_q"], 8, D, gcol=gx, gname="gx")
        load_w(w_o_b, "w_o_b", wl["w_o"], 8, D)
        CS = [dict(i=0, sq=sq, ss=ss, hb=hb, hT=hT, qT=qT0, PT=PT0, rden=rden0, oT=oT0),
              dict(i=1, sq=c_sq1, ss=c_ss1, hb=c_hb1, hT=c_hT1, qT=qT1, PT=PT1, rden=rden1, oT=oT1)]
        cbanks = [(pA, "pA"), (pB, "pB"), (pC, "pC"), (pD, "pD"), (pE, "pE"), (pF, "pF"), (pG, "pG")]
        cb_i = [0]

        def cbank():
            cb_i[0] += 1
            return cbanks[cb_i[0] % 7]

        def c_norm(src, sname, B):
            bi = B["i"]
            k.act(B["sq"][:], src, AF.Square, [sname], [("c_sq%d" % bi, None)])
            k.ve("dve", "reduce_sum", [("c_sq%d" % bi, None)], [("c_ss%d" % bi, None)], out=B["ss"][:, 0:1], in_=B["sq"][:], axis=AX.X)
            k.act(B["ss"][:, 1:2], B["ss"][:, 0:1], AF.Ln, [("c_ss%d" % bi, None)], [("c_ss%d" % bi, None)], bias=epsc[:, 0:1],
                  scale=1.0 / D)
            k.act(B["ss"][:, 1:2], B["ss"][:, 1:2], AF.Exp, [("c_ss%d" % bi, None)], [("c_ss%d" % bi, None)], scale=-0.5)
            k.ve("dve", "tensor_scalar", [sname, ("c_ss%d" % bi, None)], [("c_hb%d" % bi, None)], out=B["hb"][:], in0=src,
                 scalar1=B["ss"][:, 1:2], scalar2=None, op0=ALU.mult)

        def gen_c1(it, B):
            bi = B["i"]
            hbn, hTn, qTn, PTn, rdn, oTn = [("c_%s%d" % (s, bi), None) for s in ("hb", "hT", "qT", "PT", "rden", "oT")]
            c_norm(x2all[:, it, :], ("x2all", it), B)
            yield
            for c in range(8):
                k.tr(pT[:, c * 128:(c + 1) * 128], B["hb"][:, c * 128:(c + 1) * 128], ident[:], [hbn], [("pT", c)])
            k.ve("dve", "tensor_copy", [("pT", None)], [hTn], out=B["hT"][:].rearrange("p c t -> p (c t)"), in_=pT[:, :])
            yield
            for c in range(8):
                pp, pn = cbank()
                for kc in range(8):
                    k.mm(pp[:, 0:128], w_q_b[:, kc, c * 128:(c + 1) * 128], B["hT"][:, kc, :], kc == 0, kc == 7,
                         [("w_q_b", kc), hTn], [(pn, None)])
                k.act(B["qT"][:, c, :], pp[:, 0:128], AF.Copy, [(pn, None)], [("c_qT%d" % bi, c)])
                if c % 2 == 1:
                    yield
            for h in range(4):
                pp, pn = cbank()
                for mc in range(2):
                    for dc in range(2):
                        k.mm(pp[:, mc * 128:(mc + 1) * 128], kmT[:, 2 * h + dc, mc * 128:(mc + 1) * 128],
                             B["qT"][:, 2 * h + dc, :], dc == 0, dc == 1, [("kmT", None), qTn], [(pn, None)])
                k.act(B["PT"][:].rearrange("p c t -> p (c t)"), pp[:, 0:256], AF.Exp, [(pn, None)], [PTn], scale=1.0 / 16.0)
                yield
                pp, pn = cbank()
                for mc in range(2):
                    k.mm(pp[:, 0:128], ones_b[:, :], B["PT"][:, mc, :], mc == 0, mc == 1, [("ones_b", None), PTn], [(pn, None)])
                k.ve("dve", "reciprocal", [(pn, None)], [rdn], out=B["rden"][:], in_=pp[:, 0:128])
                for dc in range(2):
                    pp, pn = cbank()
                    for mc in range(2):
                        k.mm(pp[:, 0:128], Vm[:, mc, h * 256 + dc * 128:h * 256 + (dc + 1) * 128], B["PT"][:, mc, :], mc == 0,
                             mc == 1, [("Vm", None), PTn], [(pn, None)])
                    k.ve("dve", "tensor_tensor", [(pn, None), rdn], [("c_oT%d" % bi, 2 * h + dc)], out=B["oT"][:, 2 * h + dc, :],
                         in0=pp[:, 0:128], in1=B["rden"][:], op=ALU.mult)
                yield
            for hf in range(2):
                pp, pn = cbank()
                for kc in range(8):
                    k.mm(pp[:, :], B["oT"][:, kc, :], w_o_b[:, kc, hf * 512:(hf + 1) * 512], kc == 0, kc == 7,
                         [oTn, ("w_o_b", kc)], [(pn, None)])
                k.ve("dve", "tensor_tensor", [(pn, None), ("x2all", it)], [("x2all", it)],
                     out=x2all[:, it, hf * 512:(hf + 1) * 512], in0=pp[:, :], in1=x2all[:, it, hf * 512:(hf + 1) * 512],
                     op=ALU.add)
                yield
            c_norm(x2all[:, it, :], ("x2all", it), B)
            yield
            for c in range(8):
                k.tr(pT[:, c * 128:(c + 1) * 128], B["hb"][:, c * 128:(c + 1) * 128], ident[:], [hbn], [("pT", c)])
            k.ve("dve", "tensor_copy", [("pT", None)], [("h2T", it)], out=h2T[:, it, :, :].rearrange("p c t -> p (c t)"),
                 in_=pT[:, :])
            yield

        actC = []
        nxtC = 0
        while True:
            while nxtC < NOWN and len(actC) < 2:
                used = {a[0] for a in actC}
                bsi = 0 if 0 not in used else 1
                actC.append((bsi, gen_c1(nxtC, CS[bsi])))
                nxtC += 1
            if not actC:
                break
            for a in list(actC):
                try:
                    next(a[1])
                except StopIteration:
                    actC.remove(a)

    es_c1.close()
    eo.close()
    S.fence()
    with nc.sbuf_tensor("wg_b", [128, 8, 1408], BF16) as wg_b, \
            nc.sbuf_tensor("wu_b", [128, 8, 1408], BF16) as wu_b, \
            nc.sbuf_tensor("wd_b", [128, 11, D], BF16) as wd_b, \
            nc.sbuf_tensor("sg", [128, 128], F32) as sg, \
            nc.sbuf_tensor("sg2", [128, 128], F32) as sg2, \
            nc.sbuf_tensor("aT", [128, 11, 128], BF16) as aT, \
            nc.sbuf_tensor("yo", [128, D], F32) as yo:
        for half in range(2):
            load_w(wg_b, "wg_b", wl["w_gate"], 8, 1408, gcol=gffn, gname="gffn", c_lo=half * 1408)
            load_w(wu_b, "wu_b", wl["w_up"], 8, 1408, gcol=gffn, gname="gffn", c_lo=half * 1408)
            load_w(wd_b, "wd_b", wl["w_down"], 11, D, kc_lo=half * 11)
            for it in range(NOWN):
                for fc in range(11):
                    pg, pgn = (pC, "pC") if fc % 2 == 0 else (pE, "pE")
                    pu, pun = (pD, "pD") if fc % 2 == 0 else (pF, "pF")
                    sgt, sgn = (sg, "sg") if fc % 2 == 0 else (sg2, "sg2")
                    for kc in range(8):
                        k.mm(pg[:, 0:128], wg_b[:, kc, fc * 128:(fc + 1) * 128], h2T[:, it, kc, :], kc == 0, kc == 7,
                             [("wg_b", kc), ("h2T", it)], [(pgn, None)])
                    for kc in range(8):
                        k.mm(pu[:, 0:128], wu_b[:, kc, fc * 128:(fc + 1) * 128], h2T[:, it, kc, :], kc == 0, kc == 7,
                             [("wu_b", kc), ("h2T", it)], [(pun, None)])
                    k.act(sgt[:], pg[:, 0:128], AF.Silu, [(pgn, None)], [(sgn, None)])
                    k.ve("dve", "tensor_tensor", [(sgn, None), (pun, None)], [("aT", fc)], out=aT[:, fc, :], in0=sgt[:],
                         in1=pu[:, 0:128], op=ALU.mult)
                for hf in range(2):
                    pp, pn = (pA, "pA") if hf == 0 else (pB, "pB")
                    for fc in range(11):
                        k.mm(pp[:, :], aT[:, fc, :], wd_b[:, fc, hf * 512:(hf + 1) * 512], fc == 0, fc == 10,
                             [("aT", None), ("wd_b", fc)], [(pn, None)])
                    k.ve("dve", "tensor_tensor", [(pn, None), ("x2all", it)], [("x2all", it)],
                         out=x2all[:, it, hf * 512:(hf + 1) * 512], in0=pp[:, :],
                         in1=x2all[:, it, hf * 512:(hf + 1) * 512], op=ALU.add)
                if half == 1:
                    k.act(sq[:], x2all[:, it, :], AF.Square, [("x2all", it)], [("sq", None)])
                    k.ve("dve", "reduce_sum", [("sq", None)], [("ss", None)], out=ss[:, 0:1], in_=sq[:], axis=AX.X)
                    k.act(ss[:, 1:2], ss[:, 0:1], AF.Ln, [("ss", None)], [("ss", None)], bias=epsc[:, 0:1], scale=1.0 / D)
                    k.act(ss[:, 1:2], ss[:, 1:2], AF.Exp, [("ss", None)], [("ss", None)], scale=-0.5)
                    k.ve("dve", "tensor_scalar", [("x2all", it), ("ss", None)], [("yo", None)], out=yo[:],
                         in0=x2all[:, it, :], scalar1=ss[:, 1:2], scalar2=None, op0=ALU.mult)
                    k.ve("pool", "tensor_tensor", [("yo", None), ("gfb", None)], [("yo", None)], out=yo[:], in0=yo[:],
                         in1=gfb[:], op=ALU.mult)
                    S.dma("sp", y[it * 128:(it + 1) * 128, :], yo[:], [("yo", None)], [])
    S.finish()
    return nc


def _lay_w(w, kc):
    return np.ascontiguousarray(w.reshape(kc, 128, w.shape[1]).transpose(1, 0, 2))


def _lay_v(v):
    return np.ascontiguousarray(v.reshape(-1, 128).T)


NSA_COLS = 512 + 6 * 128 + 24


def _rwkv_consts():
    f = np.float32
    r = np.arange(128)
    t = np.arange(512)
    rst = np.broadcast_to((t % 64 != 0).astype(f), (128, 512))
    same = (r[:, None] // 64) == (r[None, :] // 64)
    mus = (same & ((r[:, None] % 64) < (r[None, :] % 64))).astype(f)
    mui = (same & ((r[:, None] % 64) <= (r[None, :] % 64))).astype(f)
    mls = (same & ((r[:, None] % 64) > (r[None, :] % 64))).astype(f)
    ma = np.concatenate([mus, mui, mus, mui], 1)
    ml4 = np.concatenate([mls] * 4, 1)
    i4 = np.concatenate([np.eye(128, dtype=f)] * 4, 1)
    return np.ascontiguousarray(np.stack([rst, ma, ml4, i4], 1)), same.astype(f)


def _rwkv_inputs(inp):
    f = np.float32
    g = lambda n: np.asarray(inp[n], f)[0]
    w_in = g("w_in")
    rm, bo = _rwkv_consts()
    lnw = g("rwkv_lnx_w").reshape(4, 2, 64)
    lnb = g("rwkv_lnx_b").reshape(4, 2, 64)
    rows_h = np.arange(128) // 64
    return {
        "w_in_r": _lay_w(w_in[:, NSA_COLS:NSA_COLS + 1792], 8),
        "g_mix": _lay_v(g("norm_mix_g")),
        "mu": _lay_v(g("rwkv_mu")),
        "rvecs": np.ascontiguousarray(np.stack([_lay_v(g("rwkv_w0")), _lay_v(g("rwkv_a0")), _lay_v(g("rwkv_k_k")),
                                                _lay_v(g("rwkv_k_a")), _lay_v(g("rwkv_r_k").reshape(-1))], 1)),
        "lora_up": np.ascontiguousarray(np.concatenate([g("rwkv_w_up"), g("rwkv_a_up")], 0)),
        "g_up": np.ascontiguousarray(g("rwkv_g_up")),
        "lnw": np.ascontiguousarray(lnw[:, rows_h, :].transpose(1, 0, 2)),
        "lnb": np.ascontiguousarray(lnb[:, rows_h, :].transpose(1, 0, 2)),
        "rmasks": rm,
        "blockones": bo,
    }


def _t5_bucket(n):
    n = np.asarray(n)
    nf = np.maximum(n, 1).astype(np.float32)
    large = 16 + (np.log(nf / np.float32(16)) / np.float32(math.log(2048 / 16)) * np.float32(16)).astype(np.int32)
    return np.where(n < 16, n, np.minimum(large, 31))


def _nsa_consts():
    f = np.float32
    def onehot(delta, valid):
        b = _t5_bucket(np.maximum(delta, 0))
        oh = np.zeros((32, delta.shape[0]), f)
        idx = np.nonzero(valid)[0]
        oh[b[idx], idx] = 1.0
        return oh
    u = np.arange(2048)
    OHa = onehot(u - 127, (u - 127) >= 0)
    u = np.arange(256)
    OHw = onehot(512 + u - 127, (u - 127) < 0)
    u = np.arange(6144)
    OHc = onehot(u - 2064, (u - 2064) >= 0)
    jj = np.arange(128)[:, None, None]
    kt = np.arange(64)[None, :, None]
    kk = np.arange(128)[None, None, :]
    Ebig = (jj == 2 * kt + kk // 64).astype(f)
    return OHa, OHw, OHc, np.ascontiguousarray(Ebig)


def _nsa_inputs(inp, j):
    f = np.float32
    g = lambda n: np.asarray(inp[n], f)[0]
    w_in = g("w_in")
    nd = (3 - j) * 128
    qperm = np.concatenate([np.r_[a * 64:(a + 1) * 64, (4 + a) * 64:(5 + a) * 64] for a in range(4)])
    cols = np.concatenate([qperm, np.arange(512, 640), np.arange(640, 768), np.arange(768, 896), np.arange(1024, 1152),
                           np.arange(896, 1024), np.arange(1152, 1280), np.arange(1280, 1304)])
    OHa, OHw, OHc, Ebig = _nsa_consts()
    def w1lay(w):
        a = w.reshape(32, 64, 256).transpose(1, 0, 2)
        return np.ascontiguousarray(np.concatenate([a, a], 0))
    def pelay(pe):
        a = pe.T
        return np.ascontiguousarray(np.concatenate([a, a], 0))
    w2k = g("cmp_k_w2").reshape(2, 128, 64).transpose(1, 0, 2)
    w2v = g("cmp_v_w2").reshape(2, 128, 64).transpose(1, 0, 2)
    tpos = np.arange(64)[None, :] * 128 + np.arange(128)[:, None]
    realtok = (tpos >= nd).astype(f)
    c = np.arange(4)[None, :] * 128 + np.arange(128)[:, None]
    realblk = ((16 * c >= nd) & (c <= 510)).astype(f)
    jb = np.arange(128)[None, None, :]
    ov = ((16 * c[:, :, None] <= 64 * jb + 63) & (16 * c[:, :, None] + 31 >= 64 * jb)).astype(f)
    ovlm = np.concatenate([realblk[:, :, None], ov * realblk[:, :, None]], 2)
    it = np.arange(16)[None, :, None]
    q = np.arange(128)[:, None, None]
    t = 128 * (4 * it + 3) + q
    cur = t // 64
    blk0 = nd // 64
    bad = (jb < blk0) | (64 * jb > t)
    forced = (jb == blk0) | (jb == cur) | (jb == cur - 1)
    addmask = np.where(bad, -1e9, np.where(forced, 1e4, 0.0)).astype(f)
    return {
        "w_in_n": _lay_w(np.ascontiguousarray(w_in[:, cols]), 8),
        "g_mix": _lay_v(g("norm_mix_g")),
        "gate_b": np.ascontiguousarray(np.broadcast_to(g("nsa_gate_b")[None, :], (128, 24))),
        "realtok": np.ascontiguousarray(realtok),
        "cmp_w1k": w1lay(g("cmp_k_w1")), "cmp_w1v": w1lay(g("cmp_v_w1")),
        "cmp_peTk": pelay(g("cmp_pe_k")), "cmp_peTv": pelay(g("cmp_pe_v")),
        "cmp_b1k": _lay_v(g("cmp_k_b1")), "cmp_b1v": _lay_v(g("cmp_v_b1")),
        "cmp_w2k": np.ascontiguousarray(np.concatenate([w2k, w2k], 2)),
        "cmp_w2v": np.ascontiguousarray(w2v),
        "ovlm": np.ascontiguousarray(ovlm.astype(f)),
        "relb_rep": np.ascontiguousarray(np.broadcast_to(np.asarray(inp["rel_bias"], f)[:, :, None], (32, 8, 128))),
        "OHa": OHa, "OHw": OHw, "OHc": OHc, "Ebig": Ebig,
        "addmask": np.ascontiguousarray(addmask),
    }


def _core_inputs(c, inp):
    b, j = c // 4, c % 4
    f = np.float32
    x = np.asarray(inp["x"], f)[b]
    nd = (3 - j) * 128
    xs = np.zeros((8192, D), f)
    xs[nd:] = x[: 8192 - nd]
    m = {
        "xs": xs,
        "ident": np.eye(128, dtype=f),
        "g_f": np.asarray(inp["norm_f_g"], f).reshape(1, D),
        "g_x": _lay_v(np.asarray(inp["norm_x_g"], f)[0]),
        "g_ffn": _lay_v(np.asarray(inp["norm_ffn_g"], f)[0]),
        "g_mem": _lay_v(np.asarray(inp["norm_mem_g"], f)[0]),
        "mem": np.ascontiguousarray(np.asarray(inp["mem"], f)[b]),
        "w_out": _lay_w(np.asarray(inp["w_out"], f)[0], 8),
        "w_q": _lay_w(np.asarray(inp["w_q_x"], f)[0], 8),
        "w_kv": _lay_w(np.asarray(inp["w_kv_x"], f)[0], 8),
        "w_o": _lay_w(np.asarray(inp["w_o_x"], f)[0], 8),
        "w_gate": _lay_w(np.asarray(inp["w_gate"], f)[0], 8),
        "w_up": _lay_w(np.asarray(inp["w_up"], f)[0], 8),
        "w_down": _lay_w(np.asarray(inp["w_down"], f)[0], 22),
    }
    return m


def kernel(**inp):
    nc = build_program(dbg={"rwkv": 1, "nsa": 1})
    rin = _rwkv_inputs(inp)
    in_maps = []
    for c in range(8):
        m = _core_inputs(c, inp)
        m.update(rin)
        m.update(_nsa_inputs(inp, c % 4))
        in_maps.append(m)
    res = run_bass_kernel_spmd(nc, in_maps, core_ids=list(range(8)))
    out = np.zeros((2, 8192, D), np.float32)
    for c in range(8):
        b, j = c // 4, c % 4
        yv = np.asarray(res.results[c]["y"]).reshape(NOWN, 128, D)
        out[b].reshape(64, 128, D)[j::4] = yv
    return out
```

```python
import math
import numpy as np
from contextlib import ExitStack
import concourse.bass as bass
import concourse.mybir as mybir
from concourse.bass_utils import run_bass_kernel_spmd

F32 = mybir.dt.float32
BF16 = mybir.dt.bfloat16
AF = mybir.ActivationFunctionType
ALU = mybir.AluOpType
AX = mybir.AxisListType

D = 1024
DFF = 2816
NOWN = 16
RMS_EPS = 1e-6


class Sync:
    EPOCH = 24000
    NDMA = 12

    def __init__(self, nc):
        self.nc = nc
        self.eng = {"pe": nc.tensor, "act": nc.scalar, "dve": nc.vector, "pool": nc.gpsimd, "sp": nc.sync}
        self.cnt = {e: 0 for e in self.eng}
        self.sems = {}
        self.known = {e: {} for e in self.eng}
        self.vc = {}
        self.dma_sems = {}
        self.dma_val = {}
        self.dma_rr = {}
        self.state = {}

    def _sem(self, e, ep):
        k = (e, ep)
        if k not in self.sems:
            self.sems[k] = self.nc.alloc_semaphore("s_%s_%d" % (e, ep))
        return self.sems[k]

    def _wait(self, e, tok):
        kn = self.known[e]
        if tok[0] == "c":
            _, f, n = tok
            if f == "pe" and e == "pe":
                return
            if kn.get(f, 0) >= n:
                return
            ep = (n - 1) // self.EPOCH
            self.eng[e].wait_ge(self._sem(f, ep), n - ep * self.EPOCH)
            kn[f] = n
        else:
            _, q, slot, v = tok
            key = ("d", q, slot)
            if kn.get(key, 0) >= v:
                return
            self.eng[e].wait_ge(self.dma_sems[(q, slot)], v)
            kn[key] = v
        snap = self.vc.get(tok)
        if snap:
            for k2, v2 in snap.items():
                if kn.get(k2, 0) < v2:
                    kn[k2] = v2

    @staticmethod
    def _rkey(tok):
        return tok[1] if tok[0] == "c" else (tok[1], tok[2])

    def _deps(self, reads, writes):
        deps = []
        for (name, sub) in reads:
            st = self.state.get(name)
            if st:
                for s2, ent in st.items():
                    if (sub is None or s2 is None or s2 == sub) and ent[0] is not None:
                        deps.append(ent[0])
        for (name, sub) in writes:
            st = self.state.get(name)
            if st:
                for s2, ent in st.items():
                    if sub is None or s2 is None or s2 == sub:
                        if ent[0] is not None:
                            deps.append(ent[0])
                        deps.extend(ent[1].values())
        return deps

    def _update(self, tok, reads, writes):
        rk = self._rkey(tok)
        for (name, sub) in reads:
            st = self.state.setdefault(name, {})
            ent = st.get(sub)
            if ent is None:
                ent = st[sub] = [None, {}]
            ent[1][rk] = tok
        for (name, sub) in writes:
            st = self.state.setdefault(name, {})
            if sub is None:
                st.clear()
            st[sub] = [tok, {}]

    PSUM = ("pA", "pB", "pC", "pD", "pE", "pF", "pG", "pT", "pJ0", "pJ1", "pM0", "pM1", "pS0", "pS1", "pS2")

    def op(self, e, fn, reads=(), writes=()):
        if e != "pe":
            writes = list(writes) + [(r[0], None) for r in reads if r[0] in self.PSUM]
        for d in self._deps(reads, writes):
            self._wait(e, d)
        ins = fn()
        self.cnt[e] += 1
        n = self.cnt[e]
        ep = (n - 1) // self.EPOCH
        ins.then_inc(self._sem(e, ep), 1)
        tok = ("c", e, n)
        self.vc[tok] = dict(self.known[e])
        self._update(tok, reads, writes)
        return tok

    def dma(self, q, out, in_, reads=(), writes=()):
        slot = self.dma_rr.get(q, 0) % self.NDMA
        self.dma_rr[q] = self.dma_rr.get(q, 0) + 1
        k = (q, slot)
        if k not in self.dma_sems:
            self.dma_sems[k] = self.nc.alloc_semaphore("d_%s_%d" % (q, slot))
            self.dma_val[k] = 0
        if self.dma_val[k] > 0:
            self._wait(q, ("d", q, slot, self.dma_val[k]))
        for d in self._deps(reads, writes):
            self._wait(q, d)
        ins = self.eng[q].dma_start(out=out, in_=in_)
        self.dma_val[k] += 16
        ins.then_inc(self.dma_sems[k], 16)
        tok = ("d", q, slot, self.dma_val[k])
        self.vc[tok] = dict(self.known[q])
        self._update(tok, reads, writes)
        return tok

    def fence(self):
        for e in self.eng:
            for f in self.eng:
                if self.cnt[f] > 0:
                    self._wait(e, ("c", f, self.cnt[f])) if not (e == "pe" and f == "pe") else None
            for (q, slot), v in list(self.dma_val.items()):
                if v > 0:
                    self._wait(e, ("d", q, slot, v))

    def finish(self):
        for (q, slot), v in list(self.dma_val.items()):
            if v > 0:
                self._wait(q, ("d", q, slot, v))


class K:
    def __init__(self, nc):
        self.nc = nc
        self.S = Sync(nc)
        self.rr = 0

    def mm(self, out, lhsT, rhs, start, stop, r, w):
        nc = self.nc
        return self.S.op("pe", lambda: nc.tensor.matmul(out, lhsT=lhsT, rhs=rhs, start=start, stop=stop), r, w)

    def tr(self, out, in_, ident, r, w):
        nc = self.nc
        return self.S.op("pe", lambda: nc.tensor.transpose(out, in_, ident), r + [("ident", None)], w)

    def act(self, out, in_, func, r, w, bias=None, scale=None):
        nc = self.nc
        kw = {}
        if bias is not None:
            kw["bias"] = bias
        if scale is not None:
            kw["scale"] = scale
        return self.S.op("act", lambda: nc.scalar.activation(out=out, in_=in_, func=func, **kw), r, w)

    def ve(self, e, name, r, w, *a, **kw):
        eng = self.S.eng[e]
        return self.S.op(e, lambda: getattr(eng, name)(*a, **kw), r, w)

    def alt(self):
        self.rr += 1
        return "dve" if self.rr % 2 else "pool"


def build_program(dbg=None):
    dbg = dbg or {}
    nc = bass.Bass("TRN2", target_bir_lowering=False)
    k = K(nc)
    S = k.S

    def din(name, shape, dt=F32):
        return nc.dram_tensor(name, list(shape), dt, kind="ExternalInput").ap()

    xs = din("xs", [8192, D])
    ident_d = din("ident", [128, 128])
    g_f = din("g_f", [1, D])
    gl = {n: din(n, [128, 8]) for n in ["g_x", "g_ffn", "g_mem"]}
    mem = din("mem", [256, D])
    wl = {
        "w_out": din("w_out", [128, 8, D]),
        "w_q": din("w_q", [128, 8, D]),
        "w_kv": din("w_kv", [128, 8, 2 * D]),
        "w_o": din("w_o", [128, 8, D]),
        "w_gate": din("w_gate", [128, 8, DFF]),
        "w_up": din("w_up", [128, 8, DFF]),
        "w_down": din("w_down", [128, 22, D]),
    }
    y = nc.dram_tensor("y", [NOWN * 128, D], F32, kind="ExternalOutput").ap()
    if "omix" in dbg:
        omix_d = din("omix", [NOWN * 128, D])
        dx1 = nc.dram_tensor("dx1", [NOWN * 128, D], F32, kind="ExternalOutput").ap()
        dx2 = nc.dram_tensor("dx2", [NOWN * 128, D], F32, kind="ExternalOutput").ap()
        dq = nc.dram_tensor("dq", [128, 1024], BF16, kind="ExternalOutput").ap()
        dk = nc.dram_tensor("dk", [128, 2048], BF16, kind="ExternalOutput").ap()
        dP = nc.dram_tensor("dP", [128, 256], BF16, kind="ExternalOutput").ap()
        dr = nc.dram_tensor("dr", [128, 128], F32, kind="ExternalOutput").ap()
        dh = nc.dram_tensor("dh", [128, 1024], BF16, kind="ExternalOutput").ap()
        dV = nc.dram_tensor("dV", [128, 2048], BF16, kind="ExternalOutput").ap()
        do = nc.dram_tensor("do", [128, 1024], BF16, kind="ExternalOutput").ap()

    ident = nc.alloc_sbuf_tensor("ident_b", [128, 128], BF16)
    identf = nc.alloc_sbuf_tensor("ident_f", [128, 128], F32)
    S.dma("sp", identf[:], ident_d[:, :], [], [("identf", None)])
    k.ve("dve", "tensor_copy", [("identf", None)], [("ident", None)], out=ident[:], in_=identf[:])
    ones_b = nc.alloc_sbuf_tensor("ones_b", [128, 128], BF16)
    k.ve("dve", "memset", [], [("ones_b", None)], ones_b[:], 1.0)
    epsc = nc.alloc_sbuf_tensor("epsc", [128, 1], F32)
    k.ve("dve", "memset", [], [("epsc", None)], epsc[:], RMS_EPS)
    ones_f = nc.alloc_sbuf_tensor("ones_f", [1, 128], F32)
    k.ve("dve", "memset", [], [("ones_f", None)], ones_f[:], 1.0)

    pT = nc.alloc_psum_tensor("pT", [128, 1024], BF16)
    pA = nc.alloc_psum_tensor("pA", [128, 512], F32)
    pB = nc.alloc_psum_tensor("pB", [128, 512], F32)
    pC = nc.alloc_psum_tensor("pC", [128, 512], F32)
    pD = nc.alloc_psum_tensor("pD", [128, 512], F32)
    pE = nc.alloc_psum_tensor("pE", [128, 512], F32)
    pF = nc.alloc_psum_tensor("pF", [128, 512], F32)
    pG = nc.alloc_psum_tensor("pG", [128, 512], F32)

    stg = [None, None]

    def alloc_stg(es, tag, side=None):
        for i in range(2):
            kw = {"side": side} if side else {}
            stg[i] = es.enter_context(nc.sbuf_tensor("stg%d%s" % (i, tag), [128, 1024], F32, **kw))
    stg_i = [0]

    def load_w(dst, dname, src, KC, ncols, gcol=None, gname=None, c_lo=0, kc_lo=0):
        for kc in range(KC):
            for c0 in range(0, ncols, 1024):
                c1 = min(ncols, c0 + 1024)
                i = stg_i[0] % 2
                stg_i[0] += 1
                q = "sp" if i == 0 else "pool"
                S.dma(q, stg[i][:, 0:c1 - c0], src[:, kc_lo + kc, c_lo + c0:c_lo + c1], [], [("stg%d" % i, None)])
                e = "dve" if i == 0 else "act"
                if gcol is not None:
                    if e == "dve":
                        k.ve("dve", "tensor_scalar", [("stg%d" % i, None), (gname, None)], [(dname, kc)],
                             out=dst[:, kc, c0:c1], in0=stg[i][:, 0:c1 - c0], scalar1=gcol[:, kc_lo + kc:kc_lo + kc + 1],
                             scalar2=None, op0=ALU.mult)
                    else:
                        k.act(dst[:, kc, c0:c1], stg[i][:, 0:c1 - c0], AF.Copy, [("stg%d" % i, None), (gname, None)],
                              [(dname, kc)], scale=gcol[:, kc_lo + kc:kc_lo + kc + 1])
                else:
                    if e == "dve":
                        k.ve("dve", "tensor_copy", [("stg%d" % i, None)], [(dname, kc)], out=dst[:, kc, c0:c1],
                             in_=stg[i][:, 0:c1 - c0])
                    else:
                        k.act(dst[:, kc, c0:c1], stg[i][:, 0:c1 - c0], AF.Copy, [("stg%d" % i, None)], [(dname, kc)])

    def rmsnorm(src, sname, dst, dname, sq, ss, tag):
        nc_ = nc
        S.op("act", lambda: nc_.scalar.activation(out=sq[:], in_=src, func=AF.Square, accum_out=ss[:, 0:1]),
             [sname], [("sq" + tag, None), ("ss" + tag, None)])
        k.act(ss[:, 1:2], ss[:, 0:1], AF.Ln, [("ss" + tag, None)], [("ss" + tag, None)], bias=epsc[:, 0:1], scale=1.0 / D)
        k.act(ss[:, 1:2], ss[:, 1:2], AF.Exp, [("ss" + tag, None)], [("ss" + tag, None)], scale=-0.5)
        k.ve("dve", "tensor_scalar", [sname, ("ss" + tag, None)], [dname], out=dst, in0=src, scalar1=ss[:, 1:2],
             scalar2=None, op0=ALU.mult)

    def transpose8(src, sname, dstT, dname, nchunk=8):
        for c in range(nchunk):
            k.tr(pT[:, c * 128:(c + 1) * 128], src[:, c * 128:(c + 1) * 128], ident[:], [sname], [("pT", c)])
        k.ve("dve", "tensor_copy", [("pT", None)], [dname], out=dstT, in_=pT[:, 0:nchunk * 128])

    pJ0, pJ1, pM0, pM1, pS0, pS1, pS2 = pA, pB, pC, pD, pE, pF, pG
    epsl = nc.alloc_sbuf_tensor("epsl", [128, 1], F32)
    k.ve("dve", "memset", [], [("epsl", None)], epsl[:], 64e-5)
    sq2 = [nc.alloc_sbuf_tensor("sq0", [128, D], BF16), None]
    ss2 = [nc.alloc_sbuf_tensor("ss0", [128, 2], F32), None]
    hb2 = [nc.alloc_sbuf_tensor("hb0", [128, D], BF16), None]
    sq, ss, hb = sq2[0], ss2[0], hb2[0]
    eo = ExitStack()
    oT_r = eo.enter_context(nc.sbuf_tensor("oT_r", [128, NOWN, 4, 128], BF16, side="right"))
    LNX_EPS = 64e-5
    if "rwkv" in dbg or "nsa" in dbg:
        gmix_d = din("g_mix", [128, 8])
    if "rwkv" in dbg:
        w_in_r = din("w_in_r", [128, 8, 1792])
        mu_d = din("mu", [128, 14])
        rv_d = din("rvecs", [128, 5, 4])
        lora_up_d = din("lora_up", [128, 512])
        g_up_d = din("g_up", [128, 512])
        lnw_d = din("lnw", [128, 4, 64])
        lnb_d = din("lnb", [128, 4, 64])
        rmask_d = din("rmasks", [128, 4, 512])
        bones_d = din("blockones", [128, 128])
        er = ExitStack()
        GT = 256
        NG = 8192 // GT
        NCH = GT // 64
        NSLOT = 3

        def T(name, shape, dt):
            return er.enter_context(nc.sbuf_tensor("r_" + name, shape, dt))

        alloc_stg(er, "r")
        sq2[1] = T("sq1", [128, D], BF16)
        ss2[1] = T("ss1", [128, 2], F32)
        hb2[1] = T("hb1", [128, D], BF16)
        w_in_b = T("w_in_b", [128, 8, 1792], BF16)
        gmix = T("gmix", [128, 8], F32)
        mu = T("mu", [128, 14], F32)
        rv = T("rv", [128, 5, 4], F32)
        omk = T("omk", [128, 4], F32)
        lup_b = T("lup_b", [128, 512], BF16)
        gup_b = T("gup_b", [128, 512], BF16)
        lnw = T("lnw", [128, 4, 64], F32)
        lnb = T("lnb", [128, 4, 64], F32)
        RSTt = T("RSTt", [128, GT], F32)
        rmb = T("rmb", [128, 3, 512], BF16)
        bones = T("bones", [128, 128], F32)
        onesc = T("onesc", [128, 1], BF16)
        xnT1 = T("xnT0", [128, 8, GT], BF16)
        xnT = [xnT1, xnT1]
        xr1 = T("xr0", [128, D], F32)
        xr = [xr1, xr1]
        carry = T("carry", [128, 14], F32)
        RWl = T("RWl", [128, 2, GT], F32)
        Pstl = T("Pstl", [128, GT + 1], F32)
        LT = [T("LT%d" % i, [128, GT], BF16) for i in range(2)]
        SGD = [T("SGD%d" % i, [128, NCH, 2, 64], BF16) for i in range(2)]
        WS = []
        for i in range(2):
            WS.append(dict(
                i=i,
                Pst=T("Pst%d" % i, [128, GT + 1], F32),
                RW3=T("RW3_%d" % i, [128, 3, GT], F32),
                Wk=[T("Wk%d_%d" % (i, j), [128, GT], F32) for j in range(7)],
            ))
        BTm = [T("BT%d" % i, [128, 4, 4, NCH, 128], BF16) for i in range(2)]
        BTo = T("BTo", [128, 2, 4, 2, 128], BF16)
        GC = [T("GC%d" % i, [128, 4, NCH], F32) for i in range(2)]
        SL = []
        for i in range(NSLOT):
            SL.append(dict(
                i=i,
                TT=T("TT%d" % i, [128, 3, 512], BF16),
                X1=T("X1_%d" % i, [128, 2, 2, 2, 128], BF16),
                X2=T("X2_%d" % i, [128, 2, 2, 2, 128], BF16),
                X3=T("X3_%d" % i, [128, 512], BF16),
                Ak=[T("Ak%d_%d" % (i, j), [128, 512], BF16) for j in range(2)],
                Bk=[T("Bk%d_%d" % (i, j), [128, 512], BF16) for j in range(2)],
                Wt=[T("Wt%d_%d" % (i, j), [128, 512], BF16) for j in range(2)],
                Zt=[T("Zt%d_%d" % (i, j), [128, 512], BF16) for j in range(2)],
            ))
        Hf = T("Hf", [128, 512], F32)
        Hb = T("Hb", [128, 512], BF16)
        Hs = T("Hs", [128, 512], F32)
        RHSb = T("RHSb", [128, 512], BF16)
        Ub = T("Ub", [128, 512], BF16)
        Yc = T("Yc", [128, 4, 64], F32)
        Gc = T("Gc", [128, 4, 64], F32)
        V2c = T("V2c", [128, 4, 64], F32)
        Ycen = [T("Ycen%d" % i, [128, 64], F32) for i in range(4)]
        Ysq = [T("Ysq%d" % i, [128, 64], F32) for i in range(4)]
        st = [T("st%d" % i, [128, 8], F32) for i in range(4)]
        stb = T("stb", [128, 4], F32)
        Oblk = T("Oblk", [128, 4, 128], BF16)

        S.dma("sp", gmix[:], gmix_d[:, :], [], [("gmix", None)])
        S.dma("sp", mu[:], mu_d[:, :], [], [("mu", None)])
        S.dma("sp", rv[:], rv_d[:, :, :], [], [("rv", None)])
        S.dma("sp", stg[0][:, 0:512], lora_up_d[:, :], [], [("stg0", None)])
        k.ve("dve", "tensor_copy", [("stg0", None)], [("lup_b", None)], out=lup_b[:], in_=stg[0][:, 0:512])
        S.dma("sp", stg[1][:, 0:512], g_up_d[:, :], [], [("stg1", None)])
        k.ve("dve", "tensor_copy", [("stg1", None)], [("gup_b", None)], out=gup_b[:], in_=stg[1][:, 0:512])
        S.dma("sp", lnw[:], lnw_d[:, :, :], [], [("lnw", None)])
        S.dma("sp", lnb[:], lnb_d[:, :, :], [], [("lnb", None)])
        S.dma("sp", RSTt[:], rmask_d[:, 0, 0:GT], [], [("rmask", None)])
        for mi in range(3):
            S.dma("sp", stg[mi % 2][:, 0:512], rmask_d[:, 1 + mi, :], [], [("stg%d" % (mi % 2), None)])
            k.ve("dve", "tensor_copy", [("stg%d" % (mi % 2), None)], [("rmb", mi)], out=rmb[:, mi, :], in_=stg[mi % 2][:, 0:512])
        S.dma("pool", bones[:], bones_d[:, :], [], [("bones", None)])
        k.ve("dve", "memset", [], [("onesc", None)], onesc[:], 1.0)
        k.ve("dve", "memset", [], [("carry", None)], carry[:], 0.0)
        k.ve("dve", "memset", [], [("Hf", None)], Hf[:], 0.0)
        k.ve("dve", "memset", [], [("Hb", None)], Hb[:], 0.0)
        for par in range(2):
            for ty in range(4):
                k.ve("pool", "memset", [], [("BT%d" % par, None)], BTm[par][:, ty, :, :, :].rearrange("p b c d -> p (b c d)"), 0.0)
        k.ve("pool", "memset", [], [("BTo", None)], BTo[:].rearrange("p a b c d -> p (a b c d)"), 0.0)
        k.ve("pool", "memset", [], [("Oblk", None)], Oblk[:].rearrange("p a b -> p (a b)"), 0.0)
        k.ve("dve", "tensor_scalar", [("rv", None)], [("omk", None)], out=omk[:], in0=rv[:, 3, :], scalar1=-1.0,
             scalar2=1.0, op0=ALU.mult, op1=ALU.add)
        load_w(w_in_b, "w_in_b", w_in_r, 8, 1792, gcol=gmix, gname="gmix")

        MA = rmb[:, 0, :]
        ML4 = rmb[:, 1, :]
        I4 = rmb[:, 2, :]
        RST = RSTt[:, :]
        mm_i = [0]
        pj_i = [0]

        rbanks = [(pJ0, "pJ0"), (pJ1, "pJ1"), (pM0, "pM0"), (pM1, "pM1"), (pS0, "pS0"), (pS1, "pS1"), (pS2, "pS2")]

        def pM():
            mm_i[0] += 1
            return rbanks[mm_i[0] % 7]

        pJ = pM

        def v3(ap):
            return ap.rearrange("q (c t) -> q c t", c=NCH)

        def halves(e, name, r, w, out3, mk):
            for hh in range(2):
                lo, hi = hh * 64, hh * 64 + 64
                k.ve(e, name, r, w, out=out3[lo:hi, :, lo:hi], **mk(lo, hi))

        def inproj(par, ci, Pst, pstn, dst, dname):
            pp, pn = pJ()
            for kc in range(8):
                k.mm(pp[:, 0:GT], w_in_b[:, kc, ci * 128:(ci + 1) * 128], xnT[par][:, kc, :], kc == 0, kc == 7,
                     [("w_in_b", kc), ("xnT0", None)], [(pn, None)])
            k.act(Pst[:, 0:1], carry[:, ci:ci + 1], AF.Copy, [("carry", ci)], [(pstn, 0)])
            k.act(Pst[:, 1:GT + 1], pp[:, 0:GT], AF.Copy, [(pn, None)], [(pstn, 1)])
            k.act(carry[:, ci:ci + 1], pp[:, GT - 1:GT], AF.Copy, [(pn, None)], [("carry", ci)])
            k.ve("dve", "tensor_tensor", [(pstn, None)], [dname], out=dst, in0=Pst[:, 0:GT], in1=Pst[:, 1:GT + 1],
                 op=ALU.subtract)
            k.ve("dve", "scalar_tensor_tensor", [dname, (pstn, None), ("mu", None)], [dname], out=dst, in0=dst,
                 scalar=mu[:, ci:ci + 1], in1=Pst[:, 1:GT + 1], op0=ALU.mult, op1=ALU.add)

        def gen_load(g):
            par = g % 2
            for tl in range(GT // 128):
                pos = (GT // 128) * g + tl
                b = tl % 2
                S.dma("sp", xr[b][:], xs[pos * 128:(pos + 1) * 128, :], [], [("xr0", None)])
                rmsnorm(xr[b][:], ("xr0", None), hb2[b][:], ("hb%d" % b, None), sq2[b], ss2[b], str(b))
                yield
                for c in range(8):
                    k.tr(pT[:, c * 128:(c + 1) * 128], hb2[b][:, c * 128:(c + 1) * 128], ident[:], [("hb%d" % b, None)], [("pT", c)])
                k.ve("dve", "tensor_copy", [("pT", None)], [("xnT0", tl)], out=xnT[par][:, :, tl * 128:(tl + 1) * 128],
                     in_=pT[:, :].rearrange("p (c t) -> p c t", c=8))
                yield
            inproj(par, 12, Pstl, "Pstl", RWl[:, 0, :], ("RWl", 0))
            yield
            inproj(par, 13, Pstl, "Pstl", RWl[:, 1, :], ("RWl", 1))
            yield
            k.act(LT[par][0:64, :], RWl[0:64, 0, :], AF.Tanh, [("RWl", 0)], [("LT%d" % par, None)])
            k.act(LT[par][64:128, :], RWl[64:128, 0, :], AF.Copy, [("RWl", 0)], [("LT%d" % par, None)])
            for dup in range(2):
                k.act(SGD[par][:, :, dup, :], RWl[:, 1, :].rearrange("q (c t) -> q c t", c=NCH), AF.Sigmoid, [("RWl", 1)],
                      [("SGD%d" % par, None)])
            yield

        def gen_prep(g, p, ws):
            par = g % 2
            wi = ws["i"]
            Wk = ws["Wk"]
            RW3 = ws["RW3"]
            wn = lambda j: ("Wk%d_%d" % (wi, j), None)
            rn = lambda j: ("RW3_%d" % wi, j)
            btn = lambda ty: ("BT%d" % par, (ty, p))
            B_ = BTm[par]
            inproj(par, p, ws["Pst"], "Pst%d" % wi, RW3[:, 0, :], rn(0))
            yield
            inproj(par, 4 + p, ws["Pst"], "Pst%d" % wi, RW3[:, 1, :], rn(1))
            yield
            inproj(par, 8 + p, ws["Pst"], "Pst%d" % wi, RW3[:, 2, :], rn(2))
            yield
            k_ = RW3[:, 1, :]
            cs = slice(p * 128, (p + 1) * 128)
            pp, pn = pJ()
            k.mm(pp[:, 0:GT], lup_b[0:64, cs], LT[par][0:64, :], True, True, [("lup_b", None), ("LT%d" % par, None)], [(pn, None)])
            k.act(Wk[0][:], pp[:, 0:GT], AF.Sigmoid, [(pn, None), ("rv", None)], [wn(0)], bias=rv[:, 0, p:p + 1])
            yield
            k.ve("dve", "tensor_scalar", [wn(0)], [wn(0)], out=Wk[0][:], in0=Wk[0][:], scalar1=-0.6065306597126334,
                 scalar2=None, op0=ALU.mult)
            yield
            k.ve("dve", "tensor_tensor_scan", [wn(0), ("rmask", None)], [wn(1)], out=Wk[1][:], data0=RST, data1=Wk[0][:],
                 initial=0.0, op0=ALU.mult, op1=ALU.add)
            yield
            k.ve("pool", "tensor_tensor", [wn(0), wn(1)], [wn(0)], out=Wk[0][:], in0=Wk[1][:], in1=Wk[0][:], op=ALU.subtract)
            k.act(Wk[2][:], Wk[1][:], AF.Exp, [wn(1)], [wn(2)])
            k.act(Wk[3][:], Wk[1][:], AF.Exp, [wn(1)], [wn(3)], scale=-1.0)
            yield
            k.act(Wk[0][:], Wk[0][:], AF.Exp, [wn(0)], [wn(0)])
            k.ve("pool", "tensor_copy", [wn(2)], [("GC%d" % par, p)], out=GC[par][:, p, :],
                 in_=Wk[2][:].rearrange("q (c t) -> q c t", c=NCH)[:, :, 63])
            yield
            pp, pn = pJ()
            k.mm(pp[:, 0:GT], lup_b[64:128, cs], LT[par][64:128, :], True, True, [("lup_b", None), ("LT%d" % par, None)],
                 [(pn, None)])
            k.act(Wk[4][:], pp[:, 0:GT], AF.Sigmoid, [(pn, None), ("rv", None)], [wn(4)], bias=rv[:, 1, p:p + 1])
            k.ve("dve", "tensor_scalar", [rn(1), ("rv", None)], [wn(5)], out=Wk[5][:], in0=k_, scalar1=rv[:, 2, p:p + 1],
                 scalar2=None, op0=ALU.mult)
            yield
            k.ve("pool", "tensor_tensor", [wn(5)], [wn(6)], out=Wk[6][:], in0=Wk[5][:], in1=Wk[5][:], op=ALU.mult)
            yield
            pp, pn = pJ()
            k.mm(pp[:, 0:GT], bones[:, :], Wk[6][:], True, True, [("bones", None), wn(6)], [(pn, None)])
            k.ve("dve", "tensor_scalar", [(pn, None)], [wn(6)], out=Wk[6][:], in0=pp[:, 0:GT], scalar1=1e-24, scalar2=None,
                 op0=ALU.max)
            yield
            k.act(Wk[6][:], Wk[6][:], AF.Ln, [wn(6)], [wn(6)])
            yield
            k.act(Wk[6][:], Wk[6][:], AF.Exp, [wn(6)], [wn(6)], scale=-0.5)
            yield
            k.ve("dve", "tensor_tensor", [wn(5), wn(6)], [wn(5)], out=Wk[5][:], in0=Wk[5][:], in1=Wk[6][:], op=ALU.mult)
            k.ve("dve", "tensor_scalar", [wn(4), ("rv", None), ("omk", None)], [wn(6)], out=Wk[6][:], in0=Wk[4][:],
                 scalar1=rv[:, 3, p:p + 1], scalar2=omk[:, p:p + 1], op0=ALU.mult, op1=ALU.add)
            yield
            k.ve("pool", "tensor_tensor", [wn(6), rn(1)], [wn(6)], out=Wk[6][:], in0=Wk[6][:], in1=k_, op=ALU.mult)
            k.ve("pool", "tensor_tensor", [wn(5), wn(4)], [wn(4)], out=Wk[4][:], in0=Wk[5][:], in1=Wk[4][:], op=ALU.mult)
            yield
            if g % 2 == 1:
                halves("dve", "tensor_tensor", [rn(0), wn(2)], [("BTo", (0, p))], BTo[:, 0, p, :, :],
                       lambda lo, hi: dict(in0=v3(RW3[lo:hi, 0, :])[:, 2:4, :], in1=v3(Wk[2][lo:hi, :])[:, 2:4, :], op=ALU.mult))
            halves("dve", "scalar_tensor_tensor", [wn(5), wn(0)], [btn(1)], B_[:, 0, p, :, :],
                   lambda lo, hi: dict(in0=v3(Wk[5][lo:hi, :]), scalar=-1.0, in1=v3(Wk[0][lo:hi, :]), op0=ALU.mult,
                                       op1=ALU.mult))
            yield
            halves("pool", "tensor_tensor", [wn(4), wn(3)], [btn(2)], B_[:, 1, p, :, :],
                   lambda lo, hi: dict(in0=v3(Wk[4][lo:hi, :]), in1=v3(Wk[3][lo:hi, :]), op=ALU.mult))
            halves("pool", "tensor_tensor", [wn(6), wn(3)], [btn(3)], B_[:, 2, p, :, :],
                   lambda lo, hi: dict(in0=v3(Wk[6][lo:hi, :]), in1=v3(Wk[3][lo:hi, :]), op=ALU.mult))
            yield
            halves("pool", "tensor_copy", [rn(2)], [btn(4)], B_[:, 3, p, :, :],
                   lambda lo, hi: dict(in_=v3(RW3[lo:hi, 2, :])))
            if g % 2 == 1:
                halves("dve", "scalar_tensor_tensor", [rn(0), wn(6), ("rv", None)], [("BTo", (1, p))], BTo[:, 1, p, :, :],
                       lambda lo, hi: dict(in0=v3(RW3[lo:hi, 0, :])[:, 2:4, :], scalar=rv[lo:hi, 4, p:p + 1],
                                           in1=v3(Wk[6][lo:hi, :])[:, 2:4, :], op0=ALU.mult, op1=ALU.mult))
            yield

        def gen_tt(g, c, sl):
            par = g % 2
            B_ = BTm[par]
            si_ = sl["i"]
            btn = lambda ty, p: ("BT%d" % par, (ty, p))
            TT, X2 = sl["TT"], sl["X2"]
            own = (g % 2 == 1) and c >= 2
            for j, ty in enumerate((2, 3, 4)):
                pp, pn = pM()
                for p in range(4):
                    k.mm(pp[:, p * 128:(p + 1) * 128], B_[:, ty - 1, p, c, :], ident[:], True, True,
                         [btn(ty, p), ("ident", None)], [(pn, None)])
                if j >= 1:
                    k.act(TT[:, j, :], pp[:, :], AF.Copy, [(pn, None)], [("TT%d" % si_, j)])
                else:
                    k.ve("dve", "tensor_copy", [(pn, None)], [("TT%d" % si_, j)], out=TT[:, j, :], in_=pp[:, :])
                yield
            for j in range(2):
                pp, pn = pM()
                for a in range(2):
                    p = 2 * j + a
                    k.mm(pp[:, a * 256:a * 256 + 128], B_[:, 2, p, c, :], B_[:, 0, p, c, :], True, True, [btn(3, p), btn(1, p)],
                         [(pn, None)])
                    if own:
                        k.mm(pp[:, a * 256 + 128:a * 256 + 256], B_[:, 2, p, c, :], BTo[:, 0, p, c - 2, :], True, True,
                             [btn(3, p), ("BTo", (0, p))], [(pn, None)])
                if own:
                    k.ve("dve", "tensor_tensor", [(pn, None), ("rmb", None)], [("X2_%d" % si_, j)],
                         out=X2[:, j, :, :, :].rearrange("q a t s -> q (a t s)"), in0=pp[:, :], in1=MA, op=ALU.mult)
                else:
                    k.ve("dve", "tensor_tensor", [(pn, None), ("rmb", None)], [("X2_%d" % si_, j)],
                         out=X2[:, j, :, 0, :], in0=pp[:, :].rearrange("q (a t s) -> q a t s", a=2, t=2)[:, :, 0, :],
                         in1=MA.rearrange("q (a t s) -> q a t s", a=2, t=2)[:, :, 0, :], op=ALU.mult)
                yield

        def gen_pre(g, c, sl):
            par = g % 2
            B_ = BTm[par]
            si_ = sl["i"]
            btn = lambda ty, p: ("BT%d" % par, (ty, p))
            X1, X3 = sl["X1"], sl["X3"]
            Ak, Bk, Wt, Zt = sl["Ak"], sl["Bk"], sl["Wt"], sl["Zt"]
            own = (g % 2 == 1) and c >= 2
            n_ = lambda base, j=None: ("%s%d" % (base, si_) if j is None else "%s%d_%d" % (base, si_, j), None)
            pp, pn = pM()
            for p in range(4):
                k.mm(pp[:, p * 128:(p + 1) * 128], B_[:, 0, p, c, :], B_[:, 1, p, c, :], True, True, [btn(1, p), btn(2, p)],
                     [(pn, None)])
            k.ve("dve", "tensor_tensor", [(pn, None), ("rmb", None)], [("X3_%d" % si_, None)], out=X3[:], in0=pp[:, :], in1=ML4,
                 op=ALU.mult)
            for j in range(2):
                pp, pn = pM()
                for a in range(2):
                    p = 2 * j + a
                    k.mm(pp[:, a * 256:a * 256 + 128], B_[:, 1, p, c, :], B_[:, 0, p, c, :], True, True, [btn(2, p), btn(1, p)],
                         [(pn, None)])
                    if own:
                        k.mm(pp[:, a * 256 + 128:a * 256 + 256], B_[:, 1, p, c, :], BTo[:, 0, p, c - 2, :], True, True,
                             [btn(2, p), ("BTo", (0, p))], [(pn, None)])
                if own:
                    k.ve("dve", "tensor_tensor", [(pn, None), ("rmb", None)], [("X1_%d" % si_, j)],
                         out=X1[:, j, :, :, :].rearrange("q a t s -> q (a t s)"), in0=pp[:, :], in1=MA, op=ALU.mult)
                else:
                    k.ve("dve", "tensor_tensor", [(pn, None), ("rmb", None)], [("X1_%d" % si_, j)],
                         out=X1[:, j, :, 0, :], in0=pp[:, :].rearrange("q (a t s) -> q a t s", a=2, t=2)[:, :, 0, :],
                         in1=MA.rearrange("q (a t s) -> q a t s", a=2, t=2)[:, :, 0, :], op=ALU.mult)
            yield
            B1 = X1[:, :, :, 0, :]
            k.ve("dve", "tensor_tensor", [("X1_%d" % si_, None), ("rmb", None)], [n_("Wt", 0)],
                 out=Wt[0][:].rearrange("q (j a s) -> q j a s", j=2, a=2), in0=B1,
                 in1=I4.rearrange("q (j a s) -> q j a s", j=2, a=2), op=ALU.add)
            k.ve("dve", "tensor_tensor", [("X3_%d" % si_, None), ("rmb", None)], [n_("Zt", 0)], out=Zt[0][:], in0=X3[:], in1=I4,
                 op=ALU.add)

            def Bsl(lv, p):
                if lv == 0:
                    return X1[:, p // 2, p % 2, 0, :], ("X1_%d" % si_, None)
                return Bk[lv % 2][:, p * 128:(p + 1) * 128], n_("Bk", lv % 2)

            def Asl(lv, p):
                if lv == 0:
                    return X3[:, p * 128:(p + 1) * 128], ("X3_%d" % si_, None)
                return Ak[lv % 2][:, p * 128:(p + 1) * 128], n_("Ak", lv % 2)

            def squares(lv):
                ppB, pnB = pM()
                for p in range(4):
                    a_ap, a_n = Asl(lv - 1, p)
                    b_ap, b_n = Bsl(lv - 1, p)
                    k.mm(ppB[:, p * 128:(p + 1) * 128], a_ap, b_ap, True, True, [a_n, b_n], [(pnB, None)])
                k.act(Bk[lv % 2][:], ppB[:, :], AF.Copy, [(pnB, None)], [n_("Bk", lv % 2)])
                if lv < 5:
                    ppA, pnA = pM()
                    for p in range(4):
                        a_ap, a_n = Asl(lv - 1, p)
                        b_ap, b_n = Bsl(lv - 1, p)
                        k.mm(ppA[:, p * 128:(p + 1) * 128], b_ap, a_ap, True, True, [a_n, b_n], [(pnA, None)])
                    k.act(Ak[lv % 2][:], ppA[:, :], AF.Copy, [(pnA, None)], [n_("Ak", lv % 2)])

            def products(lv):
                wi, wo = (lv - 1) % 2, lv % 2
                ppW, pnW = pM()
                for p in range(4):
                    b_ap, b_n = Bsl(lv, p)
                    k.mm(ppW[:, p * 128:(p + 1) * 128], Zt[wi][:, p * 128:(p + 1) * 128], b_ap, True, True,
                         [n_("Zt", wi), b_n], [(pnW, None)])
                k.ve("dve", "tensor_tensor", [(pnW, None), n_("Wt", wi)], [n_("Wt", wo)], out=Wt[wo][:], in0=ppW[:, :],
                     in1=Wt[wi][:], op=ALU.add)
                if lv < 5:
                    ppZ, pnZ = pM()
                    for p in range(4):
                        a_ap, a_n = Asl(lv, p)
                        k.mm(ppZ[:, p * 128:(p + 1) * 128], Wt[wi][:, p * 128:(p + 1) * 128], a_ap, True, True,
                             [n_("Wt", wi), a_n], [(pnZ, None)])
                    k.ve("dve", "tensor_tensor", [(pnZ, None), n_("Zt", wi)], [n_("Zt", wo)], out=Zt[wo][:], in0=ppZ[:, :],
                         in1=Zt[wi][:], op=ALU.add)

            squares(1)
            yield
            for lv in range(1, 6):
                if lv < 5:
                    squares(lv + 1)
                products(lv)
                yield

        def gen_seq(g, c, sl):
            par = g % 2
            B_ = BTm[par]
            si_ = sl["i"]
            btn = lambda ty, p: ("BT%d" % par, (ty, p))
            TT, X1, X2 = sl["TT"], sl["X1"], sl["X2"]
            Wf, Wfn = sl["Wt"][1], ("Wt%d_1" % si_, None)
            ttn = lambda j: ("TT%d" % si_, j)
            x1n, x2n = ("X1_%d" % si_, None), ("X2_%d" % si_, None)
            BBT, KBT, VBT = TT[:, 0, :], TT[:, 1, :], TT[:, 2, :]
            own = (g % 2 == 1) and c >= 2
            it = g // 2
            b0, b0n = pM()
            for p in range(4):
                ps = slice(p * 128, (p + 1) * 128)
                k.mm(b0[:, ps], B_[:, 0, p, c, :], Hb[:, ps], True, False, [btn(1, p), ("Hb", None)], [(b0n, None)])
                k.mm(b0[:, ps], X2[:, p // 2, p % 2, 0, :], VBT[:, ps], False, True, [x2n, ttn(2)], [(b0n, None)])
            k.act(RHSb[:], b0[:, :], AF.Copy, [(b0n, None)], [("RHSb", None)])
            yield
            b1, b1n = pM()
            for p in range(4):
                ps = slice(p * 128, (p + 1) * 128)
                k.mm(b1[:, ps], Wf[:, ps], RHSb[:, ps], True, True, [Wfn, ("RHSb", None)], [(b1n, None)])
            k.act(Ub[:], b1[:, :], AF.Copy, [(b1n, None)], [("Ub", None)])
            yield
            if own:
                ch = c - 2
                pS2, pS2n = pM()
                for p in range(4):
                    ps = slice(p * 128, (p + 1) * 128)
                    k.mm(pS2[:, ps], BTo[:, 0, p, c - 2, :], Hb[:, ps], True, False, [("BTo", (0, p)), ("Hb", None)], [(pS2n, None)])
                    k.mm(pS2[:, ps], X1[:, p // 2, p % 2, 1, :], Ub[:, ps], False, False, [x1n, ("Ub", None)], [(pS2n, None)])
                    k.mm(pS2[:, ps], X2[:, p // 2, p % 2, 1, :], VBT[:, ps], False, True, [x2n, ttn(2)], [(pS2n, None)])
            b3, b3n = pM()
            for p in range(4):
                ps = slice(p * 128, (p + 1) * 128)
                k.mm(b3[:, ps], BBT[:, ps], Ub[:, ps], True, False, [ttn(0), ("Ub", None)], [(b3n, None)])
                k.mm(b3[:, ps], KBT[:, ps], VBT[:, ps], False, True, [ttn(1), ttn(2)], [(b3n, None)])
            k.ve("dve", "tensor_tensor", [(b3n, None), ("Hf", None)], [("Hs", None)], out=Hs[:], in0=b3[:, :], in1=Hf[:],
                 op=ALU.add)
            yield
            for p in range(4):
                ps = slice(p * 128, (p + 1) * 128)
                k.act(Hf[:, ps], Hs[:, ps], AF.Copy, [("Hs", None), ("GC%d" % par, p)], [("Hf", None)], scale=GC[par][:, p, c:c + 1])
            k.ve("dve", "tensor_copy", [("Hf", None)], [("Hb", None)], out=Hb[:], in_=Hf[:])
            yield
            if own:
                for hh in range(2):
                    lo, hi = hh * 64, hh * 64 + 64
                    k.ve("dve", "tensor_copy", [(pS2n, None)], [("Yc", None)], out=Yc[lo:hi, :, :],
                         in_=pS2[lo:hi, :].rearrange("q (p s) -> q p s", p=4)[:, :, lo:hi])
                    k.ve("pool", "tensor_copy", [ttn(2)], [("V2c", None)], out=V2c[lo:hi, :, :],
                         in_=TT[lo:hi, 2, :].rearrange("q (p s) -> q p s", p=4)[:, :, lo:hi])
                yield
                pp, pn = pJ()
                for p in range(4):
                    k.mm(pp[:, p * 128:(p + 1) * 128], SGD[par][:, c, :, :].rearrange("q a t -> q (a t)"),
                         gup_b[:, p * 128:(p + 1) * 128], True, True, [("SGD%d" % par, None), ("gup_b", None)], [(pn, None)])
                for hh in range(2):
                    lo, hi = hh * 64, hh * 64 + 64
                    k.act(Gc[lo:hi, :, :], pp[lo:hi, :].rearrange("q (p s) -> q p s", p=4)[:, :, lo:hi], AF.Copy,
                          [(pn, None)], [("Gc", None)])
                yield
                pp, pn = pJ()
                for p in range(4):
                    k.mm(pp[:, p:p + 1], BTo[:, 1, p, c - 2, :], onesc[:, 0:1], True, True, [("BTo", (1, p)), ("onesc", None)], [(pn, None)])
                k.ve("dve", "tensor_copy", [(pn, None)], [("stb", None)], out=stb[:, 0:4], in_=pp[:, 0:4])
                yield
                steps = []
                for p in range(4):
                    yc = Yc[:, p, :]
                    sn, cn, qn = ("st%d" % p, None), ("Ycen%d" % p, None), ("Ysq%d" % p, None)
                    s_, c_, q_ = st[p], Ycen[p], Ysq[p]
                    steps.append([
                        lambda yc=yc, s_=s_, sn=sn: k.ve("dve", "reduce_sum", [("Yc", None)], [sn], out=s_[:, 0:1], in_=yc, axis=AX.X),
                        lambda s_=s_, sn=sn: k.ve("dve", "tensor_scalar", [sn], [sn], out=s_[:, 0:1], in0=s_[:, 0:1],
                                                  scalar1=-1.0 / 64, scalar2=None, op0=ALU.mult),
                        lambda yc=yc, s_=s_, sn=sn, c_=c_, cn=cn: k.ve("dve", "tensor_scalar", [("Yc", None), sn], [cn], out=c_[:],
                                                                      in0=yc, scalar1=s_[:, 0:1], scalar2=None, op0=ALU.add),
                        lambda c_=c_, cn=cn, q_=q_, qn=qn: k.ve("pool", "tensor_tensor", [cn], [qn], out=q_[:], in0=c_[:], in1=c_[:],
                                                              op=ALU.mult),
                        lambda q_=q_, qn=qn, s_=s_, sn=sn: k.ve("dve", "reduce_sum", [qn], [sn], out=s_[:, 1:2], in_=q_[:], axis=AX.X),
                        lambda s_=s_, sn=sn: k.act(s_[:, 2:3], s_[:, 1:2], AF.Ln, [sn, ("epsl", None)], [sn], bias=epsl[:, 0:1],
                                                   scale=1.0 / 64),
                        lambda s_=s_, sn=sn: k.act(s_[:, 2:3], s_[:, 2:3], AF.Exp, [sn], [sn], scale=-0.5),
                        lambda s_=s_, sn=sn, c_=c_, cn=cn: k.ve("dve", "tensor_scalar", [cn, sn], [cn], out=c_[:], in0=c_[:],
                                                              scalar1=s_[:, 2:3], scalar2=None, op0=ALU.mult),
                        lambda c_=c_, cn=cn, p=p: k.ve("pool", "tensor_tensor", [cn, ("lnw", None)], [cn], out=c_[:], in0=c_[:],
                                                      in1=lnw[:, p, :], op=ALU.mult),
                        lambda c_=c_, cn=cn, p=p: k.ve("pool", "tensor_tensor", [cn, ("lnb", None)], [cn], out=c_[:], in0=c_[:],
                                                      in1=lnb[:, p, :], op=ALU.add),
                        lambda c_=c_, cn=cn, p=p: k.ve("dve", "scalar_tensor_tensor", [("V2c", None), ("stb", None), cn], [cn],
                                                      out=c_[:], in0=V2c[:, p, :], scalar=stb[:, p:p + 1], in1=c_[:],
                                                      op0=ALU.mult, op1=ALU.add),
                    ])
                for si2 in range(len(steps[0])):
                    for p in range(4):
                        steps[p][si2]()
                    yield
                for p in range(4):
                    for hh in range(2):
                        lo, hi = hh * 64, hh * 64 + 64
                        k.ve("dve" if hh == 0 else "pool", "tensor_tensor", [("Ycen%d" % p, None), ("Gc", None)], [("Oblk", None)],
                             out=Oblk[lo:hi, p, lo:hi], in0=Ycen[p][lo:hi, :], in1=Gc[lo:hi, p, :], op=ALU.mult)
                yield
                for p in range(4):
                    k.tr(pT[:, p * 128:(p + 1) * 128], Oblk[:, p, :], ident[:], [("Oblk", None)], [("pT", p)])
                for hh in range(2):
                    lo, hi = hh * 64, hh * 64 + 64
                    k.ve("dve", "tensor_copy", [("pT", None)], [("oT_r", it)], out=oT_r[lo:hi, it, :, ch * 64:(ch + 1) * 64],
                         in_=pT[lo:hi, 0:512].rearrange("q (p s) -> q p s", p=4)[:, :, lo:hi])
                yield

        NCC = NG * NCH
        done = {"load": -1, "prep": -1, "pre": -1, "seq": -1}
        active = []
        nxt = {"load": 0, "prepg": 0, "prepp": 0, "pre": 0, "seq": 0}
        prep_done_pairs = {}

        def admit():
            if nxt["load"] < NG and not any(a[0] == "load" for a in active) and nxt["load"] <= done["prep"] + 1 \
                    and done["seq"] >= (nxt["load"] - 1) * NCH - 1:
                g = nxt["load"]
                active.append(("load", g, gen_load(g)))
                nxt["load"] += 1
            while nxt["prepg"] < NG:
                g = nxt["prepg"]
                if done["load"] < g or done["seq"] < (g - 1) * NCH - 1:
                    break
                inflight = [a for a in active if a[0] == "prep"]
                if len(inflight) >= 2:
                    break
                used = {a[3] for a in inflight}
                wsi = 0 if 0 not in used else 1
                p = nxt["prepp"]
                active.append(("prep", (g, p), gen_prep(g, p, WS[wsi]), wsi))
                nxt["prepp"] += 1
                if nxt["prepp"] == 4:
                    nxt["prepp"] = 0
                    nxt["prepg"] += 1
            while nxt["pre"] < NCC:
                cc = nxt["pre"]
                if done["prep"] < cc // NCH:
                    break
                if cc - (done["seq"] + 1) >= NSLOT:
                    break
                if len([a for a in active if a[0] == "pre"]) >= 2:
                    break
                active.append(("pre", cc, gen_pre(cc // NCH, cc % NCH, SL[cc % NSLOT])))
                active.append(("tt", cc, gen_tt(cc // NCH, cc % NCH, SL[cc % NSLOT])))
                nxt["pre"] += 1
            if nxt["seq"] < NCC and not any(a[0] == "seq" for a in active) and done["pre"] >= nxt["seq"]:
                cc = nxt["seq"]
                active.append(("seq", cc, gen_seq(cc // NCH, cc % NCH, SL[cc % NSLOT])))
                nxt["seq"] += 1

        pre_done_set = set()
        pre_cnt = {}
        while True:
            admit()
            if not active:
                break
            for a in list(active):
                try:
                    next(a[2])
                except StopIteration:
                    active.remove(a)
                    kind = a[0]
                    if kind == "load":
                        done["load"] = a[1]
                    elif kind == "prep":
                        gq, pq = a[1]
                        prep_done_pairs[gq] = prep_done_pairs.get(gq, 0) + 1
                        if prep_done_pairs[gq] == 4:
                            done["prep"] = gq
                    elif kind in ("pre", "tt"):
                        pre_cnt[a[1]] = pre_cnt.get(a[1], 0) + 1
                        if pre_cnt[a[1]] == 2:
                            pre_done_set.add(a[1])
                        while done["pre"] + 1 in pre_done_set:
                            done["pre"] += 1
                    else:
                        done["seq"] = a[1]
        assert done["seq"] == NCC - 1, done
        if "orw" in dbg:
            dorw = nc.dram_tensor("dorw", [128, NOWN * 4 * 128], BF16, kind="ExternalOutput").ap()
            S.dma("sp", dorw[:, :], oT_r[:].rearrange("q a b c -> q (a b c)"), [("oT_r", None)], [])
        S.fence()
        er.close()
        S.fence()

    oT_n = eo.enter_context(nc.sbuf_tensor("oT_n", [128, NOWN, 4, 128], BF16, side="right"))
    if "nsa" in dbg:
        w_in_n = din("w_in_n", [128, 8, 1304])
        gateb_d = din("gate_b", [128, 24])
        realtok_d = din("realtok", [128, 64])
        w1_d = {"k": din("cmp_w1k", [128, 32, 256]), "v": din("cmp_w1v", [128, 32, 256])}
        peT_d = {"k": din("cmp_peTk", [128, 32]), "v": din("cmp_peTv", [128, 32])}
        b1_d = {"k": din("cmp_b1k", [128, 2]), "v": din("cmp_b1v", [128, 2])}
        w2k_d = din("cmp_w2k", [128, 2, 128])
        w2v_d = din("cmp_w2v", [128, 2, 64])
        ovlm_d = din("ovlm", [128, 4, 129])
        relb_d = din("relb_rep", [32, 8, 128])
        OHa_d = din("OHa", [32, 2048])
        OHw_d = din("OHw", [32, 256])
        OHc_d = din("OHc", [32, 6144])
        Ebig_d = din("Ebig", [128, 64, 128])
        addm_d = din("addmask", [128, 16, 128])
        scr_a = nc.dram_tensor("scr_a", [8, 128, 2048], BF16)
        scr_w = nc.dram_tensor("scr_w", [8, 128, 256], BF16)
        scr_c = nc.dram_tensor("scr_c", [8, 128, 6144], BF16)
        en = ExitStack()
        ens = ExitStack()
        alloc_stg(ens, "n", side="right")

        def TN(name, shape, dt):
            return en.enter_context(nc.sbuf_tensor("n_" + name, shape, dt))

        kselT = TN("kselT", [128, 8192], BF16)
        kwinT = TN("kwinT", [128, 8192], BF16)
        Vs = TN("Vs", [128, 64, 2, 65], BF16)
        Vw = TN("Vw", [128, 64, 2, 65], BF16)
        QT = TN("QT", [128, NOWN, 4, 128], BF16)
        GATES = TN("GATES", [128, NOWN, 24], F32)
        KcT = TN("KcT", [128, 512], BF16)
        Vca = TN("Vca", [128, 4, 2, 65], BF16)
        Ovl = TN("Ovl", [128, 4, 128], BF16)

        en1 = ExitStack()

        def TN1(name, shape, dt):
            return en1.enter_context(nc.sbuf_tensor("n_" + name, shape, dt))

        ovlm = TN1("ovlm", [128, 4, 129], F32)
        realtok = TN1("realtok", [128, 64], F32)
        gateb = TN1("gateb", [128, 24], F32)
        S.dma("sp", gateb[:], gateb_d[:, :], [], [("gateb", None)])
        S.dma("sp", realtok[:], realtok_d[:, :], [], [("realtok", None)])
        S.dma("sp", ovlm[:], ovlm_d[:, :, :], [], [("ovlm", None)])
        k.ve("pool", "tensor_copy", [("ovlm", None)], [("Ovl", None)], out=Ovl[:], in_=ovlm[:, :, 1:129])
        kcmpT = TN1("kcmpT", [128, 8192 + 32], BF16)
        vcmpT = TN1("vcmpT", [128, 8192 + 32], BF16)
        en1b = ExitStack()
        w_in_nb = en1b.enter_context(nc.sbuf_tensor("n_w_in_nb", [128, 8, 1304], BF16))
        gmixn = en1b.enter_context(nc.sbuf_tensor("n_gmixn", [128, 8], F32))
        xnT2 = [en1b.enter_context(nc.sbuf_tensor("n_xnT%d" % i, [128, 8, 512], BF16)) for i in range(2)]
        xr_n = en1b.enter_context(nc.sbuf_tensor("n_xr0", [128, D], F32))
        xr2 = [xr_n, xr_n]
        sqn = [sq, sq]
        ssn = [ss, ss]
        hbn = [hb, en1b.enter_context(nc.sbuf_tensor("n_hbx1", [128, D], BF16))]
        S.dma("sp", gmixn[:], gmix_d[:, :], [], [("gmixn", None)])
        load_w(w_in_nb, "w_in_nb", w_in_n, 8, 1304, gcol=gmixn, gname="gmixn")
        k.ve("pool", "memset", [], [("kcmpT", None)], kcmpT[:, 8192:8224], 0.0)
        k.ve("pool", "memset", [], [("vcmpT", None)], vcmpT[:, 8192:8224], 0.0)
        nbanks = [(pA, "pA"), (pB, "pB"), (pC, "pC"), (pD, "pD"), (pE, "pE"), (pF, "pF"), (pG, "pG")]
        nb_i = [0]

        def nbank():
            nb_i[0] += 1
            return nbanks[nb_i[0] % 7]

        def gen_n1(g):
            par = g % 2
            xnT = xnT2[par]
            xn_ = "n_xnT%d" % par
            for tl in range(4):
                pos = 4 * g + tl
                b2 = par
                S.dma("sp", xr2[b2][:], xs[pos * 128:(pos + 1) * 128, :], [], [("n_xr0", None)])
                rmsnorm(xr2[b2][:], ("n_xr0", None), hbn[b2][:], ("n_hb%d" % b2, None), sqn[b2], ssn[b2], "n")
                yield
                for c in range(8):
                    k.tr(pT[:, c * 128:(c + 1) * 128], hbn[b2][:, c * 128:(c + 1) * 128], ident[:], [("n_hb%d" % b2, None)], [("pT", c)])
                k.ve("dve", "tensor_copy", [("pT", None)], [(xn_, tl)], out=xnT[:, :, tl * 128:(tl + 1) * 128],
                     in_=pT[:, :].rearrange("p (c t) -> p c t", c=8))
                yield
            tok = slice(g * 512, (g + 1) * 512)
            for ci, (dst, dn) in enumerate(((kcmpT, "kcmpT"), (vcmpT, "vcmpT"), (kselT, "kselT"), (kwinT, "kwinT"))):
                pp, pn = nbank()
                for kc in range(8):
                    k.mm(pp[:, :], w_in_nb[:, kc, 512 + ci * 128:512 + (ci + 1) * 128], xnT[:, kc, :], kc == 0, kc == 7,
                         [("w_in_nb", kc), (xn_, None)], [(pn, None)])
                if ci % 2 == 0:
                    k.act(dst[:, tok], pp[:, :], AF.Copy, [(pn, None)], [(dn, g)])
                else:
                    k.ve("dve", "tensor_copy", [(pn, None)], [(dn, g)], out=dst[:, tok], in_=pp[:, :])
                yield
            for tl in range(4):
                pos = 4 * g + tl
                pp, pn = nbank()
                for kc in range(8):
                    k.mm(pp[:, 0:256], xnT[:, kc, tl * 128:(tl + 1) * 128], w_in_nb[:, kc, 1024:1280], kc == 0, kc == 7,
                         [("w_in_nb", kc), (xn_, None)], [(pn, None)])
                k.act(Vs[:, pos, :, 0:64], pp[:, 0:128].rearrange("p (a b) -> p a b", a=2), AF.Copy, [(pn, None)], [("Vs", pos)])
                k.ve("dve", "tensor_copy", [(pn, None)], [("Vw", pos)], out=Vw[:, pos, :, 0:64],
                     in_=pp[:, 128:256].rearrange("p (a b) -> p a b", a=2))
                for kv in range(2):
                    k.ve("pool", "tensor_copy", [("realtok", None)], [("Vs", pos)], out=Vs[:, pos, kv, 64:65],
                         in_=realtok[:, pos:pos + 1])
                    k.ve("pool", "tensor_copy", [("realtok", None)], [("Vw", pos)], out=Vw[:, pos, kv, 64:65],
                         in_=realtok[:, pos:pos + 1])
                yield
            for a in range(4):
                pp, pn = nbank()
                for kc in range(8):
                    k.mm(pp[:, 0:128], w_in_nb[:, kc, a * 128:(a + 1) * 128], xnT[:, kc, 384:512], kc == 0, kc == 7,
                         [("w_in_nb", kc), (xn_, None)], [(pn, None)])
                k.act(QT[:, g, a, :], pp[:, 0:128], AF.Copy, [(pn, None)], [("QT", g)])
                if a % 2 == 1:
                    yield
            pp, pn = nbank()
            for kc in range(8):
                k.mm(pp[:, 0:24], xnT[:, kc, 384:512], w_in_nb[:, kc, 1280:1304], kc == 0, kc == 7,
                     [("w_in_nb", kc), (xn_, None)], [(pn, None)])
            k.ve("dve", "tensor_tensor", [(pn, None), ("gateb", None)], [("GATES", g)], out=GATES[:, g, :], in0=pp[:, 0:24],
                 in1=gateb[:], op=ALU.add)
            k.act(GATES[:, g, :], GATES[:, g, :], AF.Sigmoid, [("GATES", g)], [("GATES", g)])
            yield

        actn = []
        nxtn = 0
        while True:
            while nxtn < 16 and len(actn) < 2:
                actn.append((nxtn, gen_n1(nxtn)))
                nxtn += 1
            if not actn:
                break
            for a in list(actn):
                try:
                    next(a[1])
                except StopIteration:
                    actn.remove(a)

        S.fence()
        en1b.close()
        S.fence()
        with nc.sbuf_tensor("n_w1b", [128, 32, 256], BF16) as w1b, \
                nc.sbuf_tensor("n_peT", [128, 32], F32) as peTf, \
                nc.sbuf_tensor("n_peTb", [128, 32], BF16) as peTb, \
                nc.sbuf_tensor("n_b1", [128, 2], F32) as b1t, \
                nc.sbuf_tensor("n_w2f", [128, 2, 128], F32) as w2f, \
                nc.sbuf_tensor("n_w2b", [128, 2, 128], BF16) as w2b, \
                nc.sbuf_tensor("n_hb1", [128, 2], F32) as hbias, \
                nc.sbuf_tensor("n_z", [128, 512], F32) as zt, \
                nc.sbuf_tensor("n_z2", [128, 512], F32) as z2t, \
                nc.sbuf_tensor("n_ge", [128, 2, 512], BF16) as ge:
            for X, srcT, sn in (("k", kcmpT, "kcmpT"), ("v", vcmpT, "vcmpT")):
                for part in range(8):
                    i = part % 2
                    S.dma("sp" if i == 0 else "pool", stg[i][:, :],
                          w1_d[X][:, part * 4:(part + 1) * 4, :].rearrange("p a b -> p (a b)"), [], [("stg%d" % i, None)])
                    k.ve("dve" if i == 0 else "pool", "tensor_copy", [("stg%d" % i, None)], [("w1b", part)],
                         out=w1b[:, part * 4:(part + 1) * 4, :].rearrange("p a b -> p (a b)"), in_=stg[i][:, :])
                S.dma("sp", peTf[:], peT_d[X][:, :], [], [("peTf", None)])
                k.ve("dve", "tensor_copy", [("peTf", None)], [("peTb", None)], out=peTb[:], in_=peTf[:])
                S.dma("sp", b1t[:], b1_d[X][:, :], [], [("b1t", None)])
                if X == "k":
                    S.dma("sp", w2f[:], w2k_d[:, :, :], [], [("w2f", None)])
                    k.ve("dve", "tensor_copy", [("w2f", None)], [("w2b", None)], out=w2b[:], in_=w2f[:])
                else:
                    S.dma("sp", w2f[:, :, 0:64], w2v_d[:, :, :], [], [("w2f", None)])
                    k.ve("dve", "tensor_copy", [("w2f", None)], [("w2b", None)], out=w2b[:, :, 0:64], in_=w2f[:, :, 0:64])
                for hc in range(2):
                    for l in range(32):
                        k.mm(pD[:, hc:hc + 1], w1b[0:64, l, hc * 128:(hc + 1) * 128], peTb[0:64, l:l + 1], l == 0, l == 31,
                             [("w1b", None), ("peTb", None)], [("pD", None)])
                k.ve("dve", "tensor_tensor", [("pD", None), ("b1t", None)], [("hbias", None)], out=hbias[:], in0=pD[:, 0:2],
                     in1=b1t[:], op=ALU.add)
                for kv in range(2):
                    rows = slice(kv * 64, kv * 64 + 64)
                    for hc in range(2):
                        pp, pn = (pA, "pA") if hc == 0 else (pB, "pB")
                        for l in range(32):
                            rhs = srcT[rows, l:l + 8192].rearrange("q (c s) -> q c s", s=16)[:, :, 0]
                            k.mm(pp[:, :], w1b[rows, l, hc * 128:(hc + 1) * 128], rhs, l == 0, l == 31,
                                 [("w1b", None), (sn, None)], [(pn, None)])
                        k.act(zt[:], pp[:, :], AF.Identity, [(pn, None), ("hbias", None)], [("zt", None)], bias=hbias[:, hc:hc + 1])
                        k.ve("dve", "tensor_tensor", [("zt", None)], [("z2t", None)], out=z2t[:], in0=zt[:], in1=zt[:], op=ALU.mult)
                        k.ve("dve", "tensor_scalar", [("z2t", None)], [("z2t", None)], out=z2t[:], in0=z2t[:], scalar1=0.044715,
                             scalar2=1.0, op0=ALU.mult, op1=ALU.add)
                        k.ve("pool", "tensor_tensor", [("z2t", None), ("zt", None)], [("z2t", None)], out=z2t[:], in0=z2t[:],
                             in1=zt[:], op=ALU.mult)
                        k.act(z2t[:], z2t[:], AF.Sigmoid, [("z2t", None)], [("z2t", None)], scale=1.5957691216057308)
                        k.ve("dve", "tensor_tensor", [("z2t", None), ("zt", None)], [("ge", hc)], out=ge[:, hc, :], in0=z2t[:],
                             in1=zt[:], op=ALU.mult)
                    if X == "k":
                        for hc in range(2):
                            k.mm(pC[:, :], w2b[:, hc, :], ge[:, hc, :], hc == 0, hc == 1, [("w2b", None), ("ge", None)],
                                 [("pC", None)])
                        k.ve("dve", "tensor_copy", [("pC", None)], [("KcT", kv)], out=KcT[rows, :], in_=pC[rows, :])
                    else:
                        for ct in range(4):
                            for hc in range(2):
                                k.mm(pC[:, 0:64], ge[:, hc, ct * 128:(ct + 1) * 128], w2b[:, hc, 0:64], hc == 0, hc == 1,
                                     [("w2b", None), ("ge", None)], [("pC", None)])
                            k.ve("dve", "tensor_scalar", [("pC", None), ("ovlm", None)], [("Vca", None)],
                                 out=Vca[:, ct, kv, 0:64], in0=pC[:, 0:64], scalar1=ovlm[:, ct, 0:1], scalar2=None, op0=ALU.mult)
                            k.ve("pool", "tensor_copy", [("ovlm", None)], [("Vca", None)], out=Vca[:, ct, kv, 64:65],
                                 in_=ovlm[:, ct, 0:1])
        S.fence()
        en1.close()
        S.fence()

        MBs = TN("MBs", [128, 8, 14, 128], BF16)
        MBw4 = TN("MBw4", [128, 8, 128], BF16)
        MBc = TN("MBc", [128, 8, 8, 128], BF16)
        Ebig = TN("Ebig", [128, 64, 128], BF16)
        for c8 in range(8):
            i = c8 % 2
            S.dma("sp" if i == 0 else "pool", stg[i][:, :], Ebig_d[:, c8 * 8:(c8 + 1) * 8, :].rearrange("p a b -> p (a b)"),
                  [], [("stg%d" % i, None)])
            k.ve("dve" if i == 0 else "pool", "tensor_copy", [("stg%d" % i, None)], [("Ebig", c8)],
                 out=Ebig[:, c8 * 8:(c8 + 1) * 8, :].rearrange("p a b -> p (a b)"), in_=stg[i][:, :])
        with nc.sbuf_tensor("n_relb", [32, 8, 128], F32) as relb, \
                nc.sbuf_tensor("n_OH0", [32, 512], F32) as OH0, \
                nc.sbuf_tensor("n_OH1", [32, 512], F32) as OH1, \
                nc.sbuf_tensor("n_Fv0", [128, 512], BF16) as Fv0, \
                nc.sbuf_tensor("n_Fv1", [128, 512], BF16) as Fv1:
            S.dma("sp", relb[:], relb_d[:, :, :], [], [("relb", None)])
            k.act(relb[:].rearrange("p a b -> p (a b)"), relb[:].rearrange("p a b -> p (a b)"), AF.Exp, [("relb", None)],
                  [("relb", None)])
            oi = 0
            for (OHd, L, scr, sname) in ((OHa_d, 2048, scr_a, "scr_a"), (OHw_d, 256, scr_w, "scr_w"), (OHc_d, 6144, scr_c, "scr_c")):
                for h in range(8):
                    for c0 in range(0, L, 512):
                        c1 = min(L, c0 + 512)
                        oi += 1
                        OH, OHn = (OH0, "OH0") if oi % 2 else (OH1, "OH1")
                        Fv, Fvn = (Fv0, "Fv0") if oi % 2 else (Fv1, "Fv1")
                        pp, pn = (pA, "pA") if oi % 2 else (pB, "pB")
                        S.dma("sp", OH[:, 0:c1 - c0], OHd[:, c0:c1], [], [(OHn, None)])
                        k.mm(pp[:, 0:c1 - c0], relb[:, h, :], OH[:, 0:c1 - c0], True, True, [("relb", None), (OHn, None)],
                             [(pn, None)])
                        if oi % 2:
                            k.act(Fv[:, 0:c1 - c0], pp[:, 0:c1 - c0], AF.Copy, [(pn, None)], [(Fvn, None)])
                        else:
                            k.ve("dve", "tensor_copy", [(pn, None)], [(Fvn, None)], out=Fv[:, 0:c1 - c0], in_=pp[:, 0:c1 - c0])
                        S.dma("pool", scr.ap()[h, :, c0:c1], Fv[:, 0:c1 - c0], [(Fvn, None)], [(sname, h)])
            for h in range(8):
                src = bass.AP(scr_a, h * 128 * 2048 + 127, [[2047, 128], [128, 14], [1, 128]])
                S.dma("pool", MBs[:, h, :, :], src, [("scr_a", h)], [("MBs", h)])
                src = bass.AP(scr_w, h * 128 * 256 + 127, [[255, 128], [1, 128]])
                S.dma("pool", MBw4[:, h, :], src, [("scr_w", h)], [("MBw4", h)])
                src = bass.AP(scr_c, h * 128 * 6144 + 128 * 3 + 2033, [[6144 - 16, 128], [512, 8], [1, 128]])
                S.dma("pool", MBc[:, h, :, :], src, [("scr_c", h)], [("MBc", h)])
        S.fence()

        S.fence()
        ens.close()
        S.fence()
        en2 = ExitStack()

        def TN2(name, shape, dt):
            return en2.enter_context(nc.sbuf_tensor("n_" + name, shape, dt))

        Pt = [TN2("Pt%d" % i, [128, 4, 128], BF16) for i in range(4)]
        AB_ = []
        for i in range(2):
            AB_.append(dict(
                i=i,
                Eall=TN2("Eall%d" % i, [128, 4, 4, 128], BF16),
                score=TN2("score%d" % i, [128, 128], F32),
                scr2=TN2("scr2%d" % i, [128, 128], F32),
                mx8=TN2("mx8%d" % i, [128, 8], F32),
                thr=TN2("thr%d" % i, [128, 1], F32),
                selb=TN2("selb%d" % i, [128, 128], BF16),
                selbT=TN2("selbT%d" % i, [128, 4, 128], BF16),
                densA=TN2("densA%d" % i, [128, 4], F32),
                coefA=TN2("coefA%d" % i, [128, 4], F32),
                Ccon=TN2("Ccon%d" % i, [128, 4, 64], F32),
            ))
        addm2 = [TN2("addm%d" % i, [128, 128], F32) for i in range(2)]
        dens = TN2("dens", [128, 8], F32)
        coef = TN2("coef", [128, 8], F32)
        Oacc = TN2("Oacc", [128, 8, 64], F32)
        Ob = TN2("Ob", [128, 512], BF16)
        sbanks = ((pA, "pA"), (pB, "pB"), (pF, "pF"), (pG, "pG"))
        sb_i = [0]

        def sbank():
            sb_i[0] += 1
            return sbanks[sb_i[0] % 4], sb_i[0] % 4

        def gen_A(n):
            it, gk = n // 2, n % 2
            Q = 4 * it + 3
            B = AB_[n % 2]
            bi = B["i"]
            nm = lambda s: (s + str(bi), None)
            Eall, score, scr2, mx8, thr, selb, selbT = B["Eall"], B["score"], B["scr2"], B["mx8"], B["thr"], B["selb"], B["selbT"]
            densA, coefA, Ccon = B["densA"], B["coefA"], B["Ccon"]
            addm = addm2[it % 2]
            addn = ("addm%d" % (it % 2), None)
            if gk == 0:
                S.dma("sp", addm[:], addm_d[:, it, :], [], [addn])
            rows = slice(gk * 64, gk * 64 + 64)
            qrhs = QT[rows, it, :, :].rearrange("q a t -> q (a t)")
            hs = slice(4 * gk, 4 * gk + 4)
            nct = min(4, Q // 16 + 1)
            for ct in range(nct):
                pp, pn = (pC, "pC") if ct % 2 == 0 else (pD, "pD")
                k.mm(pp[:, :], KcT[rows, ct * 128:(ct + 1) * 128], qrhs, True, True, [("KcT", None), ("QT", it)], [(pn, None)])
                k.act(Eall[:, ct, :, :].rearrange("p a t -> p (a t)"), pp[:, :], AF.Exp, [(pn, None)], [("Eall%d" % bi, ct)],
                      scale=0.125)
                n_ = min((Q - 16 * ct - 3) // 4, 7)
                k.ve("pool", "tensor_tensor", [("Eall%d" % bi, ct), ("MBc", None)], [("Eall%d" % bi, ct)], out=Eall[:, ct, :, :],
                     in0=Eall[:, ct, :, :], in1=MBc[:, hs, n_, :], op=ALU.mult)
                yield
            k.ve("dve", "memset", [], [("pC", None)], pC[:, 0:260], 0.0)
            k.ve("dve", "memset", [], [("pD", None)], pD[:, :], 0.0)
            for a in range(4):
                for ct in range(nct):
                    k.mm(pC[:, a * 65:(a + 1) * 65], Eall[:, ct, a, :], Vca[:, ct, gk, :], False, ct == nct - 1,
                         [("Eall%d" % bi, ct), ("Vca", None)], [("pC", None)])
                for ct in range(nct):
                    k.mm(pD[:, a * 128:(a + 1) * 128], Eall[:, ct, a, :], Ovl[:, ct, :], False, ct == nct - 1,
                         [("Eall%d" % bi, ct), ("Ovl", None)], [("pD", None)])
            yield
            k.ve("dve", "tensor_scalar", [("pC", None)], [nm("densA")], out=densA[:, 0:4],
                 in0=pC[:, 0:260].rearrange("p (a d) -> p a d", a=4)[:, :, 64], scalar1=1e-30, scalar2=None, op0=ALU.max)
            yield
            k.ve("dve", "reciprocal", [nm("densA")], [nm("densA")], out=densA[:, 0:4], in_=densA[:, 0:4])
            k.ve("pool", "tensor_copy", [addn], [nm("score")], out=score[:], in_=addm[:])
            yield
            for a in range(4):
                k.ve("dve", "scalar_tensor_tensor", [("pD", None), nm("densA"), nm("score")], [nm("score")],
                     out=score[:], in0=pD[:, a * 128:(a + 1) * 128], scalar=densA[:, a:a + 1], in1=score[:], op0=ALU.mult,
                     op1=ALU.add)
                yield
            k.ve("dve", "max", [nm("score")], [nm("mx8")], out=mx8[:], in_=score[:])
            yield
            k.ve("dve", "match_replace", [nm("score"), nm("mx8")], [nm("scr2")], out=scr2[:],
                 in_to_replace=mx8[:], in_values=score[:], imm_value=-3.0e9)
            yield
            k.ve("dve", "max", [nm("scr2")], [nm("mx8")], out=mx8[:], in_=scr2[:])
            yield
            k.ve("dve", "tensor_reduce", [nm("mx8")], [nm("thr")], out=thr[:, 0:1], in_=mx8[:], axis=AX.X, op=ALU.min)
            yield
            k.ve("dve", "tensor_scalar", [nm("score"), nm("thr")], [nm("selb")], out=selb[:], in0=score[:],
                 scalar1=thr[:, 0:1], scalar2=-30000.0, op0=ALU.is_lt, op1=ALU.mult)
            yield
            for a in range(4):
                k.tr(pT[:, a * 128:(a + 1) * 128], selb[:], ident[:], [nm("selb")], [("pT", a)])
            k.ve("dve", "tensor_copy", [("pT", None)], [nm("selbT")], out=selbT[:].rearrange("p a t -> p (a t)"),
                 in_=pT[:, 0:512])
            yield
            k.ve("pool", "tensor_tensor", [nm("densA"), ("GATES", it)], [nm("coefA")], out=coefA[:, 0:4], in0=densA[:, 0:4],
                 in1=GATES[:, it, :].rearrange("p (h j) -> p h j", j=3)[:, hs, 0], op=ALU.mult)
            yield
            for a in range(4):
                k.ve("dve", "tensor_scalar", [("pC", None), nm("coefA")], [nm("Ccon")], out=Ccon[:, a, :],
                     in0=pC[:, a * 65:a * 65 + 64], scalar1=coefA[:, a:a + 1], scalar2=None, op0=ALU.mult)
            yield

        def gen_B(n):
            it, gk = n // 2, n % 2
            Q = 4 * it + 3
            B = AB_[n % 2]
            bi = B["i"]
            selbT, Ccon = B["selbT"], B["Ccon"]
            rows = slice(gk * 64, gk * 64 + 64)
            qrhs = QT[rows, it, :, :].rearrange("q a t -> q (a t)")
            hs = slice(4 * gk, 4 * gk + 4)
            for bq, br in enumerate((1, 0)):
                if br == 0:
                    kT, kTn, Vt, Vn, kts = kselT, "kselT", Vs, "Vs", list(range(0, Q + 1))
                else:
                    kT, kTn, Vt, Vn, kts = kwinT, "kwinT", Vw, "Vw", list(range(max(Q - 4, 0), Q + 1))
                k.ve("dve", "memset", [], [("pE", None)], pE[:, 0:260], 0.0)
                SK = 3
                nk = len(kts)
                slots = []
                for step in range(nk + SK):
                    if step < nk:
                        kt = kts[step]
                        m = Q - kt
                        (pp, pn), bidx = sbank()
                        slots.append(bidx)
                        k.mm(pp[:, :], kT[rows, kt * 128:(kt + 1) * 128], qrhs, True, br == 1, [(kTn, None), ("QT", it)],
                             [(pn, None)])
                        if br == 0:
                            k.mm(pp[:, :], Ebig[:, kt, :], selbT[:].rearrange("p a t -> p (a t)"), False, True,
                                 [("Ebig", None), ("selbT%d" % bi, None)], [(pn, None)])
                        P = Pt[bidx]
                        Pn = "Pt%d" % bidx
                        k.act(P[:].rearrange("p a t -> p (a t)"), pp[:, :], AF.Exp, [(pn, None)], [(Pn, None)], scale=0.125)
                        if br == 1 and m == 4:
                            mb = MBw4[:, hs, :]
                            mbn = ("MBw4", None)
                        else:
                            mb = MBs[:, hs, min(m, 13), :]
                            mbn = ("MBs", None)
                        k.ve("dve", "tensor_tensor", [(Pn, None), mbn], [(Pn, None)], out=P[:],
                             in0=P[:], in1=mb, op=ALU.mult)
                    if step >= SK:
                        kt = kts[step - SK]
                        bidx = slots[step - SK]
                        P = Pt[bidx]
                        Pn = "Pt%d" % bidx
                        for a in range(4):
                            k.mm(pE[:, a * 65:(a + 1) * 65], P[:, a, :], Vt[:, kt, gk, :], False, kt == kts[-1],
                                 [(Pn, None), (Vn, kt)], [("pE", None)])
                    yield
                c0 = 4 * bq
                k.ve("dve", "tensor_scalar", [("pE", None)], [("dens", None)], out=dens[:, c0:c0 + 4],
                     in0=pE[:, 0:260].rearrange("p (a d) -> p a d", a=4)[:, :, 64], scalar1=1e-30, scalar2=None, op0=ALU.max)
                k.ve("dve", "reciprocal", [("dens", None)], [("dens", None)], out=dens[:, c0:c0 + 4], in_=dens[:, c0:c0 + 4])
                k.ve("dve", "tensor_tensor", [("dens", None), ("GATES", it)], [("coef", None)], out=coef[:, c0:c0 + 4],
                     in0=dens[:, c0:c0 + 4], in1=GATES[:, it, :].rearrange("p (h j) -> p h j", j=3)[:, hs, 1 + br], op=ALU.mult)
                for a in range(4):
                    if bq == 0:
                        in1, in1n = Ccon[:, a, :], ("Ccon%d" % bi, None)
                    else:
                        in1, in1n = Oacc[:, 4 * gk + a, :], ("Oacc", None)
                    k.ve("dve", "scalar_tensor_tensor", [("pE", None), ("coef", None), in1n], [("Oacc", None)],
                         out=Oacc[:, 4 * gk + a, :], in0=pE[:, a * 65:a * 65 + 64], scalar=coef[:, c0 + a:c0 + a + 1],
                         in1=in1, op0=ALU.mult, op1=ALU.add)
                yield
            if gk == 1:
                k.ve("dve", "tensor_copy", [("Oacc", None)], [("Ob", None)], out=Ob[:], in_=Oacc[:].rearrange("p h d -> p (h d)"))
                for c in range(4):
                    k.tr(pT[:, c * 128:(c + 1) * 128], Ob[:, c * 128:(c + 1) * 128], ident[:], [("Ob", None)], [("pT", c)])
                k.ve("dve", "tensor_copy", [("pT", None)], [("oT_n", it)], out=oT_n[:, it, :, :].rearrange("p c t -> p (c t)"),
                     in_=pT[:, 0:512])
                yield

        NN = 2 * NOWN
        doneA, doneB = -1, -1
        nxtA, nxtB = 0, 0
        activeN = []
        while True:
            if nxtA < NN and not any(a[0] == "A" for a in activeN) and nxtA <= doneB + 2:
                activeN.append(("A", nxtA, gen_A(nxtA)))
                nxtA += 1
            if nxtB < NN and not any(a[0] == "B" for a in activeN) and doneA >= nxtB:
                activeN.append(("B", nxtB, gen_B(nxtB)))
                nxtB += 1
            if not activeN:
                break
            for a in list(activeN):
                try:
                    next(a[2])
                except StopIteration:
                    activeN.remove(a)
                    if a[0] == "A":
                        doneA = a[1]
                    else:
                        doneB = a[1]
        assert doneB == NN - 1
        if "onsa" in dbg:
            donsa = nc.dram_tensor("donsa", [128, NOWN * 4 * 128], BF16, kind="ExternalOutput").ap()
            S.dma("sp", donsa[:, :], oT_n[:].rearrange("q a b c -> q (a b c)"), [("oT_n", None)], [])
        S.fence()
        en2.close()
        en.close()
        S.fence()

    if "stop_r" in dbg:
        S.finish()
        return nc
    es_stgc = ExitStack()
    alloc_stg(es_stgc, "c")
    gx = nc.alloc_sbuf_tensor("gx", [128, 8], F32)
    gffn = nc.alloc_sbuf_tensor("gffn", [128, 8], F32)
    gmem = nc.alloc_sbuf_tensor("gmem", [128, 8], F32)
    S.dma("sp", gx[:], gl["g_x"][:, :], [], [("gx", None)])
    S.dma("sp", gffn[:], gl["g_ffn"][:, :], [], [("gffn", None)])
    S.dma("sp", gmem[:], gl["g_mem"][:, :], [], [("gmem", None)])

    gf1 = nc.alloc_sbuf_tensor("gf1", [1, D], F32)
    gfb = nc.alloc_sbuf_tensor("gfb", [128, D], F32)
    S.dma("sp", gf1[:], g_f[:, :], [], [("gf1", None)])
    for h in range(2):
        pp = pA if h == 0 else pB
        k.mm(pp[:, :], ones_f[0:1, :], gf1[0:1, h * 512:(h + 1) * 512], True, True,
             [("ones_f", None), ("gf1", None)], [("pA" if h == 0 else "pB", None)])
        k.ve("dve", "tensor_copy", [("pA" if h == 0 else "pB", None)], [("gfb", None)],
             out=gfb[:, h * 512:(h + 1) * 512], in_=pp[:, :])

    x2all = nc.alloc_sbuf_tensor("x2all", [128, NOWN, D], F32)
    h2T = nc.alloc_sbuf_tensor("h2T", [128, NOWN, 8, 128], BF16)
    hT = nc.alloc_sbuf_tensor("hT", [128, 8, 128], BF16)

    if "omix" not in dbg:
        with nc.sbuf_tensor("w_out_b0", [128, 8, D], BF16) as w_out_b0, \
                nc.sbuf_tensor("xt0", [128, D], F32) as xt0:
            load_w(w_out_b0, "w_out_b0", wl["w_out"], 8, D)
            for it in range(NOWN):
                pos = 4 * it + 3
                S.dma("sp", xt0[:], xs[pos * 128:(pos + 1) * 128, :], [], [("xt0", None)])
                for hf in range(2):
                    pp, pn = (pA, "pA") if hf == 0 else (pB, "pB")
                    for kc in range(8):
                        if kc < 4:
                            lh, ln = oT_n[:, it, kc, :], ("oT_n", it)
                        else:
                            lh, ln = oT_r[:, it, kc - 4, :], ("oT_r", it)
                        k.mm(pp[:, :], lh, w_out_b0[:, kc, hf * 512:(hf + 1) * 512], kc == 0, kc == 7,
                             [ln, ("w_out_b0", kc)], [(pn, None)])
                    k.ve("dve", "tensor_tensor", [(pn, None), ("xt0", None)], [("x2all", it)],
                         out=x2all[:, it, hf * 512:(hf + 1) * 512], in0=pp[:, :], in1=xt0[:, hf * 512:(hf + 1) * 512],
                         op=ALU.add)
        S.fence()
        eo.close()
        S.fence()
    es_c1 = ExitStack()
    kmT = es_c1.enter_context(nc.sbuf_tensor("kmT", [128, 8, 256], BF16))
    Vm = es_c1.enter_context(nc.sbuf_tensor("Vm", [128, 2, D], BF16))
    with nc.sbuf_tensor("w_kv_b", [128, 8, 2 * D], BF16) as w_kv_b, \
            nc.sbuf_tensor("memf", [128, D], F32) as memf, \
            nc.sbuf_tensor("memT", [128, 8, 256], BF16) as memT:
        load_w(w_kv_b, "w_kv_b", wl["w_kv"], 8, 2 * D, gcol=gmem, gname="gmem")
        for mt in range(2):
            S.dma("sp", memf[:], mem[mt * 128:(mt + 1) * 128, :], [], [("memf", None)])
            rmsnorm(memf[:], ("memf", None), hb[:], ("hb", None), sq, ss, "")
            for c in range(8):
                k.tr(pT[:, c * 128:(c + 1) * 128], hb[:, c * 128:(c + 1) * 128], ident[:], [("hb", None)], [("pT", c)])
            k.ve("dve", "tensor_copy", [("pT", None)], [("memT", mt)],
                 out=memT[:, :, mt * 128:(mt + 1) * 128], in_=pT[:, :].rearrange("p (c t) -> p c t", c=8))
        for c in range(8):
            for kc in range(8):
                k.mm(pA[:, 0:256], w_kv_b[:, kc, c * 128:(c + 1) * 128], memT[:, kc, :], kc == 0, kc == 7,
                     [("w_kv_b", kc), ("memT", None)], [("pA", None)])
            k.ve("dve", "tensor_copy", [("pA", None)], [("kmT", c)], out=kmT[:, c, :], in_=pA[:, 0:256])
        for mt in range(2):
            for hf in range(2):
                for kc in range(8):
                    k.mm(pA[:, :], memT[:, kc, mt * 128:(mt + 1) * 128],
                         w_kv_b[:, kc, D + hf * 512:D + (hf + 1) * 512], kc == 0, kc == 7,
                         [("w_kv_b", kc), ("memT", None)], [("pA", None)])
                k.ve("dve", "tensor_copy", [("pA", None)], [("Vm", mt)], out=Vm[:, mt, hf * 512:(hf + 1) * 512],
                     in_=pA[:, :])
    S.fence()
    with nc.sbuf_tensor("w_q_b", [128, 8, D], BF16) as w_q_b, \
            nc.sbuf_tensor("w_o_b", [128, 8, D], BF16) as w_o_b, \
            nc.sbuf_tensor("c_sq1", [128, D], BF16) as c_sq1, \
            nc.sbuf_tensor("c_ss1", [128, 2], F32) as c_ss1, \
            nc.sbuf_tensor("c_hb1", [128, D], BF16) as c_hb1, \
            nc.sbuf_tensor("c_hT1", [128, 8, 128], BF16) as c_hT1, \
            nc.sbuf_tensor("qT0", [128, 8, 128], BF16) as qT0, \
            nc.sbuf_tensor("qT1", [128, 8, 128], BF16) as qT1, \
            nc.sbuf_tensor("PT0", [128, 2, 128], BF16) as PT0, \
            nc.sbuf_tensor("PT1", [128, 2, 128], BF16) as PT1, \
            nc.sbuf_tensor("rden0", [128, 128], F32) as rden0, \
            nc.sbuf_tensor("rden1", [128, 128], F32) as rden1, \
            nc.sbuf_tensor("oT0", [128, 8, 128], BF16) as oT0, \
            nc.sbuf_tensor("oT1", [128, 8, 128], BF16) as oT1:
        load_w(w_q_b, "w_q_b", wl["w_q"], 8, D, gcol=gx, gname="gx")
        load_w(w_o_b, "w_o_b", wl["w_o"], 8, D)
        CS = [dict(i=0, sq=sq, ss=ss, hb=hb, hT=hT, qT=qT0, PT=PT0, rden=rden0, oT=oT0),
              dict(i=1, sq=c_sq1, ss=c_ss1, hb=c_hb1, hT=c_hT1, qT=qT1, PT=PT1, rden=rden1, oT=oT1)]
        cbanks = [(pA, "pA"), (pB, "pB"), (pC, "pC"), (pD, "pD"), (pE, "pE"), (pF, "pF"), (pG, "pG")]
        cb_i = [0]

        def cbank():
            cb_i[0] += 1
            return cbanks[cb_i[0] % 7]

        def c_norm(src, sname, B):
            bi = B["i"]
            k.act(B["sq"][:], src, AF.Square, [sname], [("c_sq%d" % bi, None)])
            k.ve("dve", "reduce_sum", [("c_sq%d" % bi, None)], [("c_ss%d" % bi, None)], out=B["ss"][:, 0:1], in_=B["sq"][:], axis=AX.X)
            k.act(B["ss"][:, 1:2], B["ss"][:, 0:1], AF.Ln, [("c_ss%d" % bi, None)], [("c_ss%d" % bi, None)], bias=epsc[:, 0:1],
                  scale=1.0 / D)
            k.act(B["ss"][:, 1:2], B["ss"][:, 1:2], AF.Exp, [("c_ss%d" % bi, None)], [("c_ss%d" % bi, None)], scale=-0.5)
            k.ve("dve", "tensor_scalar", [sname, ("c_ss%d" % bi, None)], [("c_hb%d" % bi, None)], out=B["hb"][:], in0=src,
                 scalar1=B["ss"][:, 1:2], scalar2=None, op0=ALU.mult)

        def gen_c1(it, B):
            bi = B["i"]
            hbn, hTn, qTn, PTn, rdn, oTn = [("c_%s%d" % (s, bi), None) for s in ("hb", "hT", "qT", "PT", "rden", "oT")]
            c_norm(x2all[:, it, :], ("x2all", it), B)
            yield
            for c in range(8):
                k.tr(pT[:, c * 128:(c + 1) * 128], B["hb"][:, c * 128:(c + 1) * 128], ident[:], [hbn], [("pT", c)])
            k.ve("dve", "tensor_copy", [("pT", None)], [hTn], out=B["hT"][:].rearrange("p c t -> p (c t)"), in_=pT[:, :])
            yield
            for c in range(8):
                pp, pn = cbank()
                for kc in range(8):
                    k.mm(pp[:, 0:128], w_q_b[:, kc, c * 128:(c + 1) * 128], B["hT"][:, kc, :], kc == 0, kc == 7,
                         [("w_q_b", kc), hTn], [(pn, None)])
                k.act(B["qT"][:, c, :], pp[:, 0:128], AF.Copy, [(pn, None)], [("c_qT%d" % bi, c)])
                if c % 2 == 1:
                    yield
            for h in range(4):
                pp, pn = cbank()
                for mc in range(2):
                    for dc in range(2):
                        k.mm(pp[:, mc * 128:(mc + 1) * 128], kmT[:, 2 * h + dc, mc * 128:(mc + 1) * 128],
                             B["qT"][:, 2 * h + dc, :], dc == 0, dc == 1, [("kmT", None), qTn], [(pn, None)])
                k.act(B["PT"][:].rearrange("p c t -> p (c t)"), pp[:, 0:256], AF.Exp, [(pn, None)], [PTn], scale=1.0 / 16.0)
                yield
                pp, pn = cbank()
                for mc in range(2):
                    k.mm(pp[:, 0:128], ones_b[:, :], B["PT"][:, mc, :], mc == 0, mc == 1, [("ones_b", None), PTn], [(pn, None)])
                k.ve("dve", "reciprocal", [(pn, None)], [rdn], out=B["rden"][:], in_=pp[:, 0:128])
                for dc in range(2):
                    pp, pn = cbank()
                    for mc in range(2):
                        k.mm(pp[:, 0:128], Vm[:, mc, h * 256 + dc * 128:h * 256 + (dc + 1) * 128], B["PT"][:, mc, :], mc == 0,
                             mc == 1, [("Vm", None), PTn], [(pn, None)])
                    k.ve("dve", "tensor_tensor", [(pn, None), rdn], [("c_oT%d" % bi, 2 * h + dc)], out=B["oT"][:, 2 * h + dc, :],
                         in0=pp[:, 0:128], in1=B["rden"][:], op=ALU.mult)
                yield
            for hf in range(2):
                pp, pn = cbank()
                for kc in range(8):
                    k.mm(pp[:, :], B["oT"][:, kc, :], w_o_b[:, kc, hf * 512:(hf + 1) * 512], kc == 0, kc == 7,
                         [oTn, ("w_o_b", kc)], [(pn, None)])
                k.ve("dve", "tensor_tensor", [(pn, None), ("x2all", it)], [("x2all", it)],
                     out=x2all[:, it, hf * 512:(hf + 1) * 512], in0=pp[:, :], in1=x2all[:, it, hf * 512:(hf + 1) * 512],
                     op=ALU.add)
                yield
            c_norm(x2all[:, it, :], ("x2all", it), B)
            yield
            for c in range(8):
                k.tr(pT[:, c * 128:(c + 1) * 128], B["hb"][:, c * 128:(c + 1) * 128], ident[:], [hbn], [("pT", c)])
            k.ve("dve", "tensor_copy", [("pT", None)], [("h2T", it)], out=h2T[:, it, :, :].rearrange("p c t -> p (c t)"),
                 in_=pT[:, :])
            yield

        actC = []
        nxtC = 0
        while True:
            while nxtC < NOWN and len(actC) < 2:
                used = {a[0] for a in actC}
                bsi = 0 if 0 not in used else 1
                actC.append((bsi, gen_c1(nxtC, CS[bsi])))
                nxtC += 1
            if not actC:
                break
            for a in list(actC):
                try:
                    next(a[1])
                except StopIteration:
                    actC.remove(a)

    es_c1.close()
    eo.close()
    S.fence()
    with nc.sbuf_tensor("wg_b", [128, 8, 1408], BF16) as wg_b, \
            nc.sbuf_tensor("wu_b", [128, 8, 1408], BF16) as wu_b, \
            nc.sbuf_tensor("wd_b", [128, 11, D], BF16) as wd_b, \
            nc.sbuf_tensor("sg", [128, 128], F32) as sg, \
            nc.sbuf_tensor("sg2", [128, 128], F32) as sg2, \
            nc.sbuf_tensor("aT", [128, 11, 128], BF16) as aT, \
            nc.sbuf_tensor("yo", [128, D], F32) as yo:
        for half in range(2):
            load_w(wg_b, "wg_b", wl["w_gate"], 8, 1408, gcol=gffn, gname="gffn", c_lo=half * 1408)
            load_w(wu_b, "wu_b", wl["w_up"], 8, 1408, gcol=gffn, gname="gffn", c_lo=half * 1408)
            load_w(wd_b, "wd_b", wl["w_down"], 11, D, kc_lo=half * 11)
            for it in range(NOWN):
                for fc in range(11):
                    pg, pgn = (pC, "pC") if fc % 2 == 0 else (pE, "pE")
                    pu, pun = (pD, "pD") if fc % 2 == 0 else (pF, "pF")
                    sgt, sgn = (sg, "sg") if fc % 2 == 0 else (sg2, "sg2")
                    for kc in range(8):
                        k.mm(pg[:, 0:128], wg_b[:, kc, fc * 128:(fc + 1) * 128], h2T[:, it, kc, :], kc == 0, kc == 7,
                             [("wg_b", kc), ("h2T", it)], [(pgn, None)])
                    for kc in range(8):
                        k.mm(pu[:, 0:128], wu_b[:, kc, fc * 128:(fc + 1) * 128], h2T[:, it, kc, :], kc == 0, kc == 7,
                             [("wu_b", kc), ("h2T", it)], [(pun, None)])
                    k.act(sgt[:], pg[:, 0:128], AF.Silu, [(pgn, None)], [(sgn, None)])
                    k.ve("dve", "tensor_tensor", [(sgn, None), (pun, None)], [("aT", fc)], out=aT[:, fc, :], in0=sgt[:],
                         in1=pu[:, 0:128], op=ALU.mult)
                for hf in range(2):
                    pp, pn = (pA, "pA") if hf == 0 else (pB, "pB")
                    for fc in range(11):
                        k.mm(pp[:, :], aT[:, fc, :], wd_b[:, fc, hf * 512:(hf + 1) * 512], fc == 0, fc == 10,
                             [("aT", None), ("wd_b", fc)], [(pn, None)])
                    k.ve("dve", "tensor_tensor", [(pn, None), ("x2all", it)], [("x2all", it)],
                         out=x2all[:, it, hf * 512:(hf + 1) * 512], in0=pp[:, :],
                         in1=x2all[:, it, hf * 512:(hf + 1) * 512], op=ALU.add)
                if half == 1:
                    k.act(sq[:], x2all[:, it, :], AF.Square, [("x2all", it)], [("sq", None)])
                    k.ve("dve", "reduce_sum", [("sq", None)], [("ss", None)], out=ss[:, 0:1], in_=sq[:], axis=AX.X)
                    k.act(ss[:, 1:2], ss[:, 0:1], AF.Ln, [("ss", None)], [("ss", None)], bias=epsc[:, 0:1], scale=1.0 / D)
                    k.act(ss[:, 1:2], ss[:, 1:2], AF.Exp, [("ss", None)], [("ss", None)], scale=-0.5)
                    k.ve("dve", "tensor_scalar", [("x2all", it), ("ss", None)], [("yo", None)], out=yo[:],
                         in0=x2all[:, it, :], scalar1=ss[:, 1:2], scalar2=None, op0=ALU.mult)
                    k.ve("pool", "tensor_tensor", [("yo", None), ("gfb", None)], [("yo", None)], out=yo[:], in0=yo[:],
                         in1=gfb[:], op=ALU.mult)
                    S.dma("sp", y[it * 128:(it + 1) * 128, :], yo[:], [("yo", None)], [])
    S.finish()
    return nc


def _lay_w(w, kc):
    return np.ascontiguousarray(w.reshape(kc, 128, w.shape[1]).transpose(1, 0, 2))


def _lay_v(v):
    return np.ascontiguousarray(v.reshape(-1, 128).T)


NSA_COLS = 512 + 6 * 128 + 24


def _rwkv_consts():
    f = np.float32
    r = np.arange(128)
    t = np.arange(512)
    rst = np.broadcast_to((t % 64 != 0).astype(f), (128, 512))
    same = (r[:, None] // 64) == (r[None, :] // 64)
    mus = (same & ((r[:, None] % 64) < (r[None, :] % 64))).astype(f)
    mui = (same & ((r[:, None] % 64) <= (r[None, :] % 64))).astype(f)
    mls = (same & ((r[:, None] % 64) > (r[None, :] % 64))).astype(f)
    ma = np.concatenate([mus, mui, mus, mui], 1)
    ml4 = np.concatenate([mls] * 4, 1)
    i4 = np.concatenate([np.eye(128, dtype=f)] * 4, 1)
    return np.ascontiguousarray(np.stack([rst, ma, ml4, i4], 1)), same.astype(f)


def _rwkv_inputs(inp):
    f = np.float32
    g = lambda n: np.asarray(inp[n], f)[0]
    w_in = g("w_in")
    rm, bo = _rwkv_consts()
    lnw = g("rwkv_lnx_w").reshape(4, 2, 64)
    lnb = g("rwkv_lnx_b").reshape(4, 2, 64)
    rows_h = np.arange(128) // 64
    return {
        "w_in_r": _lay_w(w_in[:, NSA_COLS:NSA_COLS + 1792], 8),
        "g_mix": _lay_v(g("norm_mix_g")),
        "mu": _lay_v(g("rwkv_mu")),
        "rvecs": np.ascontiguousarray(np.stack([_lay_v(g("rwkv_w0")), _lay_v(g("rwkv_a0")), _lay_v(g("rwkv_k_k")),
                                                _lay_v(g("rwkv_k_a")), _lay_v(g("rwkv_r_k").reshape(-1))], 1)),
        "lora_up": np.ascontiguousarray(np.concatenate([g("rwkv_w_up"), g("rwkv_a_up")], 0)),
        "g_up": np.ascontiguousarray(g("rwkv_g_up")),
        "lnw": np.ascontiguousarray(lnw[:, rows_h, :].transpose(1, 0, 2)),
        "lnb": np.ascontiguousarray(lnb[:, rows_h, :].transpose(1, 0, 2)),
        "rmasks": rm,
        "blockones": bo,
    }


def _t5_bucket(n):
    n = np.asarray(n)
    nf = np.maximum(n, 1).astype(np.float32)
    large = 16 + (np.log(nf / np.float32(16)) / np.float32(math.log(2048 / 16)) * np.float32(16)).astype(np.int32)
    return np.where(n < 16, n, np.minimum(large, 31))


def _nsa_consts():
    f = np.float32
    def onehot(delta, valid):
        b = _t5_bucket(np.maximum(delta, 0))
        oh = np.zeros((32, delta.shape[0]), f)
        idx = np.nonzero(valid)[0]
        oh[b[idx], idx] = 1.0
        return oh
    u = np.arange(2048)
    OHa = onehot(u - 127, (u - 127) >= 0)
    u = np.arange(256)
    OHw = onehot(512 + u - 127, (u - 127) < 0)
    u = np.arange(6144)
    OHc = onehot(u - 2064, (u - 2064) >= 0)
    jj = np.arange(128)[:, None, None]
    kt = np.arange(64)[None, :, None]
    kk = np.arange(128)[None, None, :]
    Ebig = (jj == 2 * kt + kk // 64).astype(f)
    return OHa, OHw, OHc, np.ascontiguousarray(Ebig)


def _nsa_inputs(inp, j):
    f = np.float32
    g = lambda n: np.asarray(inp[n], f)[0]
    w_in = g("w_in")
    nd = (3 - j) * 128
    qperm = np.concatenate([np.r_[a * 64:(a + 1) * 64, (4 + a) * 64:(5 + a) * 64] for a in range(4)])
    cols = np.concatenate([qperm, np.arange(512, 640), np.arange(640, 768), np.arange(768, 896), np.arange(1024, 1152),
                           np.arange(896, 1024), np.arange(1152, 1280), np.arange(1280, 1304)])
    OHa, OHw, OHc, Ebig = _nsa_consts()
    def w1lay(w):
        a = w.reshape(32, 64, 256).transpose(1, 0, 2)
        return np.ascontiguousarray(np.concatenate([a, a], 0))
    def pelay(pe):
        a = pe.T
        return np.ascontiguousarray(np.concatenate([a, a], 0))
    w2k = g("cmp_k_w2").reshape(2, 128, 64).transpose(1, 0, 2)
    w2v = g("cmp_v_w2").reshape(2, 128, 64).transpose(1, 0, 2)
    tpos = np.arange(64)[None, :] * 128 + np.arange(128)[:, None]
    realtok = (tpos >= nd).astype(f)
    c = np.arange(4)[None, :] * 128 + np.arange(128)[:, None]
    realblk = ((16 * c >= nd) & (c <= 510)).astype(f)
    jb = np.arange(128)[None, None, :]
    ov = ((16 * c[:, :, None] <= 64 * jb + 63) & (16 * c[:, :, None] + 31 >= 64 * jb)).astype(f)
    ovlm = np.concatenate([realblk[:, :, None], ov * realblk[:, :, None]], 2)
    it = np.arange(16)[None, :, None]
    q = np.arange(128)[:, None, None]
    t = 128 * (4 * it + 3) + q
    cur = t // 64
    blk0 = nd // 64
    bad = (jb < blk0) | (64 * jb > t)
    forced = (jb == blk0) | (jb == cur) | (jb == cur - 1)
    addmask = np.where(bad, -1e9, np.where(forced, 1e4, 0.0)).astype(f)
    return {
        "w_in_n": _lay_w(np.ascontiguousarray(w_in[:, cols]), 8),
        "g_mix": _lay_v(g("norm_mix_g")),
        "gate_b": np.ascontiguousarray(np.broadcast_to(g("nsa_gate_b")[None, :], (128, 24))),
        "realtok": np.ascontiguousarray(realtok),
        "cmp_w1k": w1lay(g("cmp_k_w1")), "cmp_w1v": w1lay(g("cmp_v_w1")),
        "cmp_peTk": pelay(g("cmp_pe_k")), "cmp_peTv": pelay(g("cmp_pe_v")),
        "cmp_b1k": _lay_v(g("cmp_k_b1")), "cmp_b1v": _lay_v(g("cmp_v_b1")),
        "cmp_w2k": np.ascontiguousarray(np.concatenate([w2k, w2k], 2)),
        "cmp_w2v": np.ascontiguousarray(w2v),
        "ovlm": np.ascontiguousarray(ovlm.astype(f)),
        "relb_rep": np.ascontiguousarray(np.broadcast_to(np.asarray(inp["rel_bias"], f)[:, :, None], (32, 8, 128))),
        "OHa": OHa, "OHw": OHw, "OHc": OHc, "Ebig": Ebig,
        "addmask": np.ascontiguousarray(addmask),
    }


def _core_inputs(c, inp):
    b, j = c // 4, c % 4
    f = np.float32
    x = np.asarray(inp["x"], f)[b]
    nd = (3 - j) * 128
    xs = np.zeros((8192, D), f)
    xs[nd:] = x[: 8192 - nd]
    m = {
        "xs": xs,
        "ident": np.eye(128, dtype=f),
        "g_f": np.asarray(inp["norm_f_g"], f).reshape(1, D),
        "g_x": _lay_v(np.asarray(inp["norm_x_g"], f)[0]),
        "g_ffn": _lay_v(np.asarray(inp["norm_ffn_g"], f)[0]),
        "g_mem": _lay_v(np.asarray(inp["norm_mem_g"], f)[0]),
        "mem": np.ascontiguousarray(np.asarray(inp["mem"], f)[b]),
        "w_out": _lay_w(np.asarray(inp["w_out"], f)[0], 8),
        "w_q": _lay_w(np.asarray(inp["w_q_x"], f)[0], 8),
        "w_kv": _lay_w(np.asarray(inp["w_kv_x"], f)[0], 8),
        "w_o": _lay_w(np.asarray(inp["w_o_x"], f)[0], 8),
        "w_gate": _lay_w(np.asarray(inp["w_gate"], f)[0], 8),
        "w_up": _lay_w(np.asarray(inp["w_up"], f)[0], 8),
        "w_down": _lay_w(np.asarray(inp["w_down"], f)[0], 22),
    }
    return m


def kernel(**inp):
    nc = build_program(dbg={"rwkv": 1, "nsa": 1})
    rin = _rwkv_inputs(inp)
    in_maps = []
    for c in range(8):
        m = _core_inputs(c, inp)
        m.update(rin)
        m.update(_nsa_inputs(inp, c % 4))
        in_maps.append(m)
    res = run_bass_kernel_spmd(nc, in_maps, core_ids=list(range(8)))
    out = np.zeros((2, 8192, D), np.float32)
    for c in range(8):
        b, j = c // 4, c % 4
        yv = np.asarray(res.results[c]["y"]).reshape(NOWN, 128, D)
        out[b].reshape(64, 128, D)[j::4] = yv
    return out
```

```python
import math
import numpy as np
from contextlib import ExitStack
import concourse.bass as bass
import concourse.mybir as mybir
from concourse.bass_utils import run_bass_kernel_spmd

F32 = mybir.dt.float32
BF16 = mybir.dt.bfloat16
AF = mybir.ActivationFunctionType
ALU = mybir.AluOpType
AX = mybir.AxisListType

D = 1024
DFF = 2816
NOWN = 16
RMS_EPS = 1e-6


class Sync:
    EPOCH = 24000
    NDMA = 12

    def __init__(self, nc):
        self.nc = nc
        self.eng = {"pe": nc.tensor, "act": nc.scalar, "dve": nc.vector, "pool": nc.gpsimd, "sp": nc.sync}
        self.cnt = {e: 0 for e in self.eng}
        self.sems = {}
        self.known = {e: {} for e in self.eng}
        self.vc = {}
        self.dma_sems = {}
        self.dma_val = {}
        self.dma_rr = {}
        self.state = {}

    def _sem(self, e, ep):
        k = (e, ep)
        if k not in self.sems:
            self.sems[k] = self.nc.alloc_semaphore("s_%s_%d" % (e, ep))
        return self.sems[k]

    def _wait(self, e, tok):
        kn = self.known[e]
        if tok[0] == "c":
            _, f, n = tok
            if f == "pe" and e == "pe":
                return
            if kn.get(f, 0) >= n:
                return
            ep = (n - 1) // self.EPOCH
            self.eng[e].wait_ge(self._sem(f, ep), n - ep * self.EPOCH)
            kn[f] = n
        else:
            _, q, slot, v = tok
            key = ("d", q, slot)
            if kn.get(key, 0) >= v:
                return
            self.eng[e].wait_ge(self.dma_sems[(q, slot)], v)
            kn[key] = v
        snap = self.vc.get(tok)
        if snap:
            for k2, v2 in snap.items():
                if kn.get(k2, 0) < v2:
                    kn[k2] = v2

    @staticmethod
    def _rkey(tok):
        return tok[1] if tok[0] == "c" else (tok[1], tok[2])

    def _deps(self, reads, writes):
        deps = []
        for (name, sub) in reads:
            st = self.state.get(name)
            if st:
                for s2, ent in st.items():
                    if (sub is None or s2 is None or s2 == sub) and ent[0] is not None:
                        deps.append(ent[0])
        for (name, sub) in writes:
            st = self.state.get(name)
            if st:
                for s2, ent in st.items():
                    if sub is None or s2 is None or s2 == sub:
                        if ent[0] is not None:
                            deps.append(ent[0])
                        deps.extend(ent[1].values())
        return deps

    def _update(self, tok, reads, writes):
        rk = self._rkey(tok)
        for (name, sub) in reads:
            st = self.state.setdefault(name, {})
            ent = st.get(sub)
            if ent is None:
                ent = st[sub] = [None, {}]
            ent[1][rk] = tok
        for (name, sub) in writes:
            st = self.state.setdefault(name, {})
            if sub is None:
                st.clear()
            st[sub] = [tok, {}]

    PSUM = ("pA", "pB", "pC", "pD", "pE", "pF", "pG", "pT", "pJ0", "pJ1", "pM0", "pM1", "pS0", "pS1", "pS2")

    def op(self, e, fn, reads=(), writes=()):
        if e != "pe":
            writes = list(writes) + [(r[0], None) for r in reads if r[0] in self.PSUM]
        for d in self._deps(reads, writes):
            self._wait(e, d)
        ins = fn()
        self.cnt[e] += 1
        n = self.cnt[e]
        ep = (n - 1) // self.EPOCH
        ins.then_inc(self._sem(e, ep), 1)
        tok = ("c", e, n)
        self.vc[tok] = dict(self.known[e])
        self._update(tok, reads, writes)
        return tok

    def dma(self, q, out, in_, reads=(), writes=()):
        slot = self.dma_rr.get(q, 0) % self.NDMA
        self.dma_rr[q] = self.dma_rr.get(q, 0) + 1
        k = (q, slot)
        if k not in self.dma_sems:
            self.dma_sems[k] = self.nc.alloc_semaphore("d_%s_%d" % (q, slot))
            self.dma_val[k] = 0
        if self.dma_val[k] > 0:
            self._wait(q, ("d", q, slot, self.dma_val[k]))
        for d in self._deps(reads, writes):
            self._wait(q, d)
        ins = self.eng[q].dma_start(out=out, in_=in_)
        self.dma_val[k] += 16
        ins.then_inc(self.dma_sems[k], 16)
        tok = ("d", q, slot, self.dma_val[k])
        self.vc[tok] = dict(self.known[q])
        self._update(tok, reads, writes)
        return tok

    def fence(self):
        for e in self.eng:
            for f in self.eng:
                if self.cnt[f] > 0:
                    self._wait(e, ("c", f, self.cnt[f])) if not (e == "pe" and f == "pe") else None
            for (q, slot), v in list(self.dma_val.items()):
                if v > 0:
                    self._wait(e, ("d", q, slot, v))

    def finish(self):
        for (q, slot), v in list(self.dma_val.items()):
            if v > 0:
                self._wait(q, ("d", q, slot, v))


class K:
    def __init__(self, nc):
        self.nc = nc
        self.S = Sync(nc)
        self.rr = 0

    def mm(self, out, lhsT, rhs, start, stop, r, w):
        nc = self.nc
        return self.S.op("pe", lambda: nc.tensor.matmul(out, lhsT=lhsT, rhs=rhs, start=start, stop=stop), r, w)

    def tr(self, out, in_, ident, r, w):
        nc = self.nc
        return self.S.op("pe", lambda: nc.tensor.transpose(out, in_, ident), r + [("ident", None)], w)

    def act(self, out, in_, func, r, w, bias=None, scale=None):
        nc = self.nc
        kw = {}
        if bias is not None:
            kw["bias"] = bias
        if scale is not None:
            kw["scale"] = scale
        return self.S.op("act", lambda: nc.scalar.activation(out=out, in_=in_, func=func, **kw), r, w)

    def ve(self, e, name, r, w, *a, **kw):
        eng = self.S.eng[e]
        return self.S.op(e, lambda: getattr(eng, name)(*a, **kw), r, w)

    def alt(self):
        self.rr += 1
        return "dve" if self.rr % 2 else "pool"


def build_program(dbg=None):
    dbg = dbg or {}
    nc = bass.Bass("TRN2", target_bir_lowering=False)
    k = K(nc)
    S = k.S

    def din(name, shape, dt=F32):
        return nc.dram_tensor(name, list(shape), dt, kind="ExternalInput").ap()

    xs = din("xs", [8192, D])
    ident_d = din("ident", [128, 128])
    g_f = din("g_f", [1, D])
    gl = {n: din(n, [128, 8]) for n in ["g_x", "g_ffn", "g_mem"]}
    mem = din("mem", [256, D])
    wl = {
        "w_out": din("w_out", [128, 8, D]),
        "w_q": din("w_q", [128, 8, D]),
        "w_kv": din("w_kv", [128, 8, 2 * D]),
        "w_o": din("w_o", [128, 8, D]),
        "w_gate": din("w_gate", [128, 8, DFF]),
        "w_up": din("w_up", [128, 8, DFF]),
        "w_down": din("w_down", [128, 22, D]),
    }
    y = nc.dram_tensor("y", [NOWN * 128, D], F32, kind="ExternalOutput").ap()
    if "omix" in dbg:
        omix_d = din("omix", [NOWN * 128, D])
        dx1 = nc.dram_tensor("dx1", [NOWN * 128, D], F32, kind="ExternalOutput").ap()
        dx2 = nc.dram_tensor("dx2", [NOWN * 128, D], F32, kind="ExternalOutput").ap()
        dq = nc.dram_tensor("dq", [128, 1024], BF16, kind="ExternalOutput").ap()
        dk = nc.dram_tensor("dk", [128, 2048], BF16, kind="ExternalOutput").ap()
        dP = nc.dram_tensor("dP", [128, 256], BF16, kind="ExternalOutput").ap()
        dr = nc.dram_tensor("dr", [128, 128], F32, kind="ExternalOutput").ap()
        dh = nc.dram_tensor("dh", [128, 1024], BF16, kind="ExternalOutput").ap()
        dV = nc.dram_tensor("dV", [128, 2048], BF16, kind="ExternalOutput").ap()
        do = nc.dram_tensor("do", [128, 1024], BF16, kind="ExternalOutput").ap()

    ident = nc.alloc_sbuf_tensor("ident_b", [128, 128], BF16)
    identf = nc.alloc_sbuf_tensor("ident_f", [128, 128], F32)
    S.dma("sp", identf[:], ident_d[:, :], [], [("identf", None)])
    k.ve("dve", "tensor_copy", [("identf", None)], [("ident", None)], out=ident[:], in_=identf[:])
    ones_b = nc.alloc_sbuf_tensor("ones_b", [128, 128], BF16)
    k.ve("dve", "memset", [], [("ones_b", None)], ones_b[:], 1.0)
    epsc = nc.alloc_sbuf_tensor("epsc", [128, 1], F32)
    k.ve("dve", "memset", [], [("epsc", None)], epsc[:], RMS_EPS)
    ones_f = nc.alloc_sbuf_tensor("ones_f", [1, 128], F32)
    k.ve("dve", "memset", [], [("ones_f", None)], ones_f[:], 1.0)

    pT = nc.alloc_psum_tensor("pT", [128, 1024], BF16)
    pA = nc.alloc_psum_tensor("pA", [128, 512], F32)
    pB = nc.alloc_psum_tensor("pB", [128, 512], F32)
    pC = nc.alloc_psum_tensor("pC", [128, 512], F32)
    pD = nc.alloc_psum_tensor("pD", [128, 512], F32)
    pE = nc.alloc_psum_tensor("pE", [128, 512], F32)
    pF = nc.alloc_psum_tensor("pF", [128, 512], F32)
    pG = nc.alloc_psum_tensor("pG", [128, 512], F32)

    stg = [None, None]

    def alloc_stg(es, tag, side=None):
        for i in range(2):
            kw = {"side": side} if side else {}
            stg[i] = es.enter_context(nc.sbuf_tensor("stg%d%s" % (i, tag), [128, 1024], F32, **kw))
    stg_i = [0]

    def load_w(dst, dname, src, KC, ncols, gcol=None, gname=None, c_lo=0, kc_lo=0):
        for kc in range(KC):
            for c0 in range(0, ncols, 1024):
                c1 = min(ncols, c0 + 1024)
                i = stg_i[0] % 2
                stg_i[0] += 1
                q = "sp" if i == 0 else "pool"
                S.dma(q, stg[i][:, 0:c1 - c0], src[:, kc_lo + kc, c_lo + c0:c_lo + c1], [], [("stg%d" % i, None)])
                e = "dve" if i == 0 else "act"
                if gcol is not None:
                    if e == "dve":
                        k.ve("dve", "tensor_scalar", [("stg%d" % i, None), (gname, None)], [(dname, kc)],
                             out=dst[:, kc, c0:c1], in0=stg[i][:, 0:c1 - c0], scalar1=gcol[:, kc_lo + kc:kc_lo + kc + 1],
                             scalar2=None, op0=ALU.mult)
                    else:
                        k.act(dst[:, kc, c0:c1], stg[i][:, 0:c1 - c0], AF.Copy, [("stg%d" % i, None), (gname, None)],
                              [(dname, kc)], scale=gcol[:, kc_lo + kc:kc_lo + kc + 1])
                else:
                    if e == "dve":
                        k.ve("dve", "tensor_copy", [("stg%d" % i, None)], [(dname, kc)], out=dst[:, kc, c0:c1],
                             in_=stg[i][:, 0:c1 - c0])
                    else:
                        k.act(dst[:, kc, c0:c1], stg[i][:, 0:c1 - c0], AF.Copy, [("stg%d" % i, None)], [(dname, kc)])

    def rmsnorm(src, sname, dst, dname, sq, ss, tag):
        nc_ = nc
        S.op("act", lambda: nc_.scalar.activation(out=sq[:], in_=src, func=AF.Square, accum_out=ss[:, 0:1]),
             [sname], [("sq" + tag, None), ("ss" + tag, None)])
        k.act(ss[:, 1:2], ss[:, 0:1], AF.Ln, [("ss" + tag, None)], [("ss" + tag, None)], bias=epsc[:, 0:1], scale=1.0 / D)
        k.act(ss[:, 1:2], ss[:, 1:2], AF.Exp, [("ss" + tag, None)], [("ss" + tag, None)], scale=-0.5)
        k.ve("dve", "tensor_scalar", [sname, ("ss" + tag, None)], [dname], out=dst, in0=src, scalar1=ss[:, 1:2],
             scalar2=None, op0=ALU.mult)

    def transpose8(src, sname, dstT, dname, nchunk=8):
        for c in range(nchunk):
            k.tr(pT[:, c * 128:(c + 1) * 128], src[:, c * 128:(c + 1) * 128], ident[:], [sname], [("pT", c)])
        k.ve("dve", "tensor_copy", [("pT", None)], [dname], out=dstT, in_=pT[:, 0:nchunk * 128])

    pJ0, pJ1, pM0, pM1, pS0, pS1, pS2 = pA, pB, pC, pD, pE, pF, pG
    epsl = nc.alloc_sbuf_tensor("epsl", [128, 1], F32)
    k.ve("dve", "memset", [], [("epsl", None)], epsl[:], 64e-5)
    sq2 = [nc.alloc_sbuf_tensor("sq0", [128, D], BF16), None]
    ss2 = [nc.alloc_sbuf_tensor("ss0", [128, 2], F32), None]
    hb2 = [nc.alloc_sbuf_tensor("hb0", [128, D], BF16), None]
    sq, ss, hb = sq2[0], ss2[0], hb2[0]
    eo = ExitStack()
    oT_r = eo.enter_context(nc.sbuf_tensor("oT_r", [128, NOWN, 4, 128], BF16, side="right"))
    LNX_EPS = 64e-5
    if "rwkv" in dbg or "nsa" in dbg:
        gmix_d = din("g_mix", [128, 8])
    if "rwkv" in dbg:
        w_in_r = din("w_in_r", [128, 8, 1792])
        mu_d = din("mu", [128, 14])
        rv_d = din("rvecs", [128, 5, 4])
        lora_up_d = din("lora_up", [128, 512])
        g_up_d = din("g_up", [128, 512])
        lnw_d = din("lnw", [128, 4, 64])
        lnb_d = din("lnb", [128, 4, 64])
        rmask_d = din("rmasks", [128, 4, 512])
        bones_d = din("blockones", [128, 128])
        er = ExitStack()
        GT = 256
        NG = 8192 // GT
        NCH = GT // 64
        NSLOT = 3

        def T(name, shape, dt):
            return er.enter_context(nc.sbuf_tensor("r_" + name, shape, dt))

        alloc_stg(er, "r")
        sq2[1] = T("sq1", [128, D], BF16)
        ss2[1] = T("ss1", [128, 2], F32)
        hb2[1] = T("hb1", [128, D], BF16)
        w_in_b = T("w_in_b", [128, 8, 1792], BF16)
        gmix = T("gmix", [128, 8], F32)
        mu = T("mu", [128, 14], F32)
        rv = T("rv", [128, 5, 4], F32)
        omk = T("omk", [128, 4], F32)
        lup_b = T("lup_b", [128, 512], BF16)
        gup_b = T("gup_b", [128, 512], BF16)
        lnw = T("lnw", [128, 4, 64], F32)
        lnb = T("lnb", [128, 4, 64], F32)
        RSTt = T("RSTt", [128, GT], F32)
        rmb = T("rmb", [128, 3, 512], BF16)
        bones = T("bones", [128, 128], F32)
        onesc = T("onesc", [128, 1], BF16)
        xnT1 = T("xnT0", [128, 8, GT], BF16)
        xnT = [xnT1, xnT1]
        xr1 = T("xr0", [128, D], F32)
        xr = [xr1, xr1]
        carry = T("carry", [128, 14], F32)
        RWl = T("RWl", [128, 2, GT], F32)
        Pstl = T("Pstl", [128, GT + 1], F32)
        LT = [T("LT%d" % i, [128, GT], BF16) for i in range(2)]
        SGD = [T("SGD%d" % i, [128, NCH, 2, 64], BF16) for i in range(2)]
        WS = []
        for i in range(2):
            WS.append(dict(
                i=i,
                Pst=T("Pst%d" % i, [128, GT + 1], F32),
                RW3=T("RW3_%d" % i, [128, 3, GT], F32),
                Wk=[T("Wk%d_%d" % (i, j), [128, GT], F32) for j in range(7)],
            ))
        BTm = [T("BT%d" % i, [128, 4, 4, NCH, 128], BF16) for i in range(2)]
        BTo = T("BTo", [128, 2, 4, 2, 128], BF16)
        GC = [T("GC%d" % i, [128, 4, NCH], F32) for i in range(2)]
        SL = []
        for i in range(NSLOT):
            SL.append(dict(
                i=i,
                TT=T("TT%d" % i, [128, 3, 512], BF16),
                X1=T("X1_%d" % i, [128, 2, 2, 2, 128], BF16),
                X2=T("X2_%d" % i, [128, 2, 2, 2, 128], BF16),
                X3=T("X3_%d" % i, [128, 512], BF16),
                Ak=[T("Ak%d_%d" % (i, j), [128, 512], BF16) for j in range(2)],
                Bk=[T("Bk%d_%d" % (i, j), [128, 512], BF16) for j in range(2)],
                Wt=[T("Wt%d_%d" % (i, j), [128, 512], BF16) for j in range(2)],
                Zt=[T("Zt%d_%d" % (i, j), [128, 512], BF16) for j in range(2)],
            ))
        Hf = T("Hf", [128, 512], F32)
        Hb = T("Hb", [128, 512], BF16)
        Hs = T("Hs", [128, 512], F32)
        RHSb = T("RHSb", [128, 512], BF16)
        Ub = T("Ub", [128, 512], BF16)
        Yc = T("Yc", [128, 4, 64], F32)
        Gc = T("Gc", [128, 4, 64], F32)
        V2c = T("V2c", [128, 4, 64], F32)
        Ycen = [T("Ycen%d" % i, [128, 64], F32) for i in range(4)]
        Ysq = [T("Ysq%d" % i, [128, 64], F32) for i in range(4)]
        st = [T("st%d" % i, [128, 8], F32) for i in range(4)]
        stb = T("stb", [128, 4], F32)
        Oblk = T("Oblk", [128, 4, 128], BF16)

        S.dma("sp", gmix[:], gmix_d[:, :], [], [("gmix", None)])
        S.dma("sp", mu[:], mu_d[:, :], [], [("mu", None)])
        S.dma("sp", rv[:], rv_d[:, :, :], [], [("rv", None)])
        S.dma("sp", stg[0][:, 0:512], lora_up_d[:, :], [], [("stg0", None)])
        k.ve("dve", "tensor_copy", [("stg0", None)], [("lup_b", None)], out=lup_b[:], in_=stg[0][:, 0:512])
        S.dma("sp", stg[1][:, 0:512], g_up_d[:, :], [], [("stg1", None)])
        k.ve("dve", "tensor_copy", [("stg1", None)], [("gup_b", None)], out=gup_b[:], in_=stg[1][:, 0:512])
        S.dma("sp", lnw[:], lnw_d[:, :, :], [], [("lnw", None)])
        S.dma("sp", lnb[:], lnb_d[:, :, :], [], [("lnb", None)])
        S.dma("sp", RSTt[:], rmask_d[:, 0, 0:GT], [], [("rmask", None)])
        for mi in range(3):
            S.dma("sp", stg[mi % 2][:, 0:512], rmask_d[:, 1 + mi, :], [], [("stg%d" % (mi % 2), None)])
            k.ve("dve", "tensor_copy", [("stg%d" % (mi % 2), None)], [("rmb", mi)], out=rmb[:, mi, :], in_=stg[mi % 2][:, 0:512])
        S.dma("pool", bones[:], bones_d[:, :], [], [("bones", None)])
        k.ve("dve", "memset", [], [("onesc", None)], onesc[:], 1.0)
        k.ve("dve", "memset", [], [("carry", None)], carry[:], 0.0)
        k.ve("dve", "memset", [], [("Hf", None)], Hf[:], 0.0)
        k.ve("dve", "memset", [], [("Hb", None)], Hb[:], 0.0)
        for par in range(2):
            for ty in range(4):
                k.ve("pool", "memset", [], [("BT%d" % par, None)], BTm[par][:, ty, :, :, :].rearrange("p b c d -> p (b c d)"), 0.0)
        k.ve("pool", "memset", [], [("BTo", None)], BTo[:].rearrange("p a b c d -> p (a b c d)"), 0.0)
        k.ve("pool", "memset", [], [("Oblk", None)], Oblk[:].rearrange("p a b -> p (a b)"), 0.0)
        k.ve("dve", "tensor_scalar", [("rv", None)], [("omk", None)], out=omk[:], in0=rv[:, 3, :], scalar1=-1.0,
             scalar2=1.0, op0=ALU.mult, op1=ALU.add)
        load_w(w_in_b, "w_in_b", w_in_r, 8, 1792, gcol=gmix, gname="gmix")

        MA = rmb[:, 0, :]
        ML4 = rmb[:, 1, :]
        I4 = rmb[:, 2, :]
        RST = RSTt[:, :]
        mm_i = [0]
        pj_i = [0]

        rbanks = [(pJ0, "pJ0"), (pJ1, "pJ1"), (pM0, "pM0"), (pM1, "pM1"), (pS0, "pS0"), (pS1, "pS1"), (pS2, "pS2")]

        def pM():
            mm_i[0] += 1
            return rbanks[mm_i[0] % 7]

        pJ = pM

        def v3(ap):
            return ap.rearrange("q (c t) -> q c t", c=NCH)

        def halves(e, name, r, w, out3, mk):
            for hh in range(2):
                lo, hi = hh * 64, hh * 64 + 64
                k.ve(e, name, r, w, out=out3[lo:hi, :, lo:hi], **mk(lo, hi))

        def inproj(par, ci, Pst, pstn, dst, dname):
            pp, pn = pJ()
            for kc in range(8):
                k.mm(pp[:, 0:GT], w_in_b[:, kc, ci * 128:(ci + 1) * 128], xnT[par][:, kc, :], kc == 0, kc == 7,
                     [("w_in_b", kc), ("xnT0", None)], [(pn, None)])
            k.act(Pst[:, 0:1], carry[:, ci:ci + 1], AF.Copy, [("carry", ci)], [(pstn, 0)])
            k.act(Pst[:, 1:GT + 1], pp[:, 0:GT], AF.Copy, [(pn, None)], [(pstn, 1)])
            k.act(carry[:, ci:ci + 1], pp[:, GT - 1:GT], AF.Copy, [(pn, None)], [("carry", ci)])
            k.ve("dve", "tensor_tensor", [(pstn, None)], [dname], out=dst, in0=Pst[:, 0:GT], in1=Pst[:, 1:GT + 1],
                 op=ALU.subtract)
            k.ve("dve", "scalar_tensor_tensor", [dname, (pstn, None), ("mu", None)], [dname], out=dst, in0=dst,
                 scalar=mu[:, ci:ci + 1], in1=Pst[:, 1:GT + 1], op0=ALU.mult, op1=ALU.add)

        def gen_load(g):
            par = g % 2
            for tl in range(GT // 128):
                pos = (GT // 128) * g + tl
                b = tl % 2
                S.dma("sp", xr[b][:], xs[pos * 128:(pos + 1) * 128, :], [], [("xr0", None)])
                rmsnorm(xr[b][:], ("xr0", None), hb2[b][:], ("hb%d" % b, None), sq2[b], ss2[b], str(b))
                yield
                for c in range(8):
                    k.tr(pT[:, c * 128:(c + 1) * 128], hb2[b][:, c * 128:(c + 1) * 128], ident[:], [("hb%d" % b, None)], [("pT", c)])
                k.ve("dve", "tensor_copy", [("pT", None)], [("xnT0", tl)], out=xnT[par][:, :, tl * 128:(tl + 1) * 128],
                     in_=pT[:, :].rearrange("p (c t) -> p c t", c=8))
                yield
            inproj(par, 12, Pstl, "Pstl", RWl[:, 0, :], ("RWl", 0))
            yield
            inproj(par, 13, Pstl, "Pstl", RWl[:, 1, :], ("RWl", 1))
            yield
            k.act(LT[par][0:64, :], RWl[0:64, 0, :], AF.Tanh, [("RWl", 0)], [("LT%d" % par, None)])
            k.act(LT[par][64:128, :], RWl[64:128, 0, :], AF.Copy, [("RWl", 0)], [("LT%d" % par, None)])
            for dup in range(2):
                k.act(SGD[par][:, :, dup, :], RWl[:, 1, :].rearrange("q (c t) -> q c t", c=NCH), AF.Sigmoid, [("RWl", 1)],
                      [("SGD%d" % par, None)])
            yield

        def gen_prep(g, p, ws):
            par = g % 2
            wi = ws["i"]
            Wk = ws["Wk"]
            RW3 = ws["RW3"]
            wn = lambda j: ("Wk%d_%d" % (wi, j), None)
            rn = lambda j: ("RW3_%d" % wi, j)
            btn = lambda ty: ("BT%d" % par, (ty, p))
            B_ = BTm[par]
            inproj(par, p, ws["Pst"], "Pst%d" % wi, RW3[:, 0, :], rn(0))
            yield
            inproj(par, 4 + p, ws["Pst"], "Pst%d" % wi, RW3[:, 1, :], rn(1))
            yield
            inproj(par, 8 + p, ws["Pst"], "Pst%d" % wi, RW3[:, 2, :], rn(2))
            yield
            k_ = RW3[:, 1, :]
            cs = slice(p * 128, (p + 1) * 128)
            pp, pn = pJ()
            k.mm(pp[:, 0:GT], lup_b[0:64, cs], LT[par][0:64, :], True, True, [("lup_b", None), ("LT%d" % par, None)], [(pn, None)])
            k.act(Wk[0][:], pp[:, 0:GT], AF.Sigmoid, [(pn, None), ("rv", None)], [wn(0)], bias=rv[:, 0, p:p + 1])
            yield
            k.ve("dve", "tensor_scalar", [wn(0)], [wn(0)], out=Wk[0][:], in0=Wk[0][:], scalar1=-0.6065306597126334,
                 scalar2=None, op0=ALU.mult)
            yield
            k.ve("dve", "tensor_tensor_scan", [wn(0), ("rmask", None)], [wn(1)], out=Wk[1][:], data0=RST, data1=Wk[0][:],
                 initial=0.0, op0=ALU.mult, op1=ALU.add)
            yield
            k.ve("pool", "tensor_tensor", [wn(0), wn(1)], [wn(0)], out=Wk[0][:], in0=Wk[1][:], in1=Wk[0][:], op=ALU.subtract)
            k.act(Wk[2][:], Wk[1][:], AF.Exp, [wn(1)], [wn(2)])
            k.act(Wk[3][:], Wk[1][:], AF.Exp, [wn(1)], [wn(3)], scale=-1.0)
            yield
            k.act(Wk[0][:], Wk[0][:], AF.Exp, [wn(0)], [wn(0)])
            k.ve("pool", "tensor_copy", [wn(2)], [("GC%d" % par, p)], out=GC[par][:, p, :],
                 in_=Wk[2][:].rearrange("q (c t) -> q c t", c=NCH)[:, :, 63])
            yield
            pp, pn = pJ()
            k.mm(pp[:, 0:GT], lup_b[64:128, cs], LT[par][64:128, :], True, True, [("lup_b", None), ("LT%d" % par, None)],
                 [(pn, None)])
            k.act(Wk[4][:], pp[:, 0:GT], AF.Sigmoid, [(pn, None), ("rv", None)], [wn(4)], bias=rv[:, 1, p:p + 1])
            k.ve("dve", "tensor_scalar", [rn(1), ("rv", None)], [wn(5)], out=Wk[5][:], in0=k_, scalar1=rv[:, 2, p:p + 1],
                 scalar2=None, op0=ALU.mult)
            yield
            k.ve("pool", "tensor_tensor", [wn(5)], [wn(6)], out=Wk[6][:], in0=Wk[5][:], in1=Wk[5][:], op=ALU.mult)
            yield
            pp, pn = pJ()
            k.mm(pp[:, 0:GT], bones[:, :], Wk[6][:], True, True, [("bones", None), wn(6)], [(pn, None)])
            k.ve("dve", "tensor_scalar", [(pn, None)], [wn(6)], out=Wk[6][:], in0=pp[:, 0:GT], scalar1=1e-24, scalar2=None,
                 op0=ALU.max)
            yield
            k.act(Wk[6][:], Wk[6][:], AF.Ln, [wn(6)], [wn(6)])
            yield
            k.act(Wk[6][:], Wk[6][:], AF.Exp, [wn(6)], [wn(6)], scale=-0.5)
            yield
            k.ve("dve", "tensor_tensor", [wn(5), wn(6)], [wn(5)], out=Wk[5][:], in0=Wk[5][:], in1=Wk[6][:], op=ALU.mult)
            k.ve("dve", "tensor_scalar", [wn(4), ("rv", None), ("omk", None)], [wn(6)], out=Wk[6][:], in0=Wk[4][:],
                 scalar1=rv[:, 3, p:p + 1], scalar2=omk[:, p:p + 1], op0=ALU.mult, op1=ALU.add)
            yield
            k.ve("pool", "tensor_tensor", [wn(6), rn(1)], [wn(6)], out=Wk[6][:], in0=Wk[6][:], in1=k_, op=ALU.mult)
            k.ve("pool", "tensor_tensor", [wn(5), wn(4)], [wn(4)], out=Wk[4][:], in0=Wk[5][:], in1=Wk[4][:], op=ALU.mult)
            yield
            if g % 2 == 1:
                halves("dve", "tensor_tensor", [rn(0), wn(2)], [("BTo", (0, p))], BTo[:, 0, p, :, :],
                       lambda lo, hi: dict(in0=v3(RW3[lo:hi, 0, :])[:, 2:4, :], in1=v3(Wk[2][lo:hi, :])[:, 2:4, :], op=ALU.mult))
            halves("dve", "scalar_tensor_tensor", [wn(5), wn(0)], [btn(1)], B_[:, 0, p, :, :],
                   lambda lo, hi: dict(in0=v3(Wk[5][lo:hi, :]), scalar=-1.0, in1=v3(Wk[0][lo:hi, :]), op0=ALU.mult,
                                       op1=ALU.mult))
            yield
            halves("pool", "tensor_tensor", [wn(4), wn(3)], [btn(2)], B_[:, 1, p, :, :],
                   lambda lo, hi: dict(in0=v3(Wk[4][lo:hi, :]), in1=v3(Wk[3][lo:hi, :]), op=ALU.mult))
            halves("pool", "tensor_tensor", [wn(6), wn(3)], [btn(3)], B_[:, 2, p, :, :],
                   lambda lo, hi: dict(in0=v3(Wk[6][lo:hi, :]), in1=v3(Wk[3][lo:hi, :]), op=ALU.mult))
            yield
            halves("pool", "tensor_copy", [rn(2)], [btn(4)], B_[:, 3, p, :, :],
                   lambda lo, hi: dict(in_=v3(RW3[lo:hi, 2, :])))
            if g % 2 == 1:
                halves("dve", "scalar_tensor_tensor", [rn(0), wn(6), ("rv", None)], [("BTo", (1, p))], BTo[:, 1, p, :, :],
                       lambda lo, hi: dict(in0=v3(RW3[lo:hi, 0, :])[:, 2:4, :], scalar=rv[lo:hi, 4, p:p + 1],
                                           in1=v3(Wk[6][lo:hi, :])[:, 2:4, :], op0=ALU.mult, op1=ALU.mult))
            yield

        def gen_tt(g, c, sl):
            par = g % 2
            B_ = BTm[par]
            si_ = sl["i"]
            btn = lambda ty, p: ("BT%d" % par, (ty, p))
            TT, X2 = sl["TT"], sl["X2"]
            own = (g % 2 == 1) and c >= 2
            for j, ty in enumerate((2, 3, 4)):
                pp, pn = pM()
                for p in range(4):
                    k.mm(pp[:, p * 128:(p + 1) * 128], B_[:, ty - 1, p, c, :], ident[:], True, True,
                         [btn(ty, p), ("ident", None)], [(pn, None)])
                if j >= 1:
                    k.act(TT[:, j, :], pp[:, :], AF.Copy, [(pn, None)], [("TT%d" % si_, j)])
                else:
                    k.ve("dve", "tensor_copy", [(pn, None)], [("TT%d" % si_, j)], out=TT[:, j, :], in_=pp[:, :])
                yield
            for j in range(2):
                pp, pn = pM()
                for a in range(2):
                    p = 2 * j + a
                    k.mm(pp[:, a * 256:a * 256 + 128], B_[:, 2, p, c, :], B_[:, 0, p, c, :], True, True, [btn(3, p), btn(1, p)],
                         [(pn, None)])
                    if own:
                        k.mm(pp[:, a * 256 + 128:a * 256 + 256], B_[:, 2, p, c, :], BTo[:, 0, p, c - 2, :], True, True,
                             [btn(3, p), ("BTo", (0, p))], [(pn, None)])
                if own:
                    k.ve("dve", "tensor_tensor", [(pn, None), ("rmb", None)], [("X2_%d" % si_, j)],
                         out=X2[:, j, :, :, :].rearrange("q a t s -> q (a t s)"), in0=pp[:, :], in1=MA, op=ALU.mult)
                else:
                    k.ve("dve", "tensor_tensor", [(pn, None), ("rmb", None)], [("X2_%d" % si_, j)],
                         out=X2[:, j, :, 0, :], in0=pp[:, :].rearrange("q (a t s) -> q a t s", a=2, t=2)[:, :, 0, :],
                         in1=MA.rearrange("q (a t s) -> q a t s", a=2, t=2)[:, :, 0, :], op=ALU.mult)
                yield

        def gen_pre(g, c, sl):
            par = g % 2
            B_ = BTm[par]
            si_ = sl["i"]
            btn = lambda ty, p: ("BT%d" % par, (ty, p))
            X1, X3 = sl["X1"], sl["X3"]
            Ak, Bk, Wt, Zt = sl["Ak"], sl["Bk"], sl["Wt"], sl["Zt"]
            own = (g % 2 == 1) and c >= 2
            n_ = lambda base, j=None: ("%s%d" % (base, si_) if j is None else "%s%d_%d" % (base, si_, j), None)
            pp, pn = pM()
            for p in range(4):
                k.mm(pp[:, p * 128:(p + 1) * 128], B_[:, 0, p, c, :], B_[:, 1, p, c, :], True, True, [btn(1, p), btn(2, p)],
                     [(pn, None)])
            k.ve("dve", "tensor_tensor", [(pn, None), ("rmb", None)], [("X3_%d" % si_, None)], out=X3[:], in0=pp[:, :], in1=ML4,
                 op=ALU.mult)
            for j in range(2):
                pp, pn = pM()
                for a in range(2):
                    p = 2 * j + a
                    k.mm(pp[:, a * 256:a * 256 + 128], B_[:, 1, p, c, :], B_[:, 0, p, c, :], True, True, [btn(2, p), btn(1, p)],
                         [(pn, None)])
                    if own:
                        k.mm(pp[:, a * 256 + 128:a * 256 + 256], B_[:, 1, p, c, :], BTo[:, 0, p, c - 2, :], True, True,
                             [btn(2, p), ("BTo", (0, p))], [(pn, None)])
                if own:
                    k.ve("dve", "tensor_tensor", [(pn, None), ("rmb", None)], [("X1_%d" % si_, j)],
                         out=X1[:, j, :, :, :].rearrange("q a t s -> q (a t s)"), in0=pp[:, :], in1=MA, op=ALU.mult)
                else:
                    k.ve("dve", "tensor_tensor", [(pn, None), ("rmb", None)], [("X1_%d" % si_, j)],
                         out=X1[:, j, :, 0, :], in0=pp[:, :].rearrange("q (a t s) -> q a t s", a=2, t=2)[:, :, 0, :],
                         in1=MA.rearrange("q (a t s) -> q a t s", a=2, t=2)[:, :, 0, :], op=ALU.mult)
            yield
            B1 = X1[:, :, :, 0, :]
            k.ve("dve", "tensor_tensor", [("X1_%d" % si_, None), ("rmb", None)], [n_("Wt", 0)],
                 out=Wt[0][:].rearrange("q (j a s) -> q j a s", j=2, a=2), in0=B1,
                 in1=I4.rearrange("q (j a s) -> q j a s", j=2, a=2), op=ALU.add)
            k.ve("dve", "tensor_tensor", [("X3_%d" % si_, None), ("rmb", None)], [n_("Zt", 0)], out=Zt[0][:], in0=X3[:], in1=I4,
                 op=ALU.add)

            def Bsl(lv, p):
                if lv == 0:
                    return X1[:, p // 2, p % 2, 0, :], ("X1_%d" % si_, None)
                return Bk[lv % 2][:, p * 128:(p + 1) * 128], n_("Bk", lv % 2)

            def Asl(lv, p):
                if lv == 0:
                    return X3[:, p * 128:(p + 1) * 128], ("X3_%d" % si_, None)
                return Ak[lv % 2][:, p * 128:(p + 1) * 128], n_("Ak", lv % 2)

            def squares(lv):
                ppB, pnB = pM()
                for p in range(4):
                    a_ap, a_n = Asl(lv - 1, p)
                    b_ap, b_n = Bsl(lv - 1, p)
                    k.mm(ppB[:, p * 128:(p + 1) * 128], a_ap, b_ap, True, True, [a_n, b_n], [(pnB, None)])
                k.act(Bk[lv % 2][:], ppB[:, :], AF.Copy, [(pnB, None)], [n_("Bk", lv % 2)])
                if lv < 5:
                    ppA, pnA = pM()
                    for p in range(4):
                        a_ap, a_n = Asl(lv - 1, p)
                        b_ap, b_n = Bsl(lv - 1, p)
                        k.mm(ppA[:, p * 128:(p + 1) * 128], b_ap, a_ap, True, True, [a_n, b_n], [(pnA, None)])
                    k.act(Ak[lv % 2][:], ppA[:, :], AF.Copy, [(pnA, None)], [n_("Ak", lv % 2)])

            def products(lv):
                wi, wo = (lv - 1) % 2, lv % 2
                ppW, pnW = pM()
                for p in range(4):
                    b_ap, b_n = Bsl(lv, p)
                    k.mm(ppW[:, p * 128:(p + 1) * 128], Zt[wi][:, p * 128:(p + 1) * 128], b_ap, True, True,
                         [n_("Zt", wi), b_n], [(pnW, None)])
                k.ve("dve", "tensor_tensor", [(pnW, None), n_("Wt", wi)], [n_("Wt", wo)], out=Wt[wo][:], in0=ppW[:, :],
                     in1=Wt[wi][:], op=ALU.add)
                if lv < 5:
                    ppZ, pnZ = pM()
                    for p in range(4):
                        a_ap, a_n = Asl(lv, p)
                        k.mm(ppZ[:, p * 128:(p + 1) * 128], Wt[wi][:, p * 128:(p + 1) * 128], a_ap, True, True,
                             [n_("Wt", wi), a_n], [(pnZ, None)])
                    k.ve("dve", "tensor_tensor", [(pnZ, None), n_("Zt", wi)], [n_("Zt", wo)], out=Zt[wo][:], in0=ppZ[:, :],
                         in1=Zt[wi][:], op=ALU.add)

            squares(1)
            yield
            for lv in range(1, 6):
                if lv < 5:
                    squares(lv + 1)
                products(lv)
                yield

        def gen_seq(g, c, sl):
            par = g % 2
            B_ = BTm[par]
            si_ = sl["i"]
            btn = lambda ty, p: ("BT%d" % par, (ty, p))
            TT, X1, X2 = sl["TT"], sl["X1"], sl["X2"]
            Wf, Wfn = sl["Wt"][1], ("Wt%d_1" % si_, None)
            ttn = lambda j: ("TT%d" % si_, j)
            x1n, x2n = ("X1_%d" % si_, None), ("X2_%d" % si_, None)
            BBT, KBT, VBT = TT[:, 0, :], TT[:, 1, :], TT[:, 2, :]
            own = (g % 2 == 1) and c >= 2
            it = g // 2
            b0, b0n = pM()
            for p in range(4):
                ps = slice(p * 128, (p + 1) * 128)
                k.mm(b0[:, ps], B_[:, 0, p, c, :], Hb[:, ps], True, False, [btn(1, p), ("Hb", None)], [(b0n, None)])
                k.mm(b0[:, ps], X2[:, p // 2, p % 2, 0, :], VBT[:, ps], False, True, [x2n, ttn(2)], [(b0n, None)])
            k.act(RHSb[:], b0[:, :], AF.Copy, [(b0n, None)], [("RHSb", None)])
            yield
            b1, b1n = pM()
            for p in range(4):
                ps = slice(p * 128, (p + 1) * 128)
                k.mm(b1[:, ps], Wf[:, ps], RHSb[:, ps], True, True, [Wfn, ("RHSb", None)], [(b1n, None)])
            k.act(Ub[:], b1[:, :], AF.Copy, [(b1n, None)], [("Ub", None)])
            yield
            if own:
                ch = c - 2
                pS2, pS2n = pM()
                for p in range(4):
                    ps = slice(p * 128, (p + 1) * 128)
                    k.mm(pS2[:, ps], BTo[:, 0, p, c - 2, :], Hb[:, ps], True, False, [("BTo", (0, p)), ("Hb", None)], [(pS2n, None)])
                    k.mm(pS2[:, ps], X1[:, p // 2, p % 2, 1, :], Ub[:, ps], False, False, [x1n, ("Ub", None)], [(pS2n, None)])
                    k.mm(pS2[:, ps], X2[:, p // 2, p % 2, 1, :], VBT[:, ps], False, True, [x2n, ttn(2)], [(pS2n, None)])
            b3, b3n = pM()
            for p in range(4):
                ps = slice(p * 128, (p + 1) * 128)
                k.mm(b3[:, ps], BBT[:, ps], Ub[:, ps], True, False, [ttn(0), ("Ub", None)], [(b3n, None)])
                k.mm(b3[:, ps], KBT[:, ps], VBT[:, ps], False, True, [ttn(1), ttn(2)], [(b3n, None)])
            k.ve("dve", "tensor_tensor", [(b3n, None), ("Hf", None)], [("Hs", None)], out=Hs[:], in0=b3[:, :], in1=Hf[:],
                 op=ALU.add)
            yield
            for p in range(4):
                ps = slice(p * 128, (p + 1) * 128)
                k.act(Hf[:, ps], Hs[:, ps], AF.Copy, [("Hs", None), ("GC%d" % par, p)], [("Hf", None)], scale=GC[par][:, p, c:c + 1])
            k.ve("dve", "tensor_copy", [("Hf", None)], [("Hb", None)], out=Hb[:], in_=Hf[:])
            yield
            if own:
                for hh in range(2):
                    lo, hi = hh * 64, hh * 64 + 64
                    k.ve("dve", "tensor_copy", [(pS2n, None)], [("Yc", None)], out=Yc[lo:hi, :, :],
                         in_=pS2[lo:hi, :].rearrange("q (p s) -> q p s", p=4)[:, :, lo:hi])
                    k.ve("pool", "tensor_copy", [ttn(2)], [("V2c", None)], out=V2c[lo:hi, :, :],
                         in_=TT[lo:hi, 2, :].rearrange("q (p s) -> q p s", p=4)[:, :, lo:hi])
                yield
                pp, pn = pJ()
                for p in range(4):
                    k.mm(pp[:, p * 128:(p + 1) * 128], SGD[par][:, c, :, :].rearrange("q a t -> q (a t)"),
                         gup_b[:, p * 128:(p + 1) * 128], True, True, [("SGD%d" % par, None), ("gup_b", None)], [(pn, None)])
                for hh in range(2):
                    lo, hi = hh * 64, hh * 64 + 64
                    k.act(Gc[lo:hi, :, :], pp[lo:hi, :].rearrange("q (p s) -> q p s", p=4)[:, :, lo:hi], AF.Copy,
                          [(pn, None)], [("Gc", None)])
                yield
                pp, pn = pJ()
                for p in range(4):
                    k.mm(pp[:, p:p + 1], BTo[:, 1, p, c - 2, :], onesc[:, 0:1], True, True, [("BTo", (1, p)), ("onesc", None)], [(pn, None)])
                k.ve("dve", "tensor_copy", [(pn, None)], [("stb", None)], out=stb[:, 0:4], in_=pp[:, 0:4])
                yield
                steps = []
                for p in range(4):
                    yc = Yc[:, p, :]
                    sn, cn, qn = ("st%d" % p, None), ("Ycen%d" % p, None), ("Ysq%d" % p, None)
                    s_, c_, q_ = st[p], Ycen[p], Ysq[p]
                    steps.append([
                        lambda yc=yc, s_=s_, sn=sn: k.ve("dve", "reduce_sum", [("Yc", None)], [sn], out=s_[:, 0:1], in_=yc, axis=AX.X),
                        lambda s_=s_, sn=sn: k.ve("dve", "tensor_scalar", [sn], [sn], out=s_[:, 0:1], in0=s_[:, 0:1],
                                                  scalar1=-1.0 / 64, scalar2=None, op0=ALU.mult),
                        lambda yc=yc, s_=s_, sn=sn, c_=c_, cn=cn: k.ve("dve", "tensor_scalar", [("Yc", None), sn], [cn], out=c_[:],
                                                                      in0=yc, scalar1=s_[:, 0:1], scalar2=None, op0=ALU.add),
                        lambda c_=c_, cn=cn, q_=q_, qn=qn: k.ve("pool", "tensor_tensor", [cn], [qn], out=q_[:], in0=c_[:], in1=c_[:],
                                                              op=ALU.mult),
                        lambda q_=q_, qn=qn, s_=s_, sn=sn: k.ve("dve", "reduce_sum", [qn], [sn], out=s_[:, 1:2], in_=q_[:], axis=AX.X),
                        lambda s_=s_, sn=sn: k.act(s_[:, 2:3], s_[:, 1:2], AF.Ln, [sn, ("epsl", None)], [sn], bias=epsl[:, 0:1],
                                                   scale=1.0 / 64),
                        lambda s_=s_, sn=sn: k.act(s_[:, 2:3], s_[:, 2:3], AF.Exp, [sn], [sn], scale=-0.5),
                        lambda s_=s_, sn=sn, c_=c_, cn=cn: k.ve("dve", "tensor_scalar", [cn, sn], [cn], out=c_[:], in0=c_[:],
                                                              scalar1=s_[:, 2:3], scalar2=None, op0=ALU.mult),
                        lambda c_=c_, cn=cn, p=p: k.ve("pool", "tensor_tensor", [cn, ("lnw", None)], [cn], out=c_[:], in0=c_[:],
                                                      in1=lnw[:, p, :], op=ALU.mult),
                        lambda c_=c_, cn=cn, p=p: k.ve("pool", "tensor_tensor", [cn, ("lnb", None)], [cn], out=c_[:], in0=c_[:],
                                                      in1=lnb[:, p, :], op=ALU.add),
                        lambda c_=c_, cn=cn, p=p: k.ve("dve", "scalar_tensor_tensor", [("V2c", None), ("stb", None), cn], [cn],
                                                      out=c_[:], in0=V2c[:, p, :], scalar=stb[:, p:p + 1], in1=c_[:],
                                                      op0=ALU.mult, op1=ALU.add),
                    ])
                for si2 in range(len(steps[0])):
                    for p in range(4):
                        steps[p][si2]()
                    yield
                for p in range(4):
                    for hh in range(2):
                        lo, hi = hh * 64, hh * 64 + 64
                        k.ve("dve" if hh == 0 else "pool", "tensor_tensor", [("Ycen%d" % p, None), ("Gc", None)], [("Oblk", None)],
                             out=Oblk[lo:hi, p, lo:hi], in0=Ycen[p][lo:hi, :], in1=Gc[lo:hi, p, :], op=ALU.mult)
                yield
                for p in range(4):
                    k.tr(pT[:, p * 128:(p + 1) * 128], Oblk[:, p, :], ident[:], [("Oblk", None)], [("pT", p)])
                for hh in range(2):
                    lo, hi = hh * 64, hh * 64 + 64
                    k.ve("dve", "tensor_copy", [("pT", None)], [("oT_r", it)], out=oT_r[lo:hi, it, :, ch * 64:(ch + 1) * 64],
                         in_=pT[lo:hi, 0:512].rearrange("q (p s) -> q p s", p=4)[:, :, lo:hi])
                yield

        NCC = NG * NCH
        done = {"load": -1, "prep": -1, "pre": -1, "seq": -1}
        active = []
        nxt = {"load": 0, "prepg": 0, "prepp": 0, "pre": 0, "seq": 0}
        prep_done_pairs = {}

        def admit():
            if nxt["load"] < NG and not any(a[0] == "load" for a in active) and nxt["load"] <= done["prep"] + 1 \
                    and done["seq"] >= (nxt["load"] - 1) * NCH - 1:
                g = nxt["load"]
                active.append(("load", g, gen_load(g)))
                nxt["load"] += 1
            while nxt["prepg"] < NG:
                g = nxt["prepg"]
                if done["load"] < g or done["seq"] < (g - 1) * NCH - 1:
                    break
                inflight = [a for a in active if a[0] == "prep"]
                if len(inflight) >= 2:
                    break
                used = {a[3] for a in inflight}
                wsi = 0 if 0 not in used else 1
                p = nxt["prepp"]
                active.append(("prep", (g, p), gen_prep(g, p, WS[wsi]), wsi))
                nxt["prepp"] += 1
                if nxt["prepp"] == 4:
                    nxt["prepp"] = 0
                    nxt["prepg"] += 1
            while nxt["pre"] < NCC:
                cc = nxt["pre"]
                if done["prep"] < cc // NCH:
                    break
                if cc - (done["seq"] + 1) >= NSLOT:
                    break
                if len([a for a in active if a[0] == "pre"]) >= 2:
                    break
                active.append(("pre", cc, gen_pre(cc // NCH, cc % NCH, SL[cc % NSLOT])))
                active.append(("tt", cc, gen_tt(cc // NCH, cc % NCH, SL[cc % NSLOT])))
                nxt["pre"] += 1
            if nxt["seq"] < NCC and not any(a[0] == "seq" for a in active) and done["pre"] >= nxt["seq"]:
                cc = nxt["seq"]
                active.append(("seq", cc, gen_seq(cc // NCH, cc % NCH, SL[cc % NSLOT])))
                nxt["seq"] += 1

        pre_done_set = set()
        pre_cnt = {}
        while True:
            admit()
            if not active:
                break
            for a in list(active):
                try:
                    next(a[2])
                except StopIteration:
                    active.remove(a)
                    kind = a[0]
                    if kind == "load":
                        done["load"] = a[1]
                    elif kind == "prep":
                        gq, pq = a[1]
                        prep_done_pairs[gq] = prep_done_pairs.get(gq, 0) + 1
                        if prep_done_pairs[gq] == 4:
                            done["prep"] = gq
                    elif kind in ("pre", "tt"):
                        pre_cnt[a[1]] = pre_cnt.get(a[1], 0) + 1
                        if pre_cnt[a[1]] == 2:
                            pre_done_set.add(a[1])
                        while done["pre"] + 1 in pre_done_set:
                            done["pre"] += 1
                    else:
                        done["seq"] = a[1]
        assert done["seq"] == NCC - 1, done
        if "orw" in dbg:
            dorw = nc.dram_tensor("dorw", [128, NOWN * 4 * 128], BF16, kind="ExternalOutput").ap()
            S.dma("sp", dorw[:, :], oT_r[:].rearrange("q a b c -> q (a b c)"), [("oT_r", None)], [])
        S.fence()
        er.close()
        S.fence()

    oT_n = eo.enter_context(nc.sbuf_tensor("oT_n", [128, NOWN, 4, 128], BF16, side="right"))
    if "nsa" in dbg:
        w_in_n = din("w_in_n", [128, 8, 1304])
        gateb_d = din("gate_b", [128, 24])
        realtok_d = din("realtok", [128, 64])
        w1_d = {"k": din("cmp_w1k", [128, 32, 256]), "v": din("cmp_w1v", [128, 32, 256])}
        peT_d = {"k": din("cmp_peTk", [128, 32]), "v": din("cmp_peTv", [128, 32])}
        b1_d = {"k": din("cmp_b1k", [128, 2]), "v": din("cmp_b1v", [128, 2])}
        w2k_d = din("cmp_w2k", [128, 2, 128])
        w2v_d = din("cmp_w2v", [128, 2, 64])
        ovlm_d = din("ovlm", [128, 4, 129])
        relb_d = din("relb_rep", [32, 8, 128])
        OHa_d = din("OHa", [32, 2048])
        OHw_d = din("OHw", [32, 256])
        OHc_d = din("OHc", [32, 6144])
        Ebig_d = din("Ebig", [128, 64, 128])
        addm_d = din("addmask", [128, 16, 128])
        scr_a = nc.dram_tensor("scr_a", [8, 128, 2048], BF16)
        scr_w = nc.dram_tensor("scr_w", [8, 128, 256], BF16)
        scr_c = nc.dram_tensor("scr_c", [8, 128, 6144], BF16)
        en = ExitStack()
        ens = ExitStack()
        alloc_stg(ens, "n", side="right")

        def TN(name, shape, dt):
            return en.enter_context(nc.sbuf_tensor("n_" + name, shape, dt))

        kselT = TN("kselT", [128, 8192], BF16)
        kwinT = TN("kwinT", [128, 8192], BF16)
        Vs = TN("Vs", [128, 64, 2, 65], BF16)
        Vw = TN("Vw", [128, 64, 2, 65], BF16)
        QT = TN("QT", [128, NOWN, 4, 128], BF16)
        GATES = TN("GATES", [128, NOWN, 24], F32)
        KcT = TN("KcT", [128, 512], BF16)
        Vca = TN("Vca", [128, 4, 2, 65], BF16)
        Ovl = TN("Ovl", [128, 4, 128], BF16)

        en1 = ExitStack()

        def TN1(name, shape, dt):
            return en1.enter_context(nc.sbuf_tensor("n_" + name, shape, dt))

        ovlm = TN1("ovlm", [128, 4, 129], F32)
        realtok = TN1("realtok", [128, 64], F32)
        gateb = TN1("gateb", [128, 24], F32)
        S.dma("sp", gateb[:], gateb_d[:, :], [], [("gateb", None)])
        S.dma("sp", realtok[:], realtok_d[:, :], [], [("realtok", None)])
        S.dma("sp", ovlm[:], ovlm_d[:, :, :], [], [("ovlm", None)])
        k.ve("pool", "tensor_copy", [("ovlm", None)], [("Ovl", None)], out=Ovl[:], in_=ovlm[:, :, 1:129])
        kcmpT = TN1("kcmpT", [128, 8192 + 32], BF16)
        vcmpT = TN1("vcmpT", [128, 8192 + 32], BF16)
        en1b = ExitStack()
        w_in_nb = en1b.enter_context(nc.sbuf_tensor("n_w_in_nb", [128, 8, 1304], BF16))
        gmixn = en1b.enter_context(nc.sbuf_tensor("n_gmixn", [128, 8], F32))
        xnT2 = [en1b.enter_context(nc.sbuf_tensor("n_xnT%d" % i, [128, 8, 512], BF16)) for i in range(2)]
        xr_n = en1b.enter_context(nc.sbuf_tensor("n_xr0", [128, D], F32))
        xr2 = [xr_n, xr_n]
        sqn = [sq, sq]
        ssn = [ss, ss]
        hbn = [hb, en1b.enter_context(nc.sbuf_tensor("n_hbx1", [128, D], BF16))]
        S.dma("sp", gmixn[:], gmix_d[:, :], [], [("gmixn", None)])
        load_w(w_in_nb, "w_in_nb", w_in_n, 8, 1304, gcol=gmixn, gname="gmixn")
        k.ve("pool", "memset", [], [("kcmpT", None)], kcmpT[:, 8192:8224], 0.0)
        k.ve("pool", "memset", [], [("vcmpT", None)], vcmpT[:, 8192:8224], 0.0)
        nbanks = [(pA, "pA"), (pB, "pB"), (pC, "pC"), (pD, "pD"), (pE, "pE"), (pF, "pF"), (pG, "pG")]
        nb_i = [0]

        def nbank():
            nb_i[0] += 1
            return nbanks[nb_i[0] % 7]

        def gen_n1(g):
            par = g % 2
            xnT = xnT2[par]
            xn_ = "n_xnT%d" % par
            for tl in range(4):
                pos = 4 * g + tl
                b2 = par
                S.dma("sp", xr2[b2][:], xs[pos * 128:(pos + 1) * 128, :], [], [("n_xr0", None)])
                rmsnorm(xr2[b2][:], ("n_xr0", None), hbn[b2][:], ("n_hb%d" % b2, None), sqn[b2], ssn[b2], "n")
                yield
                for c in range(8):
                    k.tr(pT[:, c * 128:(c + 1) * 128], hbn[b2][:, c * 128:(c + 1) * 128], ident[:], [("n_hb%d" % b2, None)], [("pT", c)])
                k.ve("dve", "tensor_copy", [("pT", None)], [(xn_, tl)], out=xnT[:, :, tl * 128:(tl + 1) * 128],
                     in_=pT[:, :].rearrange("p (c t) -> p c t", c=8))
                yield
            tok = slice(g * 512, (g + 1) * 512)
            for ci, (dst, dn) in enumerate(((kcmpT, "kcmpT"), (vcmpT, "vcmpT"), (kselT, "kselT"), (kwinT, "kwinT"))):
                pp, pn = nbank()
                for kc in range(8):
                    k.mm(pp[:, :], w_in_nb[:, kc, 512 + ci * 128:512 + (ci + 1) * 128], xnT[:, kc, :], kc == 0, kc == 7,
                         [("w_in_nb", kc), (xn_, None)], [(pn, None)])
                if ci % 2 == 0:
                    k.act(dst[:, tok], pp[:, :], AF.Copy, [(pn, None)], [(dn, g)])
                else:
                    k.ve("dve", "tensor_copy", [(pn, None)], [(dn, g)], out=dst[:, tok], in_=pp[:, :])
                yield
            for tl in range(4):
                pos = 4 * g + tl
                pp, pn = nbank()
                for kc in range(8):
                    k.mm(pp[:, 0:256], xnT[:, kc, tl * 128:(tl + 1) * 128], w_in_nb[:, kc, 1024:1280], kc == 0, kc == 7,
                         [("w_in_nb", kc), (xn_, None)], [(pn, None)])
                k.act(Vs[:, pos, :, 0:64], pp[:, 0:128].rearrange("p (a b) -> p a b", a=2), AF.Copy, [(pn, None)], [("Vs", pos)])
                k.ve("dve", "tensor_copy", [(pn, None)], [("Vw", pos)], out=Vw[:, pos, :, 0:64],
                     in_=pp[:, 128:256].rearrange("p (a b) -> p a b", a=2))
                for kv in range(2):
                    k.ve("pool", "tensor_copy", [("realtok", None)], [("Vs", pos)], out=Vs[:, pos, kv, 64:65],
                         in_=realtok[:, pos:pos + 1])
                    k.ve("pool", "tensor_copy", [("realtok", None)], [("Vw", pos)], out=Vw[:, pos, kv, 64:65],
                         in_=realtok[:, pos:pos + 1])
                yield
            for a in range(4):
                pp, pn = nbank()
                for kc in range(8):
                    k.mm(pp[:, 0:128], w_in_nb[:, kc, a * 128:(a + 1) * 128], xnT[:, kc, 384:512], kc == 0, kc == 7,
                         [("w_in_nb", kc), (xn_, None)], [(pn, None)])
                k.act(QT[:, g, a, :], pp[:, 0:128], AF.Copy, [(pn, None)], [("QT", g)])
                if a % 2 == 1:
                    yield
            pp, pn = nbank()
            for kc in range(8):
                k.mm(pp[:, 0:24], xnT[:, kc, 384:512], w_in_nb[:, kc, 1280:1304], kc == 0, kc == 7,
                     [("w_in_nb", kc), (xn_, None)], [(pn, None)])
            k.ve("dve", "tensor_tensor", [(pn, None), ("gateb", None)], [("GATES", g)], out=GATES[:, g, :], in0=pp[:, 0:24],
                 in1=gateb[:], op=ALU.add)
            k.act(GATES[:, g, :], GATES[:, g, :], AF.Sigmoid, [("GATES", g)], [("GATES", g)])
            yield

        actn = []
        nxtn = 0
        while True:
            while nxtn < 16 and len(actn) < 2:
                actn.append((nxtn, gen_n1(nxtn)))
                nxtn += 1
            if not actn:
                break
            for a in list(actn):
                try:
                    next(a[1])
                except StopIteration:
                    actn.remove(a)

        S.fence()
        en1b.close()
        S.fence()
        with nc.sbuf_tensor("n_w1b", [128, 32, 256], BF16) as w1b, \
                nc.sbuf_tensor("n_peT", [128, 32], F32) as peTf, \
                nc.sbuf_tensor("n_peTb", [128, 32], BF16) as peTb, \
                nc.sbuf_tensor("n_b1", [128, 2], F32) as b1t, \
                nc.sbuf_tensor("n_w2f", [128, 2, 128], F32) as w2f, \
                nc.sbuf_tensor("n_w2b", [128, 2, 128], BF16) as w2b, \
                nc.sbuf_tensor("n_hb1", [128, 2], F32) as hbias, \
                nc.sbuf_tensor("n_z", [128, 512], F32) as zt, \
                nc.sbuf_tensor("n_z2", [128, 512], F32) as z2t, \
                nc.sbuf_tensor("n_ge", [128, 2, 512], BF16) as ge:
            for X, srcT, sn in (("k", kcmpT, "kcmpT"), ("v", vcmpT, "vcmpT")):
                for part in range(8):
                    i = part % 2
                    S.dma("sp" if i == 0 else "pool", stg[i][:, :],
                          w1_d[X][:, part * 4:(part + 1) * 4, :].rearrange("p a b -> p (a b)"), [], [("stg%d" % i, None)])
                    k.ve("dve" if i == 0 else "pool", "tensor_copy", [("stg%d" % i, None)], [("w1b", part)],
                         out=w1b[:, part * 4:(part + 1) * 4, :].rearrange("p a b -> p (a b)"), in_=stg[i][:, :])
                S.dma("sp", peTf[:], peT_d[X][:, :], [], [("peTf", None)])
                k.ve("dve", "tensor_copy", [("peTf", None)], [("peTb", None)], out=peTb[:], in_=peTf[:])
                S.dma("sp", b1t[:], b1_d[X][:, :], [], [("b1t", None)])
                if X == "k":
                    S.dma("sp", w2f[:], w2k_d[:, :, :], [], [("w2f", None)])
                    k.ve("dve", "tensor_copy", [("w2f", None)], [("w2b", None)], out=w2b[:], in_=w2f[:])
                else:
                    S.dma("sp", w2f[:, :, 0:64], w2v_d[:, :, :], [], [("w2f", None)])
                    k.ve("dve", "tensor_copy", [("w2f", None)], [("w2b", None)], out=w2b[:, :, 0:64], in_=w2f[:, :, 0:64])
                for hc in range(2):
                    for l in range(32):
                        k.mm(pD[:, hc:hc + 1], w1b[0:64, l, hc * 128:(hc + 1) * 128], peTb[0:64, l:l + 1], l == 0, l == 31,
                             [("w1b", None), ("peTb", None)], [("pD", None)])
                k.ve("dve", "tensor_tensor", [("pD", None), ("b1t", None)], [("hbias", None)], out=hbias[:], in0=pD[:, 0:2],
                     in1=b1t[:], op=ALU.add)
                for kv in range(2):
                    rows = slice(kv * 64, kv * 64 + 64)
                    for hc in range(2):
                        pp, pn = (pA, "pA") if hc == 0 else (pB, "pB")
                        for l in range(32):
                            rhs = srcT[rows, l:l + 8192].rearrange("q (c s) -> q c s", s=16)[:, :, 0]
                            k.mm(pp[:, :], w1b[rows, l, hc * 128:(hc + 1) * 128], rhs, l == 0, l == 31,
                                 [("w1b", None), (sn, None)], [(pn, None)])
                        k.act(zt[:], pp[:, :], AF.Identity, [(pn, None), ("hbias", None)], [("zt", None)], bias=hbias[:, hc:hc + 1])
                        k.ve("dve", "tensor_tensor", [("zt", None)], [("z2t", None)], out=z2t[:], in0=zt[:], in1=zt[:], op=ALU.mult)
                        k.ve("dve", "tensor_scalar", [("z2t", None)], [("z2t", None)], out=z2t[:], in0=z2t[:], scalar1=0.044715,
                             scalar2=1.0, op0=ALU.mult, op1=ALU.add)
                        k.ve("pool", "tensor_tensor", [("z2t", None), ("zt", None)], [("z2t", None)], out=z2t[:], in0=z2t[:],
                             in1=zt[:], op=ALU.mult)
                        k.act(z2t[:], z2t[:], AF.Sigmoid, [("z2t", None)], [("z2t", None)], scale=1.5957691216057308)
                        k.ve("dve", "tensor_tensor", [("z2t", None), ("zt", None)], [("ge", hc)], out=ge[:, hc, :], in0=z2t[:],
                             in1=zt[:], op=ALU.mult)
                    if X == "k":
                        for hc in range(2):
                            k.mm(pC[:, :], w2b[:, hc, :], ge[:, hc, :], hc == 0, hc == 1, [("w2b", None), ("ge", None)],
                                 [("pC", None)])
                        k.ve("dve", "tensor_copy", [("pC", None)], [("KcT", kv)], out=KcT[rows, :], in_=pC[rows, :])
                    else:
                        for ct in range(4):
                            for hc in range(2):
                                k.mm(pC[:, 0:64], ge[:, hc, ct * 128:(ct + 1) * 128], w2b[:, hc, 0:64], hc == 0, hc == 1,
                                     [("w2b", None), ("ge", None)], [("pC", None)])
                            k.ve("dve", "tensor_scalar", [("pC", None), ("ovlm", None)], [("Vca", None)],
                                 out=Vca[:, ct, kv, 0:64], in0=pC[:, 0:64], scalar1=ovlm[:, ct, 0:1], scalar2=None, op0=ALU.mult)
                            k.ve("pool", "tensor_copy", [("ovlm", None)], [("Vca", None)], out=Vca[:, ct, kv, 64:65],
                                 in_=ovlm[:, ct, 0:1])
        S.fence()
        en1.close()
        S.fence()

        MBs = TN("MBs", [128, 8, 14, 128], BF16)
        MBw4 = TN("MBw4", [128, 8, 128], BF16)
        MBc = TN("MBc", [128, 8, 8, 128], BF16)
        Ebig = TN("Ebig", [128, 64, 128], BF16)
        for c8 in range(8):
            i = c8 % 2
            S.dma("sp" if i == 0 else "pool", stg[i][:, :], Ebig_d[:, c8 * 8:(c8 + 1) * 8, :].rearrange("p a b -> p (a b)"),
                  [], [("stg%d" % i, None)])
            k.ve("dve" if i == 0 else "pool", "tensor_copy", [("stg%d" % i, None)], [("Ebig", c8)],
                 out=Ebig[:, c8 * 8:(c8 + 1) * 8, :].rearrange("p a b -> p (a b)"), in_=stg[i][:, :])
        with nc.sbuf_tensor("n_relb", [32, 8, 128], F32) as relb, \
                nc.sbuf_tensor("n_OH0", [32, 512], F32) as OH0, \
                nc.sbuf_tensor("n_OH1", [32, 512], F32) as OH1, \
                nc.sbuf_tensor("n_Fv0", [128, 512], BF16) as Fv0, \
                nc.sbuf_tensor("n_Fv1", [128, 512], BF16) as Fv1:
            S.dma("sp", relb[:], relb_d[:, :, :], [], [("relb", None)])
            k.act(relb[:].rearrange("p a b -> p (a b)"), relb[:].rearrange("p a b -> p (a b)"), AF.Exp, [("relb", None)],
                  [("relb", None)])
            oi = 0
            for (OHd, L, scr, sname) in ((OHa_d, 2048, scr_a, "scr_a"), (OHw_d, 256, scr_w, "scr_w"), (OHc_d, 6144, scr_c, "scr_c")):
                for h in range(8):
                    for c0 in range(0, L, 512):
                        c1 = min(L, c0 + 512)
                        oi += 1
                        OH, OHn = (OH0, "OH0") if oi % 2 else (OH1, "OH1")
                        Fv, Fvn = (Fv0, "Fv0") if oi % 2 else (Fv1, "Fv1")
                        pp, pn = (pA, "pA") if oi % 2 else (pB, "pB")
                        S.dma("sp", OH[:, 0:c1 - c0], OHd[:, c0:c1], [], [(OHn, None)])
                        k.mm(pp[:, 0:c1 - c0], relb[:, h, :], OH[:, 0:c1 - c0], True, True, [("relb", None), (OHn, None)],
                             [(pn, None)])
                        if oi % 2:
                            k.act(Fv[:, 0:c1 - c0], pp[:, 0:c1 - c0], AF.Copy, [(pn, None)], [(Fvn, None)])
                        else:
                            k.ve("dve", "tensor_copy", [(pn, None)], [(Fvn, None)], out=Fv[:, 0:c1 - c0], in_=pp[:, 0:c1 - c0])
                        S.dma("pool", scr.ap()[h, :, c0:c1], Fv[:, 0:c1 - c0], [(Fvn, None)], [(sname, h)])
            for h in range(8):
                src = bass.AP(scr_a, h * 128 * 2048 + 127, [[2047, 128], [128, 14], [1, 128]])
                S.dma("pool", MBs[:, h, :, :], src, [("scr_a", h)], [("MBs", h)])
                src = bass.AP(scr_w, h * 128 * 256 + 127, [[255, 128], [1, 128]])
                S.dma("pool", MBw4[:, h, :], src, [("scr_w", h)], [("MBw4", h)])
                src = bass.AP(scr_c, h * 128 * 6144 + 128 * 3 + 2033, [[6144 - 16, 128], [512, 8], [1, 128]])
                S.dma("pool", MBc[:, h, :, :], src, [("scr_c", h)], [("MBc", h)])
        S.fence()

        S.fence()
        ens.close()
        S.fence()
        en2 = ExitStack()

        def TN2(name, shape, dt):
            return en2.enter_context(nc.sbuf_tensor("n_" + name, shape, dt))

        Pt = [TN2("Pt%d" % i, [128, 4, 128], BF16) for i in range(4)]
        AB_ = []
        for i in range(2):
            AB_.append(dict(
                i=i,
                Eall=TN2("Eall%d" % i, [128, 4, 4, 128], BF16),
                score=TN2("score%d" % i, [128, 128], F32),
                scr2=TN2("scr2%d" % i, [128, 128], F32),
                mx8=TN2("mx8%d" % i, [128, 8], F32),
                thr=TN2("thr%d" % i, [128, 1], F32),
                selb=TN2("selb%d" % i, [128, 128], BF16),
                selbT=TN2("selbT%d" % i, [128, 4, 128], BF16),
                densA=TN2("densA%d" % i, [128, 4], F32),
                coefA=TN2("coefA%d" % i, [128, 4], F32),
                Ccon=TN2("Ccon%d" % i, [128, 4, 64], F32),
            ))
        addm2 = [TN2("addm%d" % i, [128, 128], F32) for i in range(2)]
        dens = TN2("dens", [128, 8], F32)
        coef = TN2("coef", [128, 8], F32)
        Oacc = TN2("Oacc", [128, 8, 64], F32)
        Ob = TN2("Ob", [128, 512], BF16)
        sbanks = ((pA, "pA"), (pB, "pB"), (pF, "pF"), (pG, "pG"))
        sb_i = [0]

        def sbank():
            sb_i[0] += 1
            return sbanks[sb_i[0] % 4], sb_i[0] % 4

        def gen_A(n):
            it, gk = n // 2, n % 2
            Q = 4 * it + 3
            B = AB_[n % 2]
            bi = B["i"]
            nm = lambda s: (s + str(bi), None)
            Eall, score, scr2, mx8, thr, selb, selbT = B["Eall"], B["score"], B["scr2"], B["mx8"], B["thr"], B["selb"], B["selbT"]
            densA, coefA, Ccon = B["densA"], B["coefA"], B["Ccon"]
            addm = addm2[it % 2]
            addn = ("addm%d" % (it % 2), None)
            if gk == 0:
                S.dma("sp", addm[:], addm_d[:, it, :], [], [addn])
            rows = slice(gk * 64, gk * 64 + 64)
            qrhs = QT[rows, it, :, :].rearrange("q a t -> q (a t)")
            hs = slice(4 * gk, 4 * gk + 4)
            nct = min(4, Q // 16 + 1)
            for ct in range(nct):
                pp, pn = (pC, "pC") if ct % 2 == 0 else (pD, "pD")
                k.mm(pp[:, :], KcT[rows, ct * 128:(ct + 1) * 128], qrhs, True, True, [("KcT", None), ("QT", it)], [(pn, None)])
                k.act(Eall[:, ct, :, :].rearrange("p a t -> p (a t)"), pp[:, :], AF.Exp, [(pn, None)], [("Eall%d" % bi, ct)],
                      scale=0.125)
                n_ = min((Q - 16 * ct - 3) // 4, 7)
                k.ve("pool", "tensor_tensor", [("Eall%d" % bi, ct), ("MBc", None)], [("Eall%d" % bi, ct)], out=Eall[:, ct, :, :],
                     in0=Eall[:, ct, :, :], in1=MBc[:, hs, n_, :], op=ALU.mult)
                yield
            k.ve("dve", "memset", [], [("pC", None)], pC[:, 0:260], 0.0)
            k.ve("dve", "memset", [], [("pD", None)], pD[:, :], 0.0)
            for a in range(4):
                for ct in range(nct):
                    k.mm(pC[:, a * 65:(a + 1) * 65], Eall[:, ct, a, :], Vca[:, ct, gk, :], False, ct == nct - 1,
                         [("Eall%d" % bi, ct), ("Vca", None)], [("pC", None)])
                for ct in range(nct):
                    k.mm(pD[:, a * 128:(a + 1) * 128], Eall[:, ct, a, :], Ovl[:, ct, :], False, ct == nct - 1,
                         [("Eall%d" % bi, ct), ("Ovl", None)], [("pD", None)])
            yield
            k.ve("dve", "tensor_scalar", [("pC", None)], [nm("densA")], out=densA[:, 0:4],
                 in0=pC[:, 0:260].rearrange("p (a d) -> p a d", a=4)[:, :, 64], scalar1=1e-30, scalar2=None, op0=ALU.max)
            yield
            k.ve("dve", "reciprocal", [nm("densA")], [nm("densA")], out=densA[:, 0:4], in_=densA[:, 0:4])
            k.ve("pool", "tensor_copy", [addn], [nm("score")], out=score[:], in_=addm[:])
            yield
            for a in range(4):
                k.ve("dve", "scalar_tensor_tensor", [("pD", None), nm("densA"), nm("score")], [nm("score")],
                     out=score[:], in0=pD[:, a * 128:(a + 1) * 128], scalar=densA[:, a:a + 1], in1=score[:], op0=ALU.mult,
                     op1=ALU.add)
                yield
            k.ve("dve", "max", [nm("score")], [nm("mx8")], out=mx8[:], in_=score[:])
            yield
            k.ve("dve", "match_replace", [nm("score"), nm("mx8")], [nm("scr2")], out=scr2[:],
                 in_to_replace=mx8[:], in_values=score[:], imm_value=-3.0e9)
            yield
            k.ve("dve", "max", [nm("scr2")], [nm("mx8")], out=mx8[:], in_=scr2[:])
            yield
            k.ve("dve", "tensor_reduce", [nm("mx8")], [nm("thr")], out=thr[:, 0:1], in_=mx8[:], axis=AX.X, op=ALU.min)
            yield
            k.ve("dve", "tensor_scalar", [nm("score"), nm("thr")], [nm("selb")], out=selb[:], in0=score[:],
                 scalar1=thr[:, 0:1], scalar2=-30000.0, op0=ALU.is_lt, op1=ALU.mult)
            yield
            for a in range(4):
                k.tr(pT[:, a * 128:(a + 1) * 128], selb[:], ident[:], [nm("selb")], [("pT", a)])
            k.ve("dve", "tensor_copy", [("pT", None)], [nm("selbT")], out=selbT[:].rearrange("p a t -> p (a t)"),
                 in_=pT[:, 0:512])
            yield
            k.ve("pool", "tensor_tensor", [nm("densA"), ("GATES", it)], [nm("coefA")], out=coefA[:, 0:4], in0=densA[:, 0:4],
                 in1=GATES[:, it, :].rearrange("p (h j) -> p h j", j=3)[:, hs, 0], op=ALU.mult)
            yield
            for a in range(4):
                k.ve("dve", "tensor_scalar", [("pC", None), nm("coefA")], [nm("Ccon")], out=Ccon[:, a, :],
                     in0=pC[:, a * 65:a * 65 + 64], scalar1=coefA[:, a:a + 1], scalar2=None, op0=ALU.mult)
            yield

        def gen_B(n):
            it, gk = n // 2, n % 2
            Q = 4 * it + 3
            B = AB_[n % 2]
            bi = B["i"]
            selbT, Ccon = B["selbT"], B["Ccon"]
            rows = slice(gk * 64, gk * 64 + 64)
            qrhs = QT[rows, it, :, :].rearrange("q a t -> q (a t)")
            hs = slice(4 * gk, 4 * gk + 4)
            for bq, br in enumerate((1, 0)):
                if br == 0:
                    kT, kTn, Vt, Vn, kts = kselT, "kselT", Vs, "Vs", list(range(0, Q + 1))
                else:
                    kT, kTn, Vt, Vn, kts = kwinT, "kwinT", Vw, "Vw", list(range(max(Q - 4, 0), Q + 1))
                k.ve("dve", "memset", [], [("pE", None)], pE[:, 0:260], 0.0)
                SK = 3
                nk = len(kts)
                slots = []
                for step in range(nk + SK):
                    if step < nk:
                        kt = kts[step]
                        m = Q - kt
                        (pp, pn), bidx = sbank()
                        slots.append(bidx)
                        k.mm(pp[:, :], kT[rows, kt * 128:(kt + 1) * 128], qrhs, True, br == 1, [(kTn, None), ("QT", it)],
                             [(pn, None)])
                        if br == 0:
                            k.mm(pp[:, :], Ebig[:, kt, :], selbT[:].rearrange("p a t -> p (a t)"), False, True,
                                 [("Ebig", None), ("selbT%d" % bi, None)], [(pn, None)])
                        P = Pt[bidx]
                        Pn = "Pt%d" % bidx
                        k.act(P[:].rearrange("p a t -> p (a t)"), pp[:, :], AF.Exp, [(pn, None)], [(Pn, None)], scale=0.125)
                        if br == 1 and m == 4:
                            mb = MBw4[:, hs, :]
                            mbn = ("MBw4", None)
                        else:
                            mb = MBs[:, hs, min(m, 13), :]
                            mbn = ("MBs", None)
                        k.ve("dve", "tensor_tensor", [(Pn, None), mbn], [(Pn, None)], out=P[:],
                             in0=P[:], in1=mb, op=ALU.mult)
                    if step >= SK:
                        kt = kts[step - SK]
                        bidx = slots[step - SK]
                        P = Pt[bidx]
                        Pn = "Pt%d" % bidx
                        for a in range(4):
                            k.mm(pE[:, a * 65:(a + 1) * 65], P[:, a, :], Vt[:, kt, gk, :], False, kt == kts[-1],
                                 [(Pn, None), (Vn, kt)], [("pE", None)])
                    yield
                c0 = 4 * bq
                k.ve("dve", "tensor_scalar", [("pE", None)], [("dens", None)], out=dens[:, c0:c0 + 4],
                     in0=pE[:, 0:260].rearrange("p (a d) -> p a d", a=4)[:, :, 64], scalar1=1e-30, scalar2=None, op0=ALU.max)
                k.ve("dve", "reciprocal", [("dens", None)], [("dens", None)], out=dens[:, c0:c0 + 4], in_=dens[:, c0:c0 + 4])
                k.ve("dve", "tensor_tensor", [("dens", None), ("GATES", it)], [("coef", None)], out=coef[:, c0:c0 + 4],
                     in0=dens[:, c0:c0 + 4], in1=GATES[:, it, :].rearrange("p (h j) -> p h j", j=3)[:, hs, 1 + br], op=ALU.mult)
                for a in range(4):
                    if bq == 0:
                        in1, in1n = Ccon[:, a, :], ("Ccon%d" % bi, None)
                    else:
                        in1, in1n = Oacc[:, 4 * gk + a, :], ("Oacc", None)
                    k.ve("dve", "scalar_tensor_tensor", [("pE", None), ("coef", None), in1n], [("Oacc", None)],
                         out=Oacc[:, 4 * gk + a, :], in0=pE[:, a * 65:a * 65 + 64], scalar=coef[:, c0 + a:c0 + a + 1],
                         in1=in1, op0=ALU.mult, op1=ALU.add)
                yield
            if gk == 1:
                k.ve("dve", "tensor_copy", [("Oacc", None)], [("Ob", None)], out=Ob[:], in_=Oacc[:].rearrange("p h d -> p (h d)"))
                for c in range(4):
                    k.tr(pT[:, c * 128:(c + 1) * 128], Ob[:, c * 128:(c + 1) * 128], ident[:], [("Ob", None)], [("pT", c)])
                k.ve("dve", "tensor_copy", [("pT", None)], [("oT_n", it)], out=oT_n[:, it, :, :].rearrange("p c t -> p (c t)"),
                     in_=pT[:, 0:512])
                yield

        NN = 2 * NOWN
        doneA, doneB = -1, -1
        nxtA, nxtB = 0, 0
        activeN = []
        while True:
            if nxtA < NN and not any(a[0] == "A" for a in activeN) and nxtA <= doneB + 2:
                activeN.append(("A", nxtA, gen_A(nxtA)))
                nxtA += 1
            if nxtB < NN and not any(a[0] == "B" for a in activeN) and doneA >= nxtB:
                activeN.append(("B", nxtB, gen_B(nxtB)))
                nxtB += 1
            if not activeN:
                break
            for a in list(activeN):
                try:
                    next(a[2])
                except StopIteration:
                    activeN.remove(a)
                    if a[0] == "A":
                        doneA = a[1]
                    else:
                        doneB = a[1]
        assert doneB == NN - 1
        if "onsa" in dbg:
            donsa = nc.dram_tensor("donsa", [128, NOWN * 4 * 128], BF16, kind="ExternalOutput").ap()
            S.dma("sp", donsa[:, :], oT_n[:].rearrange("q a b c -> q (a b c)"), [("oT_n", None)], [])
        S.fence()
        en2.close()
        en.close()
        S.fence()

    if "stop_r" in dbg:
        S.finish()
        return nc
    es_stgc = ExitStack()
    alloc_stg(es_stgc, "c")
    gx = nc.alloc_sbuf_tensor("gx", [128, 8], F32)
    gffn = nc.alloc_sbuf_tensor("gffn", [128, 8], F32)
    gmem = nc.alloc_sbuf_tensor("gmem", [128, 8], F32)
    S.dma("sp", gx[:], gl["g_x"][:, :], [], [("gx", None)])
    S.dma("sp", gffn[:], gl["g_ffn"][:, :], [], [("gffn", None)])
    S.dma("sp", gmem[:], gl["g_mem"][:, :], [], [("gmem", None)])

    gf1 = nc.alloc_sbuf_tensor("gf1", [1, D], F32)
    gfb = nc.alloc_sbuf_tensor("gfb", [128, D], F32)
    S.dma("sp", gf1[:], g_f[:, :], [], [("gf1", None)])
    for h in range(2):
        pp = pA if h == 0 else pB
        k.mm(pp[:, :], ones_f[0:1, :], gf1[0:1, h * 512:(h + 1) * 512], True, True,
             [("ones_f", None), ("gf1", None)], [("pA" if h == 0 else "pB", None)])
        k.ve("dve", "tensor_copy", [("pA" if h == 0 else "pB", None)], [("gfb", None)],
             out=gfb[:, h * 512:(h + 1) * 512], in_=pp[:, :])

    x2all = nc.alloc_sbuf_tensor("x2all", [128, NOWN, D], F32)
    h2T = nc.alloc_sbuf_tensor("h2T", [128, NOWN, 8, 128], BF16)
    hT = nc.alloc_sbuf_tensor("hT", [128, 8, 128], BF16)

    if "omix" not in dbg:
        with nc.sbuf_tensor("w_out_b0", [128, 8, D], BF16) as w_out_b0, \
                nc.sbuf_tensor("xt0", [128, D], F32) as xt0:
            load_w(w_out_b0, "w_out_b0", wl["w_out"], 8, D)
            for it in range(NOWN):
                pos = 4 * it + 3
                S.dma("sp", xt0[:], xs[pos * 128:(pos + 1) * 128, :], [], [("xt0", None)])
                for hf in range(2):
                    pp, pn = (pA, "pA") if hf == 0 else (pB, "pB")
                    for kc in range(8):
                        if kc < 4:
                            lh, ln = oT_n[:, it, kc, :], ("oT_n", it)
                        else:
                            lh, ln = oT_r[:, it, kc - 4, :], ("oT_r", it)
                        k.mm(pp[:, :], lh, w_out_b0[:, kc, hf * 512:(hf + 1) * 512], kc == 0, kc == 7,
                             [ln, ("w_out_b0", kc)], [(pn, None)])
                    k.ve("dve", "tensor_tensor", [(pn, None), ("xt0", None)], [("x2all", it)],
                         out=x2all[:, it, hf * 512:(hf + 1) * 512], in0=pp[:, :], in1=xt0[:, hf * 512:(hf + 1) * 512],
                         op=ALU.add)
        S.fence()
        eo.close()
        S.fence()
    es_c1 = ExitStack()
    kmT = es_c1.enter_context(nc.sbuf_tensor("kmT", [128, 8, 256], BF16))
    Vm = es_c1.enter_context(nc.sbuf_tensor("Vm", [128, 2, D], BF16))
    with nc.sbuf_tensor("w_kv_b", [128, 8, 2 * D], BF16) as w_kv_b, \
            nc.sbuf_tensor("memf", [128, D], F32) as memf, \
            nc.sbuf_tensor("memT", [128, 8, 256], BF16) as memT:
        load_w(w_kv_b, "w_kv_b", wl["w_kv"], 8, 2 * D, gcol=gmem, gname="gmem")
        for mt in range(2):
            S.dma("sp", memf[:], mem[mt * 128:(mt + 1) * 128, :], [], [("memf", None)])
            rmsnorm(memf[:], ("memf", None), hb[:], ("hb", None), sq, ss, "")
            for c in range(8):
                k.tr(pT[:, c * 128:(c + 1) * 128], hb[:, c * 128:(c + 1) * 128], ident[:], [("hb", None)], [("pT", c)])
            k.ve("dve", "tensor_copy", [("pT", None)], [("memT", mt)],
                 out=memT[:, :, mt * 128:(mt + 1) * 128], in_=pT[:, :].rearrange("p (c t) -> p c t", c=8))
        for c in range(8):
            for kc in range(8):
                k.mm(pA[:, 0:256], w_kv_b[:, kc, c * 128:(c + 1) * 128], memT[:, kc, :], kc == 0, kc == 7,
                     [("w_kv_b", kc), ("memT", None)], [("pA", None)])
            k.ve("dve", "tensor_copy", [("pA", None)], [("kmT", c)], out=kmT[:, c, :], in_=pA[:, 0:256])
        for mt in range(2):
            for hf in range(2):
                for kc in range(8):
                    k.mm(pA[:, :], memT[:, kc, mt * 128:(mt + 1) * 128],
                         w_kv_b[:, kc, D + hf * 512:D + (hf + 1) * 512], kc == 0, kc == 7,
                         [("w_kv_b", kc), ("memT", None)], [("pA", None)])
                k.ve("dve", "tensor_copy", [("pA", None)], [("Vm", mt)], out=Vm[:, mt, hf * 512:(hf + 1) * 512],
                     in_=pA[:, :])
    S.fence()
    with nc.sbuf_tensor("w_q_b", [128, 8, D], BF16) as w_q_b, \
            nc.sbuf_tensor("w_o_b", [128, 8, D], BF16) as w_o_b, \
            nc.sbuf_tensor("c_sq1", [128, D], BF16) as c_sq1, \
            nc.sbuf_tensor("c_ss1", [128, 2], F32) as c_ss1, \
            nc.sbuf_tensor("c_hb1", [128, D], BF16) as c_hb1, \
            nc.sbuf_tensor("c_hT1", [128, 8, 128], BF16) as c_hT1, \
            nc.sbuf_tensor("qT0", [128, 8, 128], BF16) as qT0, \
            nc.sbuf_tensor("qT1", [128, 8, 128], BF16) as qT1, \
            nc.sbuf_tensor("PT0", [128, 2, 128], BF16) as PT0, \
            nc.sbuf_tensor("PT1", [128, 2, 128], BF16) as PT1, \
            nc.sbuf_tensor("rden0", [128, 128], F32) as rden0, \
            nc.sbuf_tensor("rden1", [128, 128], F32) as rden1, \
            nc.sbuf_tensor("oT0", [128, 8, 128], BF16) as oT0, \
            nc.sbuf_tensor("oT1", [128, 8, 128], BF16) as oT1:
        load_w(w_q_b, "w_q_b", wl["w_q"], 8, D, gcol=gx, gname="gx")
        load_w(w_o_b, "w_o_b", wl["w_o"], 8, D)
        CS = [dict(i=0, sq=sq, ss=ss, hb=hb, hT=hT, qT=qT0, PT=PT0, rden=rden0, oT=oT0),
              dict(i=1, sq=c_sq1, ss=c_ss1, hb=c_hb1, hT=c_hT1, qT=qT1, PT=PT1, rden=rden1, oT=oT1)]
        cbanks = [(pA, "pA"), (pB, "pB"), (pC, "pC"), (pD, "pD"), (pE, "pE"), (pF, "pF"), (pG, "pG")]
        cb_i = [0]

        def cbank():
            cb_i[0] += 1
            return cbanks[cb_i[0] % 7]

        def c_norm(src, sname, B):
            bi = B["i"]
            S.op("act", lambda: nc.scalar.activation(out=B["sq"][:], in_=src, func=AF.Square, accum_out=B["ss"][:, 0:1]),
                 [sname], [("c_sq%d" % bi, None), ("c_ss%d" % bi, None)])
            k.act(B["ss"][:, 1:2], B["ss"][:, 0:1], AF.Ln, [("c_ss%d" % bi, None)], [("c_ss%d" % bi, None)], bias=epsc[:, 0:1],
                  scale=1.0 / D)
            k.act(B["ss"][:, 1:2], B["ss"][:, 1:2], AF.Exp, [("c_ss%d" % bi, None)], [("c_ss%d" % bi, None)], scale=-0.5)
            k.ve("dve", "tensor_scalar", [sname, ("c_ss%d" % bi, None)], [("c_hb%d" % bi, None)], out=B["hb"][:], in0=src,
                 scalar1=B["ss"][:, 1:2], scalar2=None, op0=ALU.mult)

        def gen_c1(it, B):
            bi = B["i"]
            hbn, hTn, qTn, PTn, rdn, oTn = [("c_%s%d" % (s, bi), None) for s in ("hb", "hT", "qT", "PT", "rden", "oT")]
            c_norm(x2all[:, it, :], ("x2all", it), B)
            yield
            for c in range(8):
                k.tr(pT[:, c * 128:(c + 1) * 128], B["hb"][:, c * 128:(c + 1) * 128], ident[:], [hbn], [("pT", c)])
            k.ve("dve", "tensor_copy", [("pT", None)], [hTn], out=B["hT"][:].rearrange("p c t -> p (c t)"), in_=pT[:, :])
            yield
            for c in range(8):
                pp, pn = cbank()
                for kc in range(8):
                    k.mm(pp[:, 0:128], w_q_b[:, kc, c * 128:(c + 1) * 128], B["hT"][:, kc, :], kc == 0, kc == 7,
                         [("w_q_b", kc), hTn], [(pn, None)])
                k.act(B["qT"][:, c, :], pp[:, 0:128], AF.Copy, [(pn, None)], [("c_qT%d" % bi, c)])
                if c % 2 == 1:
                    yield
            for h in range(4):
                pp, pn = cbank()
                for mc in range(2):
                    for dc in range(2):
                        k.mm(pp[:, mc * 128:(mc + 1) * 128], kmT[:, 2 * h + dc, mc * 128:(mc + 1) * 128],
                             B["qT"][:, 2 * h + dc, :], dc == 0, dc == 1, [("kmT", None), qTn], [(pn, None)])
                k.act(B["PT"][:].rearrange("p c t -> p (c t)"), pp[:, 0:256], AF.Exp, [(pn, None)], [PTn], scale=1.0 / 16.0)
                yield
                pp, pn = cbank()
                for mc in range(2):
                    k.mm(pp[:, 0:128], ones_b[:, :], B["PT"][:, mc, :], mc == 0, mc == 1, [("ones_b", None), PTn], [(pn, None)])
                k.ve("dve", "reciprocal", [(pn, None)], [rdn], out=B["rden"][:], in_=pp[:, 0:128])
                for dc in range(2):
                    pp, pn = cbank()
                    for mc in range(2):
                        k.mm(pp[:, 0:128], Vm[:, mc, h * 256 + dc * 128:h * 256 + (dc + 1) * 128], B["PT"][:, mc, :], mc == 0,
                             mc == 1, [("Vm", None), PTn], [(pn, None)])
                    k.ve("dve", "tensor_tensor", [(pn, None), rdn], [("c_oT%d" % bi, 2 * h + dc)], out=B["oT"][:, 2 * h + dc, :],
                         in0=pp[:, 0:128], in1=B["rden"][:], op=ALU.mult)
                yield
            for hf in range(2):
                pp, pn = cbank()
                for kc in range(8):
                    k.mm(pp[:, :], B["oT"][:, kc, :], w_o_b[:, kc, hf * 512:(hf + 1) * 512], kc == 0, kc == 7,
                         [oTn, ("w_o_b", kc)], [(pn, None)])
                k.ve("dve", "tensor_tensor", [(pn, None), ("x2all", it)], [("x2all", it)],
                     out=x2all[:, it, hf * 512:(hf + 1) * 512], in0=pp[:, :], in1=x2all[:, it, hf * 512:(hf + 1) * 512],
                     op=ALU.add)
                yield
            c_norm(x2all[:, it, :], ("x2all", it), B)
            yield
            for c in range(8):
                k.tr(pT[:, c * 128:(c + 1) * 128], B["hb"][:, c * 128:(c + 1) * 128], ident[:], [hbn], [("pT", c)])
            k.ve("dve", "tensor_copy", [("pT", None)], [("h2T", it)], out=h2T[:, it, :, :].rearrange("p c t -> p (c t)"),
                 in_=pT[:, :])
            yield

        actC = []
        nxtC = 0
        while True:
            while nxtC < NOWN and len(actC) < 2:
                used = {a[0] for a in actC}
                bsi = 0 if 0 not in used else 1
                actC.append((bsi, gen_c1(nxtC, CS[bsi])))
                nxtC += 1
            if not actC:
                break
            for a in list(actC):
                try:
                    next(a[1])
                except StopIteration:
                    actC.remove(a)

    es_c1.close()
    eo.close()
    S.fence()
    with nc.sbuf_tensor("wg_b", [128, 8, 1408], BF16) as wg_b, \
            nc.sbuf_tensor("wu_b", [128, 8, 1408], BF16) as wu_b, \
            nc.sbuf_tensor("wd_b", [128, 11, D], BF16) as wd_b, \
            nc.sbuf_tensor("sg", [128, 128], F32) as sg, \
            nc.sbuf_tensor("sg2", [128, 128], F32) as sg2, \
            nc.sbuf_tensor("aT", [128, 11, 128], BF16) as aT, \
            nc.sbuf_tensor("yo", [128, D], F32) as yo:
        for half in range(2):
            load_w(wg_b, "wg_b", wl["w_gate"], 8, 1408, gcol=gffn, gname="gffn", c_lo=half * 1408)
            load_w(wu_b, "wu_b", wl["w_up"], 8, 1408, gcol=gffn, gname="gffn", c_lo=half * 1408)
            load_w(wd_b, "wd_b", wl["w_down"], 11, D, kc_lo=half * 11)
            for it in range(NOWN):
                for fc in range(11):
                    pg, pgn = (pC, "pC") if fc % 2 == 0 else (pE, "pE")
                    pu, pun = (pD, "pD") if fc % 2 == 0 else (pF, "pF")
                    sgt, sgn = (sg, "sg") if fc % 2 == 0 else (sg2, "sg2")
                    for kc in range(8):
                        k.mm(pg[:, 0:128], wg_b[:, kc, fc * 128:(fc + 1) * 128], h2T[:, it, kc, :], kc == 0, kc == 7,
                             [("wg_b", kc), ("h2T", it)], [(pgn, None)])
                    for kc in range(8):
                        k.mm(pu[:, 0:128], wu_b[:, kc, fc * 128:(fc + 1) * 128], h2T[:, it, kc, :], kc == 0, kc == 7,
                             [("wu_b", kc), ("h2T", it)], [(pun, None)])
                    k.act(sgt[:], pg[:, 0:128], AF.Silu, [(pgn, None)], [(sgn, None)])
                    k.ve("dve", "tensor_tensor", [(sgn, None), (pun, None)], [("aT", fc)], out=aT[:, fc, :], in0=sgt[:],
                         in1=pu[:, 0:128], op=ALU.mult)
                for hf in range(2):
                    pp, pn = (pA, "pA") if hf == 0 else (pB, "pB")
                    for fc in range(11):
                        k.mm(pp[:, :], aT[:, fc, :], wd_b[:, fc, hf * 512:(hf + 1) * 512], fc == 0, fc == 10,
                             [("aT", None), ("wd_b", fc)], [(pn, None)])
                    k.ve("dve", "tensor_tensor", [(pn, None), ("x2all", it)], [("x2all", it)],
                         out=x2all[:, it, hf * 512:(hf + 1) * 512], in0=pp[:, :],
                         in1=x2all[:, it, hf * 512:(hf + 1) * 512], op=ALU.add)
                if half == 1:
                    S.op("act", lambda it=it: nc.scalar.activation(out=sq[:], in_=x2all[:, it, :], func=AF.Square,
                                                                   accum_out=ss[:, 0:1]),
                         [("x2all", it)], [("sq", None), ("ss", None)])
                    k.act(ss[:, 1:2], ss[:, 0:1], AF.Ln, [("ss", None)], [("ss", None)], bias=epsc[:, 0:1], scale=1.0 / D)
                    k.act(ss[:, 1:2], ss[:, 1:2], AF.Exp, [("ss", None)], [("ss", None)], scale=-0.5)
                    k.ve("dve", "tensor_scalar", [("x2all", it), ("ss", None)], [("yo", None)], out=yo[:],
                         in0=x2all[:, it, :], scalar1=ss[:, 1:2], scalar2=None, op0=ALU.mult)
                    k.ve("pool", "tensor_tensor", [("yo", None), ("gfb", None)], [("yo", None)], out=yo[:], in0=yo[:],
                         in1=gfb[:], op=ALU.mult)
                    S.dma("sp", y[it * 128:(it + 1) * 128, :], yo[:], [("yo", None)], [])
    S.finish()
    return nc


def _lay_w(w, kc):
    return np.ascontiguousarray(w.reshape(kc, 128, w.shape[1]).transpose(1, 0, 2))


def _lay_v(v):
    return np.ascontiguousarray(v.reshape(-1, 128).T)


NSA_COLS = 512 + 6 * 128 + 24


def _rwkv_consts():
    f = np.float32
    r = np.arange(128)
    t = np.arange(512)
    rst = np.broadcast_to((t % 64 != 0).astype(f), (128, 512))
    same = (r[:, None] // 64) == (r[None, :] // 64)
    mus = (same & ((r[:, None] % 64) < (r[None, :] % 64))).astype(f)
    mui = (same & ((r[:, None] % 64) <= (r[None, :] % 64))).astype(f)
    mls = (same & ((r[:, None] % 64) > (r[None, :] % 64))).astype(f)
    ma = np.concatenate([mus, mui, mus, mui], 1)
    ml4 = np.concatenate([mls] * 4, 1)
    i4 = np.concatenate([np.eye(128, dtype=f)] * 4, 1)
    return np.ascontiguousarray(np.stack([rst, ma, ml4, i4], 1)), same.astype(f)


def _rwkv_inputs(inp):
    f = np.float32
    g = lambda n: np.asarray(inp[n], f)[0]
    w_in = g("w_in")
    rm, bo = _rwkv_consts()
    lnw = g("rwkv_lnx_w").reshape(4, 2, 64)
    lnb = g("rwkv_lnx_b").reshape(4, 2, 64)
    rows_h = np.arange(128) // 64
    return {
        "w_in_r": _lay_w(w_in[:, NSA_COLS:NSA_COLS + 1792], 8),
        "g_mix": _lay_v(g("norm_mix_g")),
        "mu": _lay_v(g("rwkv_mu")),
        "rvecs": np.ascontiguousarray(np.stack([_lay_v(g("rwkv_w0")), _lay_v(g("rwkv_a0")), _lay_v(g("rwkv_k_k")),
                                                _lay_v(g("rwkv_k_a")), _lay_v(g("rwkv_r_k").reshape(-1))], 1)),
        "lora_up": np.ascontiguousarray(np.concatenate([g("rwkv_w_up"), g("rwkv_a_up")], 0)),
        "g_up": np.ascontiguousarray(g("rwkv_g_up")),
        "lnw": np.ascontiguousarray(lnw[:, rows_h, :].transpose(1, 0, 2)),
        "lnb": np.ascontiguousarray(lnb[:, rows_h, :].transpose(1, 0, 2)),
        "rmasks": rm,
        "blockones": bo,
    }


def _t5_bucket(n):
    n = np.asarray(n)
    nf = np.maximum(n, 1).astype(np.float32)
    large = 16 + (np.log(nf / np.float32(16)) / np.float32(math.log(2048 / 16)) * np.float32(16)).astype(np.int32)
    return np.where(n < 16, n, np.minimum(large, 31))


def _nsa_consts():
    f = np.float32
    def onehot(delta, valid):
        b = _t5_bucket(np.maximum(delta, 0))
        oh = np.zeros((32, delta.shape[0]), f)
        idx = np.nonzero(valid)[0]
        oh[b[idx], idx] = 1.0
        return oh
    u = np.arange(2048)
    OHa = onehot(u - 127, (u - 127) >= 0)
    u = np.arange(256)
    OHw = onehot(512 + u - 127, (u - 127) < 0)
    u = np.arange(6144)
    OHc = onehot(u - 2064, (u - 2064) >= 0)
    jj = np.arange(128)[:, None, None]
    kt = np.arange(64)[None, :, None]
    kk = np.arange(128)[None, None, :]
    Ebig = (jj == 2 * kt + kk // 64).astype(f)
    return OHa, OHw, OHc, np.ascontiguousarray(Ebig)


def _nsa_inputs(inp, j):
    f = np.float32
    g = lambda n: np.asarray(inp[n], f)[0]
    w_in = g("w_in")
    nd = (3 - j) * 128
    qperm = np.concatenate([np.r_[a * 64:(a + 1) * 64, (4 + a) * 64:(5 + a) * 64] for a in range(4)])
    cols = np.concatenate([qperm, np.arange(512, 640), np.arange(640, 768), np.arange(768, 896), np.arange(1024, 1152),
                           np.arange(896, 1024), np.arange(1152, 1280), np.arange(1280, 1304)])
    OHa, OHw, OHc, Ebig = _nsa_consts()
    def w1lay(w):
        a = w.reshape(32, 64, 256).transpose(1, 0, 2)
        return np.ascontiguousarray(np.concatenate([a, a], 0))
    def pelay(pe):
        a = pe.T
        return np.ascontiguousarray(np.concatenate([a, a], 0))
    w2k = g("cmp_k_w2").reshape(2, 128, 64).transpose(1, 0, 2)
    w2v = g("cmp_v_w2").reshape(2, 128, 64).transpose(1, 0, 2)
    tpos = np.arange(64)[None, :] * 128 + np.arange(128)[:, None]
    realtok = (tpos >= nd).astype(f)
    c = np.arange(4)[None, :] * 128 + np.arange(128)[:, None]
    realblk = ((16 * c >= nd) & (c <= 510)).astype(f)
    jb = np.arange(128)[None, None, :]
    ov = ((16 * c[:, :, None] <= 64 * jb + 63) & (16 * c[:, :, None] + 31 >= 64 * jb)).astype(f)
    ovlm = np.concatenate([realblk[:, :, None], ov * realblk[:, :, None]], 2)
    it = np.arange(16)[None, :, None]
    q = np.arange(128)[:, None, None]
    t = 128 * (4 * it + 3) + q
    cur = t // 64
    blk0 = nd // 64
    bad = (jb < blk0) | (64 * jb > t)
    forced = (jb == blk0) | (jb == cur) | (jb == cur - 1)
    addmask = np.where(bad, -1e9, np.where(forced, 1e4, 0.0)).astype(f)
    return {
        "w_in_n": _lay_w(np.ascontiguousarray(w_in[:, cols]), 8),
        "g_mix": _lay_v(g("norm_mix_g")),
        "gate_b": np.ascontiguousarray(np.broadcast_to(g("nsa_gate_b")[None, :], (128, 24))),
        "realtok": np.ascontiguousarray(realtok),
        "cmp_w1k": w1lay(g("cmp_k_w1")), "cmp_w1v": w1lay(g("cmp_v_w1")),
        "cmp_peTk": pelay(g("cmp_pe_k")), "cmp_peTv": pelay(g("cmp_pe_v")),
        "cmp_b1k": _lay_v(g("cmp_k_b1")), "cmp_b1v": _lay_v(g("cmp_v_b1")),
        "cmp_w2k": np.ascontiguousarray(np.concatenate([w2k, w2k], 2)),
        "cmp_w2v": np.ascontiguousarray(w2v),
        "ovlm": np.ascontiguousarray(ovlm.astype(f)),
        "relb_rep": np.ascontiguousarray(np.broadcast_to(np.asarray(inp["rel_bias"], f)[:, :, None], (32, 8, 128))),
        "OHa": OHa, "OHw": OHw, "OHc": OHc, "Ebig": Ebig,
        "addmask": np.ascontiguousarray(addmask),
    }


def _core_inputs(c, inp):
    b, j = c // 4, c % 4
    f = np.float32
    x = np.asarray(inp["x"], f)[b]
    nd = (3 - j) * 128
    xs = np.zeros((8192, D), f)
    xs[nd:] = x[: 8192 - nd]
    m = {
        "xs": xs,
        "ident": np.eye(128, dtype=f),
        "g_f": np.asarray(inp["norm_f_g"], f).reshape(1, D),
        "g_x": _lay_v(np.asarray(inp["norm_x_g"], f)[0]),
        "g_ffn": _lay_v(np.asarray(inp["norm_ffn_g"], f)[0]),
        "g_mem": _lay_v(np.asarray(inp["norm_mem_g"], f)[0]),
        "mem": np.ascontiguousarray(np.asarray(inp["mem"], f)[b]),
        "w_out": _lay_w(np.asarray(inp["w_out"], f)[0], 8),
        "w_q": _lay_w(np.asarray(inp["w_q_x"], f)[0], 8),
        "w_kv": _lay_w(np.asarray(inp["w_kv_x"], f)[0], 8),
        "w_o": _lay_w(np.asarray(inp["w_o_x"], f)[0], 8),
        "w_gate": _lay_w(np.asarray(inp["w_gate"], f)[0], 8),
        "w_up": _lay_w(np.asarray(inp["w_up"], f)[0], 8),
        "w_down": _lay_w(np.asarray(inp["w_down"], f)[0], 22),
    }
    return m


def kernel(**inp):
    nc = build_program(dbg={"rwkv": 1, "nsa": 1})
    rin = _rwkv_inputs(inp)
    in_maps = []
    for c in range(8):
        m = _core_inputs(c, inp)
        m.update(rin)
        m.update(_nsa_inputs(inp, c % 4))
        in_maps.append(m)
    res = run_bass_kernel_spmd(nc, in_maps, core_ids=list(range(8)))
    out = np.zeros((2, 8192, D), np.float32)
    for c in range(8):
        b, j = c // 4, c % 4
        yv = np.asarray(res.results[c]["y"]).reshape(NOWN, 128, D)
        out[b].reshape(64, 128, D)[j::4] = yv
    return out
```

```python
import math
import numpy as np
from contextlib import ExitStack
import concourse.bass as bass
import concourse.mybir as mybir
from concourse.bass_utils import run_bass_kernel_spmd

F32 = mybir.dt.float32
BF16 = mybir.dt.bfloat16
AF = mybir.ActivationFunctionType
ALU = mybir.AluOpType
AX = mybir.AxisListType

D = 1024
DFF = 2816
NOWN = 16
RMS_EPS = 1e-6


class Sync:
    EPOCH = 24000
    NDMA = 12

    def __init__(self, nc):
        self.nc = nc
        self.eng = {"pe": nc.tensor, "act": nc.scalar, "dve": nc.vector, "pool": nc.gpsimd, "sp": nc.sync}
        self.cnt = {e: 0 for e in self.eng}
        self.sems = {}
        self.known = {e: {} for e in self.eng}
        self.vc = {}
        self.dma_sems = {}
        self.dma_val = {}
        self.dma_rr = {}
        self.state = {}

    def _sem(self, e, ep):
        k = (e, ep)
        if k not in self.sems:
            self.sems[k] = self.nc.alloc_semaphore("s_%s_%d" % (e, ep))
        return self.sems[k]

    def _wait(self, e, tok):
        kn = self.known[e]
        if tok[0] == "c":
            _, f, n = tok
            if f == "pe" and e == "pe":
                return
            if kn.get(f, 0) >= n:
                return
            ep = (n - 1) // self.EPOCH
            self.eng[e].wait_ge(self._sem(f, ep), n - ep * self.EPOCH)
            kn[f] = n
        else:
            _, q, slot, v = tok
            key = ("d", q, slot)
            if kn.get(key, 0) >= v:
                return
            self.eng[e].wait_ge(self.dma_sems[(q, slot)], v)
            kn[key] = v
        snap = self.vc.get(tok)
        if snap:
            for k2, v2 in snap.items():
                if kn.get(k2, 0) < v2:
                    kn[k2] = v2

    @staticmethod
    def _rkey(tok):
        return tok[1] if tok[0] == "c" else (tok[1], tok[2])

    def _deps(self, reads, writes):
        deps = []
        for (name, sub) in reads:
            st = self.state.get(name)
            if st:
                for s2, ent in st.items():
                    if (sub is None or s2 is None or s2 == sub) and ent[0] is not None:
                        deps.append(ent[0])
        for (name, sub) in writes:
            st = self.state.get(name)
            if st:
                for s2, ent in st.items():
                    if sub is None or s2 is None or s2 == sub:
                        if ent[0] is not None:
                            deps.append(ent[0])
                        deps.extend(ent[1].values())
        return deps

    def _update(self, tok, reads, writes):
        rk = self._rkey(tok)
        for (name, sub) in reads:
            st = self.state.setdefault(name, {})
            ent = st.get(sub)
            if ent is None:
                ent = st[sub] = [None, {}]
            ent[1][rk] = tok
        for (name, sub) in writes:
            st = self.state.setdefault(name, {})
            if sub is None:
                st.clear()
            st[sub] = [tok, {}]

    PSUM = ("pA", "pB", "pC", "pD", "pE", "pF", "pG", "pT", "pJ0", "pJ1", "pM0", "pM1", "pS0", "pS1", "pS2")

    def op(self, e, fn, reads=(), writes=()):
        if e != "pe":
            writes = list(writes) + [(r[0], None) for r in reads if r[0] in self.PSUM]
        for d in self._deps(reads, writes):
            self._wait(e, d)
        ins = fn()
        self.cnt[e] += 1
        n = self.cnt[e]
        ep = (n - 1) // self.EPOCH
        ins.then_inc(self._sem(e, ep), 1)
        tok = ("c", e, n)
        self.vc[tok] = dict(self.known[e])
        self._update(tok, reads, writes)
        return tok

    def dma(self, q, out, in_, reads=(), writes=()):
        slot = self.dma_rr.get(q, 0) % self.NDMA
        self.dma_rr[q] = self.dma_rr.get(q, 0) + 1
        k = (q, slot)
        if k not in self.dma_sems:
            self.dma_sems[k] = self.nc.alloc_semaphore("d_%s_%d" % (q, slot))
            self.dma_val[k] = 0
        if self.dma_val[k] > 0:
            self._wait(q, ("d", q, slot, self.dma_val[k]))
        for d in self._deps(reads, writes):
            self._wait(q, d)
        ins = self.eng[q].dma_start(out=out, in_=in_)
        self.dma_val[k] += 16
        ins.then_inc(self.dma_sems[k], 16)
        tok = ("d", q, slot, self.dma_val[k])
        self.vc[tok] = dict(self.known[q])
        self._update(tok, reads, writes)
        return tok

    def fence(self):
        for e in self.eng:
            for f in self.eng:
                if self.cnt[f] > 0:
                    self._wait(e, ("c", f, self.cnt[f])) if not (e == "pe" and f == "pe") else None
            for (q, slot), v in list(self.dma_val.items()):
                if v > 0:
                    self._wait(e, ("d", q, slot, v))

    def finish(self):
        for (q, slot), v in list(self.dma_val.items()):
            if v > 0:
                self._wait(q, ("d", q, slot, v))


class K:
    def __init__(self, nc):
        self.nc = nc
        self.S = Sync(nc)
        self.rr = 0

    def mm(self, out, lhsT, rhs, start, stop, r, w):
        nc = self.nc
        return self.S.op("pe", lambda: nc.tensor.matmul(out, lhsT=lhsT, rhs=rhs, start=start, stop=stop), r, w)

    def tr(self, out, in_, ident, r, w):
        nc = self.nc
        return self.S.op("pe", lambda: nc.tensor.transpose(out, in_, ident), r + [("ident", None)], w)

    def act(self, out, in_, func, r, w, bias=None, scale=None):
        nc = self.nc
        kw = {}
        if bias is not None:
            kw["bias"] = bias
        if scale is not None:
            kw["scale"] = scale
        return self.S.op("act", lambda: nc.scalar.activation(out=out, in_=in_, func=func, **kw), r, w)

    def ve(self, e, name, r, w, *a, **kw):
        eng = self.S.eng[e]
        return self.S.op(e, lambda: getattr(eng, name)(*a, **kw), r, w)

    def alt(self):
        self.rr += 1
        return "dve" if self.rr % 2 else "pool"


def build_program(dbg=None):
    dbg = dbg or {}
    nc = bass.Bass("TRN2", target_bir_lowering=False)
    k = K(nc)
    S = k.S

    def din(name, shape, dt=F32):
        return nc.dram_tensor(name, list(shape), dt, kind="ExternalInput").ap()

    xs = din("xs", [8192, D])
    ident_d = din("ident", [128, 128])
    g_f = din("g_f", [1, D])
    gl = {n: din(n, [128, 8]) for n in ["g_x", "g_ffn", "g_mem"]}
    mem = din("mem", [256, D])
    wl = {
        "w_out": din("w_out", [128, 8, D]),
        "w_q": din("w_q", [128, 8, D]),
        "w_kv": din("w_kv", [128, 8, 2 * D]),
        "w_o": din("w_o", [128, 8, D]),
        "w_gate": din("w_gate", [128, 8, DFF]),
        "w_up": din("w_up", [128, 8, DFF]),
        "w_down": din("w_down", [128, 22, D]),
    }
    y = nc.dram_tensor("y", [NOWN * 128, D], F32, kind="ExternalOutput").ap()
    if "omix" in dbg:
        omix_d = din("omix", [NOWN * 128, D])
        dx1 = nc.dram_tensor("dx1", [NOWN * 128, D], F32, kind="ExternalOutput").ap()
        dx2 = nc.dram_tensor("dx2", [NOWN * 128, D], F32, kind="ExternalOutput").ap()
        dq = nc.dram_tensor("dq", [128, 1024], BF16, kind="ExternalOutput").ap()
        dk = nc.dram_tensor("dk", [128, 2048], BF16, kind="ExternalOutput").ap()
        dP = nc.dram_tensor("dP", [128, 256], BF16, kind="ExternalOutput").ap()
        dr = nc.dram_tensor("dr", [128, 128], F32, kind="ExternalOutput").ap()
        dh = nc.dram_tensor("dh", [128, 1024], BF16, kind="ExternalOutput").ap()
        dV = nc.dram_tensor("dV", [128, 2048], BF16, kind="ExternalOutput").ap()
        do = nc.dram_tensor("do", [128, 1024], BF16, kind="ExternalOutput").ap()

    ident = nc.alloc_sbuf_tensor("ident_b", [128, 128], BF16)
    identf = nc.alloc_sbuf_tensor("ident_f", [128, 128], F32)
    S.dma("sp", identf[:], ident_d[:, :], [], [("identf", None)])
    k.ve("dve", "tensor_copy", [("identf", None)], [("ident", None)], out=ident[:], in_=identf[:])
    ones_b = nc.alloc_sbuf_tensor("ones_b", [128, 128], BF16)
    k.ve("dve", "memset", [], [("ones_b", None)], ones_b[:], 1.0)
    epsc = nc.alloc_sbuf_tensor("epsc", [128, 1], F32)
    k.ve("dve", "memset", [], [("epsc", None)], epsc[:], RMS_EPS)
    ones_f = nc.alloc_sbuf_tensor("ones_f", [1, 128], F32)
    k.ve("dve", "memset", [], [("ones_f", None)], ones_f[:], 1.0)

    pT = nc.alloc_psum_tensor("pT", [128, 1024], BF16)
    pA = nc.alloc_psum_tensor("pA", [128, 512], F32)
    pB = nc.alloc_psum_tensor("pB", [128, 512], F32)
    pC = nc.alloc_psum_tensor("pC", [128, 512], F32)
    pD = nc.alloc_psum_tensor("pD", [128, 512], F32)
    pE = nc.alloc_psum_tensor("pE", [128, 512], F32)
    pF = nc.alloc_psum_tensor("pF", [128, 512], F32)
    pG = nc.alloc_psum_tensor("pG", [128, 512], F32)

    stg = [None, None]

    def alloc_stg(es, tag, side=None):
        for i in range(2):
            kw = {"side": side} if side else {}
            stg[i] = es.enter_context(nc.sbuf_tensor("stg%d%s" % (i, tag), [128, 1024], F32, **kw))
    stg_i = [0]

    def load_w(dst, dname, src, KC, ncols, gcol=None, gname=None, c_lo=0, kc_lo=0):
        for kc in range(KC):
            for c0 in range(0, ncols, 1024):
                c1 = min(ncols, c0 + 1024)
                i = stg_i[0] % 2
                stg_i[0] += 1
                q = "sp" if i == 0 else "pool"
                S.dma(q, stg[i][:, 0:c1 - c0], src[:, kc_lo + kc, c_lo + c0:c_lo + c1], [], [("stg%d" % i, None)])
                e = "dve" if i == 0 else "act"
                if gcol is not None:
                    if e == "dve":
                        k.ve("dve", "tensor_scalar", [("stg%d" % i, None), (gname, None)], [(dname, kc)],
                             out=dst[:, kc, c0:c1], in0=stg[i][:, 0:c1 - c0], scalar1=gcol[:, kc_lo + kc:kc_lo + kc + 1],
                             scalar2=None, op0=ALU.mult)
                    else:
                        k.act(dst[:, kc, c0:c1], stg[i][:, 0:c1 - c0], AF.Copy, [("stg%d" % i, None), (gname, None)],
                              [(dname, kc)], scale=gcol[:, kc_lo + kc:kc_lo + kc + 1])
                else:
                    if e == "dve":
                        k.ve("dve", "tensor_copy", [("stg%d" % i, None)], [(dname, kc)], out=dst[:, kc, c0:c1],
                             in_=stg[i][:, 0:c1 - c0])
                    else:
                        k.act(dst[:, kc, c0:c1], stg[i][:, 0:c1 - c0], AF.Copy, [("stg%d" % i, None)], [(dname, kc)])

    def rmsnorm(src, sname, dst, dname, sq, ss, tag):
        nc_ = nc
        S.op("act", lambda: nc_.scalar.activation(out=sq[:], in_=src, func=AF.Square, accum_out=ss[:, 0:1]),
             [sname], [("sq" + tag, None), ("ss" + tag, None)])
        k.act(ss[:, 1:2], ss[:, 0:1], AF.Ln, [("ss" + tag, None)], [("ss" + tag, None)], bias=epsc[:, 0:1], scale=1.0 / D)
        k.act(ss[:, 1:2], ss[:, 1:2], AF.Exp, [("ss" + tag, None)], [("ss" + tag, None)], scale=-0.5)
        k.ve("dve", "tensor_scalar", [sname, ("ss" + tag, None)], [dname], out=dst, in0=src, scalar1=ss[:, 1:2],
             scalar2=None, op0=ALU.mult)

    def transpose8(src, sname, dstT, dname, nchunk=8):
        for c in range(nchunk):
            k.tr(pT[:, c * 128:(c + 1) * 128], src[:, c * 128:(c + 1) * 128], ident[:], [sname], [("pT", c)])
        k.ve("dve", "tensor_copy", [("pT", None)], [dname], out=dstT, in_=pT[:, 0:nchunk * 128])

    pJ0, pJ1, pM0, pM1, pS0, pS1, pS2 = pA, pB, pC, pD, pE, pF, pG
    epsl = nc.alloc_sbuf_tensor("epsl", [128, 1], F32)
    k.ve("dve", "memset", [], [("epsl", None)], epsl[:], 64e-5)
    sq2 = [nc.alloc_sbuf_tensor("sq0", [128, D], BF16), None]
    ss2 = [nc.alloc_sbuf_tensor("ss0", [128, 2], F32), None]
    hb2 = [nc.alloc_sbuf_tensor("hb0", [128, D], BF16), None]
    sq, ss, hb = sq2[0], ss2[0], hb2[0]
    eo = ExitStack()
    oT_r = eo.enter_context(nc.sbuf_tensor("oT_r", [128, NOWN, 4, 128], BF16, side="right"))
    LNX_EPS = 64e-5
    if "rwkv" in dbg or "nsa" in dbg:
        gmix_d = din("g_mix", [128, 8])
    if "rwkv" in dbg:
        w_in_r = din("w_in_r", [128, 8, 1792])
        mu_d = din("mu", [128, 14])
        rv_d = din("rvecs", [128, 5, 4])
        lora_up_d = din("lora_up", [128, 512])
        g_up_d = din("g_up", [128, 512])
        lnw_d = din("lnw", [128, 4, 64])
        lnb_d = din("lnb", [128, 4, 64])
        rmask_d = din("rmasks", [128, 4, 512])
        bones_d = din("blockones", [128, 128])
        er = ExitStack()
        GT = 256
        NG = 8192 // GT
        NCH = GT // 64
        NSLOT = 3

        def T(name, shape, dt):
            return er.enter_context(nc.sbuf_tensor("r_" + name, shape, dt))

        alloc_stg(er, "r")
        sq2[1] = T("sq1", [128, D], BF16)
        ss2[1] = T("ss1", [128, 2], F32)
        hb2[1] = T("hb1", [128, D], BF16)
        w_in_b = T("w_in_b", [128, 8, 1792], BF16)
        gmix = T("gmix", [128, 8], F32)
        mu = T("mu", [128, 14], F32)
        rv = T("rv", [128, 5, 4], F32)
        omk = T("omk", [128, 4], F32)
        lup_b = T("lup_b", [128, 512], BF16)
        gup_b = T("gup_b", [128, 512], BF16)
        lnw = T("lnw", [128, 4, 64], F32)
        lnb = T("lnb", [128, 4, 64], F32)
        RSTt = T("RSTt", [128, GT], F32)
        rmb = T("rmb", [128, 3, 512], BF16)
        bones = T("bones", [128, 128], F32)
        onesc = T("onesc", [128, 1], BF16)
        xnT1 = T("xnT0", [128, 8, GT], BF16)
        xnT = [xnT1, xnT1]
        xr1 = T("xr0", [128, D], F32)
        xr = [xr1, xr1]
        carry = T("carry", [128, 14], F32)
        RWl = T("RWl", [128, 2, GT], F32)
        Pstl = T("Pstl", [128, GT + 1], F32)
        LT = [T("LT%d" % i, [128, GT], BF16) for i in range(2)]
        SGD = [T("SGD%d" % i, [128, NCH, 2, 64], BF16) for i in range(2)]
        WS = []
        for i in range(2):
            WS.append(dict(
                i=i,
                Pst=T("Pst%d" % i, [128, GT + 1], F32),
                RW3=T("RW3_%d" % i, [128, 3, GT], F32),
                Wk=[T("Wk%d_%d" % (i, j), [128, GT], F32) for j in range(7)],
            ))
        BTm = [T("BT%d" % i, [128, 4, 4, NCH, 128], BF16) for i in range(2)]
        BTo = T("BTo", [128, 2, 4, 2, 128], BF16)
        GC = [T("GC%d" % i, [128, 4, NCH], F32) for i in range(2)]
        SL = []
        for i in range(NSLOT):
            SL.append(dict(
                i=i,
                TT=T("TT%d" % i, [128, 3, 512], BF16),
                X1=T("X1_%d" % i, [128, 2, 2, 2, 128], BF16),
                X2=T("X2_%d" % i, [128, 2, 2, 2, 128], BF16),
                X3=T("X3_%d" % i, [128, 512], BF16),
                Ak=[T("Ak%d_%d" % (i, j), [128, 512], BF16) for j in range(2)],
                Bk=[T("Bk%d_%d" % (i, j), [128, 512], BF16) for j in range(2)],
                Wt=[T("Wt%d_%d" % (i, j), [128, 512], BF16) for j in range(2)],
                Zt=[T("Zt%d_%d" % (i, j), [128, 512], BF16) for j in range(2)],
            ))
        Hf = T("Hf", [128, 512], F32)
        Hb = T("Hb", [128, 512], BF16)
        Hs = T("Hs", [128, 512], F32)
        RHSb = T("RHSb", [128, 512], BF16)
        Ub = T("Ub", [128, 512], BF16)
        Yc = T("Yc", [128, 4, 64], F32)
        Gc = T("Gc", [128, 4, 64], F32)
        V2c = T("V2c", [128, 4, 64], F32)
        Ycen = [T("Ycen%d" % i, [128, 64], F32) for i in range(4)]
        Ysq = [T("Ysq%d" % i, [128, 64], F32) for i in range(4)]
        st = [T("st%d" % i, [128, 8], F32) for i in range(4)]
        stb = T("stb", [128, 4], F32)
        Oblk = T("Oblk", [128, 4, 128], BF16)

        S.dma("sp", gmix[:], gmix_d[:, :], [], [("gmix", None)])
        S.dma("sp", mu[:], mu_d[:, :], [], [("mu", None)])
        S.dma("sp", rv[:], rv_d[:, :, :], [], [("rv", None)])
        S.dma("sp", stg[0][:, 0:512], lora_up_d[:, :], [], [("stg0", None)])
        k.ve("dve", "tensor_copy", [("stg0", None)], [("lup_b", None)], out=lup_b[:], in_=stg[0][:, 0:512])
        S.dma("sp", stg[1][:, 0:512], g_up_d[:, :], [], [("stg1", None)])
        k.ve("dve", "tensor_copy", [("stg1", None)], [("gup_b", None)], out=gup_b[:], in_=stg[1][:, 0:512])
        S.dma("sp", lnw[:], lnw_d[:, :, :], [], [("lnw", None)])
        S.dma("sp", lnb[:], lnb_d[:, :, :], [], [("lnb", None)])
        S.dma("sp", RSTt[:], rmask_d[:, 0, 0:GT], [], [("rmask", None)])
        for mi in range(3):
            S.dma("sp", stg[mi % 2][:, 0:512], rmask_d[:, 1 + mi, :], [], [("stg%d" % (mi % 2), None)])
            k.ve("dve", "tensor_copy", [("stg%d" % (mi % 2), None)], [("rmb", mi)], out=rmb[:, mi, :], in_=stg[mi % 2][:, 0:512])
        S.dma("pool", bones[:], bones_d[:, :], [], [("bones", None)])
        k.ve("dve", "memset", [], [("onesc", None)], onesc[:], 1.0)
        k.ve("dve", "memset", [], [("carry", None)], carry[:], 0.0)
        k.ve("dve", "memset", [], [("Hf", None)], Hf[:], 0.0)
        k.ve("dve", "memset", [], [("Hb", None)], Hb[:], 0.0)
        for par in range(2):
            for ty in range(4):
                k.ve("pool", "memset", [], [("BT%d" % par, None)], BTm[par][:, ty, :, :, :].rearrange("p b c d -> p (b c d)"), 0.0)
        k.ve("pool", "memset", [], [("BTo", None)], BTo[:].rearrange("p a b c d -> p (a b c d)"), 0.0)
        k.ve("pool", "memset", [], [("Oblk", None)], Oblk[:].rearrange("p a b -> p (a b)"), 0.0)
        k.ve("dve", "tensor_scalar", [("rv", None)], [("omk", None)], out=omk[:], in0=rv[:, 3, :], scalar1=-1.0,
             scalar2=1.0, op0=ALU.mult, op1=ALU.add)
        load_w(w_in_b, "w_in_b", w_in_r, 8, 1792, gcol=gmix, gname="gmix")

        MA = rmb[:, 0, :]
        ML4 = rmb[:, 1, :]
        I4 = rmb[:, 2, :]
        RST = RSTt[:, :]
        mm_i = [0]
        pj_i = [0]

        rbanks = [(pJ0, "pJ0"), (pJ1, "pJ1"), (pM0, "pM0"), (pM1, "pM1"), (pS0, "pS0"), (pS1, "pS1"), (pS2, "pS2")]

        def pM():
            mm_i[0] += 1
            return rbanks[mm_i[0] % 7]

        pJ = pM

        def v3(ap):
            return ap.rearrange("q (c t) -> q c t", c=NCH)

        def halves(e, name, r, w, out3, mk):
            for hh in range(2):
                lo, hi = hh * 64, hh * 64 + 64
                k.ve(e, name, r, w, out=out3[lo:hi, :, lo:hi], **mk(lo, hi))

        def inproj(par, ci, Pst, pstn, dst, dname):
            pp, pn = pJ()
            for kc in range(8):
                k.mm(pp[:, 0:GT], w_in_b[:, kc, ci * 128:(ci + 1) * 128], xnT[par][:, kc, :], kc == 0, kc == 7,
                     [("w_in_b", kc), ("xnT0", None)], [(pn, None)])
            k.act(Pst[:, 0:1], carry[:, ci:ci + 1], AF.Copy, [("carry", ci)], [(pstn, 0)])
            k.act(Pst[:, 1:GT + 1], pp[:, 0:GT], AF.Copy, [(pn, None)], [(pstn, 1)])
            k.act(carry[:, ci:ci + 1], pp[:, GT - 1:GT], AF.Copy, [(pn, None)], [("carry", ci)])
            k.ve("dve", "tensor_tensor", [(pstn, None)], [dname], out=dst, in0=Pst[:, 0:GT], in1=Pst[:, 1:GT + 1],
                 op=ALU.subtract)
            k.ve("dve", "scalar_tensor_tensor", [dname, (pstn, None), ("mu", None)], [dname], out=dst, in0=dst,
                 scalar=mu[:, ci:ci + 1], in1=Pst[:, 1:GT + 1], op0=ALU.mult, op1=ALU.add)

        def gen_load(g):
            par = g % 2
            for tl in range(GT // 128):
                pos = (GT // 128) * g + tl
                b = tl % 2
                S.dma("sp", xr[b][:], xs[pos * 128:(pos + 1) * 128, :], [], [("xr0", None)])
                rmsnorm(xr[b][:], ("xr0", None), hb2[b][:], ("hb%d" % b, None), sq2[b], ss2[b], str(b))
                yield
                for c in range(8):
                    k.tr(pT[:, c * 128:(c + 1) * 128], hb2[b][:, c * 128:(c + 1) * 128], ident[:], [("hb%d" % b, None)], [("pT", c)])
                k.ve("dve", "tensor_copy", [("pT", None)], [("xnT0", tl)], out=xnT[par][:, :, tl * 128:(tl + 1) * 128],
                     in_=pT[:, :].rearrange("p (c t) -> p c t", c=8))
                yield
            inproj(par, 12, Pstl, "Pstl", RWl[:, 0, :], ("RWl", 0))
            yield
            inproj(par, 13, Pstl, "Pstl", RWl[:, 1, :], ("RWl", 1))
            yield
            k.act(LT[par][0:64, :], RWl[0:64, 0, :], AF.Tanh, [("RWl", 0)], [("LT%d" % par, None)])
            k.act(LT[par][64:128, :], RWl[64:128, 0, :], AF.Copy, [("RWl", 0)], [("LT%d" % par, None)])
            for dup in range(2):
                k.act(SGD[par][:, :, dup, :], RWl[:, 1, :].rearrange("q (c t) -> q c t", c=NCH), AF.Sigmoid, [("RWl", 1)],
                      [("SGD%d" % par, None)])
            yield

        def gen_prep(g, p, ws):
            par = g % 2
            wi = ws["i"]
            Wk = ws["Wk"]
            RW3 = ws["RW3"]
            wn = lambda j: ("Wk%d_%d" % (wi, j), None)
            rn = lambda j: ("RW3_%d" % wi, j)
            btn = lambda ty: ("BT%d" % par, (ty, p))
            B_ = BTm[par]
            inproj(par, p, ws["Pst"], "Pst%d" % wi, RW3[:, 0, :], rn(0))
            yield
            inproj(par, 4 + p, ws["Pst"], "Pst%d" % wi, RW3[:, 1, :], rn(1))
            yield
            inproj(par, 8 + p, ws["Pst"], "Pst%d" % wi, RW3[:, 2, :], rn(2))
            yield
            k_ = RW3[:, 1, :]
            cs = slice(p * 128, (p + 1) * 128)
            pp, pn = pJ()
            k.mm(pp[:, 0:GT], lup_b[0:64, cs], LT[par][0:64, :], True, True, [("lup_b", None), ("LT%d" % par, None)], [(pn, None)])
            k.act(Wk[0][:], pp[:, 0:GT], AF.Sigmoid, [(pn, None), ("rv", None)], [wn(0)], bias=rv[:, 0, p:p + 1])
            yield
            k.ve("dve", "tensor_scalar", [wn(0)], [wn(0)], out=Wk[0][:], in0=Wk[0][:], scalar1=-0.6065306597126334,
                 scalar2=None, op0=ALU.mult)
            yield
            k.ve("dve", "tensor_tensor_scan", [wn(0), ("rmask", None)], [wn(1)], out=Wk[1][:], data0=RST, data1=Wk[0][:],
                 initial=0.0, op0=ALU.mult, op1=ALU.add)
            yield
            k.ve("pool", "tensor_tensor", [wn(0), wn(1)], [wn(0)], out=Wk[0][:], in0=Wk[1][:], in1=Wk[0][:], op=ALU.subtract)
            k.act(Wk[2][:], Wk[1][:], AF.Exp, [wn(1)], [wn(2)])
            k.act(Wk[3][:], Wk[1][:], AF.Exp, [wn(1)], [wn(3)], scale=-1.0)
            yield
            k.act(Wk[0][:], Wk[0][:], AF.Exp, [wn(0)], [wn(0)])
            k.ve("pool", "tensor_copy", [wn(2)], [("GC%d" % par, p)], out=GC[par][:, p, :],
                 in_=Wk[2][:].rearrange("q (c t) -> q c t", c=NCH)[:, :, 63])
            yield
            pp, pn = pJ()
            k.mm(pp[:, 0:GT], lup_b[64:128, cs], LT[par][64:128, :], True, True, [("lup_b", None), ("LT%d" % par, None)],
                 [(pn, None)])
            k.act(Wk[4][:], pp[:, 0:GT], AF.Sigmoid, [(pn, None), ("rv", None)], [wn(4)], bias=rv[:, 1, p:p + 1])
            k.ve("dve", "tensor_scalar", [rn(1), ("rv", None)], [wn(5)], out=Wk[5][:], in0=k_, scalar1=rv[:, 2, p:p + 1],
                 scalar2=None, op0=ALU.mult)
            yield
            k.ve("pool", "tensor_tensor", [wn(5)], [wn(6)], out=Wk[6][:], in0=Wk[5][:], in1=Wk[5][:], op=ALU.mult)
            yield
            pp, pn = pJ()
            k.mm(pp[:, 0:GT], bones[:, :], Wk[6][:], True, True, [("bones", None), wn(6)], [(pn, None)])
            k.ve("dve", "tensor_scalar", [(pn, None)], [wn(6)], out=Wk[6][:], in0=pp[:, 0:GT], scalar1=1e-24, scalar2=None,
                 op0=ALU.max)
            yield
            k.act(Wk[6][:], Wk[6][:], AF.Ln, [wn(6)], [wn(6)])
            yield
            k.act(Wk[6][:], Wk[6][:], AF.Exp, [wn(6)], [wn(6)], scale=-0.5)
            yield
            k.ve("dve", "tensor_tensor", [wn(5), wn(6)], [wn(5)], out=Wk[5][:], in0=Wk[5][:], in1=Wk[6][:], op=ALU.mult)
            k.ve("dve", "tensor_scalar", [wn(4), ("rv", None), ("omk", None)], [wn(6)], out=Wk[6][:], in0=Wk[4][:],
                 scalar1=rv[:, 3, p:p + 1], scalar2=omk[:, p:p + 1], op0=ALU.mult, op1=ALU.add)
            yield
            k.ve("pool", "tensor_tensor", [wn(6), rn(1)], [wn(6)], out=Wk[6][:], in0=Wk[6][:], in1=k_, op=ALU.mult)
            k.ve("pool", "tensor_tensor", [wn(5), wn(4)], [wn(4)], out=Wk[4][:], in0=Wk[5][:], in1=Wk[4][:], op=ALU.mult)
            yield
            if g % 2 == 1:
                halves("dve", "tensor_tensor", [rn(0), wn(2)], [("BTo", (0, p))], BTo[:, 0, p, :, :],
                       lambda lo, hi: dict(in0=v3(RW3[lo:hi, 0, :])[:, 2:4, :], in1=v3(Wk[2][lo:hi, :])[:, 2:4, :], op=ALU.mult))
            halves("dve", "scalar_tensor_tensor", [wn(5), wn(0)], [btn(1)], B_[:, 0, p, :, :],
                   lambda lo, hi: dict(in0=v3(Wk[5][lo:hi, :]), scalar=-1.0, in1=v3(Wk[0][lo:hi, :]), op0=ALU.mult,
                                       op1=ALU.mult))
            yield
            halves("pool", "tensor_tensor", [wn(4), wn(3)], [btn(2)], B_[:, 1, p, :, :],
                   lambda lo, hi: dict(in0=v3(Wk[4][lo:hi, :]), in1=v3(Wk[3][lo:hi, :]), op=ALU.mult))
            halves("pool", "tensor_tensor", [wn(6), wn(3)], [btn(3)], B_[:, 2, p, :, :],
                   lambda lo, hi: dict(in0=v3(Wk[6][lo:hi, :]), in1=v3(Wk[3][lo:hi, :]), op=ALU.mult))
            yield
            halves("pool", "tensor_copy", [rn(2)], [btn(4)], B_[:, 3, p, :, :],
                   lambda lo, hi: dict(in_=v3(RW3[lo:hi, 2, :])))
            if g % 2 == 1:
                halves("dve", "scalar_tensor_tensor", [rn(0), wn(6), ("rv", None)], [("BTo", (1, p))], BTo[:, 1, p, :, :],
                       lambda lo, hi: dict(in0=v3(RW3[lo:hi, 0, :])[:, 2:4, :], scalar=rv[lo:hi, 4, p:p + 1],
                                           in1=v3(Wk[6][lo:hi, :])[:, 2:4, :], op0=ALU.mult, op1=ALU.mult))
            yield

        def gen_tt(g, c, sl):
            par = g % 2
            B_ = BTm[par]
            si_ = sl["i"]
            btn = lambda ty, p: ("BT%d" % par, (ty, p))
            TT, X2 = sl["TT"], sl["X2"]
            own = (g % 2 == 1) and c >= 2
            for j, ty in enumerate((2, 3, 4)):
                pp, pn = pM()
                for p in range(4):
                    k.mm(pp[:, p * 128:(p + 1) * 128], B_[:, ty - 1, p, c, :], ident[:], True, True,
                         [btn(ty, p), ("ident", None)], [(pn, None)])
                if j >= 1:
                    k.act(TT[:, j, :], pp[:, :], AF.Copy, [(pn, None)], [("TT%d" % si_, j)])
                else:
                    k.ve("dve", "tensor_copy", [(pn, None)], [("TT%d" % si_, j)], out=TT[:, j, :], in_=pp[:, :])
                yield
            for j in range(2):
                pp, pn = pM()
                for a in range(2):
                    p = 2 * j + a
                    k.mm(pp[:, a * 256:a * 256 + 128], B_[:, 2, p, c, :], B_[:, 0, p, c, :], True, True, [btn(3, p), btn(1, p)],
                         [(pn, None)])
                    if own:
                        k.mm(pp[:, a * 256 + 128:a * 256 + 256], B_[:, 2, p, c, :], BTo[:, 0, p, c - 2, :], True, True,
                             [btn(3, p), ("BTo", (0, p))], [(pn, None)])
                if own:
                    k.ve("dve", "tensor_tensor", [(pn, None), ("rmb", None)], [("X2_%d" % si_, j)],
                         out=X2[:, j, :, :, :].rearrange("q a t s -> q (a t s)"), in0=pp[:, :], in1=MA, op=ALU.mult)
                else:
                    k.ve("dve", "tensor_tensor", [(pn, None), ("rmb", None)], [("X2_%d" % si_, j)],
                         out=X2[:, j, :, 0, :], in0=pp[:, :].rearrange("q (a t s) -> q a t s", a=2, t=2)[:, :, 0, :],
                         in1=MA.rearrange("q (a t s) -> q a t s", a=2, t=2)[:, :, 0, :], op=ALU.mult)
                yield

        def gen_pre(g, c, sl):
            par = g % 2
            B_ = BTm[par]
            si_ = sl["i"]
            btn = lambda ty, p: ("BT%d" % par, (ty, p))
            X1, X3 = sl["X1"], sl["X3"]
            Ak, Bk, Wt, Zt = sl["Ak"], sl["Bk"], sl["Wt"], sl["Zt"]
            own = (g % 2 == 1) and c >= 2
            n_ = lambda base, j=None: ("%s%d" % (base, si_) if j is None else "%s%d_%d" % (base, si_, j), None)
            pp, pn = pM()
            for p in range(4):
                k.mm(pp[:, p * 128:(p + 1) * 128], B_[:, 0, p, c, :], B_[:, 1, p, c, :], True, True, [btn(1, p), btn(2, p)],
                     [(pn, None)])
            k.ve("dve", "tensor_tensor", [(pn, None), ("rmb", None)], [("X3_%d" % si_, None)], out=X3[:], in0=pp[:, :], in1=ML4,
                 op=ALU.mult)
            for j in range(2):
                pp, pn = pM()
                for a in range(2):
                    p = 2 * j + a
                    k.mm(pp[:, a * 256:a * 256 + 128], B_[:, 1, p, c, :], B_[:, 0, p, c, :], True, True, [btn(2, p), btn(1, p)],
                         [(pn, None)])
                    if own:
                        k.mm(pp[:, a * 256 + 128:a * 256 + 256], B_[:, 1, p, c, :], BTo[:, 0, p, c - 2, :], True, True,
                             [btn(2, p), ("BTo", (0, p))], [(pn, None)])
                if own:
                    k.ve("dve", "tensor_tensor", [(pn, None), ("rmb", None)], [("X1_%d" % si_, j)],
                         out=X1[:, j, :, :, :].rearrange("q a t s -> q (a t s)"), in0=pp[:, :], in1=MA, op=ALU.mult)
                else:
                    k.ve("dve", "tensor_tensor", [(pn, None), ("rmb", None)], [("X1_%d" % si_, j)],
                         out=X1[:, j, :, 0, :], in0=pp[:, :].rearrange("q (a t s) -> q a t s", a=2, t=2)[:, :, 0, :],
                         in1=MA.rearrange("q (a t s) -> q a t s", a=2, t=2)[:, :, 0, :], op=ALU.mult)
            yield
            B1 = X1[:, :, :, 0, :]
            k.ve("dve", "tensor_tensor", [("X1_%d" % si_, None), ("rmb", None)], [n_("Wt", 0)],
                 out=Wt[0][:].rearrange("q (j a s) -> q j a s", j=2, a=2), in0=B1,
                 in1=I4.rearrange("q (j a s) -> q j a s", j=2, a=2), op=ALU.add)
            k.ve("dve", "tensor_tensor", [("X3_%d" % si_, None), ("rmb", None)], [n_("Zt", 0)], out=Zt[0][:], in0=X3[:], in1=I4,
                 op=ALU.add)

            def Bsl(lv, p):
                if lv == 0:
                    return X1[:, p // 2, p % 2, 0, :], ("X1_%d" % si_, None)
                return Bk[lv % 2][:, p * 128:(p + 1) * 128], n_("Bk", lv % 2)

            def Asl(lv, p):
                if lv == 0:
                    return X3[:, p * 128:(p + 1) * 128], ("X3_%d" % si_, None)
                return Ak[lv % 2][:, p * 128:(p + 1) * 128], n_("Ak", lv % 2)

            def squares(lv):
                ppB, pnB = pM()
                for p in range(4):
                    a_ap, a_n = Asl(lv - 1, p)
                    b_ap, b_n = Bsl(lv - 1, p)
                    k.mm(ppB[:, p * 128:(p + 1) * 128], a_ap, b_ap, True, True, [a_n, b_n], [(pnB, None)])
                k.act(Bk[lv % 2][:], ppB[:, :], AF.Copy, [(pnB, None)], [n_("Bk", lv % 2)])
                if lv < 5:
                    ppA, pnA = pM()
                    for p in range(4):
                        a_ap, a_n = Asl(lv - 1, p)
                        b_ap, b_n = Bsl(lv - 1, p)
                        k.mm(ppA[:, p * 128:(p + 1) * 128], b_ap, a_ap, True, True, [a_n, b_n], [(pnA, None)])
                    k.act(Ak[lv % 2][:], ppA[:, :], AF.Copy, [(pnA, None)], [n_("Ak", lv % 2)])

            def products(lv):
                wi, wo = (lv - 1) % 2, lv % 2
                ppW, pnW = pM()
                for p in range(4):
                    b_ap, b_n = Bsl(lv, p)
                    k.mm(ppW[:, p * 128:(p + 1) * 128], Zt[wi][:, p * 128:(p + 1) * 128], b_ap, True, True,
                         [n_("Zt", wi), b_n], [(pnW, None)])
                k.ve("dve", "tensor_tensor", [(pnW, None), n_("Wt", wi)], [n_("Wt", wo)], out=Wt[wo][:], in0=ppW[:, :],
                     in1=Wt[wi][:], op=ALU.add)
                if lv < 5:
                    ppZ, pnZ = pM()
                    for p in range(4):
                        a_ap, a_n = Asl(lv, p)
                        k.mm(ppZ[:, p * 128:(p + 1) * 128], Wt[wi][:, p * 128:(p + 1) * 128], a_ap, True, True,
                             [n_("Wt", wi), a_n], [(pnZ, None)])
                    k.ve("dve", "tensor_tensor", [(pnZ, None), n_("Zt", wi)], [n_("Zt", wo)], out=Zt[wo][:], in0=ppZ[:, :],
                         in1=Zt[wi][:], op=ALU.add)

            squares(1)
            yield
            for lv in range(1, 6):
                if lv < 5:
                    squares(lv + 1)
                products(lv)
                yield

        def gen_seq(g, c, sl):
            par = g % 2
            B_ = BTm[par]
            si_ = sl["i"]
            btn = lambda ty, p: ("BT%d" % par, (ty, p))
            TT, X1, X2 = sl["TT"], sl["X1"], sl["X2"]
            Wf, Wfn = sl["Wt"][1], ("Wt%d_1" % si_, None)
            ttn = lambda j: ("TT%d" % si_, j)
            x1n, x2n = ("X1_%d" % si_, None), ("X2_%d" % si_, None)
            BBT, KBT, VBT = TT[:, 0, :], TT[:, 1, :], TT[:, 2, :]
            own = (g % 2 == 1) and c >= 2
            it = g // 2
            b0, b0n = pM()
            for p in range(4):
                ps = slice(p * 128, (p + 1) * 128)
                k.mm(b0[:, ps], B_[:, 0, p, c, :], Hb[:, ps], True, False, [btn(1, p), ("Hb", None)], [(b0n, None)])
                k.mm(b0[:, ps], X2[:, p // 2, p % 2, 0, :], VBT[:, ps], False, True, [x2n, ttn(2)], [(b0n, None)])
            k.act(RHSb[:], b0[:, :], AF.Copy, [(b0n, None)], [("RHSb", None)])
            yield
            b1, b1n = pM()
            for p in range(4):
                ps = slice(p * 128, (p + 1) * 128)
                k.mm(b1[:, ps], Wf[:, ps], RHSb[:, ps], True, True, [Wfn, ("RHSb", None)], [(b1n, None)])
            k.act(Ub[:], b1[:, :], AF.Copy, [(b1n, None)], [("Ub", None)])
            yield
            if own:
                ch = c - 2
                pS2, pS2n = pM()
                for p in range(4):
                    ps = slice(p * 128, (p + 1) * 128)
                    k.mm(pS2[:, ps], BTo[:, 0, p, c - 2, :], Hb[:, ps], True, False, [("BTo", (0, p)), ("Hb", None)], [(pS2n, None)])
                    k.mm(pS2[:, ps], X1[:, p // 2, p % 2, 1, :], Ub[:, ps], False, False, [x1n, ("Ub", None)], [(pS2n, None)])
                    k.mm(pS2[:, ps], X2[:, p // 2, p % 2, 1, :], VBT[:, ps], False, True, [x2n, ttn(2)], [(pS2n, None)])
            b3, b3n = pM()
            for p in range(4):
                ps = slice(p * 128, (p + 1) * 128)
                k.mm(b3[:, ps], BBT[:, ps], Ub[:, ps], True, False, [ttn(0), ("Ub", None)], [(b3n, None)])
                k.mm(b3[:, ps], KBT[:, ps], VBT[:, ps], False, True, [ttn(1), ttn(2)], [(b3n, None)])
            k.ve("dve", "tensor_tensor", [(b3n, None), ("Hf", None)], [("Hs", None)], out=Hs[:], in0=b3[:, :], in1=Hf[:],
                 op=ALU.add)
            yield
            for p in range(4):
                ps = slice(p * 128, (p + 1) * 128)
                k.act(Hf[:, ps], Hs[:, ps], AF.Copy, [("Hs", None), ("GC%d" % par, p)], [("Hf", None)], scale=GC[par][:, p, c:c + 1])
            k.ve("dve", "tensor_copy", [("Hf", None)], [("Hb", None)], out=Hb[:], in_=Hf[:])
            yield
            if own:
                for hh in range(2):
                    lo, hi = hh * 64, hh * 64 + 64
                    k.ve("dve", "tensor_copy", [(pS2n, None)], [("Yc", None)], out=Yc[lo:hi, :, :],
                         in_=pS2[lo:hi, :].rearrange("q (p s) -> q p s", p=4)[:, :, lo:hi])
                    k.ve("pool", "tensor_copy", [ttn(2)], [("V2c", None)], out=V2c[lo:hi, :, :],
                         in_=TT[lo:hi, 2, :].rearrange("q (p s) -> q p s", p=4)[:, :, lo:hi])
                yield
                pp, pn = pJ()
                for p in range(4):
                    k.mm(pp[:, p * 128:(p + 1) * 128], SGD[par][:, c, :, :].rearrange("q a t -> q (a t)"),
                         gup_b[:, p * 128:(p + 1) * 128], True, True, [("SGD%d" % par, None), ("gup_b", None)], [(pn, None)])
                for hh in range(2):
                    lo, hi = hh * 64, hh * 64 + 64
                    k.act(Gc[lo:hi, :, :], pp[lo:hi, :].rearrange("q (p s) -> q p s", p=4)[:, :, lo:hi], AF.Copy,
                          [(pn, None)], [("Gc", None)])
                yield
                pp, pn = pJ()
                for p in range(4):
                    k.mm(pp[:, p:p + 1], BTo[:, 1, p, c - 2, :], onesc[:, 0:1], True, True, [("BTo", (1, p)), ("onesc", None)], [(pn, None)])
                k.ve("dve", "tensor_copy", [(pn, None)], [("stb", None)], out=stb[:, 0:4], in_=pp[:, 0:4])
                yield
                steps = []
                for p in range(4):
                    yc = Yc[:, p, :]
                    sn, cn, qn = ("st%d" % p, None), ("Ycen%d" % p, None), ("Ysq%d" % p, None)
                    s_, c_, q_ = st[p], Ycen[p], Ysq[p]
                    steps.append([
                        lambda yc=yc, s_=s_, sn=sn: k.ve("dve", "reduce_sum", [("Yc", None)], [sn], out=s_[:, 0:1], in_=yc, axis=AX.X),
                        lambda s_=s_, sn=sn: k.ve("dve", "tensor_scalar", [sn], [sn], out=s_[:, 0:1], in0=s_[:, 0:1],
                                                  scalar1=-1.0 / 64, scalar2=None, op0=ALU.mult),
                        lambda yc=yc, s_=s_, sn=sn, c_=c_, cn=cn: k.ve("dve", "tensor_scalar", [("Yc", None), sn], [cn], out=c_[:],
                                                                      in0=yc, scalar1=s_[:, 0:1], scalar2=None, op0=ALU.add),
                        lambda c_=c_, cn=cn, q_=q_, qn=qn, s_=s_, sn=sn: S.op(
                            "act", lambda: nc.scalar.activation(out=q_[:], in_=c_[:], func=AF.Square, accum_out=s_[:, 1:2]),
                            [cn], [qn, sn]),
                        lambda s_=s_, sn=sn: k.act(s_[:, 2:3], s_[:, 1:2], AF.Ln, [sn, ("epsl", None)], [sn], bias=epsl[:, 0:1],
                                                   scale=1.0 / 64),
                        lambda s_=s_, sn=sn: k.act(s_[:, 2:3], s_[:, 2:3], AF.Exp, [sn], [sn], scale=-0.5),
                        lambda s_=s_, sn=sn, c_=c_, cn=cn: k.ve("dve", "tensor_scalar", [cn, sn], [cn], out=c_[:], in0=c_[:],
                                                              scalar1=s_[:, 2:3], scalar2=None, op0=ALU.mult),
                        lambda c_=c_, cn=cn, p=p: k.ve("pool", "tensor_tensor", [cn, ("lnw", None)], [cn], out=c_[:], in0=c_[:],
                                                      in1=lnw[:, p, :], op=ALU.mult),
                        lambda c_=c_, cn=cn, p=p: k.ve("pool", "tensor_tensor", [cn, ("lnb", None)], [cn], out=c_[:], in0=c_[:],
                                                      in1=lnb[:, p, :], op=ALU.add),
                        lambda c_=c_, cn=cn, p=p: k.ve("dve", "scalar_tensor_tensor", [("V2c", None), ("stb", None), cn], [cn],
                                                      out=c_[:], in0=V2c[:, p, :], scalar=stb[:, p:p + 1], in1=c_[:],
                                                      op0=ALU.mult, op1=ALU.add),
                    ])
                for si2 in range(len(steps[0])):
                    for p in range(4):
                        steps[p][si2]()
                    yield
                for p in range(4):
                    for hh in range(2):
                        lo, hi = hh * 64, hh * 64 + 64
                        k.ve("dve" if hh == 0 else "pool", "tensor_tensor", [("Ycen%d" % p, None), ("Gc", None)], [("Oblk", None)],
                             out=Oblk[lo:hi, p, lo:hi], in0=Ycen[p][lo:hi, :], in1=Gc[lo:hi, p, :], op=ALU.mult)
                yield
                for p in range(4):
                    k.tr(pT[:, p * 128:(p + 1) * 128], Oblk[:, p, :], ident[:], [("Oblk", None)], [("pT", p)])
                for hh in range(2):
                    lo, hi = hh * 64, hh * 64 + 64
                    k.ve("dve", "tensor_copy", [("pT", None)], [("oT_r", it)], out=oT_r[lo:hi, it, :, ch * 64:(ch + 1) * 64],
                         in_=pT[lo:hi, 0:512].rearrange("q (p s) -> q p s", p=4)[:, :, lo:hi])
                yield

        NCC = NG * NCH
        done = {"load": -1, "prep": -1, "pre": -1, "seq": -1}
        active = []
        nxt = {"load": 0, "prepg": 0, "prepp": 0, "pre": 0, "seq": 0}
        prep_done_pairs = {}

        def admit():
            if nxt["load"] < NG and not any(a[0] == "load" for a in active) and nxt["load"] <= done["prep"] + 1 \
                    and done["seq"] >= (nxt["load"] - 1) * NCH - 1:
                g = nxt["load"]
                active.append(("load", g, gen_load(g)))
                nxt["load"] += 1
            while nxt["prepg"] < NG:
                g = nxt["prepg"]
                if done["load"] < g or done["seq"] < (g - 1) * NCH - 1:
                    break
                inflight = [a for a in active if a[0] == "prep"]
                if len(inflight) >= 2:
                    break
                used = {a[3] for a in inflight}
                wsi = 0 if 0 not in used else 1
                p = nxt["prepp"]
                active.append(("prep", (g, p), gen_prep(g, p, WS[wsi]), wsi))
                nxt["prepp"] += 1
                if nxt["prepp"] == 4:
                    nxt["prepp"] = 0
                    nxt["prepg"] += 1
            while nxt["pre"] < NCC:
                cc = nxt["pre"]
                if done["prep"] < cc // NCH:
                    break
                if cc - (done["seq"] + 1) >= NSLOT:
                    break
                if len([a for a in active if a[0] == "pre"]) >= 2:
                    break
                active.append(("pre", cc, gen_pre(cc // NCH, cc % NCH, SL[cc % NSLOT])))
                active.append(("tt", cc, gen_tt(cc // NCH, cc % NCH, SL[cc % NSLOT])))
                nxt["pre"] += 1
            if nxt["seq"] < NCC and not any(a[0] == "seq" for a in active) and done["pre"] >= nxt["seq"]:
                cc = nxt["seq"]
                active.append(("seq", cc, gen_seq(cc // NCH, cc % NCH, SL[cc % NSLOT])))
                nxt["seq"] += 1

        pre_done_set = set()
        pre_cnt = {}
        while True:
            admit()
            if not active:
                break
            for a in list(active):
                try:
                    next(a[2])
                except StopIteration:
                    active.remove(a)
                    kind = a[0]
                    if kind == "load":
                        done["load"] = a[1]
                    elif kind == "prep":
                        gq, pq = a[1]
                        prep_done_pairs[gq] = prep_done_pairs.get(gq, 0) + 1
                        if prep_done_pairs[gq] == 4:
                            done["prep"] = gq
                    elif kind in ("pre", "tt"):
                        pre_cnt[a[1]] = pre_cnt.get(a[1], 0) + 1
                        if pre_cnt[a[1]] == 2:
                            pre_done_set.add(a[1])
                        while done["pre"] + 1 in pre_done_set:
                            done["pre"] += 1
                    else:
                        done["seq"] = a[1]
        assert done["seq"] == NCC - 1, done
        if "orw" in dbg:
            dorw = nc.dram_tensor("dorw", [128, NOWN * 4 * 128], BF16, kind="ExternalOutput").ap()
            S.dma("sp", dorw[:, :], oT_r[:].rearrange("q a b c -> q (a b c)"), [("oT_r", None)], [])
        S.fence()
        er.close()
        S.fence()

    oT_n = eo.enter_context(nc.sbuf_tensor("oT_n", [128, NOWN, 4, 128], BF16, side="right"))
    if "nsa" in dbg:
        w_in_n = din("w_in_n", [128, 8, 1304])
        gateb_d = din("gate_b", [128, 24])
        realtok_d = din("realtok", [128, 64])
        w1_d = {"k": din("cmp_w1k", [128, 32, 256]), "v": din("cmp_w1v", [128, 32, 256])}
        peT_d = {"k": din("cmp_peTk", [128, 32]), "v": din("cmp_peTv", [128, 32])}
        b1_d = {"k": din("cmp_b1k", [128, 2]), "v": din("cmp_b1v", [128, 2])}
        w2k_d = din("cmp_w2k", [128, 2, 128])
        w2v_d = din("cmp_w2v", [128, 2, 64])
        ovlm_d = din("ovlm", [128, 4, 129])
        relb_d = din("relb_rep", [32, 8, 128])
        OHa_d = din("OHa", [32, 2048])
        OHw_d = din("OHw", [32, 256])
        OHc_d = din("OHc", [32, 6144])
        Ebig_d = din("Ebig", [128, 64, 128])
        addm_d = din("addmask", [128, 16, 128])
        scr_a = nc.dram_tensor("scr_a", [8, 128, 2048], BF16)
        scr_w = nc.dram_tensor("scr_w", [8, 128, 256], BF16)
        scr_c = nc.dram_tensor("scr_c", [8, 128, 6144], BF16)
        en = ExitStack()
        ens = ExitStack()
        alloc_stg(ens, "n", side="right")

        def TN(name, shape, dt):
            return en.enter_context(nc.sbuf_tensor("n_" + name, shape, dt))

        kselT = TN("kselT", [128, 8192], BF16)
        kwinT = TN("kwinT", [128, 8192], BF16)
        Vs = TN("Vs", [128, 64, 2, 65], BF16)
        Vw = TN("Vw", [128, 64, 2, 65], BF16)
        QT = TN("QT", [128, NOWN, 4, 128], BF16)
        GATES = TN("GATES", [128, NOWN, 24], F32)
        KcT = TN("KcT", [128, 512], BF16)
        Vca = TN("Vca", [128, 4, 2, 65], BF16)
        Ovl = TN("Ovl", [128, 4, 128], BF16)

        en1 = ExitStack()

        def TN1(name, shape, dt):
            return en1.enter_context(nc.sbuf_tensor("n_" + name, shape, dt))

        ovlm = TN1("ovlm", [128, 4, 129], F32)
        realtok = TN1("realtok", [128, 64], F32)
        gateb = TN1("gateb", [128, 24], F32)
        S.dma("sp", gateb[:], gateb_d[:, :], [], [("gateb", None)])
        S.dma("sp", realtok[:], realtok_d[:, :], [], [("realtok", None)])
        S.dma("sp", ovlm[:], ovlm_d[:, :, :], [], [("ovlm", None)])
        k.ve("pool", "tensor_copy", [("ovlm", None)], [("Ovl", None)], out=Ovl[:], in_=ovlm[:, :, 1:129])
        kcmpT = TN1("kcmpT", [128, 8192 + 32], BF16)
        vcmpT = TN1("vcmpT", [128, 8192 + 32], BF16)
        en1b = ExitStack()
        w_in_nb = en1b.enter_context(nc.sbuf_tensor("n_w_in_nb", [128, 8, 1304], BF16))
        gmixn = en1b.enter_context(nc.sbuf_tensor("n_gmixn", [128, 8], F32))
        xnT2 = [en1b.enter_context(nc.sbuf_tensor("n_xnT%d" % i, [128, 8, 512], BF16)) for i in range(2)]
        xr_n = en1b.enter_context(nc.sbuf_tensor("n_xr0", [128, D], F32))
        xr2 = [xr_n, xr_n]
        sqn = [sq, sq]
        ssn = [ss, ss]
        hbn = [hb, en1b.enter_context(nc.sbuf_tensor("n_hbx1", [128, D], BF16))]
        S.dma("sp", gmixn[:], gmix_d[:, :], [], [("gmixn", None)])
        load_w(w_in_nb, "w_in_nb", w_in_n, 8, 1304, gcol=gmixn, gname="gmixn")
        k.ve("pool", "memset", [], [("kcmpT", None)], kcmpT[:, 8192:8224], 0.0)
        k.ve("pool", "memset", [], [("vcmpT", None)], vcmpT[:, 8192:8224], 0.0)
        nbanks = [(pA, "pA"), (pB, "pB"), (pC, "pC"), (pD, "pD"), (pE, "pE"), (pF, "pF"), (pG, "pG")]
        nb_i = [0]

        def nbank():
            nb_i[0] += 1
            return nbanks[nb_i[0] % 7]

        def gen_n1(g):
            par = g % 2
            xnT = xnT2[par]
            xn_ = "n_xnT%d" % par
            for tl in range(4):
                pos = 4 * g + tl
                b2 = par
                S.dma("sp", xr2[b2][:], xs[pos * 128:(pos + 1) * 128, :], [], [("n_xr0", None)])
                rmsnorm(xr2[b2][:], ("n_xr0", None), hbn[b2][:], ("n_hb%d" % b2, None), sqn[b2], ssn[b2], "n")
                yield
                for c in range(8):
                    k.tr(pT[:, c * 128:(c + 1) * 128], hbn[b2][:, c * 128:(c + 1) * 128], ident[:], [("n_hb%d" % b2, None)], [("pT", c)])
                k.ve("dve", "tensor_copy", [("pT", None)], [(xn_, tl)], out=xnT[:, :, tl * 128:(tl + 1) * 128],
                     in_=pT[:, :].rearrange("p (c t) -> p c t", c=8))
                yield
            tok = slice(g * 512, (g + 1) * 512)
            for ci, (dst, dn) in enumerate(((kcmpT, "kcmpT"), (vcmpT, "vcmpT"), (kselT, "kselT"), (kwinT, "kwinT"))):
                pp, pn = nbank()
                for kc in range(8):
                    k.mm(pp[:, :], w_in_nb[:, kc, 512 + ci * 128:512 + (ci + 1) * 128], xnT[:, kc, :], kc == 0, kc == 7,
                         [("w_in_nb", kc), (xn_, None)], [(pn, None)])
                if ci % 2 == 0:
                    k.act(dst[:, tok], pp[:, :], AF.Copy, [(pn, None)], [(dn, g)])
                else:
                    k.ve("dve", "tensor_copy", [(pn, None)], [(dn, g)], out=dst[:, tok], in_=pp[:, :])
                yield
            for tl in range(4):
                pos = 4 * g + tl
                pp, pn = nbank()
                for kc in range(8):
                    k.mm(pp[:, 0:256], xnT[:, kc, tl * 128:(tl + 1) * 128], w_in_nb[:, kc, 1024:1280], kc == 0, kc == 7,
                         [("w_in_nb", kc), (xn_, None)], [(pn, None)])
                k.act(Vs[:, pos, :, 0:64], pp[:, 0:128].rearrange("p (a b) -> p a b", a=2), AF.Copy, [(pn, None)], [("Vs", pos)])
                k.ve("dve", "tensor_copy", [(pn, None)], [("Vw", pos)], out=Vw[:, pos, :, 0:64],
                     in_=pp[:, 128:256].rearrange("p (a b) -> p a b", a=2))
                for kv in range(2):
                    k.ve("pool", "tensor_copy", [("realtok", None)], [("Vs", pos)], out=Vs[:, pos, kv, 64:65],
                         in_=realtok[:, pos:pos + 1])
                    k.ve("pool", "tensor_copy", [("realtok", None)], [("Vw", pos)], out=Vw[:, pos, kv, 64:65],
                         in_=realtok[:, pos:pos + 1])
                yield
            for a in range(4):
                pp, pn = nbank()
                for kc in range(8):
                    k.mm(pp[:, 0:128], w_in_nb[:, kc, a * 128:(a + 1) * 128], xnT[:, kc, 384:512], kc == 0, kc == 7,
                         [("w_in_nb", kc), (xn_, None)], [(pn, None)])
                k.act(QT[:, g, a, :], pp[:, 0:128], AF.Copy, [(pn, None)], [("QT", g)])
                if a % 2 == 1:
                    yield
            pp, pn = nbank()
            for kc in range(8):
                k.mm(pp[:, 0:24], xnT[:, kc, 384:512], w_in_nb[:, kc, 1280:1304], kc == 0, kc == 7,
                     [("w_in_nb", kc), (xn_, None)], [(pn, None)])
            k.ve("dve", "tensor_tensor", [(pn, None), ("gateb", None)], [("GATES", g)], out=GATES[:, g, :], in0=pp[:, 0:24],
                 in1=gateb[:], op=ALU.add)
            k.act(GATES[:, g, :], GATES[:, g, :], AF.Sigmoid, [("GATES", g)], [("GATES", g)])
            yield

        actn = []
        nxtn = 0
        while True:
            while nxtn < 16 and len(actn) < 2:
                actn.append((nxtn, gen_n1(nxtn)))
                nxtn += 1
            if not actn:
                break
            for a in list(actn):
                try:
                    next(a[1])
                except StopIteration:
                    actn.remove(a)

        S.fence()
        en1b.close()
        S.fence()
        with nc.sbuf_tensor("n_w1b", [128, 32, 256], BF16) as w1b, \
                nc.sbuf_tensor("n_peT", [128, 32], F32) as peTf, \
                nc.sbuf_tensor("n_peTb", [128, 32], BF16) as peTb, \
                nc.sbuf_tensor("n_b1", [128, 2], F32) as b1t, \
                nc.sbuf_tensor("n_w2f", [128, 2, 128], F32) as w2f, \
                nc.sbuf_tensor("n_w2b", [128, 2, 128], BF16) as w2b, \
                nc.sbuf_tensor("n_hb1", [128, 2], F32) as hbias, \
                nc.sbuf_tensor("n_z", [128, 512], F32) as zt, \
                nc.sbuf_tensor("n_z2", [128, 512], F32) as z2t, \
                nc.sbuf_tensor("n_ge", [128, 2, 512], BF16) as ge:
            for X, srcT, sn in (("k", kcmpT, "kcmpT"), ("v", vcmpT, "vcmpT")):
                for part in range(8):
                    i = part % 2
                    S.dma("sp" if i == 0 else "pool", stg[i][:, :],
                          w1_d[X][:, part * 4:(part + 1) * 4, :].rearrange("p a b -> p (a b)"), [], [("stg%d" % i, None)])
                    k.ve("dve" if i == 0 else "pool", "tensor_copy", [("stg%d" % i, None)], [("w1b", part)],
                         out=w1b[:, part * 4:(part + 1) * 4, :].rearrange("p a b -> p (a b)"), in_=stg[i][:, :])
                S.dma("sp", peTf[:], peT_d[X][:, :], [], [("peTf", None)])
                k.ve("dve", "tensor_copy", [("peTf", None)], [("peTb", None)], out=peTb[:], in_=peTf[:])
                S.dma("sp", b1t[:], b1_d[X][:, :], [], [("b1t", None)])
                if X == "k":
                    S.dma("sp", w2f[:], w2k_d[:, :, :], [], [("w2f", None)])
                    k.ve("dve", "tensor_copy", [("w2f", None)], [("w2b", None)], out=w2b[:], in_=w2f[:])
                else:
                    S.dma("sp", w2f[:, :, 0:64], w2v_d[:, :, :], [], [("w2f", None)])
                    k.ve("dve", "tensor_copy", [("w2f", None)], [("w2b", None)], out=w2b[:, :, 0:64], in_=w2f[:, :, 0:64])
                for hc in range(2):
                    for l in range(32):
                        k.mm(pD[:, hc:hc + 1], w1b[0:64, l, hc * 128:(hc + 1) * 128], peTb[0:64, l:l + 1], l == 0, l == 31,
                             [("w1b", None), ("peTb", None)], [("pD", None)])
                k.ve("dve", "tensor_tensor", [("pD", None), ("b1t", None)], [("hbias", None)], out=hbias[:], in0=pD[:, 0:2],
                     in1=b1t[:], op=ALU.add)
                for kv in range(2):
                    rows = slice(kv * 64, kv * 64 + 64)
                    for hc in range(2):
                        pp, pn = (pA, "pA") if hc == 0 else (pB, "pB")
                        for l in range(32):
                            rhs = srcT[rows, l:l + 8192].rearrange("q (c s) -> q c s", s=16)[:, :, 0]
                            k.mm(pp[:, :], w1b[rows, l, hc * 128:(hc + 1) * 128], rhs, l == 0, l == 31,
                                 [("w1b", None), (sn, None)], [(pn, None)])
                        k.act(zt[:], pp[:, :], AF.Identity, [(pn, None), ("hbias", None)], [("zt", None)], bias=hbias[:, hc:hc + 1])
                        k.ve("dve", "tensor_tensor", [("zt", None)], [("z2t", None)], out=z2t[:], in0=zt[:], in1=zt[:], op=ALU.mult)
                        k.ve("dve", "tensor_scalar", [("z2t", None)], [("z2t", None)], out=z2t[:], in0=z2t[:], scalar1=0.044715,
                             scalar2=1.0, op0=ALU.mult, op1=ALU.add)
                        k.ve("pool", "tensor_tensor", [("z2t", None), ("zt", None)], [("z2t", None)], out=z2t[:], in0=z2t[:],
                             in1=zt[:], op=ALU.mult)
                        k.act(z2t[:], z2t[:], AF.Sigmoid, [("z2t", None)], [("z2t", None)], scale=1.5957691216057308)
                        k.ve("dve", "tensor_tensor", [("z2t", None), ("zt", None)], [("ge", hc)], out=ge[:, hc, :], in0=z2t[:],
                             in1=zt[:], op=ALU.mult)
                    if X == "k":
                        for hc in range(2):
                            k.mm(pC[:, :], w2b[:, hc, :], ge[:, hc, :], hc == 0, hc == 1, [("w2b", None), ("ge", None)],
                                 [("pC", None)])
                        k.ve("dve", "tensor_copy", [("pC", None)], [("KcT", kv)], out=KcT[rows, :], in_=pC[rows, :])
                    else:
                        for ct in range(4):
                            for hc in range(2):
                                k.mm(pC[:, 0:64], ge[:, hc, ct * 128:(ct + 1) * 128], w2b[:, hc, 0:64], hc == 0, hc == 1,
                                     [("w2b", None), ("ge", None)], [("pC", None)])
                            k.ve("dve", "tensor_scalar", [("pC", None), ("ovlm", None)], [("Vca", None)],
                                 out=Vca[:, ct, kv, 0:64], in0=pC[:, 0:64], scalar1=ovlm[:, ct, 0:1], scalar2=None, op0=ALU.mult)
                            k.ve("pool", "tensor_copy", [("ovlm", None)], [("Vca", None)], out=Vca[:, ct, kv, 64:65],
                                 in_=ovlm[:, ct, 0:1])
        S.fence()
        en1.close()
        S.fence()

        MBs = TN("MBs", [128, 8, 14, 128], BF16)
        MBw4 = TN("MBw4", [128, 8, 128], BF16)
        MBc = TN("MBc", [128, 8, 8, 128], BF16)
        Ebig = TN("Ebig", [128, 64, 128], BF16)
        for c8 in range(8):
            i = c8 % 2
            S.dma("sp" if i == 0 else "pool", stg[i][:, :], Ebig_d[:, c8 * 8:(c8 + 1) * 8, :].rearrange("p a b -> p (a b)"),
                  [], [("stg%d" % i, None)])
            k.ve("dve" if i == 0 else "pool", "tensor_copy", [("stg%d" % i, None)], [("Ebig", c8)],
                 out=Ebig[:, c8 * 8:(c8 + 1) * 8, :].rearrange("p a b -> p (a b)"), in_=stg[i][:, :])
        with nc.sbuf_tensor("n_relb", [32, 8, 128], F32) as relb, \
                nc.sbuf_tensor("n_OH0", [32, 512], F32) as OH0, \
                nc.sbuf_tensor("n_OH1", [32, 512], F32) as OH1, \
                nc.sbuf_tensor("n_Fv0", [128, 512], BF16) as Fv0, \
                nc.sbuf_tensor("n_Fv1", [128, 512], BF16) as Fv1:
            S.dma("sp", relb[:], relb_d[:, :, :], [], [("relb", None)])
            k.act(relb[:].rearrange("p a b -> p (a b)"), relb[:].rearrange("p a b -> p (a b)"), AF.Exp, [("relb", None)],
                  [("relb", None)])
            oi = 0
            for (OHd, L, scr, sname) in ((OHa_d, 2048, scr_a, "scr_a"), (OHw_d, 256, scr_w, "scr_w"), (OHc_d, 6144, scr_c, "scr_c")):
                for h in range(8):
                    for c0 in range(0, L, 512):
                        c1 = min(L, c0 + 512)
                        oi += 1
                        OH, OHn = (OH0, "OH0") if oi % 2 else (OH1, "OH1")
                        Fv, Fvn = (Fv0, "Fv0") if oi % 2 else (Fv1, "Fv1")
                        pp, pn = (pA, "pA") if oi % 2 else (pB, "pB")
                        S.dma("sp", OH[:, 0:c1 - c0], OHd[:, c0:c1], [], [(OHn, None)])
                        k.mm(pp[:, 0:c1 - c0], relb[:, h, :], OH[:, 0:c1 - c0], True, True, [("relb", None), (OHn, None)],
                             [(pn, None)])
                        if oi % 2:
                            k.act(Fv[:, 0:c1 - c0], pp[:, 0:c1 - c0], AF.Copy, [(pn, None)], [(Fvn, None)])
                        else:
                            k.ve("dve", "tensor_copy", [(pn, None)], [(Fvn, None)], out=Fv[:, 0:c1 - c0], in_=pp[:, 0:c1 - c0])
                        S.dma("pool", scr.ap()[h, :, c0:c1], Fv[:, 0:c1 - c0], [(Fvn, None)], [(sname, h)])
            for h in range(8):
                src = bass.AP(scr_a, h * 128 * 2048 + 127, [[2047, 128], [128, 14], [1, 128]])
                S.dma("pool", MBs[:, h, :, :], src, [("scr_a", h)], [("MBs", h)])
                src = bass.AP(scr_w, h * 128 * 256 + 127, [[255, 128], [1, 128]])
                S.dma("pool", MBw4[:, h, :], src, [("scr_w", h)], [("MBw4", h)])
                src = bass.AP(scr_c, h * 128 * 6144 + 128 * 3 + 2033, [[6144 - 16, 128], [512, 8], [1, 128]])
                S.dma("pool", MBc[:, h, :, :], src, [("scr_c", h)], [("MBc", h)])
        S.fence()

        S.fence()
        ens.close()
        S.fence()
        en2 = ExitStack()

        def TN2(name, shape, dt):
            return en2.enter_context(nc.sbuf_tensor("n_" + name, shape, dt))

        Pt = [TN2("Pt%d" % i, [128, 4, 128], BF16) for i in range(4)]
        AB_ = []
        for i in range(2):
            AB_.append(dict(
                i=i,
                Eall=TN2("Eall%d" % i, [128, 4, 4, 128], BF16),
                score=TN2("score%d" % i, [128, 128], F32),
                scr2=TN2("scr2%d" % i, [128, 128], F32),
                mx8=TN2("mx8%d" % i, [128, 8], F32),
                thr=TN2("thr%d" % i, [128, 1], F32),
                selb=TN2("selb%d" % i, [128, 128], BF16),
                selbT=TN2("selbT%d" % i, [128, 4, 128], BF16),
                densA=TN2("densA%d" % i, [128, 4], F32),
                coefA=TN2("coefA%d" % i, [128, 4], F32),
                Ccon=TN2("Ccon%d" % i, [128, 4, 64], F32),
            ))
        addm2 = [TN2("addm%d" % i, [128, 128], F32) for i in range(2)]
        dens = TN2("dens", [128, 8], F32)
        coef = TN2("coef", [128, 8], F32)
        Oacc = TN2("Oacc", [128, 8, 64], F32)
        Ob = TN2("Ob", [128, 512], BF16)
        sbanks = ((pA, "pA"), (pB, "pB"), (pF, "pF"), (pG, "pG"))
        sb_i = [0]

        def sbank():
            sb_i[0] += 1
            return sbanks[sb_i[0] % 4], sb_i[0] % 4

        def gen_A(n):
            it, gk = n // 2, n % 2
            Q = 4 * it + 3
            B = AB_[n % 2]
            bi = B["i"]
            nm = lambda s: (s + str(bi), None)
            Eall, score, scr2, mx8, thr, selb, selbT = B["Eall"], B["score"], B["scr2"], B["mx8"], B["thr"], B["selb"], B["selbT"]
            densA, coefA, Ccon = B["densA"], B["coefA"], B["Ccon"]
            addm = addm2[it % 2]
            addn = ("addm%d" % (it % 2), None)
            if gk == 0:
                S.dma("sp", addm[:], addm_d[:, it, :], [], [addn])
            rows = slice(gk * 64, gk * 64 + 64)
            qrhs = QT[rows, it, :, :].rearrange("q a t -> q (a t)")
            hs = slice(4 * gk, 4 * gk + 4)
            nct = min(4, Q // 16 + 1)
            for ct in range(nct):
                pp, pn = (pC, "pC") if ct % 2 == 0 else (pD, "pD")
                k.mm(pp[:, :], KcT[rows, ct * 128:(ct + 1) * 128], qrhs, True, True, [("KcT", None), ("QT", it)], [(pn, None)])
                k.act(Eall[:, ct, :, :].rearrange("p a t -> p (a t)"), pp[:, :], AF.Exp, [(pn, None)], [("Eall%d" % bi, ct)],
                      scale=0.125)
                n_ = min((Q - 16 * ct - 3) // 4, 7)
                k.ve("pool", "tensor_tensor", [("Eall%d" % bi, ct), ("MBc", None)], [("Eall%d" % bi, ct)], out=Eall[:, ct, :, :],
                     in0=Eall[:, ct, :, :], in1=MBc[:, hs, n_, :], op=ALU.mult)
                yield
            k.ve("dve", "memset", [], [("pC", None)], pC[:, 0:260], 0.0)
            k.ve("dve", "memset", [], [("pD", None)], pD[:, :], 0.0)
            for a in range(4):
                for ct in range(nct):
                    k.mm(pC[:, a * 65:(a + 1) * 65], Eall[:, ct, a, :], Vca[:, ct, gk, :], False, ct == nct - 1,
                         [("Eall%d" % bi, ct), ("Vca", None)], [("pC", None)])
                for ct in range(nct):
                    k.mm(pD[:, a * 128:(a + 1) * 128], Eall[:, ct, a, :], Ovl[:, ct, :], False, ct == nct - 1,
                         [("Eall%d" % bi, ct), ("Ovl", None)], [("pD", None)])
            yield
            k.ve("dve", "tensor_scalar", [("pC", None)], [nm("densA")], out=densA[:, 0:4],
                 in0=pC[:, 0:260].rearrange("p (a d) -> p a d", a=4)[:, :, 64], scalar1=1e-30, scalar2=None, op0=ALU.max)
            yield
            k.ve("dve", "reciprocal", [nm("densA")], [nm("densA")], out=densA[:, 0:4], in_=densA[:, 0:4])
            k.ve("pool", "tensor_copy", [addn], [nm("score")], out=score[:], in_=addm[:])
            yield
            for a in range(4):
                k.ve("dve", "scalar_tensor_tensor", [("pD", None), nm("densA"), nm("score")], [nm("score")],
                     out=score[:], in0=pD[:, a * 128:(a + 1) * 128], scalar=densA[:, a:a + 1], in1=score[:], op0=ALU.mult,
                     op1=ALU.add)
                yield
            k.ve("dve", "max", [nm("score")], [nm("mx8")], out=mx8[:], in_=score[:])
            yield
            k.ve("dve", "match_replace", [nm("score"), nm("mx8")], [nm("scr2")], out=scr2[:],
                 in_to_replace=mx8[:], in_values=score[:], imm_value=-3.0e9)
            yield
            k.ve("dve", "max", [nm("scr2")], [nm("mx8")], out=mx8[:], in_=scr2[:])
            yield
            k.ve("dve", "tensor_reduce", [nm("mx8")], [nm("thr")], out=thr[:, 0:1], in_=mx8[:], axis=AX.X, op=ALU.min)
            yield
            k.ve("dve", "tensor_scalar", [nm("score"), nm("thr")], [nm("selb")], out=selb[:], in0=score[:],
                 scalar1=thr[:, 0:1], scalar2=-30000.0, op0=ALU.is_lt, op1=ALU.mult)
            yield
            for a in range(4):
                k.tr(pT[:, a * 128:(a + 1) * 128], selb[:], ident[:], [nm("selb")], [("pT", a)])
            k.ve("dve", "tensor_copy", [("pT", None)], [nm("selbT")], out=selbT[:].rearrange("p a t -> p (a t)"),
                 in_=pT[:, 0:512])
            yield
            k.ve("pool", "tensor_tensor", [nm("densA"), ("GATES", it)], [nm("coefA")], out=coefA[:, 0:4], in0=densA[:, 0:4],
                 in1=GATES[:, it, :].rearrange("p (h j) -> p h j", j=3)[:, hs, 0], op=ALU.mult)
            yield
            for a in range(4):
                k.ve("dve", "tensor_scalar", [("pC", None), nm("coefA")], [nm("Ccon")], out=Ccon[:, a, :],
                     in0=pC[:, a * 65:a * 65 + 64], scalar1=coefA[:, a:a + 1], scalar2=None, op0=ALU.mult)
            yield

        def gen_B(n):
            it, gk = n // 2, n % 2
            Q = 4 * it + 3
            B = AB_[n % 2]
            bi = B["i"]
            selbT, Ccon = B["selbT"], B["Ccon"]
            rows = slice(gk * 64, gk * 64 + 64)
            qrhs = QT[rows, it, :, :].rearrange("q a t -> q (a t)")
            hs = slice(4 * gk, 4 * gk + 4)
            for bq, br in enumerate((1, 0)):
                if br == 0:
                    kT, kTn, Vt, Vn, kts = kselT, "kselT", Vs, "Vs", list(range(0, Q + 1))
                else:
                    kT, kTn, Vt, Vn, kts = kwinT, "kwinT", Vw, "Vw", list(range(max(Q - 4, 0), Q + 1))
                k.ve("dve", "memset", [], [("pE", None)], pE[:, 0:260], 0.0)
                SK = 3
                nk = len(kts)
                slots = []
                for step in range(nk + SK):
                    if step < nk:
                        kt = kts[step]
                        m = Q - kt
                        (pp, pn), bidx = sbank()
                        slots.append(bidx)
                        k.mm(pp[:, :], kT[rows, kt * 128:(kt + 1) * 128], qrhs, True, br == 1, [(kTn, None), ("QT", it)],
                             [(pn, None)])
                        if br == 0:
                            k.mm(pp[:, :], Ebig[:, kt, :], selbT[:].rearrange("p a t -> p (a t)"), False, True,
                                 [("Ebig", None), ("selbT%d" % bi, None)], [(pn, None)])
                        P = Pt[bidx]
                        Pn = "Pt%d" % bidx
                        k.act(P[:].rearrange("p a t -> p (a t)"), pp[:, :], AF.Exp, [(pn, None)], [(Pn, None)], scale=0.125)
                        if br == 1 and m == 4:
                            mb = MBw4[:, hs, :]
                            mbn = ("MBw4", None)
                        else:
                            mb = MBs[:, hs, min(m, 13), :]
                            mbn = ("MBs", None)
                        k.ve("dve", "tensor_tensor", [(Pn, None), mbn], [(Pn, None)], out=P[:],
                             in0=P[:], in1=mb, op=ALU.mult)
                    if step >= SK:
                        kt = kts[step - SK]
                        bidx = slots[step - SK]
                        P = Pt[bidx]
                        Pn = "Pt%d" % bidx
                        for a in range(4):
                            k.mm(pE[:, a * 65:(a + 1) * 65], P[:, a, :], Vt[:, kt, gk, :], False, kt == kts[-1],
                                 [(Pn, None), (Vn, kt)], [("pE", None)])
                    yield
                c0 = 4 * bq
                k.ve("dve", "tensor_scalar", [("pE", None)], [("dens", None)], out=dens[:, c0:c0 + 4],
                     in0=pE[:, 0:260].rearrange("p (a d) -> p a d", a=4)[:, :, 64], scalar1=1e-30, scalar2=None, op0=ALU.max)
                k.ve("dve", "reciprocal", [("dens", None)], [("dens", None)], out=dens[:, c0:c0 + 4], in_=dens[:, c0:c0 + 4])
                k.ve("dve", "tensor_tensor", [("dens", None), ("GATES", it)], [("coef", None)], out=coef[:, c0:c0 + 4],
                     in0=dens[:, c0:c0 + 4], in1=GATES[:, it, :].rearrange("p (h j) -> p h j", j=3)[:, hs, 1 + br], op=ALU.mult)
                for a in range(4):
                    if bq == 0:
                        in1, in1n = Ccon[:, a, :], ("Ccon%d" % bi, None)
                    else:
                        in1, in1n = Oacc[:, 4 * gk + a, :], ("Oacc", None)
                    k.ve("dve", "scalar_tensor_tensor", [("pE", None), ("coef", None), in1n], [("Oacc", None)],
                         out=Oacc[:, 4 * gk + a, :], in0=pE[:, a * 65:a * 65 + 64], scalar=coef[:, c0 + a:c0 + a + 1],
                         in1=in1, op0=ALU.mult, op1=ALU.add)
                yield
            if gk == 1:
                k.ve("dve", "tensor_copy", [("Oacc", None)], [("Ob", None)], out=Ob[:], in_=Oacc[:].rearrange("p h d -> p (h d)"))
                for c in range(4):
                    k.tr(pT[:, c * 128:(c + 1) * 128], Ob[:, c * 128:(c + 1) * 128], ident[:], [("Ob", None)], [("pT", c)])
                k.ve("dve", "tensor_copy", [("pT", None)], [("oT_n", it)], out=oT_n[:, it, :, :].rearrange("p c t -> p (c t)"),
                     in_=pT[:, 0:512])
                yield

        NN = 2 * NOWN
        doneA, doneB = -1, -1
        nxtA, nxtB = 0, 0
        activeN = []
        while True:
            if nxtA < NN and not any(a[0] == "A" for a in activeN) and nxtA <= doneB + 2:
                activeN.append(("A", nxtA, gen_A(nxtA)))
                nxtA += 1
            if nxtB < NN and not any(a[0] == "B" for a in activeN) and doneA >= nxtB:
                activeN.append(("B", nxtB, gen_B(nxtB)))
                nxtB += 1
            if not activeN:
                break
            for a in list(activeN):
                try:
                    next(a[2])
                except StopIteration:
                    activeN.remove(a)
                    if a[0] == "A":
                        doneA = a[1]
                    else:
                        doneB = a[1]
        assert doneB == NN - 1
        if "onsa" in dbg:
            donsa = nc.dram_tensor("donsa", [128, NOWN * 4 * 128], BF16, kind="ExternalOutput").ap()
            S.dma("sp", donsa[:, :], oT_n[:].rearrange("q a b c -> q (a b c)"), [("oT_n", None)], [])
        S.fence()
        en2.close()
        en.close()
        S.fence()

    if "stop_r" in dbg:
        S.finish()
        return nc
    es_stgc = ExitStack()
    alloc_stg(es_stgc, "c")
    gx = nc.alloc_sbuf_tensor("gx", [128, 8], F32)
    gffn = nc.alloc_sbuf_tensor("gffn", [128, 8], F32)
    gmem = nc.alloc_sbuf_tensor("gmem", [128, 8], F32)
    S.dma("sp", gx[:], gl["g_x"][:, :], [], [("gx", None)])
    S.dma("sp", gffn[:], gl["g_ffn"][:, :], [], [("gffn", None)])
    S.dma("sp", gmem[:], gl["g_mem"][:, :], [], [("gmem", None)])

    gf1 = nc.alloc_sbuf_tensor("gf1", [1, D], F32)
    gfb = nc.alloc_sbuf_tensor("gfb", [128, D], F32)
    S.dma("sp", gf1[:], g_f[:, :], [], [("gf1", None)])
    for h in range(2):
        pp = pA if h == 0 else pB
        k.mm(pp[:, :], ones_f[0:1, :], gf1[0:1, h * 512:(h + 1) * 512], True, True,
             [("ones_f", None), ("gf1", None)], [("pA" if h == 0 else "pB", None)])
        k.ve("dve", "tensor_copy", [("pA" if h == 0 else "pB", None)], [("gfb", None)],
             out=gfb[:, h * 512:(h + 1) * 512], in_=pp[:, :])

    x2all = nc.alloc_sbuf_tensor("x2all", [128, NOWN, D], F32)
    h2T = nc.alloc_sbuf_tensor("h2T", [128, NOWN, 8, 128], BF16)
    hT = nc.alloc_sbuf_tensor("hT", [128, 8, 128], BF16)

    if "omix" not in dbg:
        with nc.sbuf_tensor("w_out_b0", [128, 8, D], BF16) as w_out_b0, \
                nc.sbuf_tensor("xt0", [128, D], F32) as xt0:
            load_w(w_out_b0, "w_out_b0", wl["w_out"], 8, D)
            for it in range(NOWN):
                pos = 4 * it + 3
                S.dma("sp", xt0[:], xs[pos * 128:(pos + 1) * 128, :], [], [("xt0", None)])
                for hf in range(2):
                    pp, pn = (pA, "pA") if hf == 0 else (pB, "pB")
                    for kc in range(8):
                        if kc < 4:
                            lh, ln = oT_n[:, it, kc, :], ("oT_n", it)
                        else:
                            lh, ln = oT_r[:, it, kc - 4, :], ("oT_r", it)
                        k.mm(pp[:, :], lh, w_out_b0[:, kc, hf * 512:(hf + 1) * 512], kc == 0, kc == 7,
                             [ln, ("w_out_b0", kc)], [(pn, None)])
                    k.ve("dve", "tensor_tensor", [(pn, None), ("xt0", None)], [("x2all", it)],
                         out=x2all[:, it, hf * 512:(hf + 1) * 512], in0=pp[:, :], in1=xt0[:, hf * 512:(hf + 1) * 512],
                         op=ALU.add)
        S.fence()
        eo.close()
        S.fence()
    es_c1 = ExitStack()
    kmT = es_c1.enter_context(nc.sbuf_tensor("kmT", [128, 8, 256], BF16))
    Vm = es_c1.enter_context(nc.sbuf_tensor("Vm", [128, 2, D], BF16))
    with nc.sbuf_tensor("w_kv_b", [128, 8, 2 * D], BF16) as w_kv_b, \
            nc.sbuf_tensor("memf", [128, D], F32) as memf, \
            nc.sbuf_tensor("memT", [128, 8, 256], BF16) as memT:
        load_w(w_kv_b, "w_kv_b", wl["w_kv"], 8, 2 * D, gcol=gmem, gname="gmem")
        for mt in range(2):
            S.dma("sp", memf[:], mem[mt * 128:(mt + 1) * 128, :], [], [("memf", None)])
            rmsnorm(memf[:], ("memf", None), hb[:], ("hb", None), sq, ss, "")
            for c in range(8):
                k.tr(pT[:, c * 128:(c + 1) * 128], hb[:, c * 128:(c + 1) * 128], ident[:], [("hb", None)], [("pT", c)])
            k.ve("dve", "tensor_copy", [("pT", None)], [("memT", mt)],
                 out=memT[:, :, mt * 128:(mt + 1) * 128], in_=pT[:, :].rearrange("p (c t) -> p c t", c=8))
        for c in range(8):
            for kc in range(8):
                k.mm(pA[:, 0:256], w_kv_b[:, kc, c * 128:(c + 1) * 128], memT[:, kc, :], kc == 0, kc == 7,
                     [("w_kv_b", kc), ("memT", None)], [("pA", None)])
            k.ve("dve", "tensor_copy", [("pA", None)], [("kmT", c)], out=kmT[:, c, :], in_=pA[:, 0:256])
        for mt in range(2):
            for hf in range(2):
                for kc in range(8):
                    k.mm(pA[:, :], memT[:, kc, mt * 128:(mt + 1) * 128],
                         w_kv_b[:, kc, D + hf * 512:D + (hf + 1) * 512], kc == 0, kc == 7,
                         [("w_kv_b", kc), ("memT", None)], [("pA", None)])
                k.ve("dve", "tensor_copy", [("pA", None)], [("Vm", mt)], out=Vm[:, mt, hf * 512:(hf + 1) * 512],
                     in_=pA[:, :])
    S.fence()
    with nc.sbuf_tensor("w_q_b", [128, 8, D], BF16) as w_q_b, \
            nc.sbuf_tensor("w_o_b", [128, 8, D], BF16) as w_o_b, \
            nc.sbuf_tensor("c_sq1", [128, D], BF16) as c_sq1, \
            nc.sbuf_tensor("c_ss1", [128, 2], F32) as c_ss1, \
            nc.sbuf_tensor("c_hb1", [128, D], BF16) as c_hb1, \
            nc.sbuf_tensor("c_hT1", [128, 8, 128], BF16) as c_hT1, \
            nc.sbuf_tensor("qT0", [128, 8, 128], BF16) as qT0, \
            nc.sbuf_tensor("qT1", [128, 8, 128], BF16) as qT1, \
            nc.sbuf_tensor("PT0", [128, 2, 128], BF16) as PT0, \
            nc.sbuf_tensor("PT1", [128, 2, 128], BF16) as PT1, \
            nc.sbuf_tensor("rden0", [128, 128], F32) as rden0, \
            nc.sbuf_tensor("rden1", [128, 128], F32) as rden1, \
            nc.sbuf_tensor("oT0", [128, 8, 128], BF16) as oT0, \
            nc.sbuf_tensor("oT1", [128, 8, 128], BF16) as oT1:
        load_w(w_q_b, "w_q_b", wl["w_q"], 8, D, gcol=gx, gname="gx")
        load_w(w_o_b, "w_o_b", wl["w_o"], 8, D)
        CS = [dict(i=0, sq=sq, ss=ss, hb=hb, hT=hT, qT=qT0, PT=PT0, rden=rden0, oT=oT0),
              dict(i=1, sq=c_sq1, ss=c_ss1, hb=c_hb1, hT=c_hT1, qT=qT1, PT=PT1, rden=rden1, oT=oT1)]
        cbanks = [(pA, "pA"), (pB, "pB"), (pC, "pC"), (pD, "pD"), (pE, "pE"), (pF, "pF"), (pG, "pG")]
        cb_i = [0]

        def cbank():
            cb_i[0] += 1
            return cbanks[cb_i[0] % 7]

        def c_norm(src, sname, B):
            bi = B["i"]
            S.op("act", lambda: nc.scalar.activation(out=B["sq"][:], in_=src, func=AF.Square, accum_out=B["ss"][:, 0:1]),
                 [sname], [("c_sq%d" % bi, None), ("c_ss%d" % bi, None)])
            k.act(B["ss"][:, 1:2], B["ss"][:, 0:1], AF.Ln, [("c_ss%d" % bi, None)], [("c_ss%d" % bi, None)], bias=epsc[:, 0:1],
                  scale=1.0 / D)
            k.act(B["ss"][:, 1:2], B["ss"][:, 1:2], AF.Exp, [("c_ss%d" % bi, None)], [("c_ss%d" % bi, None)], scale=-0.5)
            k.ve("dve", "tensor_scalar", [sname, ("c_ss%d" % bi, None)], [("c_hb%d" % bi, None)], out=B["hb"][:], in0=src,
                 scalar1=B["ss"][:, 1:2], scalar2=None, op0=ALU.mult)

        def gen_c1(it, B):
            bi = B["i"]
            hbn, hTn, qTn, PTn, rdn, oTn = [("c_%s%d" % (s, bi), None) for s in ("hb", "hT", "qT", "PT", "rden", "oT")]
            c_norm(x2all[:, it, :], ("x2all", it), B)
            yield
            for c in range(8):
                k.tr(pT[:, c * 128:(c + 1) * 128], B["hb"][:, c * 128:(c + 1) * 128], ident[:], [hbn], [("pT", c)])
            k.ve("dve", "tensor_copy", [("pT", None)], [hTn], out=B["hT"][:].rearrange("p c t -> p (c t)"), in_=pT[:, :])
            yield
            for c in range(8):
                pp, pn = cbank()
                for kc in range(8):
                    k.mm(pp[:, 0:128], w_q_b[:, kc, c * 128:(c + 1) * 128], B["hT"][:, kc, :], kc == 0, kc == 7,
                         [("w_q_b", kc), hTn], [(pn, None)])
                k.act(B["qT"][:, c, :], pp[:, 0:128], AF.Copy, [(pn, None)], [("c_qT%d" % bi, c)])
                if c % 2 == 1:
                    yield
            for h in range(4):
                pp, pn = cbank()
                for mc in range(2):
                    for dc in range(2):
                        k.mm(pp[:, mc * 128:(mc + 1) * 128], kmT[:, 2 * h + dc, mc * 128:(mc + 1) * 128],
                             B["qT"][:, 2 * h + dc, :], dc == 0, dc == 1, [("kmT", None), qTn], [(pn, None)])
                k.act(B["PT"][:].rearrange("p c t -> p (c t)"), pp[:, 0:256], AF.Exp, [(pn, None)], [PTn], scale=1.0 / 16.0)
                yield
                pp, pn = cbank()
                for mc in range(2):
                    k.mm(pp[:, 0:128], ones_b[:, :], B["PT"][:, mc, :], mc == 0, mc == 1, [("ones_b", None), PTn], [(pn, None)])
                k.ve("dve", "reciprocal", [(pn, None)], [rdn], out=B["rden"][:], in_=pp[:, 0:128])
                for dc in range(2):
                    pp, pn = cbank()
                    for mc in range(2):
                        k.mm(pp[:, 0:128], Vm[:, mc, h * 256 + dc * 128:h * 256 + (dc + 1) * 128], B["PT"][:, mc, :], mc == 0,
                             mc == 1, [("Vm", None), PTn], [(pn, None)])
                    k.ve("dve", "tensor_tensor", [(pn, None), rdn], [("c_oT%d" % bi, 2 * h + dc)], out=B["oT"][:, 2 * h + dc, :],
                         in0=pp[:, 0:128], in1=B["rden"][:], op=ALU.mult)
                yield
            for hf in range(2):
                pp, pn = cbank()
                for kc in range(8):
                    k.mm(pp[:, :], B["oT"][:, kc, :], w_o_b[:, kc, hf * 512:(hf + 1) * 512], kc == 0, kc == 7,
                         [oTn, ("w_o_b", kc)], [(pn, None)])
                k.ve("dve", "tensor_tensor", [(pn, None), ("x2all", it)], [("x2all", it)],
                     out=x2all[:, it, hf * 512:(hf + 1) * 512], in0=pp[:, :], in1=x2all[:, it, hf * 512:(hf + 1) * 512],
                     op=ALU.add)
                yield
            c_norm(x2all[:, it, :], ("x2all", it), B)
            yield
            for c in range(8):
                k.tr(pT[:, c * 128:(c + 1) * 128], B["hb"][:, c * 128:(c + 1) * 128], ident[:], [hbn], [("pT", c)])
            k.ve("dve", "tensor_copy", [("pT", None)], [("h2T", it)], out=h2T[:, it, :, :].rearrange("p c t -> p (c t)"),
                 in_=pT[:, :])
            yield

        actC = []
        nxtC = 0
        while True:
            while nxtC < NOWN and len(actC) < 2:
                used = {a[0] for a in actC}
                bsi = 0 if 0 not in used else 1
                actC.append((bsi, gen_c1(nxtC, CS[bsi])))
                nxtC += 1
            if not actC:
                break
            for a in list(actC):
                try:
                    next(a[1])
                except StopIteration:
                    actC.remove(a)

    es_c1.close()
    eo.close()
    S.fence()
    with nc.sbuf_tensor("wg_b", [128, 8, 1408], BF16) as wg_b, \
            nc.sbuf_tensor("wu_b", [128, 8, 1408], BF16) as wu_b, \
            nc.sbuf_tensor("wd_b", [128, 11, D], BF16) as wd_b, \
            nc.sbuf_tensor("sg", [128, 128], F32) as sg, \
            nc.sbuf_tensor("sg2", [128, 128], F32) as sg2, \
            nc.sbuf_tensor("aT", [128, 11, 128], BF16) as aT, \
            nc.sbuf_tensor("yo", [128, D], F32) as yo:
        for half in range(2):
            load_w(wg_b, "wg_b", wl["w_gate"], 8, 1408, gcol=gffn, gname="gffn", c_lo=half * 1408)
            load_w(wu_b, "wu_b", wl["w_up"], 8, 1408, gcol=gffn, gname="gffn", c_lo=half * 1408)
            load_w(wd_b, "wd_b", wl["w_down"], 11, D, kc_lo=half * 11)
            for it in range(NOWN):
                for fc in range(11):
                    pg, pgn = (pC, "pC") if fc % 2 == 0 else (pE, "pE")
                    pu, pun = (pD, "pD") if fc % 2 == 0 else (pF, "pF")
                    sgt, sgn = (sg, "sg") if fc % 2 == 0 else (sg2, "sg2")
                    for kc in range(8):
                        k.mm(pg[:, 0:128], wg_b[:, kc, fc * 128:(fc + 1) * 128], h2T[:, it, kc, :], kc == 0, kc == 7,
                             [("wg_b", kc), ("h2T", it)], [(pgn, None)])
                    for kc in range(8):
                        k.mm(pu[:, 0:128], wu_b[:, kc, fc * 128:(fc + 1) * 128], h2T[:, it, kc, :], kc == 0, kc == 7,
                             [("wu_b", kc), ("h2T", it)], [(pun, None)])
                    k.act(sgt[:], pg[:, 0:128], AF.Silu, [(pgn, None)], [(sgn, None)])
                    k.ve("dve", "tensor_tensor", [(sgn, None), (pun, None)], [("aT", fc)], out=aT[:, fc, :], in0=sgt[:],
                         in1=pu[:, 0:128], op=ALU.mult)
                for hf in range(2):
                    pp, pn = (pA, "pA") if hf == 0 else (pB, "pB")
                    for fc in range(11):
                        k.mm(pp[:, :], aT[:, fc, :], wd_b[:, fc, hf * 512:(hf + 1) * 512], fc == 0, fc == 10,
                             [("aT", None), ("wd_b", fc)], [(pn, None)])
                    k.ve("dve", "tensor_tensor", [(pn, None), ("x2all", it)], [("x2all", it)],
                         out=x2all[:, it, hf * 512:(hf + 1) * 512], in0=pp[:, :],
                         in1=x2all[:, it, hf * 512:(hf + 1) * 512], op=ALU.add)
                if half == 1:
                    S.op("act", lambda it=it: nc.scalar.activation(out=sq[:], in_=x2all[:, it, :], func=AF.Square,
                                                                   accum_out=ss[:, 0:1]),
                         [("x2all", it)], [("sq", None), ("ss", None)])
                    k.act(ss[:, 1:2], ss[:, 0:1], AF.Ln, [("ss", None)], [("ss", None)], bias=epsc[:, 0:1], scale=1.0 / D)
                    k.act(ss[:, 1:2], ss[:, 1:2], AF.Exp, [("ss", None)], [("ss", None)], scale=-0.5)
                    k.ve("dve", "tensor_scalar", [("x2all", it), ("ss", None)], [("yo", None)], out=yo[:],
                         in0=x2all[:, it, :], scalar1=ss[:, 1:2], scalar2=None, op0=ALU.mult)
                    k.ve("pool", "tensor_tensor", [("yo", None), ("gfb", None)], [("yo", None)], out=yo[:], in0=yo[:],
                         in1=gfb[:], op=ALU.mult)
                    S.dma("sp", y[it * 128:(it + 1) * 128, :], yo[:], [("yo", None)], [])
    S.finish()
    return nc


def _lay_w(w, kc):
    return np.ascontiguousarray(w.reshape(kc, 128, w.shape[1]).transpose(1, 0, 2))


def _lay_v(v):
    return np.ascontiguousarray(v.reshape(-1, 128).T)


NSA_COLS = 512 + 6 * 128 + 24


def _rwkv_consts():
    f = np.float32
    r = np.arange(128)
    t = np.arange(512)
    rst = np.broadcast_to((t % 64 != 0).astype(f), (128, 512))
    same = (r[:, None] // 64) == (r[None, :] // 64)
    mus = (same & ((r[:, None] % 64) < (r[None, :] % 64))).astype(f)
    mui = (same & ((r[:, None] % 64) <= (r[None, :] % 64))).astype(f)
    mls = (same & ((r[:, None] % 64) > (r[None, :] % 64))).astype(f)
    ma = np.concatenate([mus, mui, mus, mui], 1)
    ml4 = np.concatenate([mls] * 4, 1)
    i4 = np.concatenate([np.eye(128, dtype=f)] * 4, 1)
    return np.ascontiguousarray(np.stack([rst, ma, ml4, i4], 1)), same.astype(f)


def _rwkv_inputs(inp):
    f = np.float32
    g = lambda n: np.asarray(inp[n], f)[0]
    w_in = g("w_in")
    rm, bo = _rwkv_consts()
    lnw = g("rwkv_lnx_w").reshape(4, 2, 64)
    lnb = g("rwkv_lnx_b").reshape(4, 2, 64)
    rows_h = np.arange(128) // 64
    return {
        "w_in_r": _lay_w(w_in[:, NSA_COLS:NSA_COLS + 1792], 8),
        "g_mix": _lay_v(g("norm_mix_g")),
        "mu": _lay_v(g("rwkv_mu")),
        "rvecs": np.ascontiguousarray(np.stack([_lay_v(g("rwkv_w0")), _lay_v(g("rwkv_a0")), _lay_v(g("rwkv_k_k")),
                                                _lay_v(g("rwkv_k_a")), _lay_v(g("rwkv_r_k").reshape(-1))], 1)),
        "lora_up": np.ascontiguousarray(np.concatenate([g("rwkv_w_up"), g("rwkv_a_up")], 0)),
        "g_up": np.ascontiguousarray(g("rwkv_g_up")),
        "lnw": np.ascontiguousarray(lnw[:, rows_h, :].transpose(1, 0, 2)),
        "lnb": np.ascontiguousarray(lnb[:, rows_h, :].transpose(1, 0, 2)),
        "rmasks": rm,
        "blockones": bo,
    }


def _t5_bucket(n):
    n = np.asarray(n)
    nf = np.maximum(n, 1).astype(np.float32)
    large = 16 + (np.log(nf / np.float32(16)) / np.float32(math.log(2048 / 16)) * np.float32(16)).astype(np.int32)
    return np.where(n < 16, n, np.minimum(large, 31))


def _nsa_consts():
    f = np.float32
    def onehot(delta, valid):
        b = _t5_bucket(np.maximum(delta, 0))
        oh = np.zeros((32, delta.shape[0]), f)
        idx = np.nonzero(valid)[0]
        oh[b[idx], idx] = 1.0
        return oh
    u = np.arange(2048)
    OHa = onehot(u - 127, (u - 127) >= 0)
    u = np.arange(256)
    OHw = onehot(512 + u - 127, (u - 127) < 0)
    u = np.arange(6144)
    OHc = onehot(u - 2064, (u - 2064) >= 0)
    jj = np.arange(128)[:, None, None]
    kt = np.arange(64)[None, :, None]
    kk = np.arange(128)[None, None, :]
    Ebig = (jj == 2 * kt + kk // 64).astype(f)
    return OHa, OHw, OHc, np.ascontiguousarray(Ebig)


def _nsa_inputs(inp, j):
    f = np.float32
    g = lambda n: np.asarray(inp[n], f)[0]
    w_in = g("w_in")
    nd = (3 - j) * 128
    qperm = np.concatenate([np.r_[a * 64:(a + 1) * 64, (4 + a) * 64:(5 + a) * 64] for a in range(4)])
    cols = np.concatenate([qperm, np.arange(512, 640), np.arange(640, 768), np.arange(768, 896), np.arange(1024, 1152),
                           np.arange(896, 1024), np.arange(1152, 1280), np.arange(1280, 1304)])
    OHa, OHw, OHc, Ebig = _nsa_consts()
    def w1lay(w):
        a = w.reshape(32, 64, 256).transpose(1, 0, 2)
        return np.ascontiguousarray(np.concatenate([a, a], 0))
    def pelay(pe):
        a = pe.T
        return np.ascontiguousarray(np.concatenate([a, a], 0))
    w2k = g("cmp_k_w2").reshape(2, 128, 64).transpose(1, 0, 2)
    w2v = g("cmp_v_w2").reshape(2, 128, 64).transpose(1, 0, 2)
    tpos = np.arange(64)[None, :] * 128 + np.arange(128)[:, None]
    realtok = (tpos >= nd).astype(f)
    c = np.arange(4)[None, :] * 128 + np.arange(128)[:, None]
    realblk = ((16 * c >= nd) & (c <= 510)).astype(f)
    jb = np.arange(128)[None, None, :]
    ov = ((16 * c[:, :, None] <= 64 * jb + 63) & (16 * c[:, :, None] + 31 >= 64 * jb)).astype(f)
    ovlm = np.concatenate([realblk[:, :, None], ov * realblk[:, :, None]], 2)
    it = np.arange(16)[None, :, None]
    q = np.arange(128)[:, None, None]
    t = 128 * (4 * it + 3) + q
    cur = t // 64
    blk0 = nd // 64
    bad = (jb < blk0) | (64 * jb > t)
    forced = (jb == blk0) | (jb == cur) | (jb == cur - 1)
    addmask = np.where(bad, -1e9, np.where(forced, 1e4, 0.0)).astype(f)
    return {
        "w_in_n": _lay_w(np.ascontiguousarray(w_in[:, cols]), 8),
        "g_mix": _lay_v(g("norm_mix_g")),
        "gate_b": np.ascontiguousarray(np.broadcast_to(g("nsa_gate_b")[None, :], (128, 24))),
        "realtok": np.ascontiguousarray(realtok),
        "cmp_w1k": w1lay(g("cmp_k_w1")), "cmp_w1v": w1lay(g("cmp_v_w1")),
        "cmp_peTk": pelay(g("cmp_pe_k")), "cmp_peTv": pelay(g("cmp_pe_v")),
        "cmp_b1k": _lay_v(g("cmp_k_b1")), "cmp_b1v": _lay_v(g("cmp_v_b1")),
        "cmp_w2k": np.ascontiguousarray(np.concatenate([w2k, w2k], 2)),
        "cmp_w2v": np.ascontiguousarray(w2v),
        "ovlm": np.ascontiguousarray(ovlm.astype(f)),
        "relb_rep": np.ascontiguousarray(np.broadcast_to(np.asarray(inp["rel_bias"], f)[:, :, None], (32, 8, 128))),
        "OHa": OHa, "OHw": OHw, "OHc": OHc, "Ebig": Ebig,
        "addmask": np.ascontiguousarray(addmask),
    }


def _core_inputs(c, inp):
    b, j = c // 4, c % 4
    f = np.float32
    x = np.asarray(inp["x"], f)[b]
    nd = (3 - j) * 128
    xs = np.zeros((8192, D), f)
    xs[nd:] = x[: 8192 - nd]
    m = {
        "xs": xs,
        "ident": np.eye(128, dtype=f),
        "g_f": np.asarray(inp["norm_f_g"], f).reshape(1, D),
        "g_x": _lay_v(np.asarray(inp["norm_x_g"], f)[0]),
        "g_ffn": _lay_v(np.asarray(inp["norm_ffn_g"], f)[0]),
        "g_mem": _lay_v(np.asarray(inp["norm_mem_g"], f)[0]),
        "mem": np.ascontiguousarray(np.asarray(inp["mem"], f)[b]),
        "w_out": _lay_w(np.asarray(inp["w_out"], f)[0], 8),
        "w_q": _lay_w(np.asarray(inp["w_q_x"], f)[0], 8),
        "w_kv": _lay_w(np.asarray(inp["w_kv_x"], f)[0], 8),
        "w_o": _lay_w(np.asarray(inp["w_o_x"], f)[0], 8),
        "w_gate": _lay_w(np.asarray(inp["w_gate"], f)[0], 8),
        "w_up": _lay_w(np.asarray(inp["w_up"], f)[0], 8),
        "w_down": _lay_w(np.asarray(inp["w_down"], f)[0], 22),
    }
    return m


def kernel(**inp):
    nc = build_program(dbg={"rwkv": 1, "nsa": 1})
    rin = _rwkv_inputs(inp)
    in_maps = []
    for c in range(8):
        m = _core_inputs(c, inp)
        m.update(rin)
        m.update(_nsa_inputs(inp, c % 4))
        in_maps.append(m)
    res = run_bass_kernel_spmd(nc, in_maps, core_ids=list(range(8)))
    out = np.zeros((2, 8192, D), np.float32)
    for c in range(8):
        b, j = c // 4, c % 4
        yv = np.asarray(res.results[c]["y"]).reshape(NOWN, 128, D)
        out[b].reshape(64, 128, D)[j::4] = yv
    return out
```
